# Optimizing a Trainium2 kernel written in Bass

```python
import math
import jax, jax.numpy as jnp
from jax import lax
import numpy as np

D_MODEL = 1024
BATCH = 8
SEQ = 4096
DEPTH = 2

PLE_DIM = 256
N_EVEN = (DEPTH + 1) // 2
N_ODD = DEPTH // 2
S5_WIDTH = D_MODEL // 2
S5_GROUP = 16
S5_GROUPS = S5_WIDTH // S5_GROUP
S5_STATE = 64
SB_HEAD_DIM = 64
SB_HEADS = (D_MODEL // 2) // SB_HEAD_DIM
SB_WIDTH = SB_HEADS * SB_HEAD_DIM
MIX_WIDTH = S5_WIDTH + SB_WIDTH
IN_WIDTH = S5_WIDTH + 3 * SB_WIDTH
Q_BLOCK = 128
POOL_WINDOWS = (2, 4, 8, 16)
POOL_GROUP = D_MODEL // len(POOL_WINDOWS)
D_FF = 4 * D_MODEL
EPS = 1e-6
DT_MIN = 1e-3
DT_MAX = 1e-1

kernel_name = "hybrid_s5_stickbreak_pool_trunk"


def rms_norm(x, gain):
    xf = x.astype(jnp.float32)
    y = xf * lax.rsqrt(jnp.mean(xf * xf, axis=-1, keepdims=True) + EPS)
    return (y * gain.astype(jnp.float32)).astype(x.dtype)


def _cmul(ar, ai, br, bi):
    return ar * br - ai * bi, ar * bi + ai * br


def _s5_combine(earlier, later):
    a1r, a1i, b1r, b1i = earlier
    a2r, a2i, b2r, b2i = later
    ar, ai = _cmul(a2r, a2i, a1r, a1i)
    cr, ci = _cmul(a2r, a2i, b1r, b1i)
    return ar, ai, cr + b2r, ci + b2i


def s5_mixer(u, lam_re, lam_im, log_dt, b_re, b_im, c_re, c_im, d, w_glu):
    bsz, seqlen, _ = u.shape
    f32 = jnp.float32
    uf = u.astype(f32)
    ug = uf.reshape(bsz, seqlen, S5_GROUPS, S5_GROUP)
    lr = lam_re.astype(f32)
    li = lam_im.astype(f32)
    dt = jnp.exp(log_dt.astype(f32))[:, None]
    mag = jnp.exp(lr * dt)
    abar_r = mag * jnp.cos(li * dt)
    abar_i = mag * jnp.sin(li * dt)
    den = lr * lr + li * li
    nr = abar_r - 1.0
    ni = abar_i
    fr = (nr * lr + ni * li) / den
    fi = (ni * lr - nr * li) / den
    br = b_re.astype(f32)
    bi = b_im.astype(f32)
    bbar_r = fr[..., None] * br - fi[..., None] * bi
    bbar_i = fr[..., None] * bi + fi[..., None] * br
    bu_r = jnp.einsum('blgh,gph->blgp', ug, bbar_r)
    bu_i = jnp.einsum('blgh,gph->blgp', ug, bbar_i)
    a_r = jnp.broadcast_to(abar_r, bu_r.shape)
    a_i = jnp.broadcast_to(abar_i, bu_i.shape)
    _, _, x_r, x_i = lax.associative_scan(_s5_combine, (a_r, a_i, bu_r, bu_i), axis=1)
    y = (jnp.einsum('blgp,ghp->blgh', x_r, c_re.astype(f32))
         - jnp.einsum('blgp,ghp->blgh', x_i, c_im.astype(f32)))
    y = y.reshape(bsz, seqlen, S5_WIDTH) + d.astype(f32) * uf
    y = jax.nn.gelu(y)
    y = y * jax.nn.sigmoid(y @ w_glu.astype(f32))
    return y.astype(u.dtype)


def stick_breaking_attention(q, k, v):
    bsz, nh, seqlen, dh = q.shape
    n_blocks = seqlen // Q_BLOCK
    scale = dh ** -0.5
    qf = q.astype(jnp.float32)
    kf = k.astype(jnp.float32)
    q_blocks = qf.reshape(bsz, nh, n_blocks, Q_BLOCK, dh).transpose(2, 0, 1, 3, 4)
    key_pos = jnp.arange(seqlen)

    def one_block(args):
        qb, blk = args
        z = jnp.einsum('bhqd,bhkd->bhqk', qb, kf) * scale
        q_pos = blk * Q_BLOCK + jnp.arange(Q_BLOCK)
        causal = key_pos[None, :] < q_pos[:, None]
        log_beta = jax.nn.log_sigmoid(z)
        log_1m_beta = jnp.where(causal, jax.nn.log_sigmoid(-z), 0.0)
        tail = lax.cumsum(log_1m_beta, axis=3, reverse=True) - log_1m_beta
        w = jnp.where(causal, jnp.exp(log_beta + tail), 0.0)
        return jnp.einsum('bhqk,bhkd->bhqd', w.astype(v.dtype), v)

    out = lax.map(one_block, (q_blocks, jnp.arange(n_blocks)))
    return out.transpose(1, 2, 0, 3, 4).reshape(bsz, nh, seqlen, dh)


def causal_window_mean(x, window):
    seqlen = x.shape[1]
    cs = jnp.cumsum(x, axis=1)
    shifted = jnp.pad(cs, ((0, 0), (window, 0), (0, 0)))[:, :seqlen]
    count = jnp.minimum(jnp.arange(seqlen) + 1, window).astype(jnp.float32)
    return (cs - shifted) / count[None, :, None]


def pool_mixer(h, pool_w, pool_scale):
    bsz, seqlen, _ = h.shape
    hf = h.astype(jnp.float32)
    outs = []
    for g, window in enumerate(POOL_WINDOWS):
        xg = hf[..., g * POOL_GROUP:(g + 1) * POOL_GROUP]
        outs.append(causal_window_mean(xg, window) - xg)
    y = jnp.stack(outs, axis=2)
    y = jnp.einsum('blgc,gcd->blgd', y, pool_w.astype(jnp.float32))
    y = y.reshape(bsz, seqlen, D_MODEL) * pool_scale.astype(jnp.float32)
    return y.astype(h.dtype)


def even_mixer(h, ln, w_in, lam_re, lam_im, log_dt, b_re, b_im, c_re, c_im, d, w_glu,
               q_gain, k_gain, w_out):
    bsz, seqlen, _ = h.shape
    hn = rms_norm(h, ln)
    proj = hn @ w_in
    u = proj[..., :S5_WIDTH]
    q, k, v = jnp.split(proj[..., S5_WIDTH:], 3, axis=-1)
    to_heads = lambda t: t.reshape(bsz, seqlen, SB_HEADS, SB_HEAD_DIM).transpose(0, 2, 1, 3)
    q = rms_norm(to_heads(q), q_gain)
    k = rms_norm(to_heads(k), k_gain)
    v = to_heads(v)
    sb = stick_breaking_attention(q, k, v).transpose(0, 2, 1, 3).reshape(bsz, seqlen, SB_WIDTH)
    s5 = s5_mixer(u, lam_re, lam_im, log_dt, b_re, b_im, c_re, c_im, d, w_glu)
    mixed = jnp.concatenate([s5, sb.astype(s5.dtype)], axis=-1)
    return mixed @ w_out


def setup_inputs(seed: int = 0) -> dict:
    key = jax.random.key(seed)
    ks = jax.random.split(key, 32)
    f32 = jnp.float32
    nrm = lambda k, shape, scale: scale * jax.random.normal(k, shape, f32)
    gain = lambda k, shape: 1.0 + 0.01 * jax.random.normal(k, shape, f32)
    n = jnp.arange(S5_STATE, dtype=f32)
    return {
        'x': nrm(ks[0], (BATCH, SEQ, D_MODEL), 1.0),
        'p': nrm(ks[1], (DEPTH, BATCH, SEQ, PLE_DIM), 1.0),
        'ln_mix_even': gain(ks[2], (N_EVEN, D_MODEL)),
        'w_in_even': nrm(ks[3], (N_EVEN, D_MODEL, IN_WIDTH), D_MODEL ** -0.5),
        's5_lambda_re': -0.5 + nrm(ks[4], (N_EVEN, S5_GROUPS, S5_STATE), 0.01),
        's5_lambda_im': jnp.pi * n + nrm(ks[5], (N_EVEN, S5_GROUPS, S5_STATE), 0.01),
        's5_log_dt': jax.random.uniform(ks[6], (N_EVEN, S5_GROUPS), f32,
                                        math.log(DT_MIN), math.log(DT_MAX)),
        's5_b_re': nrm(ks[7], (N_EVEN, S5_GROUPS, S5_STATE, S5_GROUP), (2.0 * S5_GROUP) ** -0.5),
        's5_b_im': nrm(ks[8], (N_EVEN, S5_GROUPS, S5_STATE, S5_GROUP), (2.0 * S5_GROUP) ** -0.5),
        's5_c_re': nrm(ks[9], (N_EVEN, S5_GROUPS, S5_GROUP, S5_STATE), S5_STATE ** -0.5),
        's5_c_im': nrm(ks[10], (N_EVEN, S5_GROUPS, S5_GROUP, S5_STATE), S5_STATE ** -0.5),
        's5_d': nrm(ks[11], (N_EVEN, S5_WIDTH), 1.0),
        's5_w_glu': nrm(ks[12], (N_EVEN, S5_WIDTH, S5_WIDTH), S5_WIDTH ** -0.5),
        'sb_q_gain': gain(ks[13], (N_EVEN, SB_HEAD_DIM)),
        'sb_k_gain': gain(ks[14], (N_EVEN, SB_HEAD_DIM)),
        'w_out_even': nrm(ks[15], (N_EVEN, MIX_WIDTH, D_MODEL), MIX_WIDTH ** -0.5),
        'ln_mix_odd': gain(ks[16], (N_ODD, D_MODEL)),
        'pool_w': nrm(ks[17], (N_ODD, len(POOL_WINDOWS), POOL_GROUP, POOL_GROUP), POOL_GROUP ** -0.5),
        'pool_scale': gain(ks[18], (N_ODD, D_MODEL)),
        'ln_mlp': gain(ks[19], (DEPTH, D_MODEL)),
        'w_mlp_up': nrm(ks[20], (DEPTH, D_MODEL, D_FF), D_MODEL ** -0.5),
        'w_mlp_down': nrm(ks[21], (DEPTH, D_FF, D_MODEL), 0.5 * D_FF ** -0.5),
        'ln_ple': gain(ks[22], (DEPTH, D_MODEL)),
        'w_ple_gate': nrm(ks[23], (DEPTH, D_MODEL, D_MODEL), D_MODEL ** -0.5),
        'w_ple_up': nrm(ks[24], (DEPTH, PLE_DIM, D_MODEL), PLE_DIM ** -0.5),
    }


def reference(x, p, ln_mix_even, w_in_even, s5_lambda_re, s5_lambda_im, s5_log_dt,
              s5_b_re, s5_b_im, s5_c_re, s5_c_im, s5_d, s5_w_glu, sb_q_gain, sb_k_gain,
              w_out_even, ln_mix_odd, pool_w, pool_scale, ln_mlp, w_mlp_up, w_mlp_down,
              ln_ple, w_ple_gate, w_ple_up):
    h = x
    for i in range(DEPTH):
        j = i // 2
        if i % 2 == 0:
            h = h + even_mixer(h, ln_mix_even[j], w_in_even[j], s5_lambda_re[j], s5_lambda_im[j],
                               s5_log_dt[j], s5_b_re[j], s5_b_im[j], s5_c_re[j], s5_c_im[j],
                               s5_d[j], s5_w_glu[j], sb_q_gain[j], sb_k_gain[j], w_out_even[j])
        else:
            h = h + pool_mixer(rms_norm(h, ln_mix_odd[j]), pool_w[j], pool_scale[j])
        hn = rms_norm(h, ln_mlp[i])
        h = h + jnp.square(jax.nn.relu(hn @ w_mlp_up[i])) @ w_mlp_down[i]
        gate = jax.nn.sigmoid(rms_norm(h, ln_ple[i]) @ w_ple_gate[i])
        h = h + (p[i] @ w_ple_up[i]) * gate
    return h
```

```python
import math
from contextlib import ExitStack

import numpy as np
import concourse.bass as bass
import concourse.mybir as mybir
from concourse.bass_utils import run_bass_kernel_spmd

F32 = mybir.dt.float32
BF16 = mybir.dt.bfloat16
I32 = mybir.dt.int32
AF = mybir.ActivationFunctionType
ALU = mybir.AluOpType

TT = 512
EPS = 1e-6
KLIST = list(range(-7, 9))
TWO_PI = 2.0 * math.pi
import os
MLP_PIPE = bool(int(os.environ.get("K_MLPPIPE", "1")))
REC_ENG = os.environ.get("K_REC", "pool")
E_ON_POOL = bool(int(os.environ.get("K_EPOOL", "0")))
SAME_ENGINE_SYNC = set(os.environ.get("K_SES", "").split(","))


class Prog:
    def __init__(self, nc, stack):
        self.nc = nc
        self.stack = stack
        self.ops = {e: [] for e in ("pe", "act", "dve", "pool", "sp")}
        self.sems = {}
        self.cnt = {}
        self.keys = {}
        self.waited = {e: {} for e in self.ops}
        self.final = []
        self.stage = "setup"
        self.scopes = bool(int(os.environ.get("K_SCOPES", "0")))
        self.ses = set(SAME_ENGINE_SYNC)
        self.near = {"dve": int(os.environ.get("K_NEAR_DVE", "2")), "act": int(os.environ.get("K_NEAR_ACT", "1")), "pool": 2}

    def sem(self, name):
        if name not in self.sems:
            self.sems[name] = self.stack.enter_context(self.nc.semaphore(name))
            self.cnt[name] = 0
        return self.sems[name]

    def _resolve(self, eng, reads, writes, mysig):
        waits = {}

        def add(sig):
            if sig is None:
                return
            s, v = sig
            if s == eng:
                if eng == "pe":
                    return
                if eng not in self.ses and (self.cnt[eng] - v) > self.near.get(eng, 0):
                    return
            if waits.get(s, 0) < v:
                waits[s] = v

        for k in reads:
            st = self.keys.setdefault(k, [None, []])
            add(st[0])
        for k in writes:
            st = self.keys.setdefault(k, [None, []])
            add(st[0])
            for r in st[1]:
                add(r)
        out = []
        for s, v in waits.items():
            if self.waited[eng].get(s, 0) >= v:
                continue
            self.waited[eng][s] = v
            out.append((s, v))
        for k in reads:
            self.keys[k][1].append(mysig)
        for k in writes:
            self.keys[k][0] = mysig
            self.keys[k][1] = []
        return out

    def op(self, eng, fn, reads=(), writes=()):
        self.sem(eng)
        self.cnt[eng] += 1
        mysig = (eng, self.cnt[eng])
        waits = self._resolve(eng, reads, writes, mysig)
        self.ops[eng].append((waits, fn, (eng, 1), self.stage))

    def dma(self, q, semkey, fn, reads=(), writes=()):
        name = "d_" + semkey
        self.sem(name)
        self.cnt[name] += 16
        mysig = (name, self.cnt[name])
        waits = self._resolve(q, reads, writes, mysig)
        self.ops[q].append((waits, fn, (name, 16), "dma"))
        return mysig

    def finish(self, eng, sigs):
        self.final.append((eng, sigs))

    def emit(self):
        nc = self.nc
        engmap = {"pe": "tensor", "act": "scalar", "dve": "vector", "pool": "gpsimd", "sp": "sync"}
        block = self.stack.enter_context(nc.Block())
        for e, attr in engmap.items():
            ops = self.ops[e]
            finals = [s for (fe, s) in self.final if fe == e]
            if not ops and not finals:
                continue

            def body(engine, ops=ops, finals=finals):
                for waits, fn, (sname, inc), stage in ops:
                    if self.scopes:
                        with nc.named_scope(stage):
                            for s, v in waits:
                                engine.wait_ge(self.sems[s], v)
                            inst = fn(engine)
                            inst.then_inc(self.sems[sname], inc)
                    else:
                        for s, v in waits:
                            engine.wait_ge(self.sems[s], v)
                        inst = fn(engine)
                        inst.then_inc(self.sems[sname], inc)
                for sigs in finals:
                    for s, v in sigs:
                        engine.wait_ge(self.sems[s], v)

            getattr(block, attr)(body)


KG = 512


class Region:
    def __init__(self, arena, name, woff, nbytes):
        self.arena, self.name, self.woff, self.nbytes = arena, name, woff, nbytes

    def buf(self, boff, shape, dt):
        return Buf(self, boff, shape, dt)


class Buf:
    def __init__(self, region, boff, shape, dt):
        esz = 4 if dt in (F32, I32) else 2
        n = int(np.prod(shape))
        assert boff % 4 == 0 and boff + n * esz <= region.nbytes, (region.name, boff, shape, region.nbytes)
        w0 = region.woff + boff // 4
        nw = (n * esz + 3) // 4
        ap = region.arena[:, w0:w0 + nw]
        if dt != F32:
            ap = ap.bitcast(dt)
        if len(shape) > 1:
            names = " ".join("a%d" % i for i in range(len(shape)))
            kw = {"a%d" % i: shape[i] for i in range(1, len(shape))}
            ap = ap.rearrange("p (%s) -> p %s" % (names, names), **kw)
        self.ap, self.region, self.boff, self.shape, self.esz = ap, region, boff, tuple(shape), esz
        self.nbytes = n * esz
        self.blk = (n // shape[0]) * esz

    def keys(self, lo=None, hi=None):
        if lo is None:
            b0, b1 = self.boff, self.boff + self.nbytes
        else:
            hi = lo + 1 if hi is None else hi
            b0, b1 = self.boff + lo * self.blk, self.boff + hi * self.blk
        return [(self.region.name, k) for k in range(b0 // KG, (b1 + KG - 1) // KG)]


def _consts():
    c = {}
    c["ident"] = np.eye(128, dtype=np.float32)
    jj = np.arange(128)[:, None]
    tt = np.arange(128)[None, :]
    ones = np.ones((128, 128), np.float32)
    blockones = ((jj // 64) == (tt // 64)).astype(np.float32)
    negtri = -(jj >= tt).astype(np.float32)
    masklt = (jj < tt).astype(np.float32)
    c["cstb"] = np.concatenate([ones, blockones, negtri, masklt], axis=1)
    part = np.arange(128)
    g8 = part // 16
    bd = (g8[:, None, None] == np.arange(8)[None, :, None]) * np.ones((1, 1, 16))
    g2 = (part // 16) % 2
    g2col = (g2[:, None] == np.arange(2)[None, :]).astype(np.float32)
    pp3 = (part >= 96).astype(np.float32)[:, None]
    halfA = (part < 64).astype(np.float32)[:, None]
    halfB = (part >= 64).astype(np.float32)[:, None]
    kv = np.tile(np.array(KLIST, np.float32)[None, :], (128, 1))
    kf = kv / TWO_PI
    kvA = np.tile(np.arange(8, dtype=np.float32)[None, :], (128, 1))
    kfA = kvA / TWO_PI
    cnt = np.zeros((128, 4, 16), np.float32)
    for g, w in enumerate((2, 4, 8, 16)):
        cnt[:, g, :] = 1.0 / np.minimum(np.arange(16) + 1, w)
    c["cstf"] = np.concatenate([bd.reshape(128, 128).astype(np.float32), g2col, pp3, halfA, halfB,
                                kv, kf, kvA, kfA, cnt.reshape(128, 64)], axis=1).astype(np.float32)
    return c


CF = {}
_o = 0
for _n, _w in (("bd", 128), ("g2col", 2), ("pp3", 1), ("halfA", 1), ("halfB", 1), ("kv", 16), ("kf", 16),
               ("kvA", 8), ("kfA", 8), ("cnt", 64)):
    CF[_n] = (_o, _w)
    _o += _w
NCF = _o

VEC = {}
_o = 0
for _n, _w in (("ln_mix0", 8), ("ln_mlp0", 8), ("ln_ple0", 8), ("ln_mix1", 8), ("ln_mlp1", 8), ("ln_ple1", 8),
               ("pscale", 8), ("s5d", 4), ("qgA", 1), ("qgB", 1), ("kgA", 1)):
    VEC[_n] = (_o, _w)
    _o += _w
NVEC = _o


def build(NT, dbg=None):
    L = NT * TT
    nc = bass.Bass("TRN2", target_bir_lowering=False)
    D = lambda n, s: nc.dram_tensor(n, s, F32, kind="ExternalInput").ap()
    x_d = D("x", [L, 1024])
    p_d = D("p", [2, L, 256])
    w_in = D("w_in", [1024, 2048])
    w_out = D("w_out", [1024, 1024])
    w_glu = D("w_glu", [512, 512])
    w_up = D("w_up", [2, 1024, 4096])
    w_dn = D("w_dn", [2, 4096, 1024])
    w_gt = D("w_gt", [2, 1024, 1024])
    w_pu = D("w_pu", [2, 256, 1024])
    w_pool = D("w_pool", [4, 256, 256])
    vecs_d = D("vecs", [128, NVEC])
    s5b_d = D("s5B", [128, 48 + 4 * 256])
    s5a_d = D("s5A", [128, 5 * 256])
    ident_d = D("ident", [128, 128])
    cstb_d = D("cstb", [128, 512])
    cstf_d = D("cstf", [128, NCF])
    y_d = nc.dram_tensor("y", [L, 1024], F32, kind="ExternalOutput").ap()
    dbg_d = None
    if dbg:
        dbg_d = nc.dram_tensor("dbg", [128, 8, 512], F32, kind="ExternalOutput").ap()

    with ExitStack() as st:
        P = Prog(nc, st)
        sizes = [("KV", 65536), ("VS", 16384), ("YW", 16384), ("KD", 8192), ("WS", 3 * 4096), ("HT", 16384),
                 ("CST", 6144), ("STG", 8192), ("HN", 8192), ("SQ", 8192), ("AT", 8192), ("UB", 8192),
                 ("SCR", 8192), ("S5B", 12800), ("RS", 4096)]
        total_w = sum(s for _, s in sizes) // 4
        arena = st.enter_context(nc.sbuf_tensor("arena", [128, total_w], F32))
        R = {}
        wo = 0
        for n, s in sizes:
            R[n] = Region(arena, n, wo, s)
            wo += s // 4
        PB = [st.enter_context(nc.psum_tensor("pb%d" % i, [128, 512], F32)) for i in range(8)]
        PK = [[("pb%d" % i, 0)] for i in range(8)]

        KT = R["KV"].buf(0, [4, 4096], BF16)
        VC = R["KV"].buf(32768, [32, 512], BF16)
        VS = R["VS"].buf(0, [4, 2, 8, 128], BF16)
        YW = R["YW"].buf(0, [16, 8, 2, 32], BF16)
        KD = R["KD"].buf(0, [4, 8, 128], BF16)
        WS = [R["WS"].buf(i * 4096, [8, 256], BF16) for i in range(3)]
        HT = R["HT"].buf(0, [8, 512], F32)
        ident = R["CST"].buf(0, [128], F32)
        vecs = R["CST"].buf(512, [NVEC], F32)
        cstb = R["CST"].buf(1024, [4, 128], BF16)
        cstf = R["CST"].buf(2048, [NCF], F32)
        assert NCF * 4 <= 1024
        Acf = R["CST"].buf(3072, [2, 16], F32)
        Xc = R["CST"].buf(3584, [2, 16], F32)
        LB = R["CST"].buf(4096, [8, 16], F32)
        CE = R["CST"].buf(4608, [2, 2, 4], F32)
        RT = R["CST"].buf(5120, [4, 16], F32)
        EINV = R["CST"].buf(5632, [4], F32)

        def vcol(name, i=0):
            o, w = VEC[name]
            return vecs.ap[:, o + i:o + i + 1]

        def cf(name):
            o, w = CF[name]
            return cstf.ap[:, o:o + w]

        ones_b = cstb.ap[:, 0, :]
        blockones_b = cstb.ap[:, 1, :]
        negtri_b = cstb.ap[:, 2, :]
        masklt_b = cstb.ap[:, 3, :]
        CK = cstb.keys() + cstf.keys() + vecs.keys() + ident.keys()

        def dve_tt(out, in0, in1, op, r, w, eng="dve"):
            P.op(eng, lambda e: e.tensor_tensor(out=out, in0=in0, in1=in1, op=op), reads=r, writes=w)

        def dve_ts(out, in0, s1, op0, r, w, s2=None, op1=None, eng="dve"):
            if op1 is None:
                P.op(eng, lambda e: e.tensor_scalar(out=out, in0=in0, scalar1=s1, scalar2=None, op0=op0), reads=r, writes=w)
            else:
                P.op(eng, lambda e: e.tensor_scalar(out=out, in0=in0, scalar1=s1, scalar2=s2, op0=op0, op1=op1), reads=r, writes=w)

        def dve_stt(out, in0, scalar, in1, op0, op1, r, w):
            P.op("dve", lambda e: e.scalar_tensor_tensor(out=out, in0=in0, scalar=scalar, in1=in1, op0=op0, op1=op1), reads=r, writes=w)

        def act(out, in_, func, r, w, scale=1.0, bias=0.0):
            if func == AF.Copy:
                P.op("act", lambda e: e.activation(out=out, in_=in_, func=func, scale=scale), reads=r, writes=w)
            else:
                P.op("act", lambda e: e.activation(out=out, in_=in_, func=func, scale=scale, bias=bias), reads=r, writes=w)

        def cp(out, in_, r, w, eng="dve"):
            P.op(eng, lambda e: e.tensor_copy(out=out, in_=in_), reads=r, writes=w)

        P.dma("sp", "c0", lambda e: e.dma_start(out=ident.ap, in_=ident_d[:, :]), writes=ident.keys())
        P.dma("sp", "c1", lambda e: e.dma_start(out=vecs.ap, in_=vecs_d[:, :]), writes=vecs.keys())
        P.dma("sp", "c2", lambda e: e.dma_start(out=cstf.ap, in_=cstf_d[:, :]), writes=cstf.keys())
        P.dma("pool", "c3", lambda e: e.dma_start(out=cstb.ap.rearrange("p a b -> p (a b)"), in_=cstb_d[:, :]), writes=cstb.keys())

        P.op("dve", lambda e: e.memset(EINV.ap, math.exp(-1.0)), writes=EINV.keys())
        o_q = VEC["qgA"][0]
        dve_ts(vecs.ap[:, o_q:o_q + 2], vecs.ap[:, o_q:o_q + 2], 0.125, ALU.mult, vecs.keys(), vecs.keys())

        KVr = R["KV"]

        def s5_setup():
            o = [0]

            def tmp(shape, dt=F32):
                b = KVr.buf(o[0], shape, dt)
                o[0] += ((b.nbytes + 511) // 512) * 512
                return b

            inB = tmp([48 + 1024])
            P.dma("sp", "s5in", lambda e: e.dma_start(out=inB.ap, in_=s5b_d[:, :]), writes=inB.keys())
            lamr = inB.ap[:, 0:16]
            lami = inB.ap[:, 16:32]
            logdt = inB.ap[:, 32:48]
            bre = inB.ap[:, 48:304].rearrange("p (q h) -> p q h", q=16)
            bim = inB.ap[:, 304:560].rearrange("p (q h) -> p q h", q=16)
            cre = inB.ap[:, 560:816].rearrange("p (q h) -> p q h", q=16)
            cim = inB.ap[:, 816:1072].rearrange("p (q h) -> p q h", q=16)
            kin = inB.keys()
            sm = tmp([8, 16])
            smk = sm.keys()
            dt_, lrdt, lidt, den, nr, fr, fi, t0 = [sm.ap[:, i, :] for i in range(8)]
            act(dt_, logdt, AF.Exp, kin, smk)
            dve_tt(lrdt, lamr, dt_, ALU.mult, kin + smk, smk)
            dve_tt(lidt, lami, dt_, ALU.mult, kin + smk, smk)
            NK = len(KLIST)
            big = [tmp([NK, 16]) for _ in range(6)]
            mag, Tt, t1, t2, cosv, sinv = big
            allk = sum([b.keys() for b in big], [])
            kv = cf("kv")
            kf = cf("kf")
            bc_q = lambda a: a.unsqueeze(1).to_broadcast([128, NK, 16])
            bc_k = lambda a: a.unsqueeze(2).to_broadcast([128, NK, 16])
            dve_tt(mag.ap, bc_q(lrdt), bc_k(kv), ALU.mult, smk + CK, allk)
            act(mag.ap, mag.ap, AF.Exp, allk, allk)
            dve_tt(Tt.ap, bc_q(lidt), bc_k(kf), ALU.mult, smk + CK, allk)

            def sincos(dst, shift, Tsrc, a1, a2, keys):
                dve_ts(a1.ap, Tsrc.ap, shift, ALU.add, keys, keys)
                cp(a2.ap.bitcast(I32), a1.ap, keys, keys)
                cp(a2.ap, a2.ap.bitcast(I32), keys, keys)
                dve_tt(a1.ap, a1.ap, a2.ap, ALU.subtract, keys, keys)
                dve_stt(a1.ap, a1.ap, 0.0, a1.ap, ALU.is_lt, ALU.add, keys, keys)
                act(dst.ap, a1.ap, AF.Sin, keys, keys, scale=TWO_PI * (1 - 1e-6), bias=-math.pi * (1 - 1e-6))

            sincos(cosv, 0.75 + 32.0, Tt, t1, t2, allk)
            sincos(sinv, 0.5 + 32.0, Tt, t1, t2, allk)
            dve_tt(cosv.ap, cosv.ap, mag.ap, ALU.mult, allk, allk)
            dve_tt(sinv.ap, sinv.ap, mag.ap, ALU.mult, allk, allk)
            Er = lambda k: cosv.ap[:, k + 7, :]
            Ei = lambda k: sinv.ap[:, k + 7, :]
            cp(Acf.ap[:, 0, :], Er(8), allk, Acf.keys())
            cp(Acf.ap[:, 1, :], Ei(8), allk, Acf.keys())
            dve_tt(den, lamr, lamr, ALU.mult, kin, smk)
            dve_tt(t0, lami, lami, ALU.mult, kin, smk)
            dve_tt(den, den, t0, ALU.add, smk, smk)
            P.op("dve", lambda e: e.reciprocal(out=den, in_=den), reads=smk, writes=smk)
            dve_ts(nr, Er(1), -1.0, ALU.add, allk, smk)
            dve_tt(fr, nr, lamr, ALU.mult, smk + kin, smk)
            dve_tt(t0, Ei(1), lami, ALU.mult, allk + kin, smk)
            dve_tt(fr, fr, t0, ALU.add, smk, smk)
            dve_tt(fr, fr, den, ALU.mult, smk, smk)
            dve_tt(fi, Ei(1), lamr, ALU.mult, allk + kin, smk)
            dve_tt(t0, nr, lami, ALU.mult, smk + kin, smk)
            dve_tt(fi, fi, t0, ALU.subtract, smk, smk)
            dve_tt(fi, fi, den, ALU.mult, smk, smk)
            bb = tmp([4, 16, 16])
            bbk = bb.keys()
            Bbr, Bbi, tA, tB = [bb.ap[:, i] for i in range(4)]
            bch = lambda a: a.unsqueeze(2).to_broadcast([128, 16, 16])
            dve_tt(Bbr, bre, bch(fr), ALU.mult, kin + smk, bbk)
            dve_tt(tA, bim, bch(fi), ALU.mult, kin + smk, bbk)
            dve_tt(Bbr, Bbr, tA, ALU.subtract, bbk, bbk)
            dve_tt(Bbi, bim, bch(fr), ALU.mult, kin + smk, bbk)
            dve_tt(tA, bre, bch(fi), ALU.mult, kin + smk, bbk)
            dve_tt(Bbi, Bbi, tA, ALU.add, bbk, bbk)
            Rr = tmp([9, 16, 16])
            Ri = tmp([9, 16, 16])
            Rt = tmp([9, 16, 16])
            rk = Rr.keys() + Ri.keys() + Rt.keys()
            bcC = lambda a: a.unsqueeze(1).to_broadcast([128, 9, 16, 16])
            bcE = lambda a: a.unsqueeze(3).to_broadcast([128, 9, 16, 16])
            Er9 = cosv.ap[:, 7:16, :]
            Ei9 = sinv.ap[:, 7:16, :]
            dve_tt(Rr.ap, bcC(cre), bcE(Er9), ALU.mult, kin + allk, rk)
            dve_tt(Rt.ap, bcC(cim), bcE(Ei9), ALU.mult, kin + allk, rk)
            dve_tt(Rr.ap, Rr.ap, Rt.ap, ALU.subtract, rk, rk)
            dve_tt(Ri.ap, bcC(cre), bcE(Ei9), ALU.mult, kin + allk, rk)
            dve_tt(Rt.ap, bcC(cim), bcE(Er9), ALU.mult, kin + allk, rk)
            dve_tt(Ri.ap, Ri.ap, Rt.ap, ALU.add, rk, rk)
            Bz = tmp([2, 16, 128])
            bzk = Bz.keys()
            P.op("dve", lambda e: e.memset(Bz.ap, 0.0), writes=bzk)
            for half in range(2):
                ps_ = slice(half * 64, half * 64 + 64)
                for pp in range(4):
                    cs_ = slice(pp * 32 + half * 16, pp * 32 + half * 16 + 16)
                    cp(Bz.ap[ps_, 0, pp:16:4, cs_], Bbr[ps_, pp:16:4, :], bbk, bzk)
                    dve_ts(Bz.ap[ps_, 1, pp:16:4, cs_], Bbi[ps_, pp:16:4, :], -1.0, ALU.mult, bbk, bzk)
            bd = cf("bd").rearrange("p (a b) -> p a b", a=8)
            for ct in range(4):
                bank = PB[ct]

                def mm(e, ct=ct, bank=bank):
                    last = None
                    for pp in range(4):
                        q = 4 * ct + pp
                        e.matmul(out=bank[:, 0:128], lhsT=Bz.ap[:, 0, q, :], rhs=Rr.ap[:, 0:8, q, :],
                                 start=(pp == 0), stop=False)
                        last = e.matmul(out=bank[:, 0:128], lhsT=Bz.ap[:, 1, q, :], rhs=Ri.ap[:, 0:8, q, :],
                                        start=False, stop=(pp == 3))
                    return last

                P.op("pe", mm, reads=bzk + rk, writes=PK[ct])
                src = bank[:, 0:128].rearrange("p (t h) -> p t h", t=8).unsqueeze(2).to_broadcast([128, 8, 8, 16])
                msk = bd.unsqueeze(1).to_broadcast([128, 8, 8, 16])
                dst = KD.ap[:, ct].rearrange("p t (g h) -> p t g h", g=8)
                dve_tt(dst, src, msk, ALU.mult, PK[ct] + CK, KD.keys())
            P.op("dve", lambda e: e.memset(YW.ap, 0.0), writes=YW.keys())
            for half in range(2):
                ps_ = slice(half * 64, half * 64 + 64)
                cs_ = slice(half * 16, half * 16 + 16)
                srcr = Rr.ap[ps_, 1:9].rearrange("p k q h -> p q k h")
                srci = Ri.ap[ps_, 1:9].rearrange("p k q h -> p q k h")
                cp(YW.ap[ps_, :, :, 0, cs_], srcr, rk, YW.keys())
                dve_ts(YW.ap[ps_, :, :, 1, cs_], srci, -1.0, ALU.mult, rk, YW.keys())
            o[0] = 0
            inA = tmp([5, 256])
            P.dma("sp", "s5in", lambda e: e.dma_start(out=inA.ap.rearrange("p a b -> p (a b)"), in_=s5a_d[:, :]), writes=inA.keys())
            kia = inA.keys()
            lamrA, lamiA, logdtA, breA, bimA = [inA.ap[:, i, :] for i in range(5)]
            smA = tmp([8, 256])
            sak = smA.keys()
            dtA, lrdtA, lidtA, denA, nrA, frA, fiA, t0A = [smA.ap[:, i, :] for i in range(8)]
            act(dtA, logdtA, AF.Exp, kia, sak)
            dve_tt(lrdtA, lamrA, dtA, ALU.mult, kia + sak, sak)
            dve_tt(lidtA, lamiA, dtA, ALU.mult, kia + sak, sak)
            bigA = [tmp([8, 256]) for _ in range(6)]
            magA, TA, t1A, t2A, cosA, sinA = bigA
            ak = sum([b.keys() for b in bigA], [])
            bq = lambda a: a.unsqueeze(1).to_broadcast([128, 8, 256])
            bk = lambda a: a.unsqueeze(2).to_broadcast([128, 8, 256])
            dve_tt(magA.ap, bq(lrdtA), bk(cf("kvA")), ALU.mult, sak + CK, ak)
            act(magA.ap, magA.ap, AF.Exp, ak, ak)
            dve_tt(TA.ap, bq(lidtA), bk(cf("kfA")), ALU.mult, sak + CK, ak)
            sincos(cosA, 0.75 + 32.0, TA, t1A, t2A, ak)
            sincos(sinA, 0.5 + 32.0, TA, t1A, t2A, ak)
            dve_tt(cosA.ap, cosA.ap, magA.ap, ALU.mult, ak, ak)
            dve_tt(sinA.ap, sinA.ap, magA.ap, ALU.mult, ak, ak)
            dve_tt(denA, lamrA, lamrA, ALU.mult, kia, sak)
            dve_tt(t0A, lamiA, lamiA, ALU.mult, kia, sak)
            dve_tt(denA, denA, t0A, ALU.add, sak, sak)
            P.op("dve", lambda e: e.reciprocal(out=denA, in_=denA), reads=sak, writes=sak)
            dve_ts(nrA, cosA.ap[:, 1, :], -1.0, ALU.add, ak, sak)
            dve_tt(frA, nrA, lamrA, ALU.mult, sak + kia, sak)
            dve_tt(t0A, sinA.ap[:, 1, :], lamiA, ALU.mult, ak + kia, sak)
            dve_tt(frA, frA, t0A, ALU.add, sak, sak)
            dve_tt(frA, frA, denA, ALU.mult, sak, sak)
            dve_tt(fiA, sinA.ap[:, 1, :], lamrA, ALU.mult, ak + kia, sak)
            dve_tt(t0A, nrA, lamiA, ALU.mult, sak + kia, sak)
            dve_tt(fiA, fiA, t0A, ALU.subtract, sak, sak)
            dve_tt(fiA, fiA, denA, ALU.mult, sak, sak)
            BbrA, BbiA = dtA, denA
            dve_tt(t0A, bimA, fiA, ALU.mult, kia + sak, sak)
            dve_tt(BbrA, breA, frA, ALU.mult, kia + sak, sak)
            dve_tt(BbrA, BbrA, t0A, ALU.subtract, sak, sak)
            dve_tt(t0A, breA, fiA, ALU.mult, kia + sak, sak)
            dve_tt(BbiA, bimA, frA, ALU.mult, kia + sak, sak)
            dve_tt(BbiA, BbiA, t0A, ALU.add, sak, sak)
            Wr, Wi, tW = magA, TA, t1A
            dve_tt(Wr.ap, bq(BbrA), cosA.ap, ALU.mult, sak + ak, ak)
            dve_tt(tW.ap, bq(BbiA), sinA.ap, ALU.mult, sak + ak, ak)
            dve_tt(Wr.ap, Wr.ap, tW.ap, ALU.subtract, ak, ak)
            dve_tt(Wi.ap, bq(BbrA), sinA.ap, ALU.mult, sak + ak, ak)
            dve_tt(tW.ap, bq(BbiA), cosA.ap, ALU.mult, sak + ak, ak)
            dve_tt(Wi.ap, Wi.ap, tW.ap, ALU.add, ak, ak)
            g2c = cf("g2col")
            for ri, Wx in enumerate((Wr, Wi)):
                src = Wx.ap.rearrange("p k (c s) -> p c k s", c=4)
                for g2p in range(2):
                    dst = VS.ap[:, :, ri, :, g2p * 64:(g2p + 1) * 64]
                    dve_ts(dst, src, g2c[:, g2p:g2p + 1], ALU.mult, ak + CK, VS.keys())
            P.op("dve", lambda e: e.memset(Xc.ap, 0.0), writes=Xc.keys())
            allkv = [("KV", k) for k in range(65536 // KG)]
            P.op("dve", lambda e: e.memset(LB.ap, 0.0), reads=allkv,
                 writes=LB.keys() + [("KTc", i) for i in range(NT)] + [("VCc", i) for i in range(NT)])

        P.ses = set(os.environ.get("K_SES_SETUP", "").split(","))
        s5_setup()
        P.ses = set(SAME_ENGINE_SYNC)

        WSPEC = {"w_in": w_in, "w_glu": w_glu, "w_out": w_out}
        for l_ in range(2):
            WSPEC["w_up%d" % l_] = w_up[l_]
            WSPEC["w_dn%d" % l_] = w_dn[l_]
            WSPEC["w_gt%d" % l_] = w_gt[l_]
            WSPEC["w_pu%d" % l_] = w_pu[l_]
        SCRT = {}
        WNKT = {}
        for name, W in WSPEC.items():
            K_, N_ = W.shape
            nkt = min(8, K_ // 128)
            WNKT[name] = nkt
            SCRT[name] = nc.dram_tensor("s_" + name, [N_ // 256, K_ // (128 * nkt), 128, nkt * 256], BF16, kind="Internal").ap()
        SCRT["pool"] = nc.dram_tensor("s_pool", [4, 1, 128, 512], BF16, kind="Internal").ap()
        WNKT["pool"] = 2

        CONV = []

        def convert(name, chunks):
            for (c, kc) in chunks:
                CONV.append((name, c, kc))

        convert("w_in", [(c, 0) for c in range(8)])
        convert("w_glu", [(c, 0) for c in range(2)])
        convert("w_out", [(c, 0) for c in range(4)])
        for l_ in range(2):
            if l_ == 1:
                convert("pool", [(g_, 0) for g_ in range(4)])
            for qq in range(4):
                convert("w_up%d" % l_, [(qq * 4 + c, 0) for c in range(4)])
                convert("w_dn%d" % l_, [(c, qq) for c in range(4)])
            for half in range(2):
                convert("w_gt%d" % l_, [(half * 2 + c, 0) for c in range(2)])
                convert("w_pu%d" % l_, [(half * 2 + c, 0) for c in range(2)])
        CONV_IDX = {k_: i_ for i_, k_ in enumerate(CONV)}
        cvp = [0]
        NCV = 8
        LOOKAHEAD = 6

        def conv_upto(idx):
            while cvp[0] <= min(idx, len(CONV) - 1):
                i_ = cvp[0]
                cvp[0] += 1
                name, c, kc = CONV[i_]
                nkt = WNKT[name]
                if name == "pool":
                    src = w_pool[c].rearrange("(k p) n -> p k n", p=128)
                else:
                    W = WSPEC[name]
                    src = W[kc * nkt * 128:(kc + 1) * nkt * 128, c * 256:(c + 1) * 256].rearrange("(k p) n -> p k n", p=128)
                dst = SCRT[name][c, kc].rearrange("p (k n) -> p k n", k=nkt)
                P.dma("pool", "cv%d" % (i_ % NCV), lambda e, src=src, dst=dst: e.dma_start(out=dst, in_=src),
                      writes=[("scr", name, c, kc), ("cvring", i_ % NCV)])

        wctr = [0]
        NSLOT = len(WS)

        def load_w(name, c, kc=0):
            s = wctr[0] % NSLOT
            wctr[0] += 1
            slot = WS[s]
            nkt = WNKT[name]
            conv_upto(CONV_IDX[(name, c, kc)] + LOOKAHEAD)
            src = SCRT[name][c, kc].rearrange("p (k n) -> p k n", k=nkt)
            P.dma("pool", "ws%d" % s, lambda e: e.dma_start(out=slot.ap[:, 0:nkt, :], in_=src),
                  reads=[("scr", name, c, kc)], writes=slot.keys())
            return slot

        pctr = [0]

        def banks(n):
            b = [(pctr[0] + i) % 8 for i in range(n)]
            pctr[0] = (pctr[0] + n) % 8
            return b

        def gemm_gen(name, c0, nchunks, kc, rhs_fn, rkeys_fn, evac):
            nkt = WNKT[name]
            for c in range(nchunks):
                slot = load_w(name, c0 + c, kc)
                bs = banks(2)
                for m in range(2):
                    b = bs[m]

                    def mm(e, m=m, b=b, slot=slot):
                        last = None
                        for kt in range(nkt):
                            last = e.matmul(out=PB[b][:, :], lhsT=slot.ap[:, kt, m * 128:(m + 1) * 128], rhs=rhs_fn(kt),
                                            start=(kt == 0), stop=(kt == nkt - 1))
                        return last

                    rk = sum([rkeys_fn(kt) for kt in range(nkt)], [])
                    P.op("pe", mm, reads=slot.keys() + rk, writes=PK[b])
                    evac(c * 2 + m, b)
                yield c

        def gemm_fm(*a):
            for _ in gemm_gen(*a):
                pass

        def interleave(g1, g2, lead=1):
            for _ in range(lead):
                next(g1, None)
            d1 = d2 = False
            while not (d1 and d2):
                if not d2:
                    d2 = next(g2, "done") == "done"
                if not d1:
                    d1 = next(g1, "done") == "done"

        STGb = [R["STG"].buf(i * 4096, [1024], F32) for i in range(2)]
        hn = R["HN"].buf(0, [8, 512], BF16)
        mixT = R["HN"].buf(0, [8, 512], BF16)
        sq = R["SQ"].buf(0, [8, 512], BF16)
        qz = R["SQ"].buf(0, [8, 512], BF16)
        gb = R["SQ"].buf(0, [4, 512], BF16)
        sig = R["SQ"].buf(0, [4, 512], F32)
        acc = R["AT"].buf(0, [4, 512], F32)
        y32 = R["AT"].buf(0, [4, 512], F32)
        aTs = [R["AT"].buf(0, [8, 512], BF16), R["SQ"].buf(0, [8, 512], BF16)]
        uTb = R["UB"].buf(0, [4, 512], BF16)
        uM = R["UB"].buf(4096, [4, 512], BF16)
        r32 = [R["UB"].buf(i * 2048, [512], F32) for i in range(4)]
        pT = R["UB"].buf(0, [2, 512], BF16)
        e32 = R["SCR"].buf(0, [512], F32)
        spb = [R["SCR"].buf(2048 + i * 1024, [512], BF16) for i in range(3)]
        wlb = [R["SCR"].buf(5120 + i * 1024, [512], BF16) for i in range(3)]
        tmpf = [R["SCR"].buf(i * 2048, [512], F32) for i in range(4)]
        SX = R["S5B"].buf(0, [2, 16, 64], F32)
        Xp = R["S5B"].buf(8192, [2, 16, 64], BF16)
        hnp = [R["S5B"].buf(i * 2112, [528], F32) for i in range(2)]
        ptmp = [R["S5B"].buf(4224 + i * 2112, [528], F32) for i in range(2)]
        ypool = [R["S5B"].buf(8448 + i * 2048, [2, 512], BF16) for i in range(2)]
        rstd = R["RS"].buf(0, [512], F32)
        sqt = R["RS"].buf(2048, [512], F32)

        def dbg_dump(src_ap, keys, ti_sel, ti, slot=None):
            if dbg_d is None or ti != ti_sel:
                return
            dst = dbg_d[:, 0:src_ap.shape[1], :] if slot is None else dbg_d[:, slot, :]
            return P.dma("sp", "dbg", lambda e: e.dma_start(out=dst, in_=src_ap), reads=keys)

        NPOOL = int(os.environ.get("K_NPOOL", "3"))

        def rmsnorm(gain_name, out_fn, out_keys_fn, fp32_out=False):
            b = banks(1)[0]
            for dt in range(8):
                act(sq.ap[:, dt, :], HT.ap[:, dt, :], AF.Square, HT.keys(dt), sq.keys(dt))
                P.op("pe", lambda e, dt=dt: e.matmul(out=PB[b][:, :], lhsT=ones_b, rhs=sq.ap[:, dt, :], start=(dt == 0), stop=(dt == 7)),
                     reads=sq.keys(dt) + CK, writes=PK[b])
            for i, dt in enumerate(range(8 - NPOOL, 8)):
                act(tmpf[i].ap, HT.ap[:, dt, :], AF.Copy, HT.keys(dt) + CK, tmpf[i].keys(), scale=vcol(gain_name, dt))
            act(sqt.ap, PB[b][:, :], AF.Ln, PK[b], sqt.keys(), scale=1.0 / 1024.0, bias=EPS)
            act(rstd.ap, sqt.ap, AF.Exp, sqt.keys(), rstd.keys(), scale=-0.5)
            for i, dt in enumerate(range(8 - NPOOL, 8)):
                P.op("pool", lambda e, i=i, dt=dt: e.tensor_tensor(out=out_fn(dt), in0=tmpf[i].ap, in1=rstd.ap, op=ALU.mult),
                     reads=tmpf[i].keys() + rstd.keys(), writes=out_keys_fn(dt))
            for dt in range(8 - NPOOL):
                dve_stt(out_fn(dt), HT.ap[:, dt, :], vcol(gain_name, dt), rstd.ap, ALU.mult, ALU.mult,
                        HT.keys(dt) + rstd.keys() + CK, out_keys_fn(dt))

        final_sigs = []

        for ti in range(NT):
            t0 = ti * TT
            P.stage = "t%d_load" % ti
            for blk in range(4):
                sb_ = STGb[blk % 2]
                r0 = t0 + blk * 128
                P.dma("sp", "stg%d" % (blk % 2), lambda e, sb_=sb_, r0=r0: e.dma_start(out=sb_.ap, in_=x_d[r0:r0 + 128, :]), writes=sb_.keys())
                for half in range(2):
                    b = banks(1)[0]

                    def tr(e, sb_=sb_, half=half, b=b):
                        last = None
                        for j in range(4):
                            dt = half * 4 + j
                            last = e.transpose(out=PB[b][:, j * 128:(j + 1) * 128], in_=sb_.ap[:, dt * 128:(dt + 1) * 128], identity=ident.ap)
                        return last

                    P.op("pe", tr, reads=sb_.keys() + CK, writes=PK[b])
                    dst = HT.ap[:, half * 4:half * 4 + 4, blk * 128:(blk + 1) * 128]
                    src = PB[b][:, :].rearrange("p (j t) -> p j t", j=4)
                    wk = sum([HT.keys(half * 4 + j) for j in range(4)], [])
                    if half == 0:
                        act(dst, src, AF.Copy, PK[b], wk)
                    else:
                        cp(dst, src, PK[b], wk)

            for layer in range(2):
                if layer == 0:
                    P.stage = "t%d_inproj" % ti
                    rmsnorm("ln_mix0", lambda dt: hn.ap[:, dt, :], lambda dt: hn.keys(dt))
                    hn_r = lambda kt: hn.ap[:, kt, :]
                    hn_k = lambda kt: hn.keys(kt)

                    def ev_u(m, b):
                        act(uTb.ap[:, m, :], PB[b][:, :], AF.Copy, PK[b], uTb.keys(m))

                    gemm_fm("w_in", 0, 2, 0, hn_r, hn_k, ev_u)
                    dve_ts(uM.ap[64:128], uTb.ap[64:128], cf("pp3")[64:128, :], ALU.mult, uTb.keys() + CK, uM.keys())

                    def qk_evac(is_q):
                        def ev(m, b):
                            s_ = tmpf[m % 2]
                            sb16 = s_.ap.bitcast(BF16)[:, 0:512]
                            act(sb16, PB[b][:, :], AF.Square, PK[b], s_.keys())
                            b2 = banks(1)[0]
                            P.op("pe", lambda e: e.matmul(out=PB[b2][:, :], lhsT=blockones_b, rhs=sb16, start=True, stop=True),
                                 reads=s_.keys() + CK, writes=PK[b2])
                            r_ = tmpf[2 + m % 2]
                            act(r_.ap, PB[b2][:, :], AF.Ln, PK[b2], r_.keys(), scale=1.0 / 64.0, bias=EPS)
                            act(r_.ap, r_.ap, AF.Exp, r_.keys(), r_.keys(), scale=-0.5)
                            if is_q:
                                dve_stt(qz.ap[:, 2 * m, :], PB[b][:, :], vcol("qgA"), r_.ap, ALU.mult, ALU.mult,
                                        PK[b] + r_.keys() + CK, qz.keys(2 * m))
                                dve_stt(qz.ap[:, 2 * m + 1, :], PB[b][:, :], vcol("qgB"), r_.ap, ALU.mult, ALU.mult,
                                        PK[b] + r_.keys() + CK, qz.keys(2 * m + 1))
                            else:
                                dve_stt(KT.ap[:, m, t0:t0 + 512], PB[b][:, :], vcol("kgA"), r_.ap, ALU.mult, ALU.mult,
                                        PK[b] + r_.keys() + CK, [("KTc", ti)])
                        return ev

                    gemm_fm("w_in", 2, 2, 0, hn_r, hn_k, qk_evac(True))
                    gemm_fm("w_in", 4, 2, 0, hn_r, hn_k, qk_evac(False))
                    for c in range(2):
                        slot = load_w("w_in", 6 + c, 0)
                        bs = banks(2)
                        for blk in range(4):
                            b = bs[blk // 2]
                            cs = slice((blk % 2) * 256, (blk % 2) * 256 + 256)

                            def mm(e, blk=blk, b=b, cs=cs, slot=slot):
                                last = None
                                for kt in range(8):
                                    last = e.matmul(out=PB[b][:, cs], lhsT=hn.ap[:, kt, blk * 128:(blk + 1) * 128], rhs=slot.ap[:, kt, :],
                                                    start=(kt == 0), stop=(kt == 7))
                                return last

                            P.op("pe", mm, reads=slot.keys() + hn.keys(), writes=[("pbh%d" % b, blk % 2)] + PK[b])
                        for blk in range(4):
                            b = bs[blk // 2]
                            cs = slice((blk % 2) * 256, (blk % 2) * 256 + 256)
                            dst = VC.ap[:, ti * 4 + blk, c * 256:(c + 1) * 256]
                            if blk % 2 == 0:
                                act(dst, PB[b][:, cs], AF.Copy, PK[b], [("VCc", ti)])
                            else:
                                cp(dst, PB[b][:, cs], PK[b], [("VCc", ti)])

                    P.stage = "t%d_att" % ti
                    sbanks = [6, 7, 6, 7]
                    t1_, t2_, t3_, t4_ = [RT.ap[:, i, :] for i in range(4)]
                    tk = RT.keys()
                    Ar = Acf.ap[:, 0, :]
                    Ai = Acf.ap[:, 1, :]
                    sxk = SX.keys()
                    for grp in range(2):
                        def mmS(e, grp=grp):
                            last = None
                            for pp in (2 * grp, 2 * grp + 1):
                                bnk = PB[6 + pp % 2]
                                for ct in range(4):
                                    for ri in range(2):
                                        col = (ct * 2 + ri) * 64
                                        for j in range(8):
                                            k = 7 - j
                                            if pp < 3:
                                                rows = slice(pp * 32, pp * 32 + 32)
                                                rhs = uTb.ap[rows, ct, j:512:8]
                                            else:
                                                rows = slice(64, 128)
                                                rhs = uM.ap[rows, ct, j:512:8]
                                            last = e.matmul(out=bnk[:, col:col + 64], lhsT=VS.ap[rows, ct, ri, k, :], rhs=rhs,
                                                            start=(j == 0), stop=(j == 7))
                            return last

                        P.op("pe", mmS, reads=uTb.keys() + uM.keys() + VS.keys(), writes=PK[6] + PK[7])
                        for pp in (2 * grp, 2 * grp + 1):
                            bk_ = 6 + pp % 2
                            src = PB[bk_][:, :].rearrange("p (c r n) -> p r c n", c=4, r=2)
                            for ri in range(2):
                                dst = SX.ap[:, ri, pp:16:4, :]
                                cp(dst, src[:, ri], PK[bk_], sxk)
                    cp(Xp.ap[:, :, :, 0], Xc.ap, Xc.keys(), Xp.keys())

                    def rec_step(c):
                        pr = Xc.ap[:, 0, :] if c == 0 else SX.ap[:, 0, :, c - 1]
                        pi = Xc.ap[:, 1, :] if c == 0 else SX.ap[:, 1, :, c - 1]
                        rk_ = (Xc.keys() if c == 0 else []) + sxk + Acf.keys() + tk

                        def rec(e, pr=pr, pi=pi, c=c):
                            e.tensor_tensor(out=t1_, in0=Ar, in1=pr, op=ALU.mult)
                            e.tensor_tensor(out=t2_, in0=Ai, in1=pi, op=ALU.mult)
                            e.tensor_tensor(out=t3_, in0=Ar, in1=pi, op=ALU.mult)
                            e.tensor_tensor(out=t4_, in0=Ai, in1=pr, op=ALU.mult)
                            e.tensor_tensor(out=t1_, in0=t1_, in1=t2_, op=ALU.subtract)
                            e.tensor_tensor(out=t3_, in0=t3_, in1=t4_, op=ALU.add)
                            e.tensor_tensor(out=SX.ap[:, 0, :, c], in0=SX.ap[:, 0, :, c], in1=t1_, op=ALU.add)
                            return e.tensor_tensor(out=SX.ap[:, 1, :, c], in0=SX.ap[:, 1, :, c], in1=t3_, op=ALU.add)

                        P.op(REC_ENG, rec, reads=rk_, writes=sxk + tk)

                    steps = []
                    for h in range(8):
                        for kb in range(4 * ti + 3, -1, -1):
                            steps.append((h, kb))
                    nst = len(steps)
                    rec_per = (64 + nst - 1) // nst
                    rec_done = [0]

                    def geom(i):
                        h, kb = steps[i]
                        b_ = kb - 4 * ti
                        diag = b_ >= 0
                        c0 = 128 * b_ if diag else 0
                        sb0 = b_ if diag else 0
                        return h, kb, diag, c0, sb0, 512 - c0

                    def stageA(i):
                        h, kb, diag, c0, sb0, N = geom(i)
                        zb = i % 2
                        sp_ = spb[i % 3]
                        kt_l = KT.ap[:, h // 2, kb * 128:(kb + 1) * 128]
                        q_r = qz.ap[:, h, c0:512]
                        P.op("pe", lambda e: e.matmul(out=PB[zb][:, 0:N], lhsT=kt_l, rhs=q_r, start=True, stop=True),
                             reads=[("KTc", kb // 4)] + qz.keys(h), writes=PK[zb])
                        act(e32.ap[:, 0:N], PB[zb][:, 0:N], AF.Exp, PK[zb], e32.keys())
                        act(sp_.ap[:, 0:N], e32.ap[:, 0:N], AF.Ln, e32.keys(), sp_.keys(), bias=1.0)
                        if diag:
                            dve_tt(sp_.ap[:, 0:128], sp_.ap[:, 0:128], masklt_b, ALU.mult, sp_.keys() + CK, sp_.keys())

                    def stageB(i):
                        h, kb, diag, c0, sb0, N = geom(i)
                        ab = 2 + i % 2
                        sp_ = spb[i % 3]
                        wl_ = wlb[i % 3]
                        kt_l = KT.ap[:, h // 2, kb * 128:(kb + 1) * 128]
                        q_r = qz.ap[:, h, c0:512]

                        def mm2(e):
                            e.matmul(out=PB[ab][:, 0:N], lhsT=kt_l, rhs=q_r, start=True, stop=False)
                            return e.matmul(out=PB[ab][:, 0:N], lhsT=negtri_b, rhs=sp_.ap[:, 0:N], start=False, stop=True)

                        P.op("pe", mm2, reads=[("KTc", kb // 4)] + qz.keys(h) + sp_.keys() + CK, writes=PK[ab])
                        act(wl_.ap[:, 0:N], PB[ab][:, 0:N], AF.Exp, PK[ab], wl_.keys())
                        if diag:
                            dve_tt(wl_.ap[:, 0:128], wl_.ap[:, 0:128], masklt_b, ALU.mult, wl_.keys() + CK, wl_.keys())

                    def stageC(i):
                        h, kb, diag, c0, sb0, N = geom(i)
                        vb = 4 + i % 2
                        sp_ = spb[i % 3]
                        wl_ = wlb[i % 3]
                        par = h % 2
                        Cst = CE.ap[:, par, 0, :]
                        Est = CE.ap[:, par, 1, :]
                        cek = [("CE", par)]
                        nsb = 4 - sb0
                        v_r = VC.ap[:, kb, h * 64:(h + 1) * 64]

                        def mm3(e):
                            last = None
                            for ii in range(nsb):
                                sbq = sb0 + ii
                                cs = slice(ii * 128, (ii + 1) * 128)
                                e.matmul(out=PB[vb][:, sbq * 80:sbq * 80 + 64], lhsT=wl_.ap[:, cs], rhs=v_r, start=True, stop=True)
                                last = e.matmul(out=PB[vb][:, sbq * 80 + 64:sbq * 80 + 65], lhsT=sp_.ap[:, cs], rhs=ones_b[:, 0:1], start=True, stop=True)
                            return last

                        P.op("pe", mm3, reads=wl_.keys() + sp_.keys() + [("VCc", kb // 4)] + CK, writes=PK[vb])
                        cont0 = sb0 + 1 if diag else 0
                        if cont0 < 4:
                            if E_ON_POOL:
                                P.op("pool", lambda e: e.tensor_tensor(out=Est[:, cont0:4], in0=EINV.ap[:, cont0:4], in1=Cst[:, cont0:4], op=ALU.pow),
                                     reads=cek + EINV.keys(), writes=cek)
                            else:
                                act(Est[:, cont0:4], Cst[:, cont0:4], AF.Exp, cek, cek, scale=-1.0)
                        for sbq in range(sb0, 4):
                            pv = PB[vb][:, sbq * 80:sbq * 80 + 64]
                            dst = acc.ap[:, sbq, h * 64:(h + 1) * 64]
                            if diag and sbq == sb0:
                                cp(dst, pv, PK[vb], acc.keys(sbq))
                            else:
                                dve_stt(dst, pv, Est[:, sbq:sbq + 1], dst, ALU.mult, ALU.add, PK[vb] + cek + acc.keys(sbq), acc.keys(sbq))
                        pcs = PB[vb][:, 0:320].rearrange("p (s c) -> p s c", c=80)[:, :, 64]
                        if diag:
                            cp(Cst[:, sb0:sb0 + 1], pcs[:, sb0:sb0 + 1], PK[vb], cek)
                            if sb0 + 1 < 4:
                                dve_tt(Cst[:, sb0 + 1:4], Cst[:, sb0 + 1:4], pcs[:, sb0 + 1:4], ALU.add, PK[vb] + cek, cek)
                        else:
                            dve_tt(Cst, Cst, pcs, ALU.add, PK[vb] + cek, cek)
                        for _ in range(rec_per):
                            if rec_done[0] < 64:
                                rec_step(rec_done[0])
                                rec_done[0] += 1

                    for i in range(nst + 2):
                        if i < nst:
                            stageA(i)
                        if 0 <= i - 1 < nst:
                            stageB(i - 1)
                        if 0 <= i - 2 < nst:
                            stageC(i - 2)
                    while rec_done[0] < 64:
                        rec_step(rec_done[0])
                        rec_done[0] += 1
                    cp(Xp.ap[:, :, :, 1:64], SX.ap[:, :, :, 0:63], sxk, Xp.keys())
                    cp(Xc.ap, SX.ap[:, :, :, 63], sxk, Xc.keys())
                    dbg_dump(acc.ap, acc.keys(), 0, ti) if dbg == "att" else None
                    for ft in range(4):
                        b = banks(1)[0]

                        def tr(e, ft=ft, b=b):
                            last = None
                            for sbq in range(4):
                                last = e.transpose(out=PB[b][:, sbq * 128:(sbq + 1) * 128], in_=acc.ap[:, sbq, ft * 128:(ft + 1) * 128], identity=ident.ap)
                            return last

                        P.op("pe", tr, reads=acc.keys() + CK, writes=PK[b])
                        act(mixT.ap[:, 4 + ft, :], PB[b][:, :], AF.Copy, PK[b], mixT.keys(4 + ft))

                    P.stage = "t%d_s5out" % ti
                    ybanks = banks(4)
                    for ct in range(4):
                        bnk = PB[ybanks[ct]]

                        def mmY(e, ct=ct, bnk=bnk):
                            last = None
                            for ip in range(8):
                                for j in range(ip + 1):
                                    e.matmul(out=bnk[:, ip:512:8], lhsT=KD.ap[:, ct, ip - j, :], rhs=uTb.ap[:, ct, j:512:8],
                                             start=(ip == 0 and j == 0), stop=False, skip_group_check=True)
                            for ip in range(8):
                                for pp in range(4):
                                    q = 4 * ct + pp
                                    for ri in range(2):
                                        last = e.matmul(out=bnk[pp * 32:(pp + 1) * 32, ip:512:8], lhsT=YW.ap[:, q, ip, ri, :], rhs=Xp.ap[:, ri, q, :],
                                                        start=False, stop=(ri == 1 and ip == 7 and pp == 3), tile_position=(0, pp * 32), skip_group_check=True)
                            return last

                        P.op("pe", mmY, reads=uTb.keys(ct) + KD.keys() + YW.keys() + Xp.keys(), writes=PK[ybanks[ct]])
                        dve_stt(y32.ap[:, ct, :], uTb.ap[:, ct, :], vcol("s5d", ct), bnk[:, :], ALU.mult, ALU.add,
                                PK[ybanks[ct]] + uTb.keys(ct) + CK, y32.keys(ct))
                        act(y32.ap[:, ct, :], y32.ap[:, ct, :], AF.Gelu, y32.keys(ct), y32.keys(ct))
                        cp(gb.ap[:, ct, :], y32.ap[:, ct, :], y32.keys(ct), gb.keys(ct))
                    dbg_dump(y32.ap, y32.keys(), 0, ti) if dbg == "s5" else None

                    def ev_glu(m, b):
                        t_ = tmpf[m % 2]
                        act(t_.ap, PB[b][:, :], AF.Sigmoid, PK[b], t_.keys())
                        dve_tt(mixT.ap[:, m, :], y32.ap[:, m, :], t_.ap, ALU.mult, y32.keys(m) + t_.keys(), mixT.keys(m))

                    gemm_fm("w_glu", 0, 2, 0, lambda kt: gb.ap[:, kt, :], lambda kt: gb.keys(kt), ev_glu)

                    def ev_res(m, b):
                        dve_tt(HT.ap[:, m, :], HT.ap[:, m, :], PB[b][:, :], ALU.add, PK[b] + HT.keys(m), HT.keys(m))

                    gemm_fm("w_out", 0, 4, 0, lambda kt: mixT.ap[:, kt, :], lambda kt: mixT.keys(kt), ev_res)
                    if dbg == "mix0":
                        dbg_dump(HT.ap, HT.keys(), NT - 1, ti)
                else:
                    P.stage = "t%d_pool" % ti
                    act(sq.ap, HT.ap, AF.Square, HT.keys(), sq.keys())
                    b = banks(1)[0]

                    def mmn(e, b=b):
                        last = None
                        for dt in range(8):
                            last = e.matmul(out=PB[b][:, :], lhsT=ones_b, rhs=sq.ap[:, dt, :], start=(dt == 0), stop=(dt == 7))
                        return last

                    P.op("pe", mmn, reads=sq.keys() + CK, writes=PK[b])
                    act(sqt.ap, PB[b][:, :], AF.Ln, PK[b], sqt.keys(), scale=1.0 / 1024.0, bias=EPS)
                    act(rstd.ap, sqt.ap, AF.Exp, sqt.keys(), rstd.keys(), scale=-0.5)
                    slotp = None
                    for g, w in enumerate((2, 4, 8, 16)):
                        nlev = int(math.log2(w))
                        yp = ypool[g % 2]
                        for dl in range(2):
                            dt = 2 * g + dl
                            hp_ = hnp[dl]
                            a_, b_ = ptmp[0], ptmp[1]
                            cp(hp_.ap[:, 0:16], LB.ap[:, dt, :], LB.keys(), hp_.keys())
                            dve_stt(hp_.ap[:, 16:528], HT.ap[:, dt, :], vcol("ln_mix1", dt), rstd.ap, ALU.mult, ALU.mult,
                                    HT.keys(dt) + rstd.keys() + CK, hp_.keys())
                            cp(LB.ap[:, dt, :], hp_.ap[:, 512:528], hp_.keys(), LB.keys())
                            src = hp_
                            lo = 0
                            for lv in range(nlev):
                                sh = 1 << lv
                                dstb = a_ if lv % 2 == 0 else b_
                                lo2 = lo + sh
                                dve_tt(dstb.ap[:, lo2:528], src.ap[:, lo2:528], src.ap[:, lo2 - sh:528 - sh], ALU.add,
                                       src.keys() + dstb.keys(), dstb.keys())
                                src = dstb
                                lo = lo2
                            dve_stt(yp.ap[:, dl, :], src.ap[:, 16:528], 1.0 / w, hp_.ap[:, 16:528], ALU.mult, ALU.subtract,
                                    src.keys() + hp_.keys(), yp.keys())
                            if ti == 0:
                                cnt = cf("cnt").rearrange("p (g t) -> p g t", g=4)[:, g, :]
                                tfix = ptmp[0] if src is ptmp[1] else ptmp[1]
                                dve_tt(tfix.ap[:, 0:16], src.ap[:, 16:32], cnt, ALU.mult, src.keys() + CK + tfix.keys(), tfix.keys())
                                dve_tt(yp.ap[:, dl, 0:16], tfix.ap[:, 0:16], hp_.ap[:, 16:32], ALU.subtract, tfix.keys() + hp_.keys() + yp.keys(), yp.keys())
                        slot = load_w("pool", g, 0)
                        bs = banks(2)
                        for m in range(2):
                            b = bs[m]

                            def mm(e, m=m, b=b, slot=slot, yp=yp):
                                e.matmul(out=PB[b][:, :], lhsT=slot.ap[:, 0, m * 128:(m + 1) * 128], rhs=yp.ap[:, 0, :], start=True, stop=False)
                                return e.matmul(out=PB[b][:, :], lhsT=slot.ap[:, 1, m * 128:(m + 1) * 128], rhs=yp.ap[:, 1, :], start=False, stop=True)

                            P.op("pe", mm, reads=slot.keys() + yp.keys(), writes=PK[b])
                            dt = 2 * g + m
                            dve_stt(HT.ap[:, dt, :], PB[b][:, :], vcol("pscale", dt), HT.ap[:, dt, :], ALU.mult, ALU.add,
                                    PK[b] + HT.keys(dt) + CK, HT.keys(dt))

                P.stage = "t%d_mlp%d" % (ti, layer)
                rmsnorm("ln_mlp%d" % layer, lambda dt: hn.ap[:, dt, :], lambda dt: hn.keys(dt))
                def up_gen(qq):
                    aT = aTs[qq % 2]

                    def ev_up(m, b, aT=aT):
                        r_ = r32[m % 4]
                        act(r_.ap, PB[b][:, :], AF.Relu, PK[b], r_.keys())
                        dve_tt(aT.ap[:, m, :], r_.ap, r_.ap, ALU.mult, r_.keys(), aT.keys(m))

                    return gemm_gen("w_up%d" % layer, qq * 4, 4, 0, lambda kt: hn.ap[:, kt, :], lambda kt: hn.keys(kt), ev_up)

                def dn_gen(qq):
                    aT = aTs[qq % 2]

                    def ev_res(m, b):
                        dve_tt(HT.ap[:, m, :], HT.ap[:, m, :], PB[b][:, :], ALU.add, PK[b] + HT.keys(m), HT.keys(m))

                    return gemm_gen("w_dn%d" % layer, 0, 4, qq, lambda kt, aT=aT: aT.ap[:, kt, :], lambda kt, aT=aT: aT.keys(kt), ev_res)

                if MLP_PIPE:
                    for _ in up_gen(0):
                        pass
                    for qq in range(4):
                        if qq < 3:
                            interleave(up_gen(qq + 1), dn_gen(qq), lead=1)
                        else:
                            for _ in dn_gen(qq):
                                pass
                else:
                    for qq in range(4):
                        for _ in up_gen(qq):
                            pass
                        for _ in dn_gen(qq):
                            pass

                if dbg == "mlp0" and layer == 0:
                    dbg_dump(HT.ap, HT.keys(), NT - 1, ti)
                P.stage = "t%d_ple%d" % (ti, layer)
                rmsnorm("ln_ple%d" % layer, lambda dt: hn.ap[:, dt, :], lambda dt: hn.keys(dt))
                for blk in range(4):
                    sb_ = STGb[blk % 2]
                    r0 = t0 + blk * 128
                    P.dma("sp", "stg%d" % (blk % 2), lambda e, sb_=sb_, r0=r0, layer=layer: e.dma_start(out=sb_.ap[:, 0:256], in_=p_d[layer, r0:r0 + 128, :]), writes=sb_.keys())
                    b = banks(1)[0]

                    def trp(e, sb_=sb_, b=b):
                        e.transpose(out=PB[b][:, 0:128], in_=sb_.ap[:, 0:128], identity=ident.ap)
                        return e.transpose(out=PB[b][:, 128:256], in_=sb_.ap[:, 128:256], identity=ident.ap)

                    P.op("pe", trp, reads=sb_.keys() + CK, writes=PK[b])
                    act(pT.ap[:, :, blk * 128:(blk + 1) * 128], PB[b][:, 0:256].rearrange("p (k t) -> p k t", k=2), AF.Copy, PK[b], pT.keys())
                for half in range(2):
                    def ev_gate(m, b):
                        act(sig.ap[:, m, :], PB[b][:, :], AF.Sigmoid, PK[b], sig.keys(m))

                    gemm_fm("w_gt%d" % layer, half * 2, 2, 0, lambda kt: hn.ap[:, kt, :], lambda kt: hn.keys(kt), ev_gate)

                    def ev_pu(m, b, half=half):
                        t_ = tmpf[2 + m % 2]
                        dve_tt(t_.ap, PB[b][:, :], sig.ap[:, m, :], ALU.mult, PK[b] + sig.keys(m), t_.keys())
                        dt = half * 4 + m
                        dve_tt(HT.ap[:, dt, :], HT.ap[:, dt, :], t_.ap, ALU.add, t_.keys() + HT.keys(dt), HT.keys(dt))

                    gemm_fm("w_pu%d" % layer, half * 2, 2, 0, lambda kt: pT.ap[:, kt, :], lambda kt: pT.keys(), ev_pu)
                if dbg == "l0" and layer == 0:
                    dbg_dump(HT.ap, HT.keys(), NT - 1, ti)

            P.stage = "t%d_out" % ti
            for blk in range(4):
                sb_ = STGb[blk % 2]
                r0 = t0 + blk * 128
                for half in range(2):
                    b = banks(1)[0]

                    def tro(e, half=half, b=b, blk=blk):
                        last = None
                        for j in range(4):
                            dt = half * 4 + j
                            last = e.transpose(out=PB[b][:, j * 128:(j + 1) * 128], in_=HT.ap[:, dt, blk * 128:(blk + 1) * 128], identity=ident.ap)
                        return last

                    P.op("pe", tro, reads=HT.keys() + CK, writes=PK[b])
                    if half == 0:
                        act(sb_.ap[:, 0:512], PB[b][:, :], AF.Copy, PK[b], sb_.keys())
                    else:
                        cp(sb_.ap[:, 512:1024], PB[b][:, :], PK[b], sb_.keys())
                sg = P.dma("sp", "stg%d" % (blk % 2), lambda e, sb_=sb_, r0=r0: e.dma_start(out=y_d[r0:r0 + 128, :], in_=sb_.ap), reads=sb_.keys())
                final_sigs.append(sg)

        fs = {}
        for s, v in final_sigs:
            fs[s] = max(fs.get(s, 0), v)
        if dbg_d is not None and "d_dbg" in P.cnt:
            fs["d_dbg"] = P.cnt["d_dbg"]
        P.finish("sp", list(fs.items()))
        P.emit()
    return nc


def _prep_shared(inp):
    f = lambda a: np.ascontiguousarray(np.asarray(a, dtype=np.float32))
    sh = {}
    sh["w_in"] = f(inp["w_in_even"][0])
    sh["w_out"] = f(inp["w_out_even"][0])
    sh["w_glu"] = f(inp["s5_w_glu"][0])
    sh["w_up"] = f(inp["w_mlp_up"])
    sh["w_dn"] = f(inp["w_mlp_down"])
    sh["w_gt"] = f(inp["w_ple_gate"])
    sh["w_pu"] = f(inp["w_ple_up"])
    sh["w_pool"] = f(inp["pool_w"][0])
    col8 = lambda v: np.asarray(v, np.float32).reshape(-1, 128).T
    qg = np.asarray(inp["sb_q_gain"][0], np.float32)
    kg = np.asarray(inp["sb_k_gain"][0], np.float32)
    qg128 = np.concatenate([qg, qg])
    kg128 = np.concatenate([kg, kg])
    half = (np.arange(128) < 64)
    scale = np.float32(64 ** -0.5)
    vec = np.concatenate([
        col8(inp["ln_mix_even"][0]), col8(inp["ln_mlp"][0]), col8(inp["ln_ple"][0]),
        col8(inp["ln_mix_odd"][0]), col8(inp["ln_mlp"][1]), col8(inp["ln_ple"][1]),
        col8(inp["pool_scale"][0]), col8(inp["s5_d"][0]),
        np.where(half, qg128, 0)[:, None], np.where(~half, qg128, 0)[:, None], kg128[:, None]], axis=1)
    sh["vecs"] = f(vec)
    lamr = np.asarray(inp["s5_lambda_re"][0], np.float32)
    lami = np.asarray(inp["s5_lambda_im"][0], np.float32)
    logdt = np.asarray(inp["s5_log_dt"][0], np.float32)
    toB = lambda a: a.reshape(16, 2, 64).transpose(1, 2, 0).reshape(128, 16)
    logdtB = toB(np.tile(logdt[:, None], (1, 64)))
    toB3 = lambda a: a.reshape(16, 2, 64, 16).transpose(1, 2, 0, 3).reshape(128, 256)
    bre = np.asarray(inp["s5_b_re"][0], np.float32)
    bim = np.asarray(inp["s5_b_im"][0], np.float32)
    cre = np.asarray(inp["s5_c_re"][0], np.float32).transpose(0, 2, 1)
    cim = np.asarray(inp["s5_c_im"][0], np.float32).transpose(0, 2, 1)
    sh["s5B"] = f(np.concatenate([toB(lamr), toB(lami), logdtB, toB3(bre), toB3(bim), toB3(cre), toB3(cim)], axis=1))
    toA = lambda a: np.tile(a.reshape(4, 8, 1, 64), (1, 1, 16, 1)).transpose(1, 2, 0, 3).reshape(128, 256)
    toA3 = lambda a: a.reshape(4, 8, 64, 16).transpose(1, 3, 0, 2).reshape(128, 256)
    sh["s5A"] = f(np.concatenate([toA(lamr), toA(lami), toA(np.tile(logdt[:, None], (1, 64))), toA3(bre), toA3(bim)], axis=1))
    sh.update(_consts())
    sh["_scale"] = scale
    return sh


_NC_CACHE = {}


def kernel(**inputs):
    x = np.asarray(inputs["x"], np.float32)
    p = np.asarray(inputs["p"], np.float32)
    B, L, Dm = x.shape
    NT = L // TT
    sh = _prep_shared(inputs)
    sh.pop("_scale")
    if NT not in _NC_CACHE:
        _NC_CACHE[NT] = build(NT)
    nc = _NC_CACHE[NT]
    in_maps = []
    for b in range(B):
        m = dict(sh)
        m["x"] = np.ascontiguousarray(x[b])
        m["p"] = np.ascontiguousarray(p[:, b])
        in_maps.append(m)
    res = run_bass_kernel_spmd(nc, in_maps, core_ids=list(range(B)))
    out = np.stack([res.results[b]["y"] for b in range(B)], axis=0)
    return out.astype(np.float32)
```

```python
import math
from contextlib import ExitStack

import numpy as np
import concourse.bass as bass
import concourse.mybir as mybir
from concourse.bass_utils import run_bass_kernel_spmd

F32 = mybir.dt.float32
BF16 = mybir.dt.bfloat16
I32 = mybir.dt.int32
AF = mybir.ActivationFunctionType
ALU = mybir.AluOpType

TT = 512
EPS = 1e-6
KLIST = list(range(-7, 9))
TWO_PI = 2.0 * math.pi
import os
MLP_PIPE = bool(int(os.environ.get("K_MLPPIPE", "1")))
REC_ENG = os.environ.get("K_REC", "dve")
E_ON_POOL = bool(int(os.environ.get("K_EPOOL", "0")))
SAME_ENGINE_SYNC = set(os.environ.get("K_SES", "").split(","))


class Prog:
    def __init__(self, nc, stack):
        self.nc = nc
        self.stack = stack
        self.ops = {e: [] for e in ("pe", "act", "dve", "pool", "sp")}
        self.sems = {}
        self.cnt = {}
        self.keys = {}
        self.waited = {e: {} for e in self.ops}
        self.final = []
        self.stage = "setup"
        self.scopes = bool(int(os.environ.get("K_SCOPES", "0")))
        self.ses = set(SAME_ENGINE_SYNC)
        self.near = {"dve": int(os.environ.get("K_NEAR_DVE", "2")), "act": int(os.environ.get("K_NEAR_ACT", "1")), "pool": 2}

    def sem(self, name):
        if name not in self.sems:
            self.sems[name] = self.stack.enter_context(self.nc.semaphore(name))
            self.cnt[name] = 0
        return self.sems[name]

    def _resolve(self, eng, reads, writes, mysig):
        waits = {}

        def add(sig):
            if sig is None:
                return
            s, v = sig
            if s == eng:
                if eng == "pe":
                    return
                if eng not in self.ses and (self.cnt[eng] - v) > self.near.get(eng, 0):
                    return
            if waits.get(s, 0) < v:
                waits[s] = v

        for k in reads:
            st = self.keys.setdefault(k, [None, []])
            add(st[0])
        for k in writes:
            st = self.keys.setdefault(k, [None, []])
            add(st[0])
            for r in st[1]:
                add(r)
        out = []
        for s, v in waits.items():
            if self.waited[eng].get(s, 0) >= v:
                continue
            self.waited[eng][s] = v
            out.append((s, v))
        for k in reads:
            self.keys[k][1].append(mysig)
        for k in writes:
            self.keys[k][0] = mysig
            self.keys[k][1] = []
        return out

    def op(self, eng, fn, reads=(), writes=()):
        self.sem(eng)
        self.cnt[eng] += 1
        mysig = (eng, self.cnt[eng])
        waits = self._resolve(eng, reads, writes, mysig)
        self.ops[eng].append((waits, fn, (eng, 1), self.stage))

    def dma(self, q, semkey, fn, reads=(), writes=()):
        name = "d_" + semkey
        self.sem(name)
        self.cnt[name] += 16
        mysig = (name, self.cnt[name])
        waits = self._resolve(q, reads, writes, mysig)
        self.ops[q].append((waits, fn, (name, 16), "dma"))
        return mysig

    def finish(self, eng, sigs):
        self.final.append((eng, sigs))

    def emit(self):
        nc = self.nc
        engmap = {"pe": "tensor", "act": "scalar", "dve": "vector", "pool": "gpsimd", "sp": "sync"}
        block = self.stack.enter_context(nc.Block())
        for e, attr in engmap.items():
            ops = self.ops[e]
            finals = [s for (fe, s) in self.final if fe == e]
            if not ops and not finals:
                continue

            def body(engine, ops=ops, finals=finals):
                for waits, fn, (sname, inc), stage in ops:
                    if self.scopes:
                        with nc.named_scope(stage):
                            for s, v in waits:
                                engine.wait_ge(self.sems[s], v)
                            inst = fn(engine)
                            inst.then_inc(self.sems[sname], inc)
                    else:
                        for s, v in waits:
                            engine.wait_ge(self.sems[s], v)
                        inst = fn(engine)
                        inst.then_inc(self.sems[sname], inc)
                for sigs in finals:
                    for s, v in sigs:
                        engine.wait_ge(self.sems[s], v)

            getattr(block, attr)(body)


KG = 512


class Region:
    def __init__(self, arena, name, woff, nbytes):
        self.arena, self.name, self.woff, self.nbytes = arena, name, woff, nbytes

    def buf(self, boff, shape, dt):
        return Buf(self, boff, shape, dt)


class Buf:
    def __init__(self, region, boff, shape, dt):
        esz = 4 if dt in (F32, I32) else 2
        n = int(np.prod(shape))
        assert boff % 4 == 0 and boff + n * esz <= region.nbytes, (region.name, boff, shape, region.nbytes)
        w0 = region.woff + boff // 4
        nw = (n * esz + 3) // 4
        ap = region.arena[:, w0:w0 + nw]
        if dt != F32:
            ap = ap.bitcast(dt)
        if len(shape) > 1:
            names = " ".join("a%d" % i for i in range(len(shape)))
            kw = {"a%d" % i: shape[i] for i in range(1, len(shape))}
            ap = ap.rearrange("p (%s) -> p %s" % (names, names), **kw)
        self.ap, self.region, self.boff, self.shape, self.esz = ap, region, boff, tuple(shape), esz
        self.nbytes = n * esz
        self.blk = (n // shape[0]) * esz

    def keys(self, lo=None, hi=None):
        if lo is None:
            b0, b1 = self.boff, self.boff + self.nbytes
        else:
            hi = lo + 1 if hi is None else hi
            b0, b1 = self.boff + lo * self.blk, self.boff + hi * self.blk
        return [(self.region.name, k) for k in range(b0 // KG, (b1 + KG - 1) // KG)]


def _consts():
    c = {}
    c["ident"] = np.eye(128, dtype=np.float32)
    jj = np.arange(128)[:, None]
    tt = np.arange(128)[None, :]
    ones = np.ones((128, 128), np.float32)
    blockones = ((jj // 64) == (tt // 64)).astype(np.float32)
    negtri = -(jj >= tt).astype(np.float32)
    masklt = (jj < tt).astype(np.float32)
    c["cstb"] = np.concatenate([ones, blockones, negtri, masklt], axis=1)
    part = np.arange(128)
    g8 = part // 16
    bd = (g8[:, None, None] == np.arange(8)[None, :, None]) * np.ones((1, 1, 16))
    g2 = (part // 16) % 2
    g2col = (g2[:, None] == np.arange(2)[None, :]).astype(np.float32)
    pp3 = (part >= 96).astype(np.float32)[:, None]
    halfA = (part < 64).astype(np.float32)[:, None]
    halfB = (part >= 64).astype(np.float32)[:, None]
    kv = np.tile(np.array(KLIST, np.float32)[None, :], (128, 1))
    kf = kv / TWO_PI
    kvA = np.tile(np.arange(8, dtype=np.float32)[None, :], (128, 1))
    kfA = kvA / TWO_PI
    cnt = np.zeros((128, 4, 16), np.float32)
    for g, w in enumerate((2, 4, 8, 16)):
        cnt[:, g, :] = 1.0 / np.minimum(np.arange(16) + 1, w)
    c["cstf"] = np.concatenate([bd.reshape(128, 128).astype(np.float32), g2col, pp3, halfA, halfB,
                                kv, kf, kvA, kfA, cnt.reshape(128, 64)], axis=1).astype(np.float32)
    return c


CF = {}
_o = 0
for _n, _w in (("bd", 128), ("g2col", 2), ("pp3", 1), ("halfA", 1), ("halfB", 1), ("kv", 16), ("kf", 16),
               ("kvA", 8), ("kfA", 8), ("cnt", 64)):
    CF[_n] = (_o, _w)
    _o += _w
NCF = _o

VEC = {}
_o = 0
for _n, _w in (("ln_mix0", 8), ("ln_mlp0", 8), ("ln_ple0", 8), ("ln_mix1", 8), ("ln_mlp1", 8), ("ln_ple1", 8),
               ("pscale", 8), ("s5d", 4), ("qgA", 1), ("qgB", 1), ("kgA", 1)):
    VEC[_n] = (_o, _w)
    _o += _w
NVEC = _o


def build(NT, dbg=None):
    L = NT * TT
    nc = bass.Bass("TRN2", target_bir_lowering=False)
    D = lambda n, s: nc.dram_tensor(n, s, F32, kind="ExternalInput").ap()
    x_d = D("x", [L, 1024])
    p_d = D("p", [2, L, 256])
    w_in = D("w_in", [1024, 2048])
    w_out = D("w_out", [1024, 1024])
    w_glu = D("w_glu", [512, 512])
    w_up = D("w_up", [2, 1024, 4096])
    w_dn = D("w_dn", [2, 4096, 1024])
    w_gt = D("w_gt", [2, 1024, 1024])
    w_pu = D("w_pu", [2, 256, 1024])
    w_pool = D("w_pool", [4, 256, 256])
    vecs_d = D("vecs", [128, NVEC])
    s5b_d = D("s5B", [128, 48 + 4 * 256])
    s5a_d = D("s5A", [128, 5 * 256])
    ident_d = D("ident", [128, 128])
    cstb_d = D("cstb", [128, 512])
    cstf_d = D("cstf", [128, NCF])
    y_d = nc.dram_tensor("y", [L, 1024], F32, kind="ExternalOutput").ap()
    dbg_d = None
    if dbg:
        dbg_d = nc.dram_tensor("dbg", [128, 8, 512], F32, kind="ExternalOutput").ap()

    with ExitStack() as st:
        P = Prog(nc, st)
        sizes = [("KV", 65536), ("VS", 16384), ("YW", 16384), ("KD", 8192), ("WS", 3 * 4096), ("HT", 16384),
                 ("CST", 6144), ("STG", 8192), ("HN", 8192), ("SQ", 8192), ("AT", 8192), ("UB", 8192),
                 ("SCR", 8192), ("S5B", 12800), ("RS", 4096)]
        total_w = sum(s for _, s in sizes) // 4
        arena = st.enter_context(nc.sbuf_tensor("arena", [128, total_w], F32))
        R = {}
        wo = 0
        for n, s in sizes:
            R[n] = Region(arena, n, wo, s)
            wo += s // 4
        PB = [st.enter_context(nc.psum_tensor("pb%d" % i, [128, 512], F32)) for i in range(8)]
        PK = [[("pb%d" % i, 0)] for i in range(8)]

        KT = R["KV"].buf(0, [4, 4096], BF16)
        VC = R["KV"].buf(32768, [32, 512], BF16)
        VS = R["VS"].buf(0, [4, 2, 8, 128], BF16)
        YW = R["YW"].buf(0, [16, 8, 2, 32], BF16)
        KD = R["KD"].buf(0, [4, 8, 128], BF16)
        WS = [R["WS"].buf(i * 4096, [8, 256], BF16) for i in range(3)]
        HT = R["HT"].buf(0, [8, 512], F32)
        ident = R["CST"].buf(0, [128], F32)
        vecs = R["CST"].buf(512, [NVEC], F32)
        cstb = R["CST"].buf(1024, [4, 128], BF16)
        cstf = R["CST"].buf(2048, [NCF], F32)
        assert NCF * 4 <= 1024
        Acf = R["CST"].buf(3072, [2, 16], F32)
        Xc = R["CST"].buf(3584, [2, 16], F32)
        LB = R["CST"].buf(4096, [8, 16], F32)
        CE = R["CST"].buf(4608, [2, 2, 4], F32)
        RT = R["CST"].buf(5120, [4, 16], F32)
        EINV = R["CST"].buf(5632, [4], F32)

        def vcol(name, i=0):
            o, w = VEC[name]
            return vecs.ap[:, o + i:o + i + 1]

        def cf(name):
            o, w = CF[name]
            return cstf.ap[:, o:o + w]

        ones_b = cstb.ap[:, 0, :]
        blockones_b = cstb.ap[:, 1, :]
        negtri_b = cstb.ap[:, 2, :]
        masklt_b = cstb.ap[:, 3, :]
        CK = cstb.keys() + cstf.keys() + vecs.keys() + ident.keys()

        def dve_tt(out, in0, in1, op, r, w, eng="dve"):
            P.op(eng, lambda e: e.tensor_tensor(out=out, in0=in0, in1=in1, op=op), reads=r, writes=w)

        def dve_ts(out, in0, s1, op0, r, w, s2=None, op1=None, eng="dve"):
            if op1 is None:
                P.op(eng, lambda e: e.tensor_scalar(out=out, in0=in0, scalar1=s1, scalar2=None, op0=op0), reads=r, writes=w)
            else:
                P.op(eng, lambda e: e.tensor_scalar(out=out, in0=in0, scalar1=s1, scalar2=s2, op0=op0, op1=op1), reads=r, writes=w)

        def dve_stt(out, in0, scalar, in1, op0, op1, r, w):
            P.op("dve", lambda e: e.scalar_tensor_tensor(out=out, in0=in0, scalar=scalar, in1=in1, op0=op0, op1=op1), reads=r, writes=w)

        def act(out, in_, func, r, w, scale=1.0, bias=0.0):
            if func == AF.Copy:
                P.op("act", lambda e: e.activation(out=out, in_=in_, func=func, scale=scale), reads=r, writes=w)
            else:
                P.op("act", lambda e: e.activation(out=out, in_=in_, func=func, scale=scale, bias=bias), reads=r, writes=w)

        def cp(out, in_, r, w, eng="dve"):
            P.op(eng, lambda e: e.tensor_copy(out=out, in_=in_), reads=r, writes=w)

        P.dma("sp", "c0", lambda e: e.dma_start(out=ident.ap, in_=ident_d[:, :]), writes=ident.keys())
        P.dma("sp", "c1", lambda e: e.dma_start(out=vecs.ap, in_=vecs_d[:, :]), writes=vecs.keys())
        P.dma("sp", "c2", lambda e: e.dma_start(out=cstf.ap, in_=cstf_d[:, :]), writes=cstf.keys())
        P.dma("pool", "c3", lambda e: e.dma_start(out=cstb.ap.rearrange("p a b -> p (a b)"), in_=cstb_d[:, :]), writes=cstb.keys())

        P.op("dve", lambda e: e.memset(EINV.ap, math.exp(-1.0)), writes=EINV.keys())
        o_q = VEC["qgA"][0]
        dve_ts(vecs.ap[:, o_q:o_q + 2], vecs.ap[:, o_q:o_q + 2], 0.125, ALU.mult, vecs.keys(), vecs.keys())

        KVr = R["KV"]

        def s5_setup():
            o = [0]

            def tmp(shape, dt=F32):
                b = KVr.buf(o[0], shape, dt)
                o[0] += ((b.nbytes + 511) // 512) * 512
                return b

            inB = tmp([48 + 1024])
            P.dma("sp", "s5in", lambda e: e.dma_start(out=inB.ap, in_=s5b_d[:, :]), writes=inB.keys())
            lamr = inB.ap[:, 0:16]
            lami = inB.ap[:, 16:32]
            logdt = inB.ap[:, 32:48]
            bre = inB.ap[:, 48:304].rearrange("p (q h) -> p q h", q=16)
            bim = inB.ap[:, 304:560].rearrange("p (q h) -> p q h", q=16)
            cre = inB.ap[:, 560:816].rearrange("p (q h) -> p q h", q=16)
            cim = inB.ap[:, 816:1072].rearrange("p (q h) -> p q h", q=16)
            kin = inB.keys()
            sm = tmp([8, 16])
            smk = sm.keys()
            dt_, lrdt, lidt, den, nr, fr, fi, t0 = [sm.ap[:, i, :] for i in range(8)]
            act(dt_, logdt, AF.Exp, kin, smk)
            dve_tt(lrdt, lamr, dt_, ALU.mult, kin + smk, smk)
            dve_tt(lidt, lami, dt_, ALU.mult, kin + smk, smk)
            NK = len(KLIST)
            big = [tmp([NK, 16]) for _ in range(6)]
            mag, Tt, t1, t2, cosv, sinv = big
            allk = sum([b.keys() for b in big], [])
            kv = cf("kv")
            kf = cf("kf")
            bc_q = lambda a: a.unsqueeze(1).to_broadcast([128, NK, 16])
            bc_k = lambda a: a.unsqueeze(2).to_broadcast([128, NK, 16])
            dve_tt(mag.ap, bc_q(lrdt), bc_k(kv), ALU.mult, smk + CK, allk)
            act(mag.ap, mag.ap, AF.Exp, allk, allk)
            dve_tt(Tt.ap, bc_q(lidt), bc_k(kf), ALU.mult, smk + CK, allk)

            def sincos(dst, shift, Tsrc, a1, a2, keys):
                dve_ts(a1.ap, Tsrc.ap, shift, ALU.add, keys, keys)
                cp(a2.ap.bitcast(I32), a1.ap, keys, keys)
                cp(a2.ap, a2.ap.bitcast(I32), keys, keys)
                dve_tt(a1.ap, a1.ap, a2.ap, ALU.subtract, keys, keys)
                dve_stt(a1.ap, a1.ap, 0.0, a1.ap, ALU.is_lt, ALU.add, keys, keys)
                act(dst.ap, a1.ap, AF.Sin, keys, keys, scale=TWO_PI * (1 - 1e-6), bias=-math.pi * (1 - 1e-6))

            sincos(cosv, 0.75 + 32.0, Tt, t1, t2, allk)
            sincos(sinv, 0.5 + 32.0, Tt, t1, t2, allk)
            dve_tt(cosv.ap, cosv.ap, mag.ap, ALU.mult, allk, allk)
            dve_tt(sinv.ap, sinv.ap, mag.ap, ALU.mult, allk, allk)
            Er = lambda k: cosv.ap[:, k + 7, :]
            Ei = lambda k: sinv.ap[:, k + 7, :]
            cp(Acf.ap[:, 0, :], Er(8), allk, Acf.keys())
            cp(Acf.ap[:, 1, :], Ei(8), allk, Acf.keys())
            dve_tt(den, lamr, lamr, ALU.mult, kin, smk)
            dve_tt(t0, lami, lami, ALU.mult, kin, smk)
            dve_tt(den, den, t0, ALU.add, smk, smk)
            P.op("dve", lambda e: e.reciprocal(out=den, in_=den), reads=smk, writes=smk)
            dve_ts(nr, Er(1), -1.0, ALU.add, allk, smk)
            dve_tt(fr, nr, lamr, ALU.mult, smk + kin, smk)
            dve_tt(t0, Ei(1), lami, ALU.mult, allk + kin, smk)
            dve_tt(fr, fr, t0, ALU.add, smk, smk)
            dve_tt(fr, fr, den, ALU.mult, smk, smk)
            dve_tt(fi, Ei(1), lamr, ALU.mult, allk + kin, smk)
            dve_tt(t0, nr, lami, ALU.mult, smk + kin, smk)
            dve_tt(fi, fi, t0, ALU.subtract, smk, smk)
            dve_tt(fi, fi, den, ALU.mult, smk, smk)
            bb = tmp([4, 16, 16])
            bbk = bb.keys()
            Bbr, Bbi, tA, tB = [bb.ap[:, i] for i in range(4)]
            bch = lambda a: a.unsqueeze(2).to_broadcast([128, 16, 16])
            dve_tt(Bbr, bre, bch(fr), ALU.mult, kin + smk, bbk)
            dve_tt(tA, bim, bch(fi), ALU.mult, kin + smk, bbk)
            dve_tt(Bbr, Bbr, tA, ALU.subtract, bbk, bbk)
            dve_tt(Bbi, bim, bch(fr), ALU.mult, kin + smk, bbk)
            dve_tt(tA, bre, bch(fi), ALU.mult, kin + smk, bbk)
            dve_tt(Bbi, Bbi, tA, ALU.add, bbk, bbk)
            Rr = tmp([9, 16, 16])
            Ri = tmp([9, 16, 16])
            Rt = tmp([9, 16, 16])
            rk = Rr.keys() + Ri.keys() + Rt.keys()
            bcC = lambda a: a.unsqueeze(1).to_broadcast([128, 9, 16, 16])
            bcE = lambda a: a.unsqueeze(3).to_broadcast([128, 9, 16, 16])
            Er9 = cosv.ap[:, 7:16, :]
            Ei9 = sinv.ap[:, 7:16, :]
            dve_tt(Rr.ap, bcC(cre), bcE(Er9), ALU.mult, kin + allk, rk)
            dve_tt(Rt.ap, bcC(cim), bcE(Ei9), ALU.mult, kin + allk, rk)
            dve_tt(Rr.ap, Rr.ap, Rt.ap, ALU.subtract, rk, rk)
            dve_tt(Ri.ap, bcC(cre), bcE(Ei9), ALU.mult, kin + allk, rk)
            dve_tt(Rt.ap, bcC(cim), bcE(Er9), ALU.mult, kin + allk, rk)
            dve_tt(Ri.ap, Ri.ap, Rt.ap, ALU.add, rk, rk)
            Bz = tmp([2, 16, 128])
            bzk = Bz.keys()
            P.op("dve", lambda e: e.memset(Bz.ap, 0.0), writes=bzk)
            for half in range(2):
                ps_ = slice(half * 64, half * 64 + 64)
                for pp in range(4):
                    cs_ = slice(pp * 32 + half * 16, pp * 32 + half * 16 + 16)
                    cp(Bz.ap[ps_, 0, pp:16:4, cs_], Bbr[ps_, pp:16:4, :], bbk, bzk)
                    dve_ts(Bz.ap[ps_, 1, pp:16:4, cs_], Bbi[ps_, pp:16:4, :], -1.0, ALU.mult, bbk, bzk)
            bd = cf("bd").rearrange("p (a b) -> p a b", a=8)
            for ct in range(4):
                bank = PB[ct]

                def mm(e, ct=ct, bank=bank):
                    last = None
                    for pp in range(4):
                        q = 4 * ct + pp
                        e.matmul(out=bank[:, 0:128], lhsT=Bz.ap[:, 0, q, :], rhs=Rr.ap[:, 0:8, q, :],
                                 start=(pp == 0), stop=False)
                        last = e.matmul(out=bank[:, 0:128], lhsT=Bz.ap[:, 1, q, :], rhs=Ri.ap[:, 0:8, q, :],
                                        start=False, stop=(pp == 3))
                    return last

                P.op("pe", mm, reads=bzk + rk, writes=PK[ct])
                src = bank[:, 0:128].rearrange("p (t h) -> p t h", t=8).unsqueeze(2).to_broadcast([128, 8, 8, 16])
                msk = bd.unsqueeze(1).to_broadcast([128, 8, 8, 16])
                dst = KD.ap[:, ct].rearrange("p t (g h) -> p t g h", g=8)
                dve_tt(dst, src, msk, ALU.mult, PK[ct] + CK, KD.keys())
            P.op("dve", lambda e: e.memset(YW.ap, 0.0), writes=YW.keys())
            for half in range(2):
                ps_ = slice(half * 64, half * 64 + 64)
                cs_ = slice(half * 16, half * 16 + 16)
                srcr = Rr.ap[ps_, 1:9].rearrange("p k q h -> p q k h")
                srci = Ri.ap[ps_, 1:9].rearrange("p k q h -> p q k h")
                cp(YW.ap[ps_, :, :, 0, cs_], srcr, rk, YW.keys())
                dve_ts(YW.ap[ps_, :, :, 1, cs_], srci, -1.0, ALU.mult, rk, YW.keys())
            o[0] = 0
            inA = tmp([5, 256])
            P.dma("sp", "s5in", lambda e: e.dma_start(out=inA.ap.rearrange("p a b -> p (a b)"), in_=s5a_d[:, :]), writes=inA.keys())
            kia = inA.keys()
            lamrA, lamiA, logdtA, breA, bimA = [inA.ap[:, i, :] for i in range(5)]
            smA = tmp([8, 256])
            sak = smA.keys()
            dtA, lrdtA, lidtA, denA, nrA, frA, fiA, t0A = [smA.ap[:, i, :] for i in range(8)]
            act(dtA, logdtA, AF.Exp, kia, sak)
            dve_tt(lrdtA, lamrA, dtA, ALU.mult, kia + sak, sak)
            dve_tt(lidtA, lamiA, dtA, ALU.mult, kia + sak, sak)
            bigA = [tmp([8, 256]) for _ in range(6)]
            magA, TA, t1A, t2A, cosA, sinA = bigA
            ak = sum([b.keys() for b in bigA], [])
            bq = lambda a: a.unsqueeze(1).to_broadcast([128, 8, 256])
            bk = lambda a: a.unsqueeze(2).to_broadcast([128, 8, 256])
            dve_tt(magA.ap, bq(lrdtA), bk(cf("kvA")), ALU.mult, sak + CK, ak)
            act(magA.ap, magA.ap, AF.Exp, ak, ak)
            dve_tt(TA.ap, bq(lidtA), bk(cf("kfA")), ALU.mult, sak + CK, ak)
            sincos(cosA, 0.75 + 32.0, TA, t1A, t2A, ak)
            sincos(sinA, 0.5 + 32.0, TA, t1A, t2A, ak)
            dve_tt(cosA.ap, cosA.ap, magA.ap, ALU.mult, ak, ak)
            dve_tt(sinA.ap, sinA.ap, magA.ap, ALU.mult, ak, ak)
            dve_tt(denA, lamrA, lamrA, ALU.mult, kia, sak)
            dve_tt(t0A, lamiA, lamiA, ALU.mult, kia, sak)
            dve_tt(denA, denA, t0A, ALU.add, sak, sak)
            P.op("dve", lambda e: e.reciprocal(out=denA, in_=denA), reads=sak, writes=sak)
            dve_ts(nrA, cosA.ap[:, 1, :], -1.0, ALU.add, ak, sak)
            dve_tt(frA, nrA, lamrA, ALU.mult, sak + kia, sak)
            dve_tt(t0A, sinA.ap[:, 1, :], lamiA, ALU.mult, ak + kia, sak)
            dve_tt(frA, frA, t0A, ALU.add, sak, sak)
            dve_tt(frA, frA, denA, ALU.mult, sak, sak)
            dve_tt(fiA, sinA.ap[:, 1, :], lamrA, ALU.mult, ak + kia, sak)
            dve_tt(t0A, nrA, lamiA, ALU.mult, sak + kia, sak)
            dve_tt(fiA, fiA, t0A, ALU.subtract, sak, sak)
            dve_tt(fiA, fiA, denA, ALU.mult, sak, sak)
            BbrA, BbiA = dtA, denA
            dve_tt(t0A, bimA, fiA, ALU.mult, kia + sak, sak)
            dve_tt(BbrA, breA, frA, ALU.mult, kia + sak, sak)
            dve_tt(BbrA, BbrA, t0A, ALU.subtract, sak, sak)
            dve_tt(t0A, breA, fiA, ALU.mult, kia + sak, sak)
            dve_tt(BbiA, bimA, frA, ALU.mult, kia + sak, sak)
            dve_tt(BbiA, BbiA, t0A, ALU.add, sak, sak)
            Wr, Wi, tW = magA, TA, t1A
            dve_tt(Wr.ap, bq(BbrA), cosA.ap, ALU.mult, sak + ak, ak)
            dve_tt(tW.ap, bq(BbiA), sinA.ap, ALU.mult, sak + ak, ak)
            dve_tt(Wr.ap, Wr.ap, tW.ap, ALU.subtract, ak, ak)
            dve_tt(Wi.ap, bq(BbrA), sinA.ap, ALU.mult, sak + ak, ak)
            dve_tt(tW.ap, bq(BbiA), cosA.ap, ALU.mult, sak + ak, ak)
            dve_tt(Wi.ap, Wi.ap, tW.ap, ALU.add, ak, ak)
            g2c = cf("g2col")
            for ri, Wx in enumerate((Wr, Wi)):
                src = Wx.ap.rearrange("p k (c s) -> p c k s", c=4)
                for g2p in range(2):
                    dst = VS.ap[:, :, ri, :, g2p * 64:(g2p + 1) * 64]
                    dve_ts(dst, src, g2c[:, g2p:g2p + 1], ALU.mult, ak + CK, VS.keys())
            P.op("dve", lambda e: e.memset(Xc.ap, 0.0), writes=Xc.keys())
            allkv = [("KV", k) for k in range(65536 // KG)]
            P.op("dve", lambda e: e.memset(LB.ap, 0.0), reads=allkv,
                 writes=LB.keys() + [("KTc", i) for i in range(NT)] + [("VCc", i) for i in range(NT)])

        P.ses = set(os.environ.get("K_SES_SETUP", "").split(","))
        s5_setup()
        P.ses = set(SAME_ENGINE_SYNC)

        WSPEC = {"w_in": w_in, "w_glu": w_glu, "w_out": w_out}
        for l_ in range(2):
            WSPEC["w_up%d" % l_] = w_up[l_]
            WSPEC["w_dn%d" % l_] = w_dn[l_]
            WSPEC["w_gt%d" % l_] = w_gt[l_]
            WSPEC["w_pu%d" % l_] = w_pu[l_]
        SCRT = {}
        WNKT = {}
        for name, W in WSPEC.items():
            K_, N_ = W.shape
            nkt = min(8, K_ // 128)
            WNKT[name] = nkt
            SCRT[name] = nc.dram_tensor("s_" + name, [N_ // 256, K_ // (128 * nkt), 128, nkt * 256], BF16, kind="Internal").ap()
        SCRT["pool"] = nc.dram_tensor("s_pool", [4, 1, 128, 512], BF16, kind="Internal").ap()
        WNKT["pool"] = 2

        CONV = []

        def convert(name, chunks):
            for (c, kc) in chunks:
                CONV.append((name, c, kc))

        convert("w_in", [(c, 0) for c in range(8)])
        convert("w_glu", [(c, 0) for c in range(2)])
        convert("w_out", [(c, 0) for c in range(4)])
        for l_ in range(2):
            if l_ == 1:
                convert("pool", [(g_, 0) for g_ in range(4)])
            for qq in range(4):
                convert("w_up%d" % l_, [(qq * 4 + c, 0) for c in range(4)])
                convert("w_dn%d" % l_, [(c, qq) for c in range(4)])
            for half in range(2):
                convert("w_gt%d" % l_, [(half * 2 + c, 0) for c in range(2)])
                convert("w_pu%d" % l_, [(half * 2 + c, 0) for c in range(2)])
        CONV_IDX = {k_: i_ for i_, k_ in enumerate(CONV)}
        cvp = [0]
        NCV = 8
        LOOKAHEAD = 6

        def conv_upto(idx):
            while cvp[0] <= min(idx, len(CONV) - 1):
                i_ = cvp[0]
                cvp[0] += 1
                name, c, kc = CONV[i_]
                nkt = WNKT[name]
                if name == "pool":
                    src = w_pool[c].rearrange("(k p) n -> p k n", p=128)
                else:
                    W = WSPEC[name]
                    src = W[kc * nkt * 128:(kc + 1) * nkt * 128, c * 256:(c + 1) * 256].rearrange("(k p) n -> p k n", p=128)
                dst = SCRT[name][c, kc].rearrange("p (k n) -> p k n", k=nkt)
                P.dma("pool", "cv%d" % (i_ % NCV), lambda e, src=src, dst=dst: e.dma_start(out=dst, in_=src),
                      writes=[("scr", name, c, kc), ("cvring", i_ % NCV)])

        wctr = [0]
        NSLOT = len(WS)

        def load_w(name, c, kc=0):
            s = wctr[0] % NSLOT
            wctr[0] += 1
            slot = WS[s]
            nkt = WNKT[name]
            conv_upto(CONV_IDX[(name, c, kc)] + LOOKAHEAD)
            src = SCRT[name][c, kc].rearrange("p (k n) -> p k n", k=nkt)
            P.dma("pool", "ws%d" % s, lambda e: e.dma_start(out=slot.ap[:, 0:nkt, :], in_=src),
                  reads=[("scr", name, c, kc)], writes=slot.keys())
            return slot

        pctr = [0]

        def banks(n):
            b = [(pctr[0] + i) % 8 for i in range(n)]
            pctr[0] = (pctr[0] + n) % 8
            return b

        def gemm_gen(name, c0, nchunks, kc, rhs_fn, rkeys_fn, evac):
            nkt = WNKT[name]
            for c in range(nchunks):
                slot = load_w(name, c0 + c, kc)
                bs = banks(2)
                for m in range(2):
                    b = bs[m]

                    def mm(e, m=m, b=b, slot=slot):
                        last = None
                        for kt in range(nkt):
                            last = e.matmul(out=PB[b][:, :], lhsT=slot.ap[:, kt, m * 128:(m + 1) * 128], rhs=rhs_fn(kt),
                                            start=(kt == 0), stop=(kt == nkt - 1))
                        return last

                    rk = sum([rkeys_fn(kt) for kt in range(nkt)], [])
                    P.op("pe", mm, reads=slot.keys() + rk, writes=PK[b])
                    evac(c * 2 + m, b)
                yield c

        def gemm_fm(*a):
            for _ in gemm_gen(*a):
                pass

        def interleave(g1, g2, lead=1):
            for _ in range(lead):
                next(g1, None)
            d1 = d2 = False
            while not (d1 and d2):
                if not d2:
                    d2 = next(g2, "done") == "done"
                if not d1:
                    d1 = next(g1, "done") == "done"

        STGb = [R["STG"].buf(i * 4096, [1024], F32) for i in range(2)]
        hn = R["HN"].buf(0, [8, 512], BF16)
        mixT = R["HN"].buf(0, [8, 512], BF16)
        sq = R["SQ"].buf(0, [8, 512], BF16)
        qz = R["SQ"].buf(0, [8, 512], BF16)
        gb = R["SQ"].buf(0, [4, 512], BF16)
        sig = R["SQ"].buf(0, [4, 512], F32)
        acc = R["AT"].buf(0, [4, 512], F32)
        y32 = R["AT"].buf(0, [4, 512], F32)
        aTs = [R["AT"].buf(0, [8, 512], BF16), R["SQ"].buf(0, [8, 512], BF16)]
        uTb = R["UB"].buf(0, [4, 512], BF16)
        uM = R["UB"].buf(4096, [4, 512], BF16)
        r32 = [R["UB"].buf(i * 2048, [512], F32) for i in range(4)]
        pT = R["UB"].buf(0, [2, 512], BF16)
        e32 = R["SCR"].buf(0, [512], F32)
        spb = [R["SCR"].buf(2048 + i * 1024, [512], BF16) for i in range(3)]
        wlb = [R["SCR"].buf(5120 + i * 1024, [512], BF16) for i in range(3)]
        tmpf = [R["SCR"].buf(i * 2048, [512], F32) for i in range(4)]
        SX = R["S5B"].buf(0, [2, 16, 64], F32)
        Xp = R["S5B"].buf(8192, [2, 16, 64], BF16)
        hnp = [R["S5B"].buf(i * 2112, [528], F32) for i in range(2)]
        ptmp = [R["S5B"].buf(4224 + i * 2112, [528], F32) for i in range(2)]
        ypool = [R["S5B"].buf(8448 + i * 2048, [2, 512], BF16) for i in range(2)]
        rstd = R["RS"].buf(0, [512], F32)
        sqt = R["RS"].buf(2048, [512], F32)

        def dbg_dump(src_ap, keys, ti_sel, ti, slot=None):
            if dbg_d is None or ti != ti_sel:
                return
            dst = dbg_d[:, 0:src_ap.shape[1], :] if slot is None else dbg_d[:, slot, :]
            return P.dma("sp", "dbg", lambda e: e.dma_start(out=dst, in_=src_ap), reads=keys)

        NPOOL = int(os.environ.get("K_NPOOL", "3"))

        def rmsnorm(gain_name, out_fn, out_keys_fn, fp32_out=False):
            b = banks(1)[0]
            for dt in range(8):
                act(sq.ap[:, dt, :], HT.ap[:, dt, :], AF.Square, HT.keys(dt), sq.keys(dt))
                P.op("pe", lambda e, dt=dt: e.matmul(out=PB[b][:, :], lhsT=ones_b, rhs=sq.ap[:, dt, :], start=(dt == 0), stop=(dt == 7)),
                     reads=sq.keys(dt) + CK, writes=PK[b])
            for i, dt in enumerate(range(8 - NPOOL, 8)):
                act(tmpf[i].ap, HT.ap[:, dt, :], AF.Copy, HT.keys(dt) + CK, tmpf[i].keys(), scale=vcol(gain_name, dt))
            act(sqt.ap, PB[b][:, :], AF.Ln, PK[b], sqt.keys(), scale=1.0 / 1024.0, bias=EPS)
            act(rstd.ap, sqt.ap, AF.Exp, sqt.keys(), rstd.keys(), scale=-0.5)
            for i, dt in enumerate(range(8 - NPOOL, 8)):
                P.op("pool", lambda e, i=i, dt=dt: e.tensor_tensor(out=out_fn(dt), in0=tmpf[i].ap, in1=rstd.ap, op=ALU.mult),
                     reads=tmpf[i].keys() + rstd.keys(), writes=out_keys_fn(dt))
            for dt in range(8 - NPOOL):
                dve_stt(out_fn(dt), HT.ap[:, dt, :], vcol(gain_name, dt), rstd.ap, ALU.mult, ALU.mult,
                        HT.keys(dt) + rstd.keys() + CK, out_keys_fn(dt))

        final_sigs = []

        for ti in range(NT):
            t0 = ti * TT
            P.stage = "t%d_load" % ti
            for blk in range(4):
                sb_ = STGb[blk % 2]
                r0 = t0 + blk * 128
                P.dma("sp", "stg%d" % (blk % 2), lambda e, sb_=sb_, r0=r0: e.dma_start(out=sb_.ap, in_=x_d[r0:r0 + 128, :]), writes=sb_.keys())
                for half in range(2):
                    b = banks(1)[0]

                    def tr(e, sb_=sb_, half=half, b=b):
                        last = None
                        for j in range(4):
                            dt = half * 4 + j
                            last = e.transpose(out=PB[b][:, j * 128:(j + 1) * 128], in_=sb_.ap[:, dt * 128:(dt + 1) * 128], identity=ident.ap)
                        return last

                    P.op("pe", tr, reads=sb_.keys() + CK, writes=PK[b])
                    dst = HT.ap[:, half * 4:half * 4 + 4, blk * 128:(blk + 1) * 128]
                    src = PB[b][:, :].rearrange("p (j t) -> p j t", j=4)
                    wk = sum([HT.keys(half * 4 + j) for j in range(4)], [])
                    if half == 0:
                        act(dst, src, AF.Copy, PK[b], wk)
                    else:
                        cp(dst, src, PK[b], wk)

            for layer in range(2):
                if layer == 0:
                    P.stage = "t%d_inproj" % ti
                    rmsnorm("ln_mix0", lambda dt: hn.ap[:, dt, :], lambda dt: hn.keys(dt))
                    hn_r = lambda kt: hn.ap[:, kt, :]
                    hn_k = lambda kt: hn.keys(kt)

                    def ev_u(m, b):
                        act(uTb.ap[:, m, :], PB[b][:, :], AF.Copy, PK[b], uTb.keys(m))

                    gemm_fm("w_in", 0, 2, 0, hn_r, hn_k, ev_u)
                    dve_ts(uM.ap[64:128], uTb.ap[64:128], cf("pp3")[64:128, :], ALU.mult, uTb.keys() + CK, uM.keys())

                    def qk_evac(is_q):
                        def ev(m, b):
                            s_ = tmpf[m % 2]
                            sb16 = s_.ap.bitcast(BF16)[:, 0:512]
                            act(sb16, PB[b][:, :], AF.Square, PK[b], s_.keys())
                            b2 = banks(1)[0]
                            P.op("pe", lambda e: e.matmul(out=PB[b2][:, :], lhsT=blockones_b, rhs=sb16, start=True, stop=True),
                                 reads=s_.keys() + CK, writes=PK[b2])
                            r_ = tmpf[2 + m % 2]
                            act(r_.ap, PB[b2][:, :], AF.Ln, PK[b2], r_.keys(), scale=1.0 / 64.0, bias=EPS)
                            act(r_.ap, r_.ap, AF.Exp, r_.keys(), r_.keys(), scale=-0.5)
                            if is_q:
                                dve_stt(qz.ap[:, 2 * m, :], PB[b][:, :], vcol("qgA"), r_.ap, ALU.mult, ALU.mult,
                                        PK[b] + r_.keys() + CK, qz.keys(2 * m))
                                dve_stt(qz.ap[:, 2 * m + 1, :], PB[b][:, :], vcol("qgB"), r_.ap, ALU.mult, ALU.mult,
                                        PK[b] + r_.keys() + CK, qz.keys(2 * m + 1))
                            else:
                                dve_stt(KT.ap[:, m, t0:t0 + 512], PB[b][:, :], vcol("kgA"), r_.ap, ALU.mult, ALU.mult,
                                        PK[b] + r_.keys() + CK, [("KTc", ti)])
                        return ev

                    gemm_fm("w_in", 2, 2, 0, hn_r, hn_k, qk_evac(True))
                    gemm_fm("w_in", 4, 2, 0, hn_r, hn_k, qk_evac(False))
                    for c in range(2):
                        slot = load_w("w_in", 6 + c, 0)
                        bs = banks(2)
                        for blk in range(4):
                            b = bs[blk // 2]
                            cs = slice((blk % 2) * 256, (blk % 2) * 256 + 256)

                            def mm(e, blk=blk, b=b, cs=cs, slot=slot):
                                last = None
                                for kt in range(8):
                                    last = e.matmul(out=PB[b][:, cs], lhsT=hn.ap[:, kt, blk * 128:(blk + 1) * 128], rhs=slot.ap[:, kt, :],
                                                    start=(kt == 0), stop=(kt == 7))
                                return last

                            P.op("pe", mm, reads=slot.keys() + hn.keys(), writes=[("pbh%d" % b, blk % 2)] + PK[b])
                        for blk in range(4):
                            b = bs[blk // 2]
                            cs = slice((blk % 2) * 256, (blk % 2) * 256 + 256)
                            dst = VC.ap[:, ti * 4 + blk, c * 256:(c + 1) * 256]
                            if blk % 2 == 0:
                                act(dst, PB[b][:, cs], AF.Copy, PK[b], [("VCc", ti)])
                            else:
                                cp(dst, PB[b][:, cs], PK[b], [("VCc", ti)])

                    P.stage = "t%d_att" % ti
                    sbanks = [6, 7, 6, 7]
                    t1_, t2_, t3_, t4_ = [RT.ap[:, i, :] for i in range(4)]
                    tk = RT.keys()
                    Ar = Acf.ap[:, 0, :]
                    Ai = Acf.ap[:, 1, :]
                    sxk = SX.keys()
                    for grp in range(2):
                        def mmS(e, grp=grp):
                            last = None
                            for pp in (2 * grp, 2 * grp + 1):
                                bnk = PB[6 + pp % 2]
                                for ct in range(4):
                                    for ri in range(2):
                                        col = (ct * 2 + ri) * 64
                                        for j in range(8):
                                            k = 7 - j
                                            if pp < 3:
                                                rows = slice(pp * 32, pp * 32 + 32)
                                                rhs = uTb.ap[rows, ct, j:512:8]
                                            else:
                                                rows = slice(64, 128)
                                                rhs = uM.ap[rows, ct, j:512:8]
                                            last = e.matmul(out=bnk[:, col:col + 64], lhsT=VS.ap[rows, ct, ri, k, :], rhs=rhs,
                                                            start=(j == 0), stop=(j == 7))
                            return last

                        P.op("pe", mmS, reads=uTb.keys() + uM.keys() + VS.keys(), writes=PK[6] + PK[7])
                        for pp in (2 * grp, 2 * grp + 1):
                            bk_ = 6 + pp % 2
                            src = PB[bk_][:, :].rearrange("p (c r n) -> p r c n", c=4, r=2)
                            for ri in range(2):
                                dst = SX.ap[:, ri, pp:16:4, :]
                                cp(dst, src[:, ri], PK[bk_], sxk)
                    cp(Xp.ap[:, :, :, 0], Xc.ap, Xc.keys(), Xp.keys())

                    def rec_step(c):
                        pr = Xc.ap[:, 0, :] if c == 0 else SX.ap[:, 0, :, c - 1]
                        pi = Xc.ap[:, 1, :] if c == 0 else SX.ap[:, 1, :, c - 1]
                        rk_ = (Xc.keys() if c == 0 else []) + sxk + Acf.keys() + tk

                        def rec(e, pr=pr, pi=pi, c=c):
                            e.tensor_tensor(out=t1_, in0=Ar, in1=pr, op=ALU.mult)
                            e.tensor_tensor(out=t2_, in0=Ai, in1=pi, op=ALU.mult)
                            e.tensor_tensor(out=t3_, in0=Ar, in1=pi, op=ALU.mult)
                            e.tensor_tensor(out=t4_, in0=Ai, in1=pr, op=ALU.mult)
                            e.tensor_tensor(out=t1_, in0=t1_, in1=t2_, op=ALU.subtract)
                            e.tensor_tensor(out=t3_, in0=t3_, in1=t4_, op=ALU.add)
                            e.tensor_tensor(out=SX.ap[:, 0, :, c], in0=SX.ap[:, 0, :, c], in1=t1_, op=ALU.add)
                            return e.tensor_tensor(out=SX.ap[:, 1, :, c], in0=SX.ap[:, 1, :, c], in1=t3_, op=ALU.add)

                        P.op(REC_ENG, rec, reads=rk_, writes=sxk + tk)

                    steps = []
                    for h in range(8):
                        for kb in range(4 * ti + 3, -1, -1):
                            steps.append((h, kb))
                    nst = len(steps)
                    rec_per = (64 + nst - 1) // nst
                    rec_done = [0]

                    def geom(i):
                        h, kb = steps[i]
                        b_ = kb - 4 * ti
                        diag = b_ >= 0
                        c0 = 128 * b_ if diag else 0
                        sb0 = b_ if diag else 0
                        return h, kb, diag, c0, sb0, 512 - c0

                    def stageA(i):
                        h, kb, diag, c0, sb0, N = geom(i)
                        zb = i % 2
                        sp_ = spb[i % 3]
                        kt_l = KT.ap[:, h // 2, kb * 128:(kb + 1) * 128]
                        q_r = qz.ap[:, h, c0:512]
                        P.op("pe", lambda e: e.matmul(out=PB[zb][:, 0:N], lhsT=kt_l, rhs=q_r, start=True, stop=True),
                             reads=[("KTc", kb // 4)] + qz.keys(h), writes=PK[zb])
                        act(e32.ap[:, 0:N], PB[zb][:, 0:N], AF.Exp, PK[zb], e32.keys())
                        act(sp_.ap[:, 0:N], e32.ap[:, 0:N], AF.Ln, e32.keys(), sp_.keys(), bias=1.0)
                        if diag:
                            dve_tt(sp_.ap[:, 0:128], sp_.ap[:, 0:128], masklt_b, ALU.mult, sp_.keys() + CK, sp_.keys())

                    def stageB(i):
                        h, kb, diag, c0, sb0, N = geom(i)
                        ab = 2 + i % 2
                        sp_ = spb[i % 3]
                        wl_ = wlb[i % 3]
                        kt_l = KT.ap[:, h // 2, kb * 128:(kb + 1) * 128]
                        q_r = qz.ap[:, h, c0:512]

                        def mm2(e):
                            e.matmul(out=PB[ab][:, 0:N], lhsT=kt_l, rhs=q_r, start=True, stop=False)
                            return e.matmul(out=PB[ab][:, 0:N], lhsT=negtri_b, rhs=sp_.ap[:, 0:N], start=False, stop=True)

                        P.op("pe", mm2, reads=[("KTc", kb // 4)] + qz.keys(h) + sp_.keys() + CK, writes=PK[ab])
                        act(wl_.ap[:, 0:N], PB[ab][:, 0:N], AF.Exp, PK[ab], wl_.keys())
                        if diag:
                            dve_tt(wl_.ap[:, 0:128], wl_.ap[:, 0:128], masklt_b, ALU.mult, wl_.keys() + CK, wl_.keys())

                    def stageC(i):
                        h, kb, diag, c0, sb0, N = geom(i)
                        vb = 4 + i % 2
                        sp_ = spb[i % 3]
                        wl_ = wlb[i % 3]
                        par = h % 2
                        Cst = CE.ap[:, par, 0, :]
                        Est = CE.ap[:, par, 1, :]
                        cek = [("CE", par)]
                        nsb = 4 - sb0
                        v_r = VC.ap[:, kb, h * 64:(h + 1) * 64]

                        def mm3(e):
                            last = None
                            for ii in range(nsb):
                                sbq = sb0 + ii
                                cs = slice(ii * 128, (ii + 1) * 128)
                                e.matmul(out=PB[vb][:, sbq * 80:sbq * 80 + 64], lhsT=wl_.ap[:, cs], rhs=v_r, start=True, stop=True)
                                last = e.matmul(out=PB[vb][:, sbq * 80 + 64:sbq * 80 + 65], lhsT=sp_.ap[:, cs], rhs=ones_b[:, 0:1], start=True, stop=True)
                            return last

                        P.op("pe", mm3, reads=wl_.keys() + sp_.keys() + [("VCc", kb // 4)] + CK, writes=PK[vb])
                        cont0 = sb0 + 1 if diag else 0
                        if cont0 < 4:
                            if E_ON_POOL:
                                P.op("pool", lambda e: e.tensor_tensor(out=Est[:, cont0:4], in0=EINV.ap[:, cont0:4], in1=Cst[:, cont0:4], op=ALU.pow),
                                     reads=cek + EINV.keys(), writes=cek)
                            else:
                                act(Est[:, cont0:4], Cst[:, cont0:4], AF.Exp, cek, cek, scale=-1.0)
                        for sbq in range(sb0, 4):
                            pv = PB[vb][:, sbq * 80:sbq * 80 + 64]
                            dst = acc.ap[:, sbq, h * 64:(h + 1) * 64]
                            if diag and sbq == sb0:
                                cp(dst, pv, PK[vb], acc.keys(sbq))
                            else:
                                dve_stt(dst, pv, Est[:, sbq:sbq + 1], dst, ALU.mult, ALU.add, PK[vb] + cek + acc.keys(sbq), acc.keys(sbq))
                        pcs = PB[vb][:, 0:320].rearrange("p (s c) -> p s c", c=80)[:, :, 64]
                        if diag:
                            cp(Cst[:, sb0:sb0 + 1], pcs[:, sb0:sb0 + 1], PK[vb], cek)
                            if sb0 + 1 < 4:
                                dve_tt(Cst[:, sb0 + 1:4], Cst[:, sb0 + 1:4], pcs[:, sb0 + 1:4], ALU.add, PK[vb] + cek, cek)
                        else:
                            dve_tt(Cst, Cst, pcs, ALU.add, PK[vb] + cek, cek)
                        for _ in range(rec_per):
                            if rec_done[0] < 64:
                                rec_step(rec_done[0])
                                rec_done[0] += 1

                    for i in range(nst + 2):
                        if i < nst:
                            stageA(i)
                        if 0 <= i - 1 < nst:
                            stageB(i - 1)
                        if 0 <= i - 2 < nst:
                            stageC(i - 2)
                    while rec_done[0] < 64:
                        rec_step(rec_done[0])
                        rec_done[0] += 1
                    cp(Xp.ap[:, :, :, 1:64], SX.ap[:, :, :, 0:63], sxk, Xp.keys())
                    cp(Xc.ap, SX.ap[:, :, :, 63], sxk, Xc.keys())
                    dbg_dump(acc.ap, acc.keys(), 0, ti) if dbg == "att" else None
                    for ft in range(4):
                        b = banks(1)[0]

                        def tr(e, ft=ft, b=b):
                            last = None
                            for sbq in range(4):
                                last = e.transpose(out=PB[b][:, sbq * 128:(sbq + 1) * 128], in_=acc.ap[:, sbq, ft * 128:(ft + 1) * 128], identity=ident.ap)
                            return last

                        P.op("pe", tr, reads=acc.keys() + CK, writes=PK[b])
                        act(mixT.ap[:, 4 + ft, :], PB[b][:, :], AF.Copy, PK[b], mixT.keys(4 + ft))

                    P.stage = "t%d_s5out" % ti
                    ybanks = banks(4)
                    for ct in range(4):
                        bnk = PB[ybanks[ct]]

                        def mmY(e, ct=ct, bnk=bnk):
                            last = None
                            for ip in range(8):
                                for j in range(ip + 1):
                                    e.matmul(out=bnk[:, ip:512:8], lhsT=KD.ap[:, ct, ip - j, :], rhs=uTb.ap[:, ct, j:512:8],
                                             start=(ip == 0 and j == 0), stop=False, skip_group_check=True)
                            for ip in range(8):
                                for pp in range(4):
                                    q = 4 * ct + pp
                                    for ri in range(2):
                                        last = e.matmul(out=bnk[pp * 32:(pp + 1) * 32, ip:512:8], lhsT=YW.ap[:, q, ip, ri, :], rhs=Xp.ap[:, ri, q, :],
                                                        start=False, stop=(ri == 1 and ip == 7 and pp == 3), tile_position=(0, pp * 32), skip_group_check=True)
                            return last

                        P.op("pe", mmY, reads=uTb.keys(ct) + KD.keys() + YW.keys() + Xp.keys(), writes=PK[ybanks[ct]])
                        dve_stt(y32.ap[:, ct, :], uTb.ap[:, ct, :], vcol("s5d", ct), bnk[:, :], ALU.mult, ALU.add,
                                PK[ybanks[ct]] + uTb.keys(ct) + CK, y32.keys(ct))
                        act(y32.ap[:, ct, :], y32.ap[:, ct, :], AF.Gelu, y32.keys(ct), y32.keys(ct))
                        cp(gb.ap[:, ct, :], y32.ap[:, ct, :], y32.keys(ct), gb.keys(ct))
                    dbg_dump(y32.ap, y32.keys(), 0, ti) if dbg == "s5" else None

                    def ev_glu(m, b):
                        t_ = tmpf[m % 2]
                        act(t_.ap, PB[b][:, :], AF.Sigmoid, PK[b], t_.keys())
                        dve_tt(mixT.ap[:, m, :], y32.ap[:, m, :], t_.ap, ALU.mult, y32.keys(m) + t_.keys(), mixT.keys(m))

                    gemm_fm("w_glu", 0, 2, 0, lambda kt: gb.ap[:, kt, :], lambda kt: gb.keys(kt), ev_glu)

                    def ev_res(m, b):
                        dve_tt(HT.ap[:, m, :], HT.ap[:, m, :], PB[b][:, :], ALU.add, PK[b] + HT.keys(m), HT.keys(m))

                    gemm_fm("w_out", 0, 4, 0, lambda kt: mixT.ap[:, kt, :], lambda kt: mixT.keys(kt), ev_res)
                    if dbg == "mix0":
                        dbg_dump(HT.ap, HT.keys(), NT - 1, ti)
                else:
                    P.stage = "t%d_pool" % ti
                    act(sq.ap, HT.ap, AF.Square, HT.keys(), sq.keys())
                    b = banks(1)[0]

                    def mmn(e, b=b):
                        last = None
                        for dt in range(8):
                            last = e.matmul(out=PB[b][:, :], lhsT=ones_b, rhs=sq.ap[:, dt, :], start=(dt == 0), stop=(dt == 7))
                        return last

                    P.op("pe", mmn, reads=sq.keys() + CK, writes=PK[b])
                    act(sqt.ap, PB[b][:, :], AF.Ln, PK[b], sqt.keys(), scale=1.0 / 1024.0, bias=EPS)
                    act(rstd.ap, sqt.ap, AF.Exp, sqt.keys(), rstd.keys(), scale=-0.5)
                    slotp = None
                    for g, w in enumerate((2, 4, 8, 16)):
                        nlev = int(math.log2(w))
                        yp = ypool[g % 2]
                        for dl in range(2):
                            dt = 2 * g + dl
                            hp_ = hnp[dl]
                            a_, b_ = ptmp[0], ptmp[1]
                            cp(hp_.ap[:, 0:16], LB.ap[:, dt, :], LB.keys(), hp_.keys())
                            dve_stt(hp_.ap[:, 16:528], HT.ap[:, dt, :], vcol("ln_mix1", dt), rstd.ap, ALU.mult, ALU.mult,
                                    HT.keys(dt) + rstd.keys() + CK, hp_.keys())
                            cp(LB.ap[:, dt, :], hp_.ap[:, 512:528], hp_.keys(), LB.keys())
                            src = hp_
                            lo = 0
                            for lv in range(nlev):
                                sh = 1 << lv
                                dstb = a_ if lv % 2 == 0 else b_
                                lo2 = lo + sh
                                dve_tt(dstb.ap[:, lo2:528], src.ap[:, lo2:528], src.ap[:, lo2 - sh:528 - sh], ALU.add,
                                       src.keys() + dstb.keys(), dstb.keys())
                                src = dstb
                                lo = lo2
                            dve_stt(yp.ap[:, dl, :], src.ap[:, 16:528], 1.0 / w, hp_.ap[:, 16:528], ALU.mult, ALU.subtract,
                                    src.keys() + hp_.keys(), yp.keys())
                            if ti == 0:
                                cnt = cf("cnt").rearrange("p (g t) -> p g t", g=4)[:, g, :]
                                tfix = ptmp[0] if src is ptmp[1] else ptmp[1]
                                dve_tt(tfix.ap[:, 0:16], src.ap[:, 16:32], cnt, ALU.mult, src.keys() + CK + tfix.keys(), tfix.keys())
                                dve_tt(yp.ap[:, dl, 0:16], tfix.ap[:, 0:16], hp_.ap[:, 16:32], ALU.subtract, tfix.keys() + hp_.keys() + yp.keys(), yp.keys())
                        slot = load_w("pool", g, 0)
                        bs = banks(2)
                        for m in range(2):
                            b = bs[m]

                            def mm(e, m=m, b=b, slot=slot, yp=yp):
                                e.matmul(out=PB[b][:, :], lhsT=slot.ap[:, 0, m * 128:(m + 1) * 128], rhs=yp.ap[:, 0, :], start=True, stop=False)
                                return e.matmul(out=PB[b][:, :], lhsT=slot.ap[:, 1, m * 128:(m + 1) * 128], rhs=yp.ap[:, 1, :], start=False, stop=True)

                            P.op("pe", mm, reads=slot.keys() + yp.keys(), writes=PK[b])
                            dt = 2 * g + m
                            dve_stt(HT.ap[:, dt, :], PB[b][:, :], vcol("pscale", dt), HT.ap[:, dt, :], ALU.mult, ALU.add,
                                    PK[b] + HT.keys(dt) + CK, HT.keys(dt))

                P.stage = "t%d_mlp%d" % (ti, layer)
                rmsnorm("ln_mlp%d" % layer, lambda dt: hn.ap[:, dt, :], lambda dt: hn.keys(dt))
                def up_gen(qq):
                    aT = aTs[qq % 2]

                    def ev_up(m, b, aT=aT):
                        r_ = r32[m % 4]
                        act(r_.ap, PB[b][:, :], AF.Relu, PK[b], r_.keys())
                        dve_tt(aT.ap[:, m, :], r_.ap, r_.ap, ALU.mult, r_.keys(), aT.keys(m))

                    return gemm_gen("w_up%d" % layer, qq * 4, 4, 0, lambda kt: hn.ap[:, kt, :], lambda kt: hn.keys(kt), ev_up)

                def dn_gen(qq):
                    aT = aTs[qq % 2]

                    def ev_res(m, b):
                        dve_tt(HT.ap[:, m, :], HT.ap[:, m, :], PB[b][:, :], ALU.add, PK[b] + HT.keys(m), HT.keys(m))

                    return gemm_gen("w_dn%d" % layer, 0, 4, qq, lambda kt, aT=aT: aT.ap[:, kt, :], lambda kt, aT=aT: aT.keys(kt), ev_res)

                if MLP_PIPE:
                    for _ in up_gen(0):
                        pass
                    for qq in range(4):
                        if qq < 3:
                            interleave(up_gen(qq + 1), dn_gen(qq), lead=1)
                        else:
                            for _ in dn_gen(qq):
                                pass
                else:
                    for qq in range(4):
                        for _ in up_gen(qq):
                            pass
                        for _ in dn_gen(qq):
                            pass

                if dbg == "mlp0" and layer == 0:
                    dbg_dump(HT.ap, HT.keys(), NT - 1, ti)
                P.stage = "t%d_ple%d" % (ti, layer)
                rmsnorm("ln_ple%d" % layer, lambda dt: hn.ap[:, dt, :], lambda dt: hn.keys(dt))
                for blk in range(4):
                    sb_ = STGb[blk % 2]
                    r0 = t0 + blk * 128
                    P.dma("sp", "stg%d" % (blk % 2), lambda e, sb_=sb_, r0=r0, layer=layer: e.dma_start(out=sb_.ap[:, 0:256], in_=p_d[layer, r0:r0 + 128, :]), writes=sb_.keys())
                    b = banks(1)[0]

                    def trp(e, sb_=sb_, b=b):
                        e.transpose(out=PB[b][:, 0:128], in_=sb_.ap[:, 0:128], identity=ident.ap)
                        return e.transpose(out=PB[b][:, 128:256], in_=sb_.ap[:, 128:256], identity=ident.ap)

                    P.op("pe", trp, reads=sb_.keys() + CK, writes=PK[b])
                    act(pT.ap[:, :, blk * 128:(blk + 1) * 128], PB[b][:, 0:256].rearrange("p (k t) -> p k t", k=2), AF.Copy, PK[b], pT.keys())
                for half in range(2):
                    def ev_gate(m, b):
                        act(sig.ap[:, m, :], PB[b][:, :], AF.Sigmoid, PK[b], sig.keys(m))

                    gemm_fm("w_gt%d" % layer, half * 2, 2, 0, lambda kt: hn.ap[:, kt, :], lambda kt: hn.keys(kt), ev_gate)

                    def ev_pu(m, b, half=half):
                        t_ = tmpf[2 + m % 2]
                        dve_tt(t_.ap, PB[b][:, :], sig.ap[:, m, :], ALU.mult, PK[b] + sig.keys(m), t_.keys())
                        dt = half * 4 + m
                        dve_tt(HT.ap[:, dt, :], HT.ap[:, dt, :], t_.ap, ALU.add, t_.keys() + HT.keys(dt), HT.keys(dt))

                    gemm_fm("w_pu%d" % layer, half * 2, 2, 0, lambda kt: pT.ap[:, kt, :], lambda kt: pT.keys(), ev_pu)
                if dbg == "l0" and layer == 0:
                    dbg_dump(HT.ap, HT.keys(), NT - 1, ti)

            P.stage = "t%d_out" % ti
            for blk in range(4):
                sb_ = STGb[blk % 2]
                r0 = t0 + blk * 128
                for half in range(2):
                    b = banks(1)[0]

                    def tro(e, half=half, b=b, blk=blk):
                        last = None
                        for j in range(4):
                            dt = half * 4 + j
                            last = e.transpose(out=PB[b][:, j * 128:(j + 1) * 128], in_=HT.ap[:, dt, blk * 128:(blk + 1) * 128], identity=ident.ap)
                        return last

                    P.op("pe", tro, reads=HT.keys() + CK, writes=PK[b])
                    if half == 0:
                        act(sb_.ap[:, 0:512], PB[b][:, :], AF.Copy, PK[b], sb_.keys())
                    else:
                        cp(sb_.ap[:, 512:1024], PB[b][:, :], PK[b], sb_.keys())
                sg = P.dma("sp", "stg%d" % (blk % 2), lambda e, sb_=sb_, r0=r0: e.dma_start(out=y_d[r0:r0 + 128, :], in_=sb_.ap), reads=sb_.keys())
                final_sigs.append(sg)

        fs = {}
        for s, v in final_sigs:
            fs[s] = max(fs.get(s, 0), v)
        if dbg_d is not None and "d_dbg" in P.cnt:
            fs["d_dbg"] = P.cnt["d_dbg"]
        P.finish("sp", list(fs.items()))
        P.emit()
    return nc


def _prep_shared(inp):
    f = lambda a: np.ascontiguousarray(np.asarray(a, dtype=np.float32))
    sh = {}
    sh["w_in"] = f(inp["w_in_even"][0])
    sh["w_out"] = f(inp["w_out_even"][0])
    sh["w_glu"] = f(inp["s5_w_glu"][0])
    sh["w_up"] = f(inp["w_mlp_up"])
    sh["w_dn"] = f(inp["w_mlp_down"])
    sh["w_gt"] = f(inp["w_ple_gate"])
    sh["w_pu"] = f(inp["w_ple_up"])
    sh["w_pool"] = f(inp["pool_w"][0])
    col8 = lambda v: np.asarray(v, np.float32).reshape(-1, 128).T
    qg = np.asarray(inp["sb_q_gain"][0], np.float32)
    kg = np.asarray(inp["sb_k_gain"][0], np.float32)
    qg128 = np.concatenate([qg, qg])
    kg128 = np.concatenate([kg, kg])
    half = (np.arange(128) < 64)
    scale = np.float32(64 ** -0.5)
    vec = np.concatenate([
        col8(inp["ln_mix_even"][0]), col8(inp["ln_mlp"][0]), col8(inp["ln_ple"][0]),
        col8(inp["ln_mix_odd"][0]), col8(inp["ln_mlp"][1]), col8(inp["ln_ple"][1]),
        col8(inp["pool_scale"][0]), col8(inp["s5_d"][0]),
        np.where(half, qg128, 0)[:, None], np.where(~half, qg128, 0)[:, None], kg128[:, None]], axis=1)
    sh["vecs"] = f(vec)
    lamr = np.asarray(inp["s5_lambda_re"][0], np.float32)
    lami = np.asarray(inp["s5_lambda_im"][0], np.float32)
    logdt = np.asarray(inp["s5_log_dt"][0], np.float32)
    toB = lambda a: a.reshape(16, 2, 64).transpose(1, 2, 0).reshape(128, 16)
    logdtB = toB(np.tile(logdt[:, None], (1, 64)))
    toB3 = lambda a: a.reshape(16, 2, 64, 16).transpose(1, 2, 0, 3).reshape(128, 256)
    bre = np.asarray(inp["s5_b_re"][0], np.float32)
    bim = np.asarray(inp["s5_b_im"][0], np.float32)
    cre = np.asarray(inp["s5_c_re"][0], np.float32).transpose(0, 2, 1)
    cim = np.asarray(inp["s5_c_im"][0], np.float32).transpose(0, 2, 1)
    sh["s5B"] = f(np.concatenate([toB(lamr), toB(lami), logdtB, toB3(bre), toB3(bim), toB3(cre), toB3(cim)], axis=1))
    toA = lambda a: np.tile(a.reshape(4, 8, 1, 64), (1, 1, 16, 1)).transpose(1, 2, 0, 3).reshape(128, 256)
    toA3 = lambda a: a.reshape(4, 8, 64, 16).transpose(1, 3, 0, 2).reshape(128, 256)
    sh["s5A"] = f(np.concatenate([toA(lamr), toA(lami), toA(np.tile(logdt[:, None], (1, 64))), toA3(bre), toA3(bim)], axis=1))
    sh.update(_consts())
    sh["_scale"] = scale
    return sh


_NC_CACHE = {}


def kernel(**inputs):
    x = np.asarray(inputs["x"], np.float32)
    p = np.asarray(inputs["p"], np.float32)
    B, L, Dm = x.shape
    NT = L // TT
    sh = _prep_shared(inputs)
    sh.pop("_scale")
    if NT not in _NC_CACHE:
        _NC_CACHE[NT] = build(NT)
    nc = _NC_CACHE[NT]
    in_maps = []
    for b in range(B):
        m = dict(sh)
        m["x"] = np.ascontiguousarray(x[b])
        m["p"] = np.ascontiguousarray(p[:, b])
        in_maps.append(m)
    res = run_bass_kernel_spmd(nc, in_maps, core_ids=list(range(B)))
    out = np.stack([res.results[b]["y"] for b in range(B)], axis=0)
    return out.astype(np.float32)
```

```python
import math
from contextlib import ExitStack

import numpy as np
import concourse.bass as bass
import concourse.mybir as mybir
from concourse.bass_utils import run_bass_kernel_spmd

F32 = mybir.dt.float32
BF16 = mybir.dt.bfloat16
I32 = mybir.dt.int32
AF = mybir.ActivationFunctionType
ALU = mybir.AluOpType

TT = 512
EPS = 1e-6
KLIST = list(range(-7, 9))
TWO_PI = 2.0 * math.pi
import os
MLP_PIPE = bool(int(os.environ.get("K_MLPPIPE", "1")))
REC_ENG = os.environ.get("K_REC", "dve")
E_ON_POOL = bool(int(os.environ.get("K_EPOOL", "0")))
SAME_ENGINE_SYNC = set(os.environ.get("K_SES", "").split(","))


class Prog:
    def __init__(self, nc, stack):
        self.nc = nc
        self.stack = stack
        self.ops = {e: [] for e in ("pe", "act", "dve", "pool", "sp")}
        self.sems = {}
        self.cnt = {}
        self.keys = {}
        self.waited = {e: {} for e in self.ops}
        self.final = []
        self.stage = "setup"
        self.scopes = bool(int(os.environ.get("K_SCOPES", "0")))
        self.ses = set(SAME_ENGINE_SYNC)
        self.near = {"dve": int(os.environ.get("K_NEAR_DVE", "2")), "act": int(os.environ.get("K_NEAR_ACT", "1")), "pool": 2}

    def sem(self, name):
        if name not in self.sems:
            self.sems[name] = self.stack.enter_context(self.nc.semaphore(name))
            self.cnt[name] = 0
        return self.sems[name]

    def _resolve(self, eng, reads, writes, mysig):
        waits = {}

        def add(sig):
            if sig is None:
                return
            s, v = sig
            if s == eng:
                if eng == "pe":
                    return
                if eng not in self.ses and (self.cnt[eng] - v) > self.near.get(eng, 0):
                    return
            if waits.get(s, 0) < v:
                waits[s] = v

        for k in reads:
            st = self.keys.setdefault(k, [None, []])
            add(st[0])
        for k in writes:
            st = self.keys.setdefault(k, [None, []])
            add(st[0])
            for r in st[1]:
                add(r)
        out = []
        for s, v in waits.items():
            if self.waited[eng].get(s, 0) >= v:
                continue
            self.waited[eng][s] = v
            out.append((s, v))
        for k in reads:
            self.keys[k][1].append(mysig)
        for k in writes:
            self.keys[k][0] = mysig
            self.keys[k][1] = []
        return out

    def op(self, eng, fn, reads=(), writes=()):
        self.sem(eng)
        self.cnt[eng] += 1
        mysig = (eng, self.cnt[eng])
        waits = self._resolve(eng, reads, writes, mysig)
        self.ops[eng].append((waits, fn, (eng, 1), self.stage))

    def dma(self, q, semkey, fn, reads=(), writes=()):
        name = "d_" + semkey
        self.sem(name)
        self.cnt[name] += 16
        mysig = (name, self.cnt[name])
        waits = self._resolve(q, reads, writes, mysig)
        self.ops[q].append((waits, fn, (name, 16), "dma"))
        return mysig

    def finish(self, eng, sigs):
        self.final.append((eng, sigs))

    def emit(self):
        nc = self.nc
        engmap = {"pe": "tensor", "act": "scalar", "dve": "vector", "pool": "gpsimd", "sp": "sync"}
        block = self.stack.enter_context(nc.Block())
        for e, attr in engmap.items():
            ops = self.ops[e]
            finals = [s for (fe, s) in self.final if fe == e]
            if not ops and not finals:
                continue

            def body(engine, ops=ops, finals=finals):
                for waits, fn, (sname, inc), stage in ops:
                    if self.scopes:
                        with nc.named_scope(stage):
                            for s, v in waits:
                                engine.wait_ge(self.sems[s], v)
                            inst = fn(engine)
                            inst.then_inc(self.sems[sname], inc)
                    else:
                        for s, v in waits:
                            engine.wait_ge(self.sems[s], v)
                        inst = fn(engine)
                        inst.then_inc(self.sems[sname], inc)
                for sigs in finals:
                    for s, v in sigs:
                        engine.wait_ge(self.sems[s], v)

            getattr(block, attr)(body)


KG = 512


class Region:
    def __init__(self, arena, name, woff, nbytes):
        self.arena, self.name, self.woff, self.nbytes = arena, name, woff, nbytes

    def buf(self, boff, shape, dt):
        return Buf(self, boff, shape, dt)


class Buf:
    def __init__(self, region, boff, shape, dt):
        esz = 4 if dt in (F32, I32) else 2
        n = int(np.prod(shape))
        assert boff % 4 == 0 and boff + n * esz <= region.nbytes, (region.name, boff, shape, region.nbytes)
        w0 = region.woff + boff // 4
        nw = (n * esz + 3) // 4
        ap = region.arena[:, w0:w0 + nw]
        if dt != F32:
            ap = ap.bitcast(dt)
        if len(shape) > 1:
            names = " ".join("a%d" % i for i in range(len(shape)))
            kw = {"a%d" % i: shape[i] for i in range(1, len(shape))}
            ap = ap.rearrange("p (%s) -> p %s" % (names, names), **kw)
        self.ap, self.region, self.boff, self.shape, self.esz = ap, region, boff, tuple(shape), esz
        self.nbytes = n * esz
        self.blk = (n // shape[0]) * esz

    def keys(self, lo=None, hi=None):
        if lo is None:
            b0, b1 = self.boff, self.boff + self.nbytes
        else:
            hi = lo + 1 if hi is None else hi
            b0, b1 = self.boff + lo * self.blk, self.boff + hi * self.blk
        return [(self.region.name, k) for k in range(b0 // KG, (b1 + KG - 1) // KG)]


def _consts():
    c = {}
    c["ident"] = np.eye(128, dtype=np.float32)
    jj = np.arange(128)[:, None]
    tt = np.arange(128)[None, :]
    ones = np.ones((128, 128), np.float32)
    blockones = ((jj // 64) == (tt // 64)).astype(np.float32)
    negtri = -(jj >= tt).astype(np.float32)
    masklt = (jj < tt).astype(np.float32)
    c["cstb"] = np.concatenate([ones, blockones, negtri, masklt], axis=1)
    part = np.arange(128)
    g8 = part // 16
    bd = (g8[:, None, None] == np.arange(8)[None, :, None]) * np.ones((1, 1, 16))
    g2 = (part // 16) % 2
    g2col = (g2[:, None] == np.arange(2)[None, :]).astype(np.float32)
    pp3 = (part >= 96).astype(np.float32)[:, None]
    halfA = (part < 64).astype(np.float32)[:, None]
    halfB = (part >= 64).astype(np.float32)[:, None]
    kv = np.tile(np.array(KLIST, np.float32)[None, :], (128, 1))
    kf = kv / TWO_PI
    kvA = np.tile(np.arange(8, dtype=np.float32)[None, :], (128, 1))
    kfA = kvA / TWO_PI
    cnt = np.zeros((128, 4, 16), np.float32)
    for g, w in enumerate((2, 4, 8, 16)):
        cnt[:, g, :] = 1.0 / np.minimum(np.arange(16) + 1, w)
    bands = np.zeros((128, 16, 128), np.float64)
    tq = np.arange(128)[:, None]
    tt_ = np.arange(128)[None, :]
    for g, w in enumerate((2, 4, 8, 16)):
        main = ((tq <= tt_) & (tq > tt_ - w)) / float(w) - (tq == tt_)
        spill = ((tq - 128) > (tt_ - w)) / float(w)
        cntv = np.minimum(tt_ + 1, w)
        b0 = ((tq <= tt_) & (tq > tt_ - w)) / cntv - (tq == tt_)
        import ml_dtypes
        hi = b0.astype(np.float32).astype(ml_dtypes.bfloat16).astype(np.float64)
        lo = b0 - hi
        bands[:, 4 * g + 0] = main
        bands[:, 4 * g + 1] = spill
        bands[:, 4 * g + 2] = hi
        bands[:, 4 * g + 3] = lo
    c["bands"] = bands.reshape(128, 2048).astype(np.float32)
    c["cstf"] = np.concatenate([bd.reshape(128, 128).astype(np.float32), g2col, pp3, halfA, halfB,
                                kv, kf, kvA, kfA, cnt.reshape(128, 64)], axis=1).astype(np.float32)
    return c


CF = {}
_o = 0
for _n, _w in (("bd", 128), ("g2col", 2), ("pp3", 1), ("halfA", 1), ("halfB", 1), ("kv", 16), ("kf", 16),
               ("kvA", 8), ("kfA", 8), ("cnt", 64)):
    CF[_n] = (_o, _w)
    _o += _w
NCF = _o

VEC = {}
_o = 0
for _n, _w in (("ln_mix0", 8), ("ln_mlp0", 8), ("ln_ple0", 8), ("ln_mix1", 8), ("ln_mlp1", 8), ("ln_ple1", 8),
               ("pscale", 8), ("s5d", 4), ("qgA", 1), ("qgB", 1), ("kgA", 1)):
    VEC[_n] = (_o, _w)
    _o += _w
NVEC = _o


def build(NT, dbg=None):
    L = NT * TT
    nc = bass.Bass("TRN2", target_bir_lowering=False)
    D = lambda n, s: nc.dram_tensor(n, s, F32, kind="ExternalInput").ap()
    x_d = D("x", [L, 1024])
    p_d = D("p", [2, L, 256])
    w_in = D("w_in", [1024, 2048])
    w_out = D("w_out", [1024, 1024])
    w_glu = D("w_glu", [512, 512])
    w_up = D("w_up", [2, 1024, 4096])
    w_dn = D("w_dn", [2, 4096, 1024])
    w_gt = D("w_gt", [2, 1024, 1024])
    w_pu = D("w_pu", [2, 256, 1024])
    w_pool = D("w_pool", [4, 256, 256])
    vecs_d = D("vecs", [128, NVEC])
    s5b_d = D("s5B", [128, 48 + 4 * 256])
    s5a_d = D("s5A", [128, 5 * 256])
    ident_d = D("ident", [128, 128])
    cstb_d = D("cstb", [128, 512])
    cstf_d = D("cstf", [128, NCF])
    bands_d = D("bands", [128, 2048])
    carry_d = nc.dram_tensor("poolcarry", [128, 1024], BF16, kind="Internal").ap()
    y_d = nc.dram_tensor("y", [L, 1024], F32, kind="ExternalOutput").ap()
    dbg_d = None
    if dbg:
        dbg_d = nc.dram_tensor("dbg", [128, 8, 512], F32, kind="ExternalOutput").ap()

    with ExitStack() as st:
        P = Prog(nc, st)
        sizes = [("KV", 65536), ("VS", 16384), ("YW", 16384), ("KD", 8192), ("WS", 3 * 4096), ("HT", 16384),
                 ("CST", 6144), ("STG", 8192), ("HN", 8192), ("SQ", 8192), ("AT", 8192), ("UB", 8192),
                 ("SCR", 8192), ("S5B", 12800), ("RS", 4096)]
        total_w = sum(s for _, s in sizes) // 4
        arena = st.enter_context(nc.sbuf_tensor("arena", [128, total_w], F32))
        R = {}
        wo = 0
        for n, s in sizes:
            R[n] = Region(arena, n, wo, s)
            wo += s // 4
        PB = [st.enter_context(nc.psum_tensor("pb%d" % i, [128, 512], F32)) for i in range(8)]
        PK = [[("pb%d" % i, 0)] for i in range(8)]

        KT = R["KV"].buf(0, [4, 4096], BF16)
        VC = R["KV"].buf(32768, [32, 512], BF16)
        VS = R["VS"].buf(0, [4, 2, 8, 128], BF16)
        YW = R["YW"].buf(0, [16, 8, 2, 32], BF16)
        KD = R["KD"].buf(0, [4, 8, 128], BF16)
        WS = [R["WS"].buf(i * 4096, [8, 256], BF16) for i in range(3)]
        HT = R["HT"].buf(0, [8, 512], F32)
        ident = R["CST"].buf(0, [128], F32)
        vecs = R["CST"].buf(512, [NVEC], F32)
        cstb = R["CST"].buf(1024, [4, 128], BF16)
        cstf = R["CST"].buf(2048, [NCF], F32)
        assert NCF * 4 <= 1024
        Acf = R["CST"].buf(3072, [2, 16], F32)
        Xc = R["CST"].buf(3584, [2, 16], F32)
        LB = R["CST"].buf(4096, [8, 16], F32)
        CE = R["CST"].buf(4608, [2, 2, 4], F32)
        RT = R["CST"].buf(5120, [4, 16], F32)
        EINV = R["CST"].buf(5632, [4], F32)

        def vcol(name, i=0):
            o, w = VEC[name]
            return vecs.ap[:, o + i:o + i + 1]

        def cf(name):
            o, w = CF[name]
            return cstf.ap[:, o:o + w]

        ones_b = cstb.ap[:, 0, :]
        blockones_b = cstb.ap[:, 1, :]
        negtri_b = cstb.ap[:, 2, :]
        masklt_b = cstb.ap[:, 3, :]
        CK = cstb.keys() + cstf.keys() + vecs.keys() + ident.keys()

        def dve_tt(out, in0, in1, op, r, w, eng="dve"):
            P.op(eng, lambda e: e.tensor_tensor(out=out, in0=in0, in1=in1, op=op), reads=r, writes=w)

        def dve_ts(out, in0, s1, op0, r, w, s2=None, op1=None, eng="dve"):
            if op1 is None:
                P.op(eng, lambda e: e.tensor_scalar(out=out, in0=in0, scalar1=s1, scalar2=None, op0=op0), reads=r, writes=w)
            else:
                P.op(eng, lambda e: e.tensor_scalar(out=out, in0=in0, scalar1=s1, scalar2=s2, op0=op0, op1=op1), reads=r, writes=w)

        def dve_stt(out, in0, scalar, in1, op0, op1, r, w):
            P.op("dve", lambda e: e.scalar_tensor_tensor(out=out, in0=in0, scalar=scalar, in1=in1, op0=op0, op1=op1), reads=r, writes=w)

        def act(out, in_, func, r, w, scale=1.0, bias=0.0):
            if func == AF.Copy:
                P.op("act", lambda e: e.activation(out=out, in_=in_, func=func, scale=scale), reads=r, writes=w)
            else:
                P.op("act", lambda e: e.activation(out=out, in_=in_, func=func, scale=scale, bias=bias), reads=r, writes=w)

        def cp(out, in_, r, w, eng="dve"):
            P.op(eng, lambda e: e.tensor_copy(out=out, in_=in_), reads=r, writes=w)

        P.dma("sp", "c0", lambda e: e.dma_start(out=ident.ap, in_=ident_d[:, :]), writes=ident.keys())
        P.dma("sp", "c1", lambda e: e.dma_start(out=vecs.ap, in_=vecs_d[:, :]), writes=vecs.keys())
        P.dma("sp", "c2", lambda e: e.dma_start(out=cstf.ap, in_=cstf_d[:, :]), writes=cstf.keys())
        P.dma("pool", "c3", lambda e: e.dma_start(out=cstb.ap.rearrange("p a b -> p (a b)"), in_=cstb_d[:, :]), writes=cstb.keys())

        P.op("dve", lambda e: e.memset(EINV.ap, math.exp(-1.0)), writes=EINV.keys())
        o_q = VEC["qgA"][0]
        dve_ts(vecs.ap[:, o_q:o_q + 2], vecs.ap[:, o_q:o_q + 2], 0.125, ALU.mult, vecs.keys(), vecs.keys())

        KVr = R["KV"]

        def s5_setup():
            o = [0]

            def tmp(shape, dt=F32):
                b = KVr.buf(o[0], shape, dt)
                o[0] += ((b.nbytes + 511) // 512) * 512
                return b

            inB = tmp([48 + 1024])
            P.dma("sp", "s5in", lambda e: e.dma_start(out=inB.ap, in_=s5b_d[:, :]), writes=inB.keys())
            lamr = inB.ap[:, 0:16]
            lami = inB.ap[:, 16:32]
            logdt = inB.ap[:, 32:48]
            bre = inB.ap[:, 48:304].rearrange("p (q h) -> p q h", q=16)
            bim = inB.ap[:, 304:560].rearrange("p (q h) -> p q h", q=16)
            cre = inB.ap[:, 560:816].rearrange("p (q h) -> p q h", q=16)
            cim = inB.ap[:, 816:1072].rearrange("p (q h) -> p q h", q=16)
            kin = inB.keys()
            sm = tmp([8, 16])
            smk = sm.keys()
            dt_, lrdt, lidt, den, nr, fr, fi, t0 = [sm.ap[:, i, :] for i in range(8)]
            act(dt_, logdt, AF.Exp, kin, smk)
            dve_tt(lrdt, lamr, dt_, ALU.mult, kin + smk, smk)
            dve_tt(lidt, lami, dt_, ALU.mult, kin + smk, smk)
            NK = len(KLIST)
            big = [tmp([NK, 16]) for _ in range(6)]
            mag, Tt, t1, t2, cosv, sinv = big
            allk = sum([b.keys() for b in big], [])
            kv = cf("kv")
            kf = cf("kf")
            bc_q = lambda a: a.unsqueeze(1).to_broadcast([128, NK, 16])
            bc_k = lambda a: a.unsqueeze(2).to_broadcast([128, NK, 16])
            dve_tt(mag.ap, bc_q(lrdt), bc_k(kv), ALU.mult, smk + CK, allk)
            act(mag.ap, mag.ap, AF.Exp, allk, allk)
            dve_tt(Tt.ap, bc_q(lidt), bc_k(kf), ALU.mult, smk + CK, allk)

            def sincos(dst, shift, Tsrc, a1, a2, keys):
                dve_ts(a1.ap, Tsrc.ap, shift, ALU.add, keys, keys)
                cp(a2.ap.bitcast(I32), a1.ap, keys, keys)
                cp(a2.ap, a2.ap.bitcast(I32), keys, keys)
                dve_tt(a1.ap, a1.ap, a2.ap, ALU.subtract, keys, keys)
                dve_stt(a1.ap, a1.ap, 0.0, a1.ap, ALU.is_lt, ALU.add, keys, keys)
                act(dst.ap, a1.ap, AF.Sin, keys, keys, scale=TWO_PI * (1 - 1e-6), bias=-math.pi * (1 - 1e-6))

            sincos(cosv, 0.75 + 32.0, Tt, t1, t2, allk)
            sincos(sinv, 0.5 + 32.0, Tt, t1, t2, allk)
            dve_tt(cosv.ap, cosv.ap, mag.ap, ALU.mult, allk, allk)
            dve_tt(sinv.ap, sinv.ap, mag.ap, ALU.mult, allk, allk)
            Er = lambda k: cosv.ap[:, k + 7, :]
            Ei = lambda k: sinv.ap[:, k + 7, :]
            cp(Acf.ap[:, 0, :], Er(8), allk, Acf.keys())
            cp(Acf.ap[:, 1, :], Ei(8), allk, Acf.keys())
            dve_tt(den, lamr, lamr, ALU.mult, kin, smk)
            dve_tt(t0, lami, lami, ALU.mult, kin, smk)
            dve_tt(den, den, t0, ALU.add, smk, smk)
            P.op("dve", lambda e: e.reciprocal(out=den, in_=den), reads=smk, writes=smk)
            dve_ts(nr, Er(1), -1.0, ALU.add, allk, smk)
            dve_tt(fr, nr, lamr, ALU.mult, smk + kin, smk)
            dve_tt(t0, Ei(1), lami, ALU.mult, allk + kin, smk)
            dve_tt(fr, fr, t0, ALU.add, smk, smk)
            dve_tt(fr, fr, den, ALU.mult, smk, smk)
            dve_tt(fi, Ei(1), lamr, ALU.mult, allk + kin, smk)
            dve_tt(t0, nr, lami, ALU.mult, smk + kin, smk)
            dve_tt(fi, fi, t0, ALU.subtract, smk, smk)
            dve_tt(fi, fi, den, ALU.mult, smk, smk)
            bb = tmp([4, 16, 16])
            bbk = bb.keys()
            Bbr, Bbi, tA, tB = [bb.ap[:, i] for i in range(4)]
            bch = lambda a: a.unsqueeze(2).to_broadcast([128, 16, 16])
            dve_tt(Bbr, bre, bch(fr), ALU.mult, kin + smk, bbk)
            dve_tt(tA, bim, bch(fi), ALU.mult, kin + smk, bbk)
            dve_tt(Bbr, Bbr, tA, ALU.subtract, bbk, bbk)
            dve_tt(Bbi, bim, bch(fr), ALU.mult, kin + smk, bbk)
            dve_tt(tA, bre, bch(fi), ALU.mult, kin + smk, bbk)
            dve_tt(Bbi, Bbi, tA, ALU.add, bbk, bbk)
            Rr = tmp([9, 16, 16])
            Ri = tmp([9, 16, 16])
            Rt = tmp([9, 16, 16])
            rk = Rr.keys() + Ri.keys() + Rt.keys()
            bcC = lambda a: a.unsqueeze(1).to_broadcast([128, 9, 16, 16])
            bcE = lambda a: a.unsqueeze(3).to_broadcast([128, 9, 16, 16])
            Er9 = cosv.ap[:, 7:16, :]
            Ei9 = sinv.ap[:, 7:16, :]
            dve_tt(Rr.ap, bcC(cre), bcE(Er9), ALU.mult, kin + allk, rk)
            dve_tt(Rt.ap, bcC(cim), bcE(Ei9), ALU.mult, kin + allk, rk)
            dve_tt(Rr.ap, Rr.ap, Rt.ap, ALU.subtract, rk, rk)
            dve_tt(Ri.ap, bcC(cre), bcE(Ei9), ALU.mult, kin + allk, rk)
            dve_tt(Rt.ap, bcC(cim), bcE(Er9), ALU.mult, kin + allk, rk)
            dve_tt(Ri.ap, Ri.ap, Rt.ap, ALU.add, rk, rk)
            Bz = tmp([2, 16, 128])
            bzk = Bz.keys()
            P.op("dve", lambda e: e.memset(Bz.ap, 0.0), writes=bzk)
            for half in range(2):
                ps_ = slice(half * 64, half * 64 + 64)
                for pp in range(4):
                    cs_ = slice(pp * 32 + half * 16, pp * 32 + half * 16 + 16)
                    cp(Bz.ap[ps_, 0, pp:16:4, cs_], Bbr[ps_, pp:16:4, :], bbk, bzk)
                    dve_ts(Bz.ap[ps_, 1, pp:16:4, cs_], Bbi[ps_, pp:16:4, :], -1.0, ALU.mult, bbk, bzk)
            bd = cf("bd").rearrange("p (a b) -> p a b", a=8)
            for ct in range(4):
                bank = PB[ct]

                def mm(e, ct=ct, bank=bank):
                    last = None
                    for pp in range(4):
                        q = 4 * ct + pp
                        e.matmul(out=bank[:, 0:128], lhsT=Bz.ap[:, 0, q, :], rhs=Rr.ap[:, 0:8, q, :],
                                 start=(pp == 0), stop=False)
                        last = e.matmul(out=bank[:, 0:128], lhsT=Bz.ap[:, 1, q, :], rhs=Ri.ap[:, 0:8, q, :],
                                        start=False, stop=(pp == 3))
                    return last

                P.op("pe", mm, reads=bzk + rk, writes=PK[ct])
                src = bank[:, 0:128].rearrange("p (t h) -> p t h", t=8).unsqueeze(2).to_broadcast([128, 8, 8, 16])
                msk = bd.unsqueeze(1).to_broadcast([128, 8, 8, 16])
                dst = KD.ap[:, ct].rearrange("p t (g h) -> p t g h", g=8)
                dve_tt(dst, src, msk, ALU.mult, PK[ct] + CK, KD.keys())
            P.op("dve", lambda e: e.memset(YW.ap, 0.0), writes=YW.keys())
            for half in range(2):
                ps_ = slice(half * 64, half * 64 + 64)
                cs_ = slice(half * 16, half * 16 + 16)
                srcr = Rr.ap[ps_, 1:9].rearrange("p k q h -> p q k h")
                srci = Ri.ap[ps_, 1:9].rearrange("p k q h -> p q k h")
                cp(YW.ap[ps_, :, :, 0, cs_], srcr, rk, YW.keys())
                dve_ts(YW.ap[ps_, :, :, 1, cs_], srci, -1.0, ALU.mult, rk, YW.keys())
            o[0] = 0
            inA = tmp([5, 256])
            P.dma("sp", "s5in", lambda e: e.dma_start(out=inA.ap.rearrange("p a b -> p (a b)"), in_=s5a_d[:, :]), writes=inA.keys())
            kia = inA.keys()
            lamrA, lamiA, logdtA, breA, bimA = [inA.ap[:, i, :] for i in range(5)]
            smA = tmp([8, 256])
            sak = smA.keys()
            dtA, lrdtA, lidtA, denA, nrA, frA, fiA, t0A = [smA.ap[:, i, :] for i in range(8)]
            act(dtA, logdtA, AF.Exp, kia, sak)
            dve_tt(lrdtA, lamrA, dtA, ALU.mult, kia + sak, sak)
            dve_tt(lidtA, lamiA, dtA, ALU.mult, kia + sak, sak)
            bigA = [tmp([8, 256]) for _ in range(6)]
            magA, TA, t1A, t2A, cosA, sinA = bigA
            ak = sum([b.keys() for b in bigA], [])
            bq = lambda a: a.unsqueeze(1).to_broadcast([128, 8, 256])
            bk = lambda a: a.unsqueeze(2).to_broadcast([128, 8, 256])
            dve_tt(magA.ap, bq(lrdtA), bk(cf("kvA")), ALU.mult, sak + CK, ak)
            act(magA.ap, magA.ap, AF.Exp, ak, ak)
            dve_tt(TA.ap, bq(lidtA), bk(cf("kfA")), ALU.mult, sak + CK, ak)
            sincos(cosA, 0.75 + 32.0, TA, t1A, t2A, ak)
            sincos(sinA, 0.5 + 32.0, TA, t1A, t2A, ak)
            dve_tt(cosA.ap, cosA.ap, magA.ap, ALU.mult, ak, ak)
            dve_tt(sinA.ap, sinA.ap, magA.ap, ALU.mult, ak, ak)
            dve_tt(denA, lamrA, lamrA, ALU.mult, kia, sak)
            dve_tt(t0A, lamiA, lamiA, ALU.mult, kia, sak)
            dve_tt(denA, denA, t0A, ALU.add, sak, sak)
            P.op("dve", lambda e: e.reciprocal(out=denA, in_=denA), reads=sak, writes=sak)
            dve_ts(nrA, cosA.ap[:, 1, :], -1.0, ALU.add, ak, sak)
            dve_tt(frA, nrA, lamrA, ALU.mult, sak + kia, sak)
            dve_tt(t0A, sinA.ap[:, 1, :], lamiA, ALU.mult, ak + kia, sak)
            dve_tt(frA, frA, t0A, ALU.add, sak, sak)
            dve_tt(frA, frA, denA, ALU.mult, sak, sak)
            dve_tt(fiA, sinA.ap[:, 1, :], lamrA, ALU.mult, ak + kia, sak)
            dve_tt(t0A, nrA, lamiA, ALU.mult, sak + kia, sak)
            dve_tt(fiA, fiA, t0A, ALU.subtract, sak, sak)
            dve_tt(fiA, fiA, denA, ALU.mult, sak, sak)
            BbrA, BbiA = dtA, denA
            dve_tt(t0A, bimA, fiA, ALU.mult, kia + sak, sak)
            dve_tt(BbrA, breA, frA, ALU.mult, kia + sak, sak)
            dve_tt(BbrA, BbrA, t0A, ALU.subtract, sak, sak)
            dve_tt(t0A, breA, fiA, ALU.mult, kia + sak, sak)
            dve_tt(BbiA, bimA, frA, ALU.mult, kia + sak, sak)
            dve_tt(BbiA, BbiA, t0A, ALU.add, sak, sak)
            Wr, Wi, tW = magA, TA, t1A
            dve_tt(Wr.ap, bq(BbrA), cosA.ap, ALU.mult, sak + ak, ak)
            dve_tt(tW.ap, bq(BbiA), sinA.ap, ALU.mult, sak + ak, ak)
            dve_tt(Wr.ap, Wr.ap, tW.ap, ALU.subtract, ak, ak)
            dve_tt(Wi.ap, bq(BbrA), sinA.ap, ALU.mult, sak + ak, ak)
            dve_tt(tW.ap, bq(BbiA), cosA.ap, ALU.mult, sak + ak, ak)
            dve_tt(Wi.ap, Wi.ap, tW.ap, ALU.add, ak, ak)
            g2c = cf("g2col")
            for ri, Wx in enumerate((Wr, Wi)):
                src = Wx.ap.rearrange("p k (c s) -> p c k s", c=4)
                for g2p in range(2):
                    dst = VS.ap[:, :, ri, :, g2p * 64:(g2p + 1) * 64]
                    dve_ts(dst, src, g2c[:, g2p:g2p + 1], ALU.mult, ak + CK, VS.keys())
            P.op("dve", lambda e: e.memset(Xc.ap, 0.0), writes=Xc.keys())
            allkv = [("KV", k) for k in range(65536 // KG)]
            P.op("dve", lambda e: e.memset(LB.ap, 0.0), reads=allkv,
                 writes=LB.keys() + [("KTc", i) for i in range(NT)] + [("VCc", i) for i in range(NT)])

        P.ses = set(os.environ.get("K_SES_SETUP", "").split(","))
        s5_setup()
        P.ses = set(SAME_ENGINE_SYNC)

        WSPEC = {"w_in": w_in, "w_glu": w_glu, "w_out": w_out}
        for l_ in range(2):
            WSPEC["w_up%d" % l_] = w_up[l_]
            WSPEC["w_dn%d" % l_] = w_dn[l_]
            WSPEC["w_gt%d" % l_] = w_gt[l_]
            WSPEC["w_pu%d" % l_] = w_pu[l_]
        SCRT = {}
        WNKT = {}
        for name, W in WSPEC.items():
            K_, N_ = W.shape
            nkt = min(8, K_ // 128)
            WNKT[name] = nkt
            SCRT[name] = nc.dram_tensor("s_" + name, [N_ // 256, K_ // (128 * nkt), 128, nkt * 256], BF16, kind="Internal").ap()
        SCRT["pool"] = nc.dram_tensor("s_pool", [4, 1, 128, 512], BF16, kind="Internal").ap()
        WNKT["pool"] = 2

        CONV = []

        def convert(name, chunks):
            for (c, kc) in chunks:
                CONV.append((name, c, kc))

        convert("w_in", [(c, 0) for c in range(8)])
        convert("w_glu", [(c, 0) for c in range(2)])
        convert("w_out", [(c, 0) for c in range(4)])
        for l_ in range(2):
            if l_ == 1:
                convert("pool", [(g_, 0) for g_ in range(4)])
            for qq in range(4):
                convert("w_up%d" % l_, [(qq * 4 + c, 0) for c in range(4)])
                convert("w_dn%d" % l_, [(c, qq) for c in range(4)])
            for half in range(2):
                convert("w_gt%d" % l_, [(half * 2 + c, 0) for c in range(2)])
                convert("w_pu%d" % l_, [(half * 2 + c, 0) for c in range(2)])
        CONV_IDX = {k_: i_ for i_, k_ in enumerate(CONV)}
        cvp = [0]
        NCV = 8
        LOOKAHEAD = 6

        def conv_upto(idx):
            while cvp[0] <= min(idx, len(CONV) - 1):
                i_ = cvp[0]
                cvp[0] += 1
                name, c, kc = CONV[i_]
                nkt = WNKT[name]
                if name == "pool":
                    src = w_pool[c].rearrange("(k p) n -> p k n", p=128)
                else:
                    W = WSPEC[name]
                    src = W[kc * nkt * 128:(kc + 1) * nkt * 128, c * 256:(c + 1) * 256].rearrange("(k p) n -> p k n", p=128)
                dst = SCRT[name][c, kc].rearrange("p (k n) -> p k n", k=nkt)
                P.dma("pool", "cv%d" % (i_ % NCV), lambda e, src=src, dst=dst: e.dma_start(out=dst, in_=src),
                      writes=[("scr", name, c, kc), ("cvring", i_ % NCV)])

        wctr = [0]
        NSLOT = len(WS)

        def load_w(name, c, kc=0):
            s = wctr[0] % NSLOT
            wctr[0] += 1
            slot = WS[s]
            nkt = WNKT[name]
            conv_upto(CONV_IDX[(name, c, kc)] + LOOKAHEAD)
            src = SCRT[name][c, kc].rearrange("p (k n) -> p k n", k=nkt)
            P.dma("pool", "ws%d" % s, lambda e: e.dma_start(out=slot.ap[:, 0:nkt, :], in_=src),
                  reads=[("scr", name, c, kc)], writes=slot.keys())
            return slot

        pctr = [0]

        def banks(n):
            b = [(pctr[0] + i) % 8 for i in range(n)]
            pctr[0] = (pctr[0] + n) % 8
            return b

        def gemm_gen(name, c0, nchunks, kc, rhs_fn, rkeys_fn, evac):
            nkt = WNKT[name]
            for c in range(nchunks):
                slot = load_w(name, c0 + c, kc)
                bs = banks(2)
                for m in range(2):
                    b = bs[m]

                    def mm(e, m=m, b=b, slot=slot):
                        last = None
                        for kt in range(nkt):
                            last = e.matmul(out=PB[b][:, :], lhsT=slot.ap[:, kt, m * 128:(m + 1) * 128], rhs=rhs_fn(kt),
                                            start=(kt == 0), stop=(kt == nkt - 1))
                        return last

                    rk = sum([rkeys_fn(kt) for kt in range(nkt)], [])
                    P.op("pe", mm, reads=slot.keys() + rk, writes=PK[b])
                    evac(c * 2 + m, b)
                yield c

        def gemm_fm(*a):
            for _ in gemm_gen(*a):
                pass

        def interleave(g1, g2, lead=1):
            for _ in range(lead):
                next(g1, None)
            d1 = d2 = False
            while not (d1 and d2):
                if not d2:
                    d2 = next(g2, "done") == "done"
                if not d1:
                    d1 = next(g1, "done") == "done"

        STGb = [R["STG"].buf(i * 4096, [1024], F32) for i in range(2)]
        hn = R["HN"].buf(0, [8, 512], BF16)
        mixT = R["HN"].buf(0, [8, 512], BF16)
        sq = R["SQ"].buf(0, [8, 512], BF16)
        qz = R["SQ"].buf(0, [8, 512], BF16)
        gb = R["SQ"].buf(0, [4, 512], BF16)
        sig = R["SQ"].buf(0, [4, 512], F32)
        acc = R["AT"].buf(0, [4, 512], F32)
        y32 = R["AT"].buf(0, [4, 512], F32)
        aTs = [R["AT"].buf(0, [8, 512], BF16), R["SQ"].buf(0, [8, 512], BF16)]
        uTb = R["UB"].buf(0, [4, 512], BF16)
        uM = R["UB"].buf(4096, [4, 512], BF16)
        r32 = [R["UB"].buf(i * 2048, [512], F32) for i in range(4)]
        pT = R["UB"].buf(0, [2, 512], BF16)
        e32 = R["SCR"].buf(0, [512], F32)
        spb = [R["SCR"].buf(2048 + i * 1024, [512], BF16) for i in range(3)]
        wlb = [R["SCR"].buf(5120 + i * 1024, [512], BF16) for i in range(3)]
        tmpf = [R["SCR"].buf(i * 2048, [512], F32) for i in range(4)]
        SX = R["S5B"].buf(0, [2, 16, 64], F32)
        Xp = R["S5B"].buf(8192, [2, 16, 64], BF16)
        hnp = [R["S5B"].buf(i * 2112, [528], F32) for i in range(2)]
        ptmp = [R["S5B"].buf(4224 + i * 2112, [528], F32) for i in range(2)]
        ypool = [R["S5B"].buf(8448 + i * 2048, [2, 512], BF16) for i in range(2)]
        hTok = R["AT"].buf(0, [4, 1024], BF16)
        ypool8 = R["SQ"].buf(0, [8, 512], BF16)
        bandsb = R["S5B"].buf(0, [16, 128], BF16)
        carryb = R["S5B"].buf(4096, [1024], BF16)
        rstdc = R["RS"].buf(0, [4], F32)
        lnc = R["RS"].buf(512, [4], F32)
        rstd = R["RS"].buf(0, [512], F32)
        sqt = R["RS"].buf(2048, [512], F32)

        def dbg_dump(src_ap, keys, ti_sel, ti, slot=None):
            if dbg_d is None or ti != ti_sel:
                return
            dst = dbg_d[:, 0:src_ap.shape[1], :] if slot is None else dbg_d[:, slot, :]
            return P.dma("sp", "dbg", lambda e: e.dma_start(out=dst, in_=src_ap), reads=keys)

        NPOOL = int(os.environ.get("K_NPOOL", "3"))

        def rmsnorm(gain_name, out_fn, out_keys_fn, fp32_out=False):
            b = banks(1)[0]
            for dt in range(8):
                act(sq.ap[:, dt, :], HT.ap[:, dt, :], AF.Square, HT.keys(dt), sq.keys(dt))
                P.op("pe", lambda e, dt=dt: e.matmul(out=PB[b][:, :], lhsT=ones_b, rhs=sq.ap[:, dt, :], start=(dt == 0), stop=(dt == 7)),
                     reads=sq.keys(dt) + CK, writes=PK[b])
            for i, dt in enumerate(range(8 - NPOOL, 8)):
                act(tmpf[i].ap, HT.ap[:, dt, :], AF.Copy, HT.keys(dt) + CK, tmpf[i].keys(), scale=vcol(gain_name, dt))
            act(sqt.ap, PB[b][:, :], AF.Ln, PK[b], sqt.keys(), scale=1.0 / 1024.0, bias=EPS)
            act(rstd.ap, sqt.ap, AF.Exp, sqt.keys(), rstd.keys(), scale=-0.5)
            for i, dt in enumerate(range(8 - NPOOL, 8)):
                P.op("pool", lambda e, i=i, dt=dt: e.tensor_tensor(out=out_fn(dt), in0=tmpf[i].ap, in1=rstd.ap, op=ALU.mult),
                     reads=tmpf[i].keys() + rstd.keys(), writes=out_keys_fn(dt))
            for dt in range(8 - NPOOL):
                dve_stt(out_fn(dt), HT.ap[:, dt, :], vcol(gain_name, dt), rstd.ap, ALU.mult, ALU.mult,
                        HT.keys(dt) + rstd.keys() + CK, out_keys_fn(dt))

        final_sigs = []

        for ti in range(NT):
            t0 = ti * TT
            P.stage = "t%d_load" % ti
            for blk in range(4):
                sb_ = STGb[blk % 2]
                r0 = t0 + blk * 128
                P.dma("sp", "stg%d" % (blk % 2), lambda e, sb_=sb_, r0=r0: e.dma_start(out=sb_.ap, in_=x_d[r0:r0 + 128, :]), writes=sb_.keys())
                for half in range(2):
                    b = banks(1)[0]

                    def tr(e, sb_=sb_, half=half, b=b):
                        last = None
                        for j in range(4):
                            dt = half * 4 + j
                            last = e.transpose(out=PB[b][:, j * 128:(j + 1) * 128], in_=sb_.ap[:, dt * 128:(dt + 1) * 128], identity=ident.ap)
                        return last

                    P.op("pe", tr, reads=sb_.keys() + CK, writes=PK[b])
                    dst = HT.ap[:, half * 4:half * 4 + 4, blk * 128:(blk + 1) * 128]
                    src = PB[b][:, :].rearrange("p (j t) -> p j t", j=4)
                    wk = sum([HT.keys(half * 4 + j) for j in range(4)], [])
                    if half == 0:
                        act(dst, src, AF.Copy, PK[b], wk)
                    else:
                        cp(dst, src, PK[b], wk)

            for layer in range(2):
                if layer == 0:
                    P.stage = "t%d_inproj" % ti
                    rmsnorm("ln_mix0", lambda dt: hn.ap[:, dt, :], lambda dt: hn.keys(dt))
                    hn_r = lambda kt: hn.ap[:, kt, :]
                    hn_k = lambda kt: hn.keys(kt)

                    def ev_u(m, b):
                        act(uTb.ap[:, m, :], PB[b][:, :], AF.Copy, PK[b], uTb.keys(m))

                    gemm_fm("w_in", 0, 2, 0, hn_r, hn_k, ev_u)
                    dve_ts(uM.ap[64:128], uTb.ap[64:128], cf("pp3")[64:128, :], ALU.mult, uTb.keys() + CK, uM.keys())

                    def qk_evac(is_q):
                        def ev(m, b):
                            s_ = tmpf[m % 2]
                            sb16 = s_.ap.bitcast(BF16)[:, 0:512]
                            act(sb16, PB[b][:, :], AF.Square, PK[b], s_.keys())
                            b2 = banks(1)[0]
                            P.op("pe", lambda e: e.matmul(out=PB[b2][:, :], lhsT=blockones_b, rhs=sb16, start=True, stop=True),
                                 reads=s_.keys() + CK, writes=PK[b2])
                            r_ = tmpf[2 + m % 2]
                            act(r_.ap, PB[b2][:, :], AF.Ln, PK[b2], r_.keys(), scale=1.0 / 64.0, bias=EPS)
                            act(r_.ap, r_.ap, AF.Exp, r_.keys(), r_.keys(), scale=-0.5)
                            if is_q:
                                dve_stt(qz.ap[:, 2 * m, :], PB[b][:, :], vcol("qgA"), r_.ap, ALU.mult, ALU.mult,
                                        PK[b] + r_.keys() + CK, qz.keys(2 * m))
                                dve_stt(qz.ap[:, 2 * m + 1, :], PB[b][:, :], vcol("qgB"), r_.ap, ALU.mult, ALU.mult,
                                        PK[b] + r_.keys() + CK, qz.keys(2 * m + 1))
                            else:
                                dve_stt(KT.ap[:, m, t0:t0 + 512], PB[b][:, :], vcol("kgA"), r_.ap, ALU.mult, ALU.mult,
                                        PK[b] + r_.keys() + CK, [("KTc", ti)])
                        return ev

                    gemm_fm("w_in", 2, 2, 0, hn_r, hn_k, qk_evac(True))
                    gemm_fm("w_in", 4, 2, 0, hn_r, hn_k, qk_evac(False))
                    for c in range(2):
                        slot = load_w("w_in", 6 + c, 0)
                        bs = banks(2)
                        for blk in range(4):
                            b = bs[blk // 2]
                            cs = slice((blk % 2) * 256, (blk % 2) * 256 + 256)

                            def mm(e, blk=blk, b=b, cs=cs, slot=slot):
                                last = None
                                for kt in range(8):
                                    last = e.matmul(out=PB[b][:, cs], lhsT=hn.ap[:, kt, blk * 128:(blk + 1) * 128], rhs=slot.ap[:, kt, :],
                                                    start=(kt == 0), stop=(kt == 7))
                                return last

                            P.op("pe", mm, reads=slot.keys() + hn.keys(), writes=[("pbh%d" % b, blk % 2)] + PK[b])
                        for blk in range(4):
                            b = bs[blk // 2]
                            cs = slice((blk % 2) * 256, (blk % 2) * 256 + 256)
                            dst = VC.ap[:, ti * 4 + blk, c * 256:(c + 1) * 256]
                            if blk % 2 == 0:
                                act(dst, PB[b][:, cs], AF.Copy, PK[b], [("VCc", ti)])
                            else:
                                cp(dst, PB[b][:, cs], PK[b], [("VCc", ti)])

                    P.stage = "t%d_att" % ti
                    sbanks = [6, 7, 6, 7]
                    t1_, t2_, t3_, t4_ = [RT.ap[:, i, :] for i in range(4)]
                    tk = RT.keys()
                    Ar = Acf.ap[:, 0, :]
                    Ai = Acf.ap[:, 1, :]
                    sxk = SX.keys()
                    for grp in range(2):
                        def mmS(e, grp=grp):
                            last = None
                            for pp in (2 * grp, 2 * grp + 1):
                                bnk = PB[6 + pp % 2]
                                for ct in range(4):
                                    for ri in range(2):
                                        col = (ct * 2 + ri) * 64
                                        for j in range(8):
                                            k = 7 - j
                                            if pp < 3:
                                                rows = slice(pp * 32, pp * 32 + 32)
                                                rhs = uTb.ap[rows, ct, j:512:8]
                                            else:
                                                rows = slice(64, 128)
                                                rhs = uM.ap[rows, ct, j:512:8]
                                            last = e.matmul(out=bnk[:, col:col + 64], lhsT=VS.ap[rows, ct, ri, k, :], rhs=rhs,
                                                            start=(j == 0), stop=(j == 7))
                            return last

                        P.op("pe", mmS, reads=uTb.keys() + uM.keys() + VS.keys(), writes=PK[6] + PK[7])
                        for pp in (2 * grp, 2 * grp + 1):
                            bk_ = 6 + pp % 2
                            src = PB[bk_][:, :].rearrange("p (c r n) -> p r c n", c=4, r=2)
                            for ri in range(2):
                                dst = SX.ap[:, ri, pp:16:4, :]
                                cp(dst, src[:, ri], PK[bk_], sxk)
                    cp(Xp.ap[:, :, :, 0], Xc.ap, Xc.keys(), Xp.keys())

                    def rec_step(c):
                        pr = Xc.ap[:, 0, :] if c == 0 else SX.ap[:, 0, :, c - 1]
                        pi = Xc.ap[:, 1, :] if c == 0 else SX.ap[:, 1, :, c - 1]
                        rk_ = (Xc.keys() if c == 0 else []) + sxk + Acf.keys() + tk

                        def rec(e, pr=pr, pi=pi, c=c):
                            e.tensor_tensor(out=t1_, in0=Ar, in1=pr, op=ALU.mult)
                            e.tensor_tensor(out=t2_, in0=Ai, in1=pi, op=ALU.mult)
                            e.tensor_tensor(out=t3_, in0=Ar, in1=pi, op=ALU.mult)
                            e.tensor_tensor(out=t4_, in0=Ai, in1=pr, op=ALU.mult)
                            e.tensor_tensor(out=t1_, in0=t1_, in1=t2_, op=ALU.subtract)
                            e.tensor_tensor(out=t3_, in0=t3_, in1=t4_, op=ALU.add)
                            e.tensor_tensor(out=SX.ap[:, 0, :, c], in0=SX.ap[:, 0, :, c], in1=t1_, op=ALU.add)
                            return e.tensor_tensor(out=SX.ap[:, 1, :, c], in0=SX.ap[:, 1, :, c], in1=t3_, op=ALU.add)

                        P.op(REC_ENG, rec, reads=rk_, writes=sxk + tk)

                    steps = []
                    for h in range(8):
                        for kb in range(4 * ti + 3, -1, -1):
                            steps.append((h, kb))
                    nst = len(steps)
                    rec_per = (64 + nst - 1) // nst
                    rec_done = [0]

                    def geom(i):
                        h, kb = steps[i]
                        b_ = kb - 4 * ti
                        diag = b_ >= 0
                        c0 = 128 * b_ if diag else 0
                        sb0 = b_ if diag else 0
                        return h, kb, diag, c0, sb0, 512 - c0

                    def stageA(i):
                        h, kb, diag, c0, sb0, N = geom(i)
                        zb = i % 2
                        sp_ = spb[i % 3]
                        kt_l = KT.ap[:, h // 2, kb * 128:(kb + 1) * 128]
                        q_r = qz.ap[:, h, c0:512]
                        P.op("pe", lambda e: e.matmul(out=PB[zb][:, 0:N], lhsT=kt_l, rhs=q_r, start=True, stop=True),
                             reads=[("KTc", kb // 4)] + qz.keys(h), writes=PK[zb])
                        act(e32.ap[:, 0:N], PB[zb][:, 0:N], AF.Exp, PK[zb], e32.keys())
                        act(sp_.ap[:, 0:N], e32.ap[:, 0:N], AF.Ln, e32.keys(), sp_.keys(), bias=1.0)
                        if diag:
                            dve_tt(sp_.ap[:, 0:128], sp_.ap[:, 0:128], masklt_b, ALU.mult, sp_.keys() + CK, sp_.keys())

                    def stageB(i):
                        h, kb, diag, c0, sb0, N = geom(i)
                        ab = 2 + i % 2
                        sp_ = spb[i % 3]
                        wl_ = wlb[i % 3]
                        kt_l = KT.ap[:, h // 2, kb * 128:(kb + 1) * 128]
                        q_r = qz.ap[:, h, c0:512]

                        def mm2(e):
                            e.matmul(out=PB[ab][:, 0:N], lhsT=kt_l, rhs=q_r, start=True, stop=False)
                            return e.matmul(out=PB[ab][:, 0:N], lhsT=negtri_b, rhs=sp_.ap[:, 0:N], start=False, stop=True)

                        P.op("pe", mm2, reads=[("KTc", kb // 4)] + qz.keys(h) + sp_.keys() + CK, writes=PK[ab])
                        act(wl_.ap[:, 0:N], PB[ab][:, 0:N], AF.Exp, PK[ab], wl_.keys())
                        if diag:
                            dve_tt(wl_.ap[:, 0:128], wl_.ap[:, 0:128], masklt_b, ALU.mult, wl_.keys() + CK, wl_.keys())

                    def stageC(i):
                        h, kb, diag, c0, sb0, N = geom(i)
                        vb = 4 + i % 2
                        sp_ = spb[i % 3]
                        wl_ = wlb[i % 3]
                        par = h % 2
                        Cst = CE.ap[:, par, 0, :]
                        Est = CE.ap[:, par, 1, :]
                        cek = [("CE", par)]
                        nsb = 4 - sb0
                        v_r = VC.ap[:, kb, h * 64:(h + 1) * 64]

                        def mm3(e):
                            last = None
                            for ii in range(nsb):
                                sbq = sb0 + ii
                                cs = slice(ii * 128, (ii + 1) * 128)
                                e.matmul(out=PB[vb][:, sbq * 80:sbq * 80 + 64], lhsT=wl_.ap[:, cs], rhs=v_r, start=True, stop=True)
                                last = e.matmul(out=PB[vb][:, sbq * 80 + 64:sbq * 80 + 65], lhsT=sp_.ap[:, cs], rhs=ones_b[:, 0:1], start=True, stop=True)
                            return last

                        P.op("pe", mm3, reads=wl_.keys() + sp_.keys() + [("VCc", kb // 4)] + CK, writes=PK[vb])
                        cont0 = sb0 + 1 if diag else 0
                        if cont0 < 4:
                            if E_ON_POOL:
                                P.op("pool", lambda e: e.tensor_tensor(out=Est[:, cont0:4], in0=EINV.ap[:, cont0:4], in1=Cst[:, cont0:4], op=ALU.pow),
                                     reads=cek + EINV.keys(), writes=cek)
                            else:
                                act(Est[:, cont0:4], Cst[:, cont0:4], AF.Exp, cek, cek, scale=-1.0)
                        for sbq in range(sb0, 4):
                            pv = PB[vb][:, sbq * 80:sbq * 80 + 64]
                            dst = acc.ap[:, sbq, h * 64:(h + 1) * 64]
                            if diag and sbq == sb0:
                                cp(dst, pv, PK[vb], acc.keys(sbq))
                            else:
                                dve_stt(dst, pv, Est[:, sbq:sbq + 1], dst, ALU.mult, ALU.add, PK[vb] + cek + acc.keys(sbq), acc.keys(sbq))
                        pcs = PB[vb][:, 0:320].rearrange("p (s c) -> p s c", c=80)[:, :, 64]
                        if diag:
                            cp(Cst[:, sb0:sb0 + 1], pcs[:, sb0:sb0 + 1], PK[vb], cek)
                            if sb0 + 1 < 4:
                                dve_tt(Cst[:, sb0 + 1:4], Cst[:, sb0 + 1:4], pcs[:, sb0 + 1:4], ALU.add, PK[vb] + cek, cek)
                        else:
                            dve_tt(Cst, Cst, pcs, ALU.add, PK[vb] + cek, cek)
                        for _ in range(rec_per):
                            if rec_done[0] < 64:
                                rec_step(rec_done[0])
                                rec_done[0] += 1

                    for i in range(nst + 2):
                        if i < nst:
                            stageA(i)
                        if 0 <= i - 1 < nst:
                            stageB(i - 1)
                        if 0 <= i - 2 < nst:
                            stageC(i - 2)
                    while rec_done[0] < 64:
                        rec_step(rec_done[0])
                        rec_done[0] += 1
                    cp(Xp.ap[:, :, :, 1:64], SX.ap[:, :, :, 0:63], sxk, Xp.keys())
                    cp(Xc.ap, SX.ap[:, :, :, 63], sxk, Xc.keys())
                    dbg_dump(acc.ap, acc.keys(), 0, ti) if dbg == "att" else None
                    for ft in range(4):
                        b = banks(1)[0]

                        def tr(e, ft=ft, b=b):
                            last = None
                            for sbq in range(4):
                                last = e.transpose(out=PB[b][:, sbq * 128:(sbq + 1) * 128], in_=acc.ap[:, sbq, ft * 128:(ft + 1) * 128], identity=ident.ap)
                            return last

                        P.op("pe", tr, reads=acc.keys() + CK, writes=PK[b])
                        act(mixT.ap[:, 4 + ft, :], PB[b][:, :], AF.Copy, PK[b], mixT.keys(4 + ft))

                    P.stage = "t%d_s5out" % ti
                    ybanks = banks(4)
                    for ct in range(4):
                        bnk = PB[ybanks[ct]]

                        def mmY(e, ct=ct, bnk=bnk):
                            last = None
                            for ip in range(8):
                                for j in range(ip + 1):
                                    e.matmul(out=bnk[:, ip:512:8], lhsT=KD.ap[:, ct, ip - j, :], rhs=uTb.ap[:, ct, j:512:8],
                                             start=(ip == 0 and j == 0), stop=False, skip_group_check=True)
                            for ip in range(8):
                                for pp in range(4):
                                    q = 4 * ct + pp
                                    for ri in range(2):
                                        last = e.matmul(out=bnk[pp * 32:(pp + 1) * 32, ip:512:8], lhsT=YW.ap[:, q, ip, ri, :], rhs=Xp.ap[:, ri, q, :],
                                                        start=False, stop=(ri == 1 and ip == 7 and pp == 3), tile_position=(0, pp * 32), skip_group_check=True)
                            return last

                        P.op("pe", mmY, reads=uTb.keys(ct) + KD.keys() + YW.keys() + Xp.keys(), writes=PK[ybanks[ct]])
                        dve_stt(y32.ap[:, ct, :], uTb.ap[:, ct, :], vcol("s5d", ct), bnk[:, :], ALU.mult, ALU.add,
                                PK[ybanks[ct]] + uTb.keys(ct) + CK, y32.keys(ct))
                        act(y32.ap[:, ct, :], y32.ap[:, ct, :], AF.Gelu, y32.keys(ct), y32.keys(ct))
                        cp(gb.ap[:, ct, :], y32.ap[:, ct, :], y32.keys(ct), gb.keys(ct))
                    dbg_dump(y32.ap, y32.keys(), 0, ti) if dbg == "s5" else None

                    def ev_glu(m, b):
                        t_ = tmpf[m % 2]
                        act(t_.ap, PB[b][:, :], AF.Sigmoid, PK[b], t_.keys())
                        dve_tt(mixT.ap[:, m, :], y32.ap[:, m, :], t_.ap, ALU.mult, y32.keys(m) + t_.keys(), mixT.keys(m))

                    gemm_fm("w_glu", 0, 2, 0, lambda kt: gb.ap[:, kt, :], lambda kt: gb.keys(kt), ev_glu)

                    def ev_res(m, b):
                        dve_tt(HT.ap[:, m, :], HT.ap[:, m, :], PB[b][:, :], ALU.add, PK[b] + HT.keys(m), HT.keys(m))

                    gemm_fm("w_out", 0, 4, 0, lambda kt: mixT.ap[:, kt, :], lambda kt: mixT.keys(kt), ev_res)
                    if dbg == "mix0":
                        dbg_dump(HT.ap, HT.keys(), NT - 1, ti)
                else:
                    P.stage = "t%d_pool" % ti
                    P.dma("pool", "bands", lambda e: e.dma_start(out=bandsb.ap.rearrange("p a b -> p (a b)"), in_=bands_d[:, :]), writes=bandsb.keys())
                    if ti > 0:
                        P.dma("sp", "carry", lambda e: e.dma_start(out=carryb.ap, in_=carry_d[:, :]), reads=[("carryd", 0)], writes=carryb.keys())
                    bss = banks(1)[0]
                    for dt in range(8):
                        act(sq.ap[:, dt, :], HT.ap[:, dt, :], AF.Square, HT.keys(dt), sq.keys(dt))

                    def mmss(e, bss=bss):
                        last = None
                        for blk in range(4):
                            for dt in range(8):
                                last = e.matmul(out=PB[bss][:, blk:blk + 1], lhsT=sq.ap[:, dt, blk * 128:(blk + 1) * 128], rhs=ones_b[:, 0:1],
                                                start=(dt == 0), stop=(dt == 7))
                        return last

                    P.op("pe", mmss, reads=sq.keys() + CK, writes=PK[bss])
                    act(lnc.ap, PB[bss][:, 0:4], AF.Ln, PK[bss], lnc.keys(), scale=1.0 / 1024.0, bias=EPS)
                    act(rstdc.ap, lnc.ap, AF.Exp, lnc.keys(), rstdc.keys(), scale=-0.5)
                    for blk in range(4):
                        bs = banks(2)
                        for half in range(2):
                            b = bs[half]

                            def trh(e, half=half, b=b, blk=blk):
                                last = None
                                for j in range(4):
                                    dt = half * 4 + j
                                    last = e.transpose(out=PB[b][:, j * 128:(j + 1) * 128], in_=HT.ap[:, dt, blk * 128:(blk + 1) * 128], identity=ident.ap)
                                return last

                            P.op("pe", trh, reads=HT.keys() + CK, writes=PK[b])
                            dst = hTok.ap[:, blk, half * 512:(half + 1) * 512]
                            if half == 0:
                                act(dst, PB[b][:, :], AF.Copy, PK[b] + rstdc.keys(), hTok.keys(blk), scale=rstdc.ap[:, blk:blk + 1])
                            else:
                                dve_ts(dst, PB[b][:, :], rstdc.ap[:, blk:blk + 1], ALU.mult, PK[b] + rstdc.keys(), hTok.keys(blk))
                    for dt in range(8):
                        g = dt // 2
                        b = banks(1)[0]

                        def mmb(e, dt=dt, g=g, b=b, ti=ti):
                            last = None
                            cs = slice(dt * 128, (dt + 1) * 128)
                            for blk in range(4):
                                out = PB[b][:, blk * 128:(blk + 1) * 128]
                                first_seq = (ti == 0 and blk == 0)
                                has_spill = not first_seq
                                if first_seq:
                                    e.matmul(out=out, lhsT=hTok.ap[:, 0, cs], rhs=bandsb.ap[:, 4 * g + 2, :], start=True, stop=False)
                                    last = e.matmul(out=out, lhsT=hTok.ap[:, 0, cs], rhs=bandsb.ap[:, 4 * g + 3, :], start=False, stop=True)
                                else:
                                    e.matmul(out=out, lhsT=hTok.ap[:, blk, cs], rhs=bandsb.ap[:, 4 * g + 0, :], start=True, stop=False)
                                    prev = carryb.ap[:, cs] if blk == 0 else hTok.ap[:, blk - 1, cs]
                                    last = e.matmul(out=out, lhsT=prev, rhs=bandsb.ap[:, 4 * g + 1, :], start=False, stop=True)
                            return last

                        P.op("pe", mmb, reads=hTok.keys() + bandsb.keys() + carryb.keys(), writes=PK[b])
                        if dt % 2 == 0:
                            act(ypool8.ap[:, dt, :], PB[b][:, :], AF.Copy, PK[b] + CK, ypool8.keys(dt), scale=vcol("ln_mix1", dt))
                        else:
                            dve_ts(ypool8.ap[:, dt, :], PB[b][:, :], vcol("ln_mix1", dt), ALU.mult, PK[b] + CK, ypool8.keys(dt))
                    P.dma("sp", "carryw", lambda e: e.dma_start(out=carry_d[:, :], in_=hTok.ap[:, 3, :]), reads=hTok.keys(3), writes=[("carryd", 0)])
                    for g in range(4):
                        slot = load_w("pool", g, 0)
                        bs = banks(2)
                        for m in range(2):
                            b = bs[m]

                            def mm(e, m=m, b=b, slot=slot, g=g):
                                e.matmul(out=PB[b][:, :], lhsT=slot.ap[:, 0, m * 128:(m + 1) * 128], rhs=ypool8.ap[:, 2 * g, :], start=True, stop=False)
                                return e.matmul(out=PB[b][:, :], lhsT=slot.ap[:, 1, m * 128:(m + 1) * 128], rhs=ypool8.ap[:, 2 * g + 1, :], start=False, stop=True)

                            P.op("pe", mm, reads=slot.keys() + ypool8.keys(2 * g) + ypool8.keys(2 * g + 1), writes=PK[b])
                            dt = 2 * g + m
                            dve_stt(HT.ap[:, dt, :], PB[b][:, :], vcol("pscale", dt), HT.ap[:, dt, :], ALU.mult, ALU.add,
                                    PK[b] + HT.keys(dt) + CK, HT.keys(dt))

                if dbg == "pool" and layer == 1:
                    dbg_dump(HT.ap, HT.keys(), 0, ti)
                P.stage = "t%d_mlp%d" % (ti, layer)
                rmsnorm("ln_mlp%d" % layer, lambda dt: hn.ap[:, dt, :], lambda dt: hn.keys(dt))
                def up_gen(qq):
                    aT = aTs[qq % 2]

                    def ev_up(m, b, aT=aT):
                        r_ = r32[m % 4]
                        act(r_.ap, PB[b][:, :], AF.Relu, PK[b], r_.keys())
                        dve_tt(aT.ap[:, m, :], r_.ap, r_.ap, ALU.mult, r_.keys(), aT.keys(m))

                    return gemm_gen("w_up%d" % layer, qq * 4, 4, 0, lambda kt: hn.ap[:, kt, :], lambda kt: hn.keys(kt), ev_up)

                def dn_gen(qq):
                    aT = aTs[qq % 2]

                    def ev_res(m, b):
                        dve_tt(HT.ap[:, m, :], HT.ap[:, m, :], PB[b][:, :], ALU.add, PK[b] + HT.keys(m), HT.keys(m))

                    return gemm_gen("w_dn%d" % layer, 0, 4, qq, lambda kt, aT=aT: aT.ap[:, kt, :], lambda kt, aT=aT: aT.keys(kt), ev_res)

                if MLP_PIPE:
                    for _ in up_gen(0):
                        pass
                    for qq in range(4):
                        if qq < 3:
                            interleave(up_gen(qq + 1), dn_gen(qq), lead=1)
                        else:
                            for _ in dn_gen(qq):
                                pass
                else:
                    for qq in range(4):
                        for _ in up_gen(qq):
                            pass
                        for _ in dn_gen(qq):
                            pass

                if dbg == "mlp0" and layer == 0:
                    dbg_dump(HT.ap, HT.keys(), NT - 1, ti)
                P.stage = "t%d_ple%d" % (ti, layer)
                rmsnorm("ln_ple%d" % layer, lambda dt: hn.ap[:, dt, :], lambda dt: hn.keys(dt))
                for blk in range(4):
                    sb_ = STGb[blk % 2]
                    r0 = t0 + blk * 128
                    P.dma("sp", "stg%d" % (blk % 2), lambda e, sb_=sb_, r0=r0, layer=layer: e.dma_start(out=sb_.ap[:, 0:256], in_=p_d[layer, r0:r0 + 128, :]), writes=sb_.keys())
                    b = banks(1)[0]

                    def trp(e, sb_=sb_, b=b):
                        e.transpose(out=PB[b][:, 0:128], in_=sb_.ap[:, 0:128], identity=ident.ap)
                        return e.transpose(out=PB[b][:, 128:256], in_=sb_.ap[:, 128:256], identity=ident.ap)

                    P.op("pe", trp, reads=sb_.keys() + CK, writes=PK[b])
                    act(pT.ap[:, :, blk * 128:(blk + 1) * 128], PB[b][:, 0:256].rearrange("p (k t) -> p k t", k=2), AF.Copy, PK[b], pT.keys())
                for half in range(2):
                    def ev_gate(m, b):
                        act(sig.ap[:, m, :], PB[b][:, :], AF.Sigmoid, PK[b], sig.keys(m))

                    gemm_fm("w_gt%d" % layer, half * 2, 2, 0, lambda kt: hn.ap[:, kt, :], lambda kt: hn.keys(kt), ev_gate)

                    def ev_pu(m, b, half=half):
                        t_ = tmpf[2 + m % 2]
                        dve_tt(t_.ap, PB[b][:, :], sig.ap[:, m, :], ALU.mult, PK[b] + sig.keys(m), t_.keys())
                        dt = half * 4 + m
                        dve_tt(HT.ap[:, dt, :], HT.ap[:, dt, :], t_.ap, ALU.add, t_.keys() + HT.keys(dt), HT.keys(dt))

                    gemm_fm("w_pu%d" % layer, half * 2, 2, 0, lambda kt: pT.ap[:, kt, :], lambda kt: pT.keys(), ev_pu)
                if dbg == "l0" and layer == 0:
                    dbg_dump(HT.ap, HT.keys(), NT - 1, ti)

            P.stage = "t%d_out" % ti
            for blk in range(4):
                sb_ = STGb[blk % 2]
                r0 = t0 + blk * 128
                for half in range(2):
                    b = banks(1)[0]

                    def tro(e, half=half, b=b, blk=blk):
                        last = None
                        for j in range(4):
                            dt = half * 4 + j
                            last = e.transpose(out=PB[b][:, j * 128:(j + 1) * 128], in_=HT.ap[:, dt, blk * 128:(blk + 1) * 128], identity=ident.ap)
                        return last

                    P.op("pe", tro, reads=HT.keys() + CK, writes=PK[b])
                    if half == 0:
                        act(sb_.ap[:, 0:512], PB[b][:, :], AF.Copy, PK[b], sb_.keys())
                    else:
                        cp(sb_.ap[:, 512:1024], PB[b][:, :], PK[b], sb_.keys())
                sg = P.dma("sp", "stg%d" % (blk % 2), lambda e, sb_=sb_, r0=r0: e.dma_start(out=y_d[r0:r0 + 128, :], in_=sb_.ap), reads=sb_.keys())
                final_sigs.append(sg)

        fs = {}
        for s, v in final_sigs:
            fs[s] = max(fs.get(s, 0), v)
        if dbg_d is not None and "d_dbg" in P.cnt:
            fs["d_dbg"] = P.cnt["d_dbg"]
        P.finish("sp", list(fs.items()))
        P.emit()
    return nc


def _prep_shared(inp):
    f = lambda a: np.ascontiguousarray(np.asarray(a, dtype=np.float32))
    sh = {}
    sh["w_in"] = f(inp["w_in_even"][0])
    sh["w_out"] = f(inp["w_out_even"][0])
    sh["w_glu"] = f(inp["s5_w_glu"][0])
    sh["w_up"] = f(inp["w_mlp_up"])
    sh["w_dn"] = f(inp["w_mlp_down"])
    sh["w_gt"] = f(inp["w_ple_gate"])
    sh["w_pu"] = f(inp["w_ple_up"])
    sh["w_pool"] = f(inp["pool_w"][0])
    col8 = lambda v: np.asarray(v, np.float32).reshape(-1, 128).T
    qg = np.asarray(inp["sb_q_gain"][0], np.float32)
    kg = np.asarray(inp["sb_k_gain"][0], np.float32)
    qg128 = np.concatenate([qg, qg])
    kg128 = np.concatenate([kg, kg])
    half = (np.arange(128) < 64)
    scale = np.float32(64 ** -0.5)
    vec = np.concatenate([
        col8(inp["ln_mix_even"][0]), col8(inp["ln_mlp"][0]), col8(inp["ln_ple"][0]),
        col8(inp["ln_mix_odd"][0]), col8(inp["ln_mlp"][1]), col8(inp["ln_ple"][1]),
        col8(inp["pool_scale"][0]), col8(inp["s5_d"][0]),
        np.where(half, qg128, 0)[:, None], np.where(~half, qg128, 0)[:, None], kg128[:, None]], axis=1)
    sh["vecs"] = f(vec)
    lamr = np.asarray(inp["s5_lambda_re"][0], np.float32)
    lami = np.asarray(inp["s5_lambda_im"][0], np.float32)
    logdt = np.asarray(inp["s5_log_dt"][0], np.float32)
    toB = lambda a: a.reshape(16, 2, 64).transpose(1, 2, 0).reshape(128, 16)
    logdtB = toB(np.tile(logdt[:, None], (1, 64)))
    toB3 = lambda a: a.reshape(16, 2, 64, 16).transpose(1, 2, 0, 3).reshape(128, 256)
    bre = np.asarray(inp["s5_b_re"][0], np.float32)
    bim = np.asarray(inp["s5_b_im"][0], np.float32)
    cre = np.asarray(inp["s5_c_re"][0], np.float32).transpose(0, 2, 1)
    cim = np.asarray(inp["s5_c_im"][0], np.float32).transpose(0, 2, 1)
    sh["s5B"] = f(np.concatenate([toB(lamr), toB(lami), logdtB, toB3(bre), toB3(bim), toB3(cre), toB3(cim)], axis=1))
    toA = lambda a: np.tile(a.reshape(4, 8, 1, 64), (1, 1, 16, 1)).transpose(1, 2, 0, 3).reshape(128, 256)
    toA3 = lambda a: a.reshape(4, 8, 64, 16).transpose(1, 3, 0, 2).reshape(128, 256)
    sh["s5A"] = f(np.concatenate([toA(lamr), toA(lami), toA(np.tile(logdt[:, None], (1, 64))), toA3(bre), toA3(bim)], axis=1))
    sh.update(_consts())
    sh["_scale"] = scale
    return sh


_NC_CACHE = {}


def kernel(**inputs):
    x = np.asarray(inputs["x"], np.float32)
    p = np.asarray(inputs["p"], np.float32)
    B, L, Dm = x.shape
    NT = L // TT
    sh = _prep_shared(inputs)
    sh.pop("_scale")
    if NT not in _NC_CACHE:
        _NC_CACHE[NT] = build(NT)
    nc = _NC_CACHE[NT]
    in_maps = []
    for b in range(B):
        m = dict(sh)
        m["x"] = np.ascontiguousarray(x[b])
        m["p"] = np.ascontiguousarray(p[:, b])
        in_maps.append(m)
    res = run_bass_kernel_spmd(nc, in_maps, core_ids=list(range(B)))
    out = np.stack([res.results[b]["y"] for b in range(B)], axis=0)
    return out.astype(np.float32)
```

```python
import math
from contextlib import ExitStack

import numpy as np
import concourse.bass as bass
import concourse.mybir as mybir
from concourse.bass_utils import run_bass_kernel_spmd

F32 = mybir.dt.float32
BF16 = mybir.dt.bfloat16
I32 = mybir.dt.int32
AF = mybir.ActivationFunctionType
ALU = mybir.AluOpType

TT = 512
EPS = 1e-6
KLIST = list(range(-7, 9))
TWO_PI = 2.0 * math.pi
import os
MLP_PIPE = bool(int(os.environ.get("K_MLPPIPE", "1")))
REC_ENG = os.environ.get("K_REC", "dve")
E_ON_POOL = bool(int(os.environ.get("K_EPOOL", "0")))
SAME_ENGINE_SYNC = set(os.environ.get("K_SES", "").split(","))


class Prog:
    def __init__(self, nc, stack):
        self.nc = nc
        self.stack = stack
        self.ops = {e: [] for e in ("pe", "act", "dve", "pool", "sp")}
        self.sems = {}
        self.cnt = {}
        self.keys = {}
        self.waited = {e: {} for e in self.ops}
        self.final = []
        self.stage = "setup"
        self.scopes = bool(int(os.environ.get("K_SCOPES", "0")))
        self.ses = set(SAME_ENGINE_SYNC)
        self.near = {"dve": int(os.environ.get("K_NEAR_DVE", "2")), "act": int(os.environ.get("K_NEAR_ACT", "1")), "pool": 2}

    def sem(self, name):
        if name not in self.sems:
            self.sems[name] = self.stack.enter_context(self.nc.semaphore(name))
            self.cnt[name] = 0
        return self.sems[name]

    def _resolve(self, eng, reads, writes, mysig):
        waits = {}

        def add(sig):
            if sig is None:
                return
            s, v = sig
            if s == eng:
                if eng == "pe":
                    return
                if eng not in self.ses and (self.cnt[eng] - v) > self.near.get(eng, 0):
                    return
            if waits.get(s, 0) < v:
                waits[s] = v

        for k in reads:
            st = self.keys.setdefault(k, [None, []])
            add(st[0])
        for k in writes:
            st = self.keys.setdefault(k, [None, []])
            add(st[0])
            for r in st[1]:
                add(r)
        out = []
        for s, v in waits.items():
            if self.waited[eng].get(s, 0) >= v:
                continue
            self.waited[eng][s] = v
            out.append((s, v))
        for k in reads:
            self.keys[k][1].append(mysig)
        for k in writes:
            self.keys[k][0] = mysig
            self.keys[k][1] = []
        return out

    def op(self, eng, fn, reads=(), writes=()):
        self.sem(eng)
        self.cnt[eng] += 1
        mysig = (eng, self.cnt[eng])
        waits = self._resolve(eng, reads, writes, mysig)
        self.ops[eng].append((waits, fn, (eng, 1), self.stage))

    def dma(self, q, semkey, fn, reads=(), writes=()):
        name = "d_" + semkey
        self.sem(name)
        self.cnt[name] += 16
        mysig = (name, self.cnt[name])
        waits = self._resolve(q, reads, writes, mysig)
        self.ops[q].append((waits, fn, (name, 16), "dma"))
        return mysig

    def finish(self, eng, sigs):
        self.final.append((eng, sigs))

    def emit(self):
        nc = self.nc
        engmap = {"pe": "tensor", "act": "scalar", "dve": "vector", "pool": "gpsimd", "sp": "sync"}
        block = self.stack.enter_context(nc.Block())
        for e, attr in engmap.items():
            ops = self.ops[e]
            finals = [s for (fe, s) in self.final if fe == e]
            if not ops and not finals:
                continue

            def body(engine, ops=ops, finals=finals):
                for waits, fn, (sname, inc), stage in ops:
                    if self.scopes:
                        with nc.named_scope(stage):
                            for s, v in waits:
                                engine.wait_ge(self.sems[s], v)
                            inst = fn(engine)
                            inst.then_inc(self.sems[sname], inc)
                    else:
                        for s, v in waits:
                            engine.wait_ge(self.sems[s], v)
                        inst = fn(engine)
                        inst.then_inc(self.sems[sname], inc)
                for sigs in finals:
                    for s, v in sigs:
                        engine.wait_ge(self.sems[s], v)

            getattr(block, attr)(body)


KG = 512


class Region:
    def __init__(self, arena, name, woff, nbytes):
        self.arena, self.name, self.woff, self.nbytes = arena, name, woff, nbytes

    def buf(self, boff, shape, dt):
        return Buf(self, boff, shape, dt)


class Buf:
    def __init__(self, region, boff, shape, dt):
        esz = 4 if dt in (F32, I32) else 2
        n = int(np.prod(shape))
        assert boff % 4 == 0 and boff + n * esz <= region.nbytes, (region.name, boff, shape, region.nbytes)
        w0 = region.woff + boff // 4
        nw = (n * esz + 3) // 4
        ap = region.arena[:, w0:w0 + nw]
        if dt != F32:
            ap = ap.bitcast(dt)
        if len(shape) > 1:
            names = " ".join("a%d" % i for i in range(len(shape)))
            kw = {"a%d" % i: shape[i] for i in range(1, len(shape))}
            ap = ap.rearrange("p (%s) -> p %s" % (names, names), **kw)
        self.ap, self.region, self.boff, self.shape, self.esz = ap, region, boff, tuple(shape), esz
        self.nbytes = n * esz
        self.blk = (n // shape[0]) * esz

    def keys(self, lo=None, hi=None):
        if lo is None:
            b0, b1 = self.boff, self.boff + self.nbytes
        else:
            hi = lo + 1 if hi is None else hi
            b0, b1 = self.boff + lo * self.blk, self.boff + hi * self.blk
        return [(self.region.name, k) for k in range(b0 // KG, (b1 + KG - 1) // KG)]


def _consts():
    c = {}
    c["ident"] = np.eye(128, dtype=np.float32)
    jj = np.arange(128)[:, None]
    tt = np.arange(128)[None, :]
    ones = np.ones((128, 128), np.float32)
    blockones = ((jj // 64) == (tt // 64)).astype(np.float32)
    negtri = -(jj >= tt).astype(np.float32)
    masklt = (jj < tt).astype(np.float32)
    c["cstb"] = np.concatenate([ones, blockones, negtri, masklt], axis=1)
    part = np.arange(128)
    g8 = part // 16
    bd = (g8[:, None, None] == np.arange(8)[None, :, None]) * np.ones((1, 1, 16))
    g2 = (part // 16) % 2
    g2col = (g2[:, None] == np.arange(2)[None, :]).astype(np.float32)
    pp3 = (part >= 96).astype(np.float32)[:, None]
    halfA = (part < 64).astype(np.float32)[:, None]
    halfB = (part >= 64).astype(np.float32)[:, None]
    kv = np.tile(np.array(KLIST, np.float32)[None, :], (128, 1))
    kf = kv / TWO_PI
    kvA = np.tile(np.arange(8, dtype=np.float32)[None, :], (128, 1))
    kfA = kvA / TWO_PI
    cnt = np.zeros((128, 4, 16), np.float32)
    for g, w in enumerate((2, 4, 8, 16)):
        cnt[:, g, :] = 1.0 / np.minimum(np.arange(16) + 1, w)
    bands = np.zeros((128, 16, 128), np.float64)
    tq = np.arange(128)[:, None]
    tt_ = np.arange(128)[None, :]
    for g, w in enumerate((2, 4, 8, 16)):
        main = ((tq <= tt_) & (tq > tt_ - w)) / float(w) - (tq == tt_)
        spill = ((tq - 128) > (tt_ - w)) / float(w)
        cntv = np.minimum(tt_ + 1, w)
        b0 = ((tq <= tt_) & (tq > tt_ - w)) / cntv - (tq == tt_)
        import ml_dtypes
        hi = b0.astype(np.float32).astype(ml_dtypes.bfloat16).astype(np.float64)
        lo = b0 - hi
        bands[:, 4 * g + 0] = main
        bands[:, 4 * g + 1] = spill
        bands[:, 4 * g + 2] = hi
        bands[:, 4 * g + 3] = lo
    c["bands"] = bands.reshape(128, 2048).astype(np.float32)
    c["cstf"] = np.concatenate([bd.reshape(128, 128).astype(np.float32), g2col, pp3, halfA, halfB,
                                kv, kf, kvA, kfA, cnt.reshape(128, 64)], axis=1).astype(np.float32)
    return c


CF = {}
_o = 0
for _n, _w in (("bd", 128), ("g2col", 2), ("pp3", 1), ("halfA", 1), ("halfB", 1), ("kv", 16), ("kf", 16),
               ("kvA", 8), ("kfA", 8), ("cnt", 64)):
    CF[_n] = (_o, _w)
    _o += _w
NCF = _o

VEC = {}
_o = 0
for _n, _w in (("ln_mix0", 8), ("ln_mlp0", 8), ("ln_ple0", 8), ("ln_mix1", 8), ("ln_mlp1", 8), ("ln_ple1", 8),
               ("pscale", 8), ("s5d", 4), ("qgA", 1), ("qgB", 1), ("kgA", 1)):
    VEC[_n] = (_o, _w)
    _o += _w
NVEC = _o


def build(NT, dbg=None):
    L = NT * TT
    nc = bass.Bass("TRN2", target_bir_lowering=False)
    D = lambda n, s: nc.dram_tensor(n, s, F32, kind="ExternalInput").ap()
    x_d = D("x", [L, 1024])
    p_d = D("p", [2, L, 256])
    w_in = D("w_in", [1024, 2048])
    w_out = D("w_out", [1024, 1024])
    w_glu = D("w_glu", [512, 512])
    w_up = D("w_up", [2, 1024, 4096])
    w_dn = D("w_dn", [2, 4096, 1024])
    w_gt = D("w_gt", [2, 1024, 1024])
    w_pu = D("w_pu", [2, 256, 1024])
    w_pool = D("w_pool", [4, 256, 256])
    vecs_d = D("vecs", [128, NVEC])
    s5b_d = D("s5B", [128, 48 + 4 * 256])
    s5a_d = D("s5A", [128, 5 * 256])
    ident_d = D("ident", [128, 128])
    cstb_d = D("cstb", [128, 512])
    cstf_d = D("cstf", [128, NCF])
    bands_d = D("bands", [128, 2048])
    carry_d = nc.dram_tensor("poolcarry", [128, 1024], BF16, kind="Internal").ap()
    y_d = nc.dram_tensor("y", [L, 1024], F32, kind="ExternalOutput").ap()
    dbg_d = None
    if dbg:
        dbg_d = nc.dram_tensor("dbg", [128, 8, 512], F32, kind="ExternalOutput").ap()

    with ExitStack() as st:
        P = Prog(nc, st)
        sizes = [("KV", 65536), ("VS", 16384), ("YW", 16384), ("KD", 8192), ("WS", 3 * 4096), ("HT", 16384),
                 ("CST", 6144), ("STG", 8192), ("HN", 8192), ("SQ", 8192), ("AT", 8192), ("UB", 8192),
                 ("SCR", 8192), ("S5B", 12800), ("RS", 4096)]
        total_w = sum(s for _, s in sizes) // 4
        arena = st.enter_context(nc.sbuf_tensor("arena", [128, total_w], F32))
        R = {}
        wo = 0
        for n, s in sizes:
            R[n] = Region(arena, n, wo, s)
            wo += s // 4
        PB = [st.enter_context(nc.psum_tensor("pb%d" % i, [128, 512], F32)) for i in range(8)]
        PK = [[("pb%d" % i, 0)] for i in range(8)]

        KT = R["KV"].buf(0, [4, 4096], BF16)
        VC = R["KV"].buf(32768, [32, 512], BF16)
        VS = R["VS"].buf(0, [4, 2, 8, 128], BF16)
        YW = R["YW"].buf(0, [16, 8, 2, 32], BF16)
        KD = R["KD"].buf(0, [4, 8, 128], BF16)
        WS = [R["WS"].buf(i * 4096, [8, 256], BF16) for i in range(3)]
        HT = R["HT"].buf(0, [8, 512], F32)
        ident = R["CST"].buf(0, [128], F32)
        vecs = R["CST"].buf(512, [NVEC], F32)
        cstb = R["CST"].buf(1024, [4, 128], BF16)
        cstf = R["CST"].buf(2048, [NCF], F32)
        assert NCF * 4 <= 1024
        Acf = R["CST"].buf(3072, [2, 16], F32)
        Xc = R["CST"].buf(3584, [2, 16], F32)
        LB = R["CST"].buf(4096, [8, 16], F32)
        CE = R["CST"].buf(4608, [2, 2, 4], F32)
        RT = R["CST"].buf(5120, [4, 16], F32)
        EINV = R["CST"].buf(5632, [4], F32)

        def vcol(name, i=0):
            o, w = VEC[name]
            return vecs.ap[:, o + i:o + i + 1]

        def cf(name):
            o, w = CF[name]
            return cstf.ap[:, o:o + w]

        ones_b = cstb.ap[:, 0, :]
        blockones_b = cstb.ap[:, 1, :]
        negtri_b = cstb.ap[:, 2, :]
        masklt_b = cstb.ap[:, 3, :]
        CK = cstb.keys() + cstf.keys() + vecs.keys() + ident.keys()

        def dve_tt(out, in0, in1, op, r, w, eng="dve"):
            P.op(eng, lambda e: e.tensor_tensor(out=out, in0=in0, in1=in1, op=op), reads=r, writes=w)

        def dve_ts(out, in0, s1, op0, r, w, s2=None, op1=None, eng="dve"):
            if op1 is None:
                P.op(eng, lambda e: e.tensor_scalar(out=out, in0=in0, scalar1=s1, scalar2=None, op0=op0), reads=r, writes=w)
            else:
                P.op(eng, lambda e: e.tensor_scalar(out=out, in0=in0, scalar1=s1, scalar2=s2, op0=op0, op1=op1), reads=r, writes=w)

        def dve_stt(out, in0, scalar, in1, op0, op1, r, w):
            P.op("dve", lambda e: e.scalar_tensor_tensor(out=out, in0=in0, scalar=scalar, in1=in1, op0=op0, op1=op1), reads=r, writes=w)

        def act(out, in_, func, r, w, scale=1.0, bias=0.0):
            if func == AF.Copy:
                P.op("act", lambda e: e.activation(out=out, in_=in_, func=func, scale=scale), reads=r, writes=w)
            else:
                P.op("act", lambda e: e.activation(out=out, in_=in_, func=func, scale=scale, bias=bias), reads=r, writes=w)

        def cp(out, in_, r, w, eng="dve"):
            P.op(eng, lambda e: e.tensor_copy(out=out, in_=in_), reads=r, writes=w)

        P.dma("sp", "c0", lambda e: e.dma_start(out=ident.ap, in_=ident_d[:, :]), writes=ident.keys())
        P.dma("sp", "c1", lambda e: e.dma_start(out=vecs.ap, in_=vecs_d[:, :]), writes=vecs.keys())
        P.dma("sp", "c2", lambda e: e.dma_start(out=cstf.ap, in_=cstf_d[:, :]), writes=cstf.keys())
        P.dma("pool", "c3", lambda e: e.dma_start(out=cstb.ap.rearrange("p a b -> p (a b)"), in_=cstb_d[:, :]), writes=cstb.keys())

        P.op("dve", lambda e: e.memset(EINV.ap, math.exp(-1.0)), writes=EINV.keys())
        o_q = VEC["qgA"][0]
        dve_ts(vecs.ap[:, o_q:o_q + 2], vecs.ap[:, o_q:o_q + 2], 0.125, ALU.mult, vecs.keys(), vecs.keys())

        KVr = R["KV"]

        def s5_setup():
            o = [0]

            def tmp(shape, dt=F32):
                b = KVr.buf(o[0], shape, dt)
                o[0] += ((b.nbytes + 511) // 512) * 512
                return b

            inB = tmp([48 + 1024])
            P.dma("sp", "s5in", lambda e: e.dma_start(out=inB.ap, in_=s5b_d[:, :]), writes=inB.keys())
            lamr = inB.ap[:, 0:16]
            lami = inB.ap[:, 16:32]
            logdt = inB.ap[:, 32:48]
            bre = inB.ap[:, 48:304].rearrange("p (q h) -> p q h", q=16)
            bim = inB.ap[:, 304:560].rearrange("p (q h) -> p q h", q=16)
            cre = inB.ap[:, 560:816].rearrange("p (q h) -> p q h", q=16)
            cim = inB.ap[:, 816:1072].rearrange("p (q h) -> p q h", q=16)
            kin = inB.keys()
            sm = tmp([8, 16])
            smk = sm.keys()
            dt_, lrdt, lidt, den, nr, fr, fi, t0 = [sm.ap[:, i, :] for i in range(8)]
            act(dt_, logdt, AF.Exp, kin, smk)
            dve_tt(lrdt, lamr, dt_, ALU.mult, kin + smk, smk)
            dve_tt(lidt, lami, dt_, ALU.mult, kin + smk, smk)
            NK = len(KLIST)
            big = [tmp([NK, 16]) for _ in range(6)]
            mag, Tt, t1, t2, cosv, sinv = big
            allk = sum([b.keys() for b in big], [])
            kv = cf("kv")
            kf = cf("kf")
            bc_q = lambda a: a.unsqueeze(1).to_broadcast([128, NK, 16])
            bc_k = lambda a: a.unsqueeze(2).to_broadcast([128, NK, 16])
            dve_tt(mag.ap, bc_q(lrdt), bc_k(kv), ALU.mult, smk + CK, allk)
            act(mag.ap, mag.ap, AF.Exp, allk, allk)
            dve_tt(Tt.ap, bc_q(lidt), bc_k(kf), ALU.mult, smk + CK, allk)

            def sincos(dst, shift, Tsrc, a1, a2, keys):
                dve_ts(a1.ap, Tsrc.ap, shift, ALU.add, keys, keys)
                cp(a2.ap.bitcast(I32), a1.ap, keys, keys)
                cp(a2.ap, a2.ap.bitcast(I32), keys, keys)
                dve_tt(a1.ap, a1.ap, a2.ap, ALU.subtract, keys, keys)
                dve_stt(a1.ap, a1.ap, 0.0, a1.ap, ALU.is_lt, ALU.add, keys, keys)
                act(dst.ap, a1.ap, AF.Sin, keys, keys, scale=TWO_PI * (1 - 1e-6), bias=-math.pi * (1 - 1e-6))

            sincos(cosv, 0.75 + 32.0, Tt, t1, t2, allk)
            sincos(sinv, 0.5 + 32.0, Tt, t1, t2, allk)
            dve_tt(cosv.ap, cosv.ap, mag.ap, ALU.mult, allk, allk)
            dve_tt(sinv.ap, sinv.ap, mag.ap, ALU.mult, allk, allk)
            Er = lambda k: cosv.ap[:, k + 7, :]
            Ei = lambda k: sinv.ap[:, k + 7, :]
            cp(Acf.ap[:, 0, :], Er(8), allk, Acf.keys())
            cp(Acf.ap[:, 1, :], Ei(8), allk, Acf.keys())
            dve_tt(den, lamr, lamr, ALU.mult, kin, smk)
            dve_tt(t0, lami, lami, ALU.mult, kin, smk)
            dve_tt(den, den, t0, ALU.add, smk, smk)
            P.op("dve", lambda e: e.reciprocal(out=den, in_=den), reads=smk, writes=smk)
            dve_ts(nr, Er(1), -1.0, ALU.add, allk, smk)
            dve_tt(fr, nr, lamr, ALU.mult, smk + kin, smk)
            dve_tt(t0, Ei(1), lami, ALU.mult, allk + kin, smk)
            dve_tt(fr, fr, t0, ALU.add, smk, smk)
            dve_tt(fr, fr, den, ALU.mult, smk, smk)
            dve_tt(fi, Ei(1), lamr, ALU.mult, allk + kin, smk)
            dve_tt(t0, nr, lami, ALU.mult, smk + kin, smk)
            dve_tt(fi, fi, t0, ALU.subtract, smk, smk)
            dve_tt(fi, fi, den, ALU.mult, smk, smk)
            bb = tmp([4, 16, 16])
            bbk = bb.keys()
            Bbr, Bbi, tA, tB = [bb.ap[:, i] for i in range(4)]
            bch = lambda a: a.unsqueeze(2).to_broadcast([128, 16, 16])
            dve_tt(Bbr, bre, bch(fr), ALU.mult, kin + smk, bbk)
            dve_tt(tA, bim, bch(fi), ALU.mult, kin + smk, bbk)
            dve_tt(Bbr, Bbr, tA, ALU.subtract, bbk, bbk)
            dve_tt(Bbi, bim, bch(fr), ALU.mult, kin + smk, bbk)
            dve_tt(tA, bre, bch(fi), ALU.mult, kin + smk, bbk)
            dve_tt(Bbi, Bbi, tA, ALU.add, bbk, bbk)
            Rr = tmp([9, 16, 16])
            Ri = tmp([9, 16, 16])
            Rt = tmp([9, 16, 16])
            rk = Rr.keys() + Ri.keys() + Rt.keys()
            bcC = lambda a: a.unsqueeze(1).to_broadcast([128, 9, 16, 16])
            bcE = lambda a: a.unsqueeze(3).to_broadcast([128, 9, 16, 16])
            Er9 = cosv.ap[:, 7:16, :]
            Ei9 = sinv.ap[:, 7:16, :]
            dve_tt(Rr.ap, bcC(cre), bcE(Er9), ALU.mult, kin + allk, rk)
            dve_tt(Rt.ap, bcC(cim), bcE(Ei9), ALU.mult, kin + allk, rk)
            dve_tt(Rr.ap, Rr.ap, Rt.ap, ALU.subtract, rk, rk)
            dve_tt(Ri.ap, bcC(cre), bcE(Ei9), ALU.mult, kin + allk, rk)
            dve_tt(Rt.ap, bcC(cim), bcE(Er9), ALU.mult, kin + allk, rk)
            dve_tt(Ri.ap, Ri.ap, Rt.ap, ALU.add, rk, rk)
            Bz = tmp([2, 16, 128])
            bzk = Bz.keys()
            P.op("dve", lambda e: e.memset(Bz.ap, 0.0), writes=bzk)
            for half in range(2):
                ps_ = slice(half * 64, half * 64 + 64)
                for pp in range(4):
                    cs_ = slice(pp * 32 + half * 16, pp * 32 + half * 16 + 16)
                    cp(Bz.ap[ps_, 0, pp:16:4, cs_], Bbr[ps_, pp:16:4, :], bbk, bzk)
                    dve_ts(Bz.ap[ps_, 1, pp:16:4, cs_], Bbi[ps_, pp:16:4, :], -1.0, ALU.mult, bbk, bzk)
            bd = cf("bd").rearrange("p (a b) -> p a b", a=8)
            for ct in range(4):
                bank = PB[ct]

                def mm(e, ct=ct, bank=bank):
                    last = None
                    for pp in range(4):
                        q = 4 * ct + pp
                        e.matmul(out=bank[:, 0:128], lhsT=Bz.ap[:, 0, q, :], rhs=Rr.ap[:, 0:8, q, :],
                                 start=(pp == 0), stop=False)
                        last = e.matmul(out=bank[:, 0:128], lhsT=Bz.ap[:, 1, q, :], rhs=Ri.ap[:, 0:8, q, :],
                                        start=False, stop=(pp == 3))
                    return last

                P.op("pe", mm, reads=bzk + rk, writes=PK[ct])
                src = bank[:, 0:128].rearrange("p (t h) -> p t h", t=8).unsqueeze(2).to_broadcast([128, 8, 8, 16])
                msk = bd.unsqueeze(1).to_broadcast([128, 8, 8, 16])
                dst = KD.ap[:, ct].rearrange("p t (g h) -> p t g h", g=8)
                dve_tt(dst, src, msk, ALU.mult, PK[ct] + CK, KD.keys())
            P.op("dve", lambda e: e.memset(YW.ap, 0.0), writes=YW.keys())
            for half in range(2):
                ps_ = slice(half * 64, half * 64 + 64)
                cs_ = slice(half * 16, half * 16 + 16)
                srcr = Rr.ap[ps_, 1:9].rearrange("p k q h -> p q k h")
                srci = Ri.ap[ps_, 1:9].rearrange("p k q h -> p q k h")
                cp(YW.ap[ps_, :, :, 0, cs_], srcr, rk, YW.keys())
                dve_ts(YW.ap[ps_, :, :, 1, cs_], srci, -1.0, ALU.mult, rk, YW.keys())
            o[0] = 0
            inA = tmp([5, 256])
            P.dma("sp", "s5in", lambda e: e.dma_start(out=inA.ap.rearrange("p a b -> p (a b)"), in_=s5a_d[:, :]), writes=inA.keys())
            kia = inA.keys()
            lamrA, lamiA, logdtA, breA, bimA = [inA.ap[:, i, :] for i in range(5)]
            smA = tmp([8, 256])
            sak = smA.keys()
            dtA, lrdtA, lidtA, denA, nrA, frA, fiA, t0A = [smA.ap[:, i, :] for i in range(8)]
            act(dtA, logdtA, AF.Exp, kia, sak)
            dve_tt(lrdtA, lamrA, dtA, ALU.mult, kia + sak, sak)
            dve_tt(lidtA, lamiA, dtA, ALU.mult, kia + sak, sak)
            bigA = [tmp([8, 256]) for _ in range(6)]
            magA, TA, t1A, t2A, cosA, sinA = bigA
            ak = sum([b.keys() for b in bigA], [])
            bq = lambda a: a.unsqueeze(1).to_broadcast([128, 8, 256])
            bk = lambda a: a.unsqueeze(2).to_broadcast([128, 8, 256])
            dve_tt(magA.ap, bq(lrdtA), bk(cf("kvA")), ALU.mult, sak + CK, ak)
            act(magA.ap, magA.ap, AF.Exp, ak, ak)
            dve_tt(TA.ap, bq(lidtA), bk(cf("kfA")), ALU.mult, sak + CK, ak)
            sincos(cosA, 0.75 + 32.0, TA, t1A, t2A, ak)
            sincos(sinA, 0.5 + 32.0, TA, t1A, t2A, ak)
            dve_tt(cosA.ap, cosA.ap, magA.ap, ALU.mult, ak, ak)
            dve_tt(sinA.ap, sinA.ap, magA.ap, ALU.mult, ak, ak)
            dve_tt(denA, lamrA, lamrA, ALU.mult, kia, sak)
            dve_tt(t0A, lamiA, lamiA, ALU.mult, kia, sak)
            dve_tt(denA, denA, t0A, ALU.add, sak, sak)
            P.op("dve", lambda e: e.reciprocal(out=denA, in_=denA), reads=sak, writes=sak)
            dve_ts(nrA, cosA.ap[:, 1, :], -1.0, ALU.add, ak, sak)
            dve_tt(frA, nrA, lamrA, ALU.mult, sak + kia, sak)
            dve_tt(t0A, sinA.ap[:, 1, :], lamiA, ALU.mult, ak + kia, sak)
            dve_tt(frA, frA, t0A, ALU.add, sak, sak)
            dve_tt(frA, frA, denA, ALU.mult, sak, sak)
            dve_tt(fiA, sinA.ap[:, 1, :], lamrA, ALU.mult, ak + kia, sak)
            dve_tt(t0A, nrA, lamiA, ALU.mult, sak + kia, sak)
            dve_tt(fiA, fiA, t0A, ALU.subtract, sak, sak)
            dve_tt(fiA, fiA, denA, ALU.mult, sak, sak)
            BbrA, BbiA = dtA, denA
            dve_tt(t0A, bimA, fiA, ALU.mult, kia + sak, sak)
            dve_tt(BbrA, breA, frA, ALU.mult, kia + sak, sak)
            dve_tt(BbrA, BbrA, t0A, ALU.subtract, sak, sak)
            dve_tt(t0A, breA, fiA, ALU.mult, kia + sak, sak)
            dve_tt(BbiA, bimA, frA, ALU.mult, kia + sak, sak)
            dve_tt(BbiA, BbiA, t0A, ALU.add, sak, sak)
            Wr, Wi, tW = magA, TA, t1A
            dve_tt(Wr.ap, bq(BbrA), cosA.ap, ALU.mult, sak + ak, ak)
            dve_tt(tW.ap, bq(BbiA), sinA.ap, ALU.mult, sak + ak, ak)
            dve_tt(Wr.ap, Wr.ap, tW.ap, ALU.subtract, ak, ak)
            dve_tt(Wi.ap, bq(BbrA), sinA.ap, ALU.mult, sak + ak, ak)
            dve_tt(tW.ap, bq(BbiA), cosA.ap, ALU.mult, sak + ak, ak)
            dve_tt(Wi.ap, Wi.ap, tW.ap, ALU.add, ak, ak)
            g2c = cf("g2col")
            for ri, Wx in enumerate((Wr, Wi)):
                src = Wx.ap.rearrange("p k (c s) -> p c k s", c=4)
                for g2p in range(2):
                    dst = VS.ap[:, :, ri, :, g2p * 64:(g2p + 1) * 64]
                    dve_ts(dst, src, g2c[:, g2p:g2p + 1], ALU.mult, ak + CK, VS.keys())
            P.op("dve", lambda e: e.memset(Xc.ap, 0.0), writes=Xc.keys())
            allkv = [("KV", k) for k in range(65536 // KG)]
            P.op("dve", lambda e: e.memset(LB.ap, 0.0), reads=allkv,
                 writes=LB.keys() + [("KTc", i) for i in range(NT)] + [("VCc", i) for i in range(NT)])

        P.ses = set(os.environ.get("K_SES_SETUP", "").split(","))
        s5_setup()
        P.ses = set(SAME_ENGINE_SYNC)

        WSPEC = {"w_in": w_in, "w_glu": w_glu, "w_out": w_out}
        for l_ in range(2):
            WSPEC["w_up%d" % l_] = w_up[l_]
            WSPEC["w_dn%d" % l_] = w_dn[l_]
            WSPEC["w_gt%d" % l_] = w_gt[l_]
            WSPEC["w_pu%d" % l_] = w_pu[l_]
        SCRT = {}
        WNKT = {}
        for name, W in WSPEC.items():
            K_, N_ = W.shape
            nkt = min(8, K_ // 128)
            WNKT[name] = nkt
            SCRT[name] = nc.dram_tensor("s_" + name, [N_ // 256, K_ // (128 * nkt), 128, nkt * 256], BF16, kind="Internal").ap()
        SCRT["pool"] = nc.dram_tensor("s_pool", [4, 1, 128, 512], BF16, kind="Internal").ap()
        WNKT["pool"] = 2

        CONV = []

        def convert(name, chunks):
            for (c, kc) in chunks:
                CONV.append((name, c, kc))

        convert("w_in", [(c, 0) for c in range(8)])
        convert("w_glu", [(c, 0) for c in range(2)])
        convert("w_out", [(c, 0) for c in range(4)])
        for l_ in range(2):
            if l_ == 1:
                convert("pool", [(g_, 0) for g_ in range(4)])
            for qq in range(4):
                convert("w_up%d" % l_, [(qq * 4 + c, 0) for c in range(4)])
                convert("w_dn%d" % l_, [(c, qq) for c in range(4)])
            for half in range(2):
                convert("w_gt%d" % l_, [(half * 2 + c, 0) for c in range(2)])
                convert("w_pu%d" % l_, [(half * 2 + c, 0) for c in range(2)])
        CONV_IDX = {k_: i_ for i_, k_ in enumerate(CONV)}
        cvp = [0]
        NCV = 8
        LOOKAHEAD = 6

        def conv_upto(idx):
            while cvp[0] <= min(idx, len(CONV) - 1):
                i_ = cvp[0]
                cvp[0] += 1
                name, c, kc = CONV[i_]
                nkt = WNKT[name]
                if name == "pool":
                    src = w_pool[c].rearrange("(k p) n -> p k n", p=128)
                else:
                    W = WSPEC[name]
                    src = W[kc * nkt * 128:(kc + 1) * nkt * 128, c * 256:(c + 1) * 256].rearrange("(k p) n -> p k n", p=128)
                dst = SCRT[name][c, kc].rearrange("p (k n) -> p k n", k=nkt)
                P.dma("pool", "cv%d" % (i_ % NCV), lambda e, src=src, dst=dst: e.dma_start(out=dst, in_=src),
                      writes=[("scr", name, c, kc), ("cvring", i_ % NCV)])

        wctr = [0]
        NSLOT = len(WS)

        def load_w(name, c, kc=0):
            s = wctr[0] % NSLOT
            wctr[0] += 1
            slot = WS[s]
            nkt = WNKT[name]
            conv_upto(CONV_IDX[(name, c, kc)] + LOOKAHEAD)
            src = SCRT[name][c, kc].rearrange("p (k n) -> p k n", k=nkt)
            P.dma("pool", "ws%d" % s, lambda e: e.dma_start(out=slot.ap[:, 0:nkt, :], in_=src),
                  reads=[("scr", name, c, kc)], writes=slot.keys())
            return slot

        pctr = [0]

        def banks(n):
            b = [(pctr[0] + i) % 8 for i in range(n)]
            pctr[0] = (pctr[0] + n) % 8
            return b

        def gemm_gen(name, c0, nchunks, kc, rhs_fn, rkeys_fn, evac):
            nkt = WNKT[name]
            for c in range(nchunks):
                slot = load_w(name, c0 + c, kc)
                bs = banks(2)
                for m in range(2):
                    b = bs[m]

                    def mm(e, m=m, b=b, slot=slot):
                        last = None
                        for kt in range(nkt):
                            last = e.matmul(out=PB[b][:, :], lhsT=slot.ap[:, kt, m * 128:(m + 1) * 128], rhs=rhs_fn(kt),
                                            start=(kt == 0), stop=(kt == nkt - 1))
                        return last

                    rk = sum([rkeys_fn(kt) for kt in range(nkt)], [])
                    P.op("pe", mm, reads=slot.keys() + rk, writes=PK[b])
                    evac(c * 2 + m, b)
                yield c

        def gemm_fm(*a):
            for _ in gemm_gen(*a):
                pass

        def interleave(g1, g2, lead=1):
            for _ in range(lead):
                next(g1, None)
            d1 = d2 = False
            while not (d1 and d2):
                if not d2:
                    d2 = next(g2, "done") == "done"
                if not d1:
                    d1 = next(g1, "done") == "done"

        STGb = [R["STG"].buf(i * 4096, [1024], F32) for i in range(2)]
        hn = R["HN"].buf(0, [8, 512], BF16)
        mixT = R["HN"].buf(0, [8, 512], BF16)
        sq = R["SQ"].buf(0, [8, 512], BF16)
        qz = R["SQ"].buf(0, [8, 512], BF16)
        gb = R["SQ"].buf(0, [4, 512], BF16)
        sig = R["SQ"].buf(0, [4, 512], F32)
        acc = R["AT"].buf(0, [4, 512], F32)
        y32 = R["AT"].buf(0, [4, 512], F32)
        aTs = [R["AT"].buf(0, [8, 512], BF16), R["SQ"].buf(0, [8, 512], BF16)]
        uTb = R["UB"].buf(0, [4, 512], BF16)
        uM = R["UB"].buf(4096, [4, 512], BF16)
        r32 = [R["UB"].buf(i * 2048, [512], F32) for i in range(4)]
        pT = R["UB"].buf(0, [2, 512], BF16)
        e32 = R["SCR"].buf(0, [512], F32)
        spb = [R["SCR"].buf(2048 + i * 1024, [512], BF16) for i in range(3)]
        wlb = [R["SCR"].buf(5120 + i * 1024, [512], BF16) for i in range(3)]
        tmpf = [R["SCR"].buf(i * 2048, [512], F32) for i in range(4)]
        SX = R["S5B"].buf(0, [2, 16, 64], F32)
        Xp = R["S5B"].buf(8192, [2, 16, 64], BF16)
        hnp = [R["S5B"].buf(i * 2112, [528], F32) for i in range(2)]
        ptmp = [R["S5B"].buf(4224 + i * 2112, [528], F32) for i in range(2)]
        ypool = [R["S5B"].buf(8448 + i * 2048, [2, 512], BF16) for i in range(2)]
        hTok = R["AT"].buf(0, [4, 1024], BF16)
        ypool8 = R["SQ"].buf(0, [8, 512], BF16)
        bandsb = R["S5B"].buf(0, [16, 128], BF16)
        carryb = R["S5B"].buf(4096, [1024], BF16)
        rstdc = R["RS"].buf(0, [4], F32)
        lnc = R["RS"].buf(512, [4], F32)
        rstd = R["RS"].buf(0, [512], F32)
        sqt = R["RS"].buf(2048, [512], F32)

        def dbg_dump(src_ap, keys, ti_sel, ti, slot=None):
            if dbg_d is None or ti != ti_sel:
                return
            dst = dbg_d[:, 0:src_ap.shape[1], :] if slot is None else dbg_d[:, slot, :]
            return P.dma("sp", "dbg", lambda e: e.dma_start(out=dst, in_=src_ap), reads=keys)

        NPOOL = int(os.environ.get("K_NPOOL", "3"))

        def rmsnorm(gain_name, out_fn, out_keys_fn, fp32_out=False):
            b = banks(1)[0]
            for dt in range(8):
                act(sq.ap[:, dt, :], HT.ap[:, dt, :], AF.Square, HT.keys(dt), sq.keys(dt))
                P.op("pe", lambda e, dt=dt: e.matmul(out=PB[b][:, :], lhsT=ones_b, rhs=sq.ap[:, dt, :], start=(dt == 0), stop=(dt == 7)),
                     reads=sq.keys(dt) + CK, writes=PK[b])
            for i, dt in enumerate(range(8 - NPOOL, 8)):
                act(tmpf[i].ap, HT.ap[:, dt, :], AF.Copy, HT.keys(dt) + CK, tmpf[i].keys(), scale=vcol(gain_name, dt))
            act(sqt.ap, PB[b][:, :], AF.Ln, PK[b], sqt.keys(), scale=1.0 / 1024.0, bias=EPS)
            act(rstd.ap, sqt.ap, AF.Exp, sqt.keys(), rstd.keys(), scale=-0.5)
            for i, dt in enumerate(range(8 - NPOOL, 8)):
                P.op("pool", lambda e, i=i, dt=dt: e.tensor_tensor(out=out_fn(dt), in0=tmpf[i].ap, in1=rstd.ap, op=ALU.mult),
                     reads=tmpf[i].keys() + rstd.keys(), writes=out_keys_fn(dt))
            for dt in range(8 - NPOOL):
                dve_stt(out_fn(dt), HT.ap[:, dt, :], vcol(gain_name, dt), rstd.ap, ALU.mult, ALU.mult,
                        HT.keys(dt) + rstd.keys() + CK, out_keys_fn(dt))

        final_sigs = []

        for ti in range(NT):
            t0 = ti * TT
            P.stage = "t%d_load" % ti
            for blk in range(4):
                sb_ = STGb[blk % 2]
                r0 = t0 + blk * 128
                P.dma("sp", "stg%d" % (blk % 2), lambda e, sb_=sb_, r0=r0: e.dma_start(out=sb_.ap, in_=x_d[r0:r0 + 128, :]), writes=sb_.keys())
                for half in range(2):
                    b = banks(1)[0]

                    def tr(e, sb_=sb_, half=half, b=b):
                        last = None
                        for j in range(4):
                            dt = half * 4 + j
                            last = e.transpose(out=PB[b][:, j * 128:(j + 1) * 128], in_=sb_.ap[:, dt * 128:(dt + 1) * 128], identity=ident.ap)
                        return last

                    P.op("pe", tr, reads=sb_.keys() + CK, writes=PK[b])
                    dst = HT.ap[:, half * 4:half * 4 + 4, blk * 128:(blk + 1) * 128]
                    src = PB[b][:, :].rearrange("p (j t) -> p j t", j=4)
                    wk = sum([HT.keys(half * 4 + j) for j in range(4)], [])
                    if half == 0:
                        act(dst, src, AF.Copy, PK[b], wk)
                    else:
                        cp(dst, src, PK[b], wk)

            for layer in range(2):
                if layer == 0:
                    P.stage = "t%d_inproj" % ti
                    rmsnorm("ln_mix0", lambda dt: hn.ap[:, dt, :], lambda dt: hn.keys(dt))
                    hn_r = lambda kt: hn.ap[:, kt, :]
                    hn_k = lambda kt: hn.keys(kt)

                    def ev_u(m, b):
                        act(uTb.ap[:, m, :], PB[b][:, :], AF.Copy, PK[b], uTb.keys(m))

                    gemm_fm("w_in", 0, 2, 0, hn_r, hn_k, ev_u)
                    dve_ts(uM.ap[64:128], uTb.ap[64:128], cf("pp3")[64:128, :], ALU.mult, uTb.keys() + CK, uM.keys())

                    def qk_evac(is_q):
                        def ev(m, b):
                            s_ = tmpf[m % 2]
                            sb16 = s_.ap.bitcast(BF16)[:, 0:512]
                            act(sb16, PB[b][:, :], AF.Square, PK[b], s_.keys())
                            b2 = banks(1)[0]
                            P.op("pe", lambda e: e.matmul(out=PB[b2][:, :], lhsT=blockones_b, rhs=sb16, start=True, stop=True),
                                 reads=s_.keys() + CK, writes=PK[b2])
                            r_ = tmpf[2 + m % 2]
                            act(r_.ap, PB[b2][:, :], AF.Ln, PK[b2], r_.keys(), scale=1.0 / 64.0, bias=EPS)
                            act(r_.ap, r_.ap, AF.Exp, r_.keys(), r_.keys(), scale=-0.5)
                            if is_q:
                                dve_stt(qz.ap[:, 2 * m, :], PB[b][:, :], vcol("qgA"), r_.ap, ALU.mult, ALU.mult,
                                        PK[b] + r_.keys() + CK, qz.keys(2 * m))
                                dve_stt(qz.ap[:, 2 * m + 1, :], PB[b][:, :], vcol("qgB"), r_.ap, ALU.mult, ALU.mult,
                                        PK[b] + r_.keys() + CK, qz.keys(2 * m + 1))
                            else:
                                dve_stt(KT.ap[:, m, t0:t0 + 512], PB[b][:, :], vcol("kgA"), r_.ap, ALU.mult, ALU.mult,
                                        PK[b] + r_.keys() + CK, [("KTc", ti)])
                        return ev

                    gemm_fm("w_in", 2, 2, 0, hn_r, hn_k, qk_evac(True))
                    gemm_fm("w_in", 4, 2, 0, hn_r, hn_k, qk_evac(False))
                    for c in range(2):
                        slot = load_w("w_in", 6 + c, 0)
                        bs = banks(2)
                        for blk in range(4):
                            b = bs[blk // 2]
                            cs = slice((blk % 2) * 256, (blk % 2) * 256 + 256)

                            def mm(e, blk=blk, b=b, cs=cs, slot=slot):
                                last = None
                                for kt in range(8):
                                    last = e.matmul(out=PB[b][:, cs], lhsT=hn.ap[:, kt, blk * 128:(blk + 1) * 128], rhs=slot.ap[:, kt, :],
                                                    start=(kt == 0), stop=(kt == 7))
                                return last

                            P.op("pe", mm, reads=slot.keys() + hn.keys(), writes=[("pbh%d" % b, blk % 2)] + PK[b])
                        for blk in range(4):
                            b = bs[blk // 2]
                            cs = slice((blk % 2) * 256, (blk % 2) * 256 + 256)
                            dst = VC.ap[:, ti * 4 + blk, c * 256:(c + 1) * 256]
                            if blk % 2 == 0:
                                act(dst, PB[b][:, cs], AF.Copy, PK[b], [("VCc", ti)])
                            else:
                                cp(dst, PB[b][:, cs], PK[b], [("VCc", ti)])

                    P.stage = "t%d_att" % ti
                    sbanks = [6, 7, 6, 7]
                    t1_, t2_, t3_, t4_ = [RT.ap[:, i, :] for i in range(4)]
                    tk = RT.keys()
                    Ar = Acf.ap[:, 0, :]
                    Ai = Acf.ap[:, 1, :]
                    sxk = SX.keys()
                    for grp in range(2):
                        def mmS(e, grp=grp):
                            last = None
                            for pp in (2 * grp, 2 * grp + 1):
                                bnk = PB[6 + pp % 2]
                                for ct in range(4):
                                    for ri in range(2):
                                        col = (ct * 2 + ri) * 64
                                        for j in range(8):
                                            k = 7 - j
                                            if pp < 3:
                                                rows = slice(pp * 32, pp * 32 + 32)
                                                rhs = uTb.ap[rows, ct, j:512:8]
                                            else:
                                                rows = slice(64, 128)
                                                rhs = uM.ap[rows, ct, j:512:8]
                                            last = e.matmul(out=bnk[:, col:col + 64], lhsT=VS.ap[rows, ct, ri, k, :], rhs=rhs,
                                                            start=(j == 0), stop=(j == 7))
                            return last

                        P.op("pe", mmS, reads=uTb.keys() + uM.keys() + VS.keys(), writes=PK[6] + PK[7])
                        for pp in (2 * grp, 2 * grp + 1):
                            bk_ = 6 + pp % 2
                            src = PB[bk_][:, :].rearrange("p (c r n) -> p r c n", c=4, r=2)
                            for ri in range(2):
                                dst = SX.ap[:, ri, pp:16:4, :]
                                cp(dst, src[:, ri], PK[bk_], sxk)
                    cp(Xp.ap[:, :, :, 0], Xc.ap, Xc.keys(), Xp.keys())

                    def rec_step(c):
                        pr = Xc.ap[:, 0, :] if c == 0 else SX.ap[:, 0, :, c - 1]
                        pi = Xc.ap[:, 1, :] if c == 0 else SX.ap[:, 1, :, c - 1]
                        rk_ = (Xc.keys() if c == 0 else []) + sxk + Acf.keys() + tk

                        def rec(e, pr=pr, pi=pi, c=c):
                            e.tensor_tensor(out=t1_, in0=Ar, in1=pr, op=ALU.mult)
                            e.tensor_tensor(out=t2_, in0=Ai, in1=pi, op=ALU.mult)
                            e.tensor_tensor(out=t3_, in0=Ar, in1=pi, op=ALU.mult)
                            e.tensor_tensor(out=t4_, in0=Ai, in1=pr, op=ALU.mult)
                            e.tensor_tensor(out=t1_, in0=t1_, in1=t2_, op=ALU.subtract)
                            e.tensor_tensor(out=t3_, in0=t3_, in1=t4_, op=ALU.add)
                            e.tensor_tensor(out=SX.ap[:, 0, :, c], in0=SX.ap[:, 0, :, c], in1=t1_, op=ALU.add)
                            return e.tensor_tensor(out=SX.ap[:, 1, :, c], in0=SX.ap[:, 1, :, c], in1=t3_, op=ALU.add)

                        P.op(REC_ENG, rec, reads=rk_, writes=sxk + tk)

                    steps = []
                    for h in range(8):
                        for kb in range(4 * ti + 3, -1, -1):
                            steps.append((h, kb))
                    nst = len(steps)
                    rec_per = (64 + nst - 1) // nst
                    rec_done = [0]

                    def geom(i):
                        h, kb = steps[i]
                        b_ = kb - 4 * ti
                        diag = b_ >= 0
                        c0 = 128 * b_ if diag else 0
                        sb0 = b_ if diag else 0
                        return h, kb, diag, c0, sb0, 512 - c0

                    def stageA(i):
                        h, kb, diag, c0, sb0, N = geom(i)
                        zb = i % 2
                        sp_ = spb[i % 3]
                        kt_l = KT.ap[:, h // 2, kb * 128:(kb + 1) * 128]
                        q_r = qz.ap[:, h, c0:512]
                        P.op("pe", lambda e: e.matmul(out=PB[zb][:, 0:N], lhsT=kt_l, rhs=q_r, start=True, stop=True),
                             reads=[("KTc", kb // 4)] + qz.keys(h), writes=PK[zb])
                        act(e32.ap[:, 0:N], PB[zb][:, 0:N], AF.Exp, PK[zb], e32.keys())
                        act(sp_.ap[:, 0:N], e32.ap[:, 0:N], AF.Ln, e32.keys(), sp_.keys(), bias=1.0)
                        if diag:
                            dve_tt(sp_.ap[:, 0:128], sp_.ap[:, 0:128], masklt_b, ALU.mult, sp_.keys() + CK, sp_.keys())

                    def stageB(i):
                        h, kb, diag, c0, sb0, N = geom(i)
                        ab = 2 + i % 2
                        sp_ = spb[i % 3]
                        wl_ = wlb[i % 3]
                        kt_l = KT.ap[:, h // 2, kb * 128:(kb + 1) * 128]
                        q_r = qz.ap[:, h, c0:512]

                        def mm2(e):
                            e.matmul(out=PB[ab][:, 0:N], lhsT=kt_l, rhs=q_r, start=True, stop=False)
                            return e.matmul(out=PB[ab][:, 0:N], lhsT=negtri_b, rhs=sp_.ap[:, 0:N], start=False, stop=True)

                        P.op("pe", mm2, reads=[("KTc", kb // 4)] + qz.keys(h) + sp_.keys() + CK, writes=PK[ab])
                        act(wl_.ap[:, 0:N], PB[ab][:, 0:N], AF.Exp, PK[ab], wl_.keys())
                        if diag:
                            dve_tt(wl_.ap[:, 0:128], wl_.ap[:, 0:128], masklt_b, ALU.mult, wl_.keys() + CK, wl_.keys())

                    def stageC(i):
                        h, kb, diag, c0, sb0, N = geom(i)
                        vb = 4 + i % 2
                        sp_ = spb[i % 3]
                        wl_ = wlb[i % 3]
                        par = h % 2
                        Cst = CE.ap[:, par, 0, :]
                        Est = CE.ap[:, par, 1, :]
                        cek = [("CE", par)]
                        nsb = 4 - sb0
                        v_r = VC.ap[:, kb, h * 64:(h + 1) * 64]

                        def mm3(e):
                            last = None
                            for ii in range(nsb):
                                sbq = sb0 + ii
                                cs = slice(ii * 128, (ii + 1) * 128)
                                e.matmul(out=PB[vb][:, sbq * 80:sbq * 80 + 64], lhsT=wl_.ap[:, cs], rhs=v_r, start=True, stop=True)
                                last = e.matmul(out=PB[vb][:, sbq * 80 + 64:sbq * 80 + 65], lhsT=sp_.ap[:, cs], rhs=ones_b[:, 0:1], start=True, stop=True)
                            return last

                        P.op("pe", mm3, reads=wl_.keys() + sp_.keys() + [("VCc", kb // 4)] + CK, writes=PK[vb])
                        cont0 = sb0 + 1 if diag else 0
                        if cont0 < 4:
                            if E_ON_POOL:
                                P.op("pool", lambda e: e.tensor_tensor(out=Est[:, cont0:4], in0=EINV.ap[:, cont0:4], in1=Cst[:, cont0:4], op=ALU.pow),
                                     reads=cek + EINV.keys(), writes=cek)
                            else:
                                act(Est[:, cont0:4], Cst[:, cont0:4], AF.Exp, cek, cek, scale=-1.0)
                        for sbq in range(sb0, 4):
                            pv = PB[vb][:, sbq * 80:sbq * 80 + 64]
                            dst = acc.ap[:, sbq, h * 64:(h + 1) * 64]
                            if diag and sbq == sb0:
                                cp(dst, pv, PK[vb], acc.keys(sbq))
                            else:
                                dve_stt(dst, pv, Est[:, sbq:sbq + 1], dst, ALU.mult, ALU.add, PK[vb] + cek + acc.keys(sbq), acc.keys(sbq))
                        pcs = PB[vb][:, 0:320].rearrange("p (s c) -> p s c", c=80)[:, :, 64]
                        if diag:
                            cp(Cst[:, sb0:sb0 + 1], pcs[:, sb0:sb0 + 1], PK[vb], cek)
                            if sb0 + 1 < 4:
                                dve_tt(Cst[:, sb0 + 1:4], Cst[:, sb0 + 1:4], pcs[:, sb0 + 1:4], ALU.add, PK[vb] + cek, cek)
                        else:
                            dve_tt(Cst, Cst, pcs, ALU.add, PK[vb] + cek, cek)
                        for _ in range(rec_per):
                            if rec_done[0] < 64:
                                rec_step(rec_done[0])
                                rec_done[0] += 1

                    for i in range(nst + 2):
                        if i < nst:
                            stageA(i)
                        if 0 <= i - 1 < nst:
                            stageB(i - 1)
                        if 0 <= i - 2 < nst:
                            stageC(i - 2)
                    while rec_done[0] < 64:
                        rec_step(rec_done[0])
                        rec_done[0] += 1
                    cp(Xp.ap[:, :, :, 1:64], SX.ap[:, :, :, 0:63], sxk, Xp.keys())
                    cp(Xc.ap, SX.ap[:, :, :, 63], sxk, Xc.keys())
                    dbg_dump(acc.ap, acc.keys(), 0, ti) if dbg == "att" else None
                    for ft in range(4):
                        b = banks(1)[0]

                        def tr(e, ft=ft, b=b):
                            last = None
                            for sbq in range(4):
                                last = e.transpose(out=PB[b][:, sbq * 128:(sbq + 1) * 128], in_=acc.ap[:, sbq, ft * 128:(ft + 1) * 128], identity=ident.ap)
                            return last

                        P.op("pe", tr, reads=acc.keys() + CK, writes=PK[b])
                        act(mixT.ap[:, 4 + ft, :], PB[b][:, :], AF.Copy, PK[b], mixT.keys(4 + ft))

                    P.stage = "t%d_s5out" % ti
                    ybanks = banks(4)
                    for ct in range(4):
                        bnk = PB[ybanks[ct]]

                        def mmY(e, ct=ct, bnk=bnk):
                            last = None
                            for ip in range(8):
                                for j in range(ip + 1):
                                    e.matmul(out=bnk[:, ip:512:8], lhsT=KD.ap[:, ct, ip - j, :], rhs=uTb.ap[:, ct, j:512:8],
                                             start=(ip == 0 and j == 0), stop=False, skip_group_check=True)
                            for ip in range(8):
                                for pp in range(4):
                                    q = 4 * ct + pp
                                    for ri in range(2):
                                        last = e.matmul(out=bnk[pp * 32:(pp + 1) * 32, ip:512:8], lhsT=YW.ap[:, q, ip, ri, :], rhs=Xp.ap[:, ri, q, :],
                                                        start=False, stop=(ri == 1 and ip == 7 and pp == 3), tile_position=(0, pp * 32), skip_group_check=True)
                            return last

                        P.op("pe", mmY, reads=uTb.keys(ct) + KD.keys() + YW.keys() + Xp.keys(), writes=PK[ybanks[ct]])
                        dve_stt(y32.ap[:, ct, :], uTb.ap[:, ct, :], vcol("s5d", ct), bnk[:, :], ALU.mult, ALU.add,
                                PK[ybanks[ct]] + uTb.keys(ct) + CK, y32.keys(ct))
                        act(y32.ap[:, ct, :], y32.ap[:, ct, :], AF.Gelu, y32.keys(ct), y32.keys(ct))
                        cp(gb.ap[:, ct, :], y32.ap[:, ct, :], y32.keys(ct), gb.keys(ct))
                    dbg_dump(y32.ap, y32.keys(), 0, ti) if dbg == "s5" else None

                    def ev_glu(m, b):
                        t_ = tmpf[m % 2]
                        act(t_.ap, PB[b][:, :], AF.Sigmoid, PK[b], t_.keys())
                        dve_tt(mixT.ap[:, m, :], y32.ap[:, m, :], t_.ap, ALU.mult, y32.keys(m) + t_.keys(), mixT.keys(m))

                    gemm_fm("w_glu", 0, 2, 0, lambda kt: gb.ap[:, kt, :], lambda kt: gb.keys(kt), ev_glu)

                    def ev_res(m, b):
                        dve_tt(HT.ap[:, m, :], HT.ap[:, m, :], PB[b][:, :], ALU.add, PK[b] + HT.keys(m), HT.keys(m))

                    gemm_fm("w_out", 0, 4, 0, lambda kt: mixT.ap[:, kt, :], lambda kt: mixT.keys(kt), ev_res)
                    if dbg == "mix0":
                        dbg_dump(HT.ap, HT.keys(), NT - 1, ti)
                else:
                    P.stage = "t%d_pool" % ti
                    P.dma("pool", "bands", lambda e: e.dma_start(out=bandsb.ap.rearrange("p a b -> p (a b)"), in_=bands_d[:, :]), writes=bandsb.keys())
                    if ti > 0:
                        P.dma("sp", "carry", lambda e: e.dma_start(out=carryb.ap, in_=carry_d[:, :]), reads=[("carryd", 0)], writes=carryb.keys())
                    bss = banks(1)[0]
                    for dt in range(8):
                        act(sq.ap[:, dt, :], HT.ap[:, dt, :], AF.Square, HT.keys(dt), sq.keys(dt))

                    def mmss(e, bss=bss):
                        last = None
                        for blk in range(4):
                            for dt in range(8):
                                last = e.matmul(out=PB[bss][:, blk:blk + 1], lhsT=sq.ap[:, dt, blk * 128:(blk + 1) * 128], rhs=ones_b[:, 0:1],
                                                start=(dt == 0), stop=(dt == 7))
                        return last

                    P.op("pe", mmss, reads=sq.keys() + CK, writes=PK[bss])
                    act(lnc.ap, PB[bss][:, 0:4], AF.Ln, PK[bss], lnc.keys(), scale=1.0 / 1024.0, bias=EPS)
                    act(rstdc.ap, lnc.ap, AF.Exp, lnc.keys(), rstdc.keys(), scale=-0.5)
                    for blk in range(4):
                        bs = banks(2)
                        for half in range(2):
                            b = bs[half]

                            def trh(e, half=half, b=b, blk=blk):
                                last = None
                                for j in range(4):
                                    dt = half * 4 + j
                                    last = e.transpose(out=PB[b][:, j * 128:(j + 1) * 128], in_=HT.ap[:, dt, blk * 128:(blk + 1) * 128], identity=ident.ap)
                                return last

                            P.op("pe", trh, reads=HT.keys() + CK, writes=PK[b])
                            dst = hTok.ap[:, blk, half * 512:(half + 1) * 512]
                            if half == 0:
                                act(dst, PB[b][:, :], AF.Copy, PK[b] + rstdc.keys(), hTok.keys(blk), scale=rstdc.ap[:, blk:blk + 1])
                            else:
                                dve_ts(dst, PB[b][:, :], rstdc.ap[:, blk:blk + 1], ALU.mult, PK[b] + rstdc.keys(), hTok.keys(blk))
                    for dt in range(8):
                        g = dt // 2
                        b = banks(1)[0]

                        def mmb(e, dt=dt, g=g, b=b, ti=ti):
                            last = None
                            cs = slice(dt * 128, (dt + 1) * 128)
                            for blk in range(4):
                                out = PB[b][:, blk * 128:(blk + 1) * 128]
                                first_seq = (ti == 0 and blk == 0)
                                has_spill = not first_seq
                                if first_seq:
                                    e.matmul(out=out, lhsT=hTok.ap[:, 0, cs], rhs=bandsb.ap[:, 4 * g + 2, :], start=True, stop=False)
                                    last = e.matmul(out=out, lhsT=hTok.ap[:, 0, cs], rhs=bandsb.ap[:, 4 * g + 3, :], start=False, stop=True)
                                else:
                                    e.matmul(out=out, lhsT=hTok.ap[:, blk, cs], rhs=bandsb.ap[:, 4 * g + 0, :], start=True, stop=False)
                                    prev = carryb.ap[:, cs] if blk == 0 else hTok.ap[:, blk - 1, cs]
                                    last = e.matmul(out=out, lhsT=prev, rhs=bandsb.ap[:, 4 * g + 1, :], start=False, stop=True)
                            return last

                        P.op("pe", mmb, reads=hTok.keys() + bandsb.keys() + carryb.keys(), writes=PK[b])
                        if dt % 2 == 0:
                            act(ypool8.ap[:, dt, :], PB[b][:, :], AF.Copy, PK[b] + CK, ypool8.keys(dt), scale=vcol("ln_mix1", dt))
                        else:
                            dve_ts(ypool8.ap[:, dt, :], PB[b][:, :], vcol("ln_mix1", dt), ALU.mult, PK[b] + CK, ypool8.keys(dt))
                    P.dma("sp", "carryw", lambda e: e.dma_start(out=carry_d[:, :], in_=hTok.ap[:, 3, :]), reads=hTok.keys(3), writes=[("carryd", 0)])
                    for g in range(4):
                        slot = load_w("pool", g, 0)
                        bs = banks(2)
                        for m in range(2):
                            b = bs[m]

                            def mm(e, m=m, b=b, slot=slot, g=g):
                                e.matmul(out=PB[b][:, :], lhsT=slot.ap[:, 0, m * 128:(m + 1) * 128], rhs=ypool8.ap[:, 2 * g, :], start=True, stop=False)
                                return e.matmul(out=PB[b][:, :], lhsT=slot.ap[:, 1, m * 128:(m + 1) * 128], rhs=ypool8.ap[:, 2 * g + 1, :], start=False, stop=True)

                            P.op("pe", mm, reads=slot.keys() + ypool8.keys(2 * g) + ypool8.keys(2 * g + 1), writes=PK[b])
                            dt = 2 * g + m
                            dve_stt(HT.ap[:, dt, :], PB[b][:, :], vcol("pscale", dt), HT.ap[:, dt, :], ALU.mult, ALU.add,
                                    PK[b] + HT.keys(dt) + CK, HT.keys(dt))

                if dbg == "pool" and layer == 1:
                    dbg_dump(HT.ap, HT.keys(), 0, ti)
                P.stage = "t%d_mlp%d" % (ti, layer)
                rmsnorm("ln_mlp%d" % layer, lambda dt: hn.ap[:, dt, :], lambda dt: hn.keys(dt))
                def up_gen(qq):
                    aT = aTs[qq % 2]

                    def ev_up(m, b, aT=aT):
                        r_ = r32[m % 4]
                        act(r_.ap, PB[b][:, :], AF.Relu, PK[b], r_.keys())
                        dve_tt(aT.ap[:, m, :], r_.ap, r_.ap, ALU.mult, r_.keys(), aT.keys(m))

                    return gemm_gen("w_up%d" % layer, qq * 4, 4, 0, lambda kt: hn.ap[:, kt, :], lambda kt: hn.keys(kt), ev_up)

                def dn_gen(qq):
                    aT = aTs[qq % 2]

                    def ev_res(m, b):
                        dve_tt(HT.ap[:, m, :], HT.ap[:, m, :], PB[b][:, :], ALU.add, PK[b] + HT.keys(m), HT.keys(m))

                    return gemm_gen("w_dn%d" % layer, 0, 4, qq, lambda kt, aT=aT: aT.ap[:, kt, :], lambda kt, aT=aT: aT.keys(kt), ev_res)

                if MLP_PIPE:
                    for _ in up_gen(0):
                        pass
                    for qq in range(4):
                        if qq < 3:
                            interleave(up_gen(qq + 1), dn_gen(qq), lead=1)
                        else:
                            for _ in dn_gen(qq):
                                pass
                else:
                    for qq in range(4):
                        for _ in up_gen(qq):
                            pass
                        for _ in dn_gen(qq):
                            pass

                if dbg == "mlp0" and layer == 0:
                    dbg_dump(HT.ap, HT.keys(), NT - 1, ti)
                P.stage = "t%d_ple%d" % (ti, layer)
                rmsnorm("ln_ple%d" % layer, lambda dt: hn.ap[:, dt, :], lambda dt: hn.keys(dt))
                for blk in range(4):
                    sb_ = STGb[blk % 2]
                    r0 = t0 + blk * 128
                    P.dma("sp", "stg%d" % (blk % 2), lambda e, sb_=sb_, r0=r0, layer=layer: e.dma_start(out=sb_.ap[:, 0:256], in_=p_d[layer, r0:r0 + 128, :]), writes=sb_.keys())
                    b = banks(1)[0]

                    def trp(e, sb_=sb_, b=b):
                        e.transpose(out=PB[b][:, 0:128], in_=sb_.ap[:, 0:128], identity=ident.ap)
                        return e.transpose(out=PB[b][:, 128:256], in_=sb_.ap[:, 128:256], identity=ident.ap)

                    P.op("pe", trp, reads=sb_.keys() + CK, writes=PK[b])
                    act(pT.ap[:, :, blk * 128:(blk + 1) * 128], PB[b][:, 0:256].rearrange("p (k t) -> p k t", k=2), AF.Copy, PK[b], pT.keys())
                sigs = [sig, R["AT"].buf(0, [4, 512], F32)]

                def gate_gen(half):
                    sg = sigs[half]

                    def ev_gate(m, b, sg=sg):
                        act(sg.ap[:, m, :], PB[b][:, :], AF.Sigmoid, PK[b], sg.keys(m))

                    return gemm_gen("w_gt%d" % layer, half * 2, 2, 0, lambda kt: hn.ap[:, kt, :], lambda kt: hn.keys(kt), ev_gate)

                def pu_gen(half):
                    sg = sigs[half]

                    def ev_pu(m, b, half=half, sg=sg):
                        t_ = tmpf[2 + m % 2]
                        dve_tt(t_.ap, PB[b][:, :], sg.ap[:, m, :], ALU.mult, PK[b] + sg.keys(m), t_.keys())
                        dt = half * 4 + m
                        dve_tt(HT.ap[:, dt, :], HT.ap[:, dt, :], t_.ap, ALU.add, t_.keys() + HT.keys(dt), HT.keys(dt))

                    return gemm_gen("w_pu%d" % layer, half * 2, 2, 0, lambda kt: pT.ap[:, kt, :], lambda kt: pT.keys(), ev_pu)

                for _ in gate_gen(0):
                    pass
                interleave(gate_gen(1), pu_gen(0), lead=1)
                for _ in pu_gen(1):
                    pass
                if dbg == "l0" and layer == 0:
                    dbg_dump(HT.ap, HT.keys(), NT - 1, ti)

            P.stage = "t%d_out" % ti
            for blk in range(4):
                sb_ = STGb[blk % 2]
                r0 = t0 + blk * 128
                for half in range(2):
                    b = banks(1)[0]

                    def tro(e, half=half, b=b, blk=blk):
                        last = None
                        for j in range(4):
                            dt = half * 4 + j
                            last = e.transpose(out=PB[b][:, j * 128:(j + 1) * 128], in_=HT.ap[:, dt, blk * 128:(blk + 1) * 128], identity=ident.ap)
                        return last

                    P.op("pe", tro, reads=HT.keys() + CK, writes=PK[b])
                    if half == 0:
                        act(sb_.ap[:, 0:512], PB[b][:, :], AF.Copy, PK[b], sb_.keys())
                    else:
                        cp(sb_.ap[:, 512:1024], PB[b][:, :], PK[b], sb_.keys())
                sg = P.dma("sp", "stg%d" % (blk % 2), lambda e, sb_=sb_, r0=r0: e.dma_start(out=y_d[r0:r0 + 128, :], in_=sb_.ap), reads=sb_.keys())
                final_sigs.append(sg)

        fs = {}
        for s, v in final_sigs:
            fs[s] = max(fs.get(s, 0), v)
        if dbg_d is not None and "d_dbg" in P.cnt:
            fs["d_dbg"] = P.cnt["d_dbg"]
        P.finish("sp", list(fs.items()))
        P.emit()
    return nc


def _prep_shared(inp):
    f = lambda a: np.ascontiguousarray(np.asarray(a, dtype=np.float32))
    sh = {}
    sh["w_in"] = f(inp["w_in_even"][0])
    sh["w_out"] = f(inp["w_out_even"][0])
    sh["w_glu"] = f(inp["s5_w_glu"][0])
    sh["w_up"] = f(inp["w_mlp_up"])
    sh["w_dn"] = f(inp["w_mlp_down"])
    sh["w_gt"] = f(inp["w_ple_gate"])
    sh["w_pu"] = f(inp["w_ple_up"])
    sh["w_pool"] = f(inp["pool_w"][0])
    col8 = lambda v: np.asarray(v, np.float32).reshape(-1, 128).T
    qg = np.asarray(inp["sb_q_gain"][0], np.float32)
    kg = np.asarray(inp["sb_k_gain"][0], np.float32)
    qg128 = np.concatenate([qg, qg])
    kg128 = np.concatenate([kg, kg])
    half = (np.arange(128) < 64)
    scale = np.float32(64 ** -0.5)
    vec = np.concatenate([
        col8(inp["ln_mix_even"][0]), col8(inp["ln_mlp"][0]), col8(inp["ln_ple"][0]),
        col8(inp["ln_mix_odd"][0]), col8(inp["ln_mlp"][1]), col8(inp["ln_ple"][1]),
        col8(inp["pool_scale"][0]), col8(inp["s5_d"][0]),
        np.where(half, qg128, 0)[:, None], np.where(~half, qg128, 0)[:, None], kg128[:, None]], axis=1)
    sh["vecs"] = f(vec)
    lamr = np.asarray(inp["s5_lambda_re"][0], np.float32)
    lami = np.asarray(inp["s5_lambda_im"][0], np.float32)
    logdt = np.asarray(inp["s5_log_dt"][0], np.float32)
    toB = lambda a: a.reshape(16, 2, 64).transpose(1, 2, 0).reshape(128, 16)
    logdtB = toB(np.tile(logdt[:, None], (1, 64)))
    toB3 = lambda a: a.reshape(16, 2, 64, 16).transpose(1, 2, 0, 3).reshape(128, 256)
    bre = np.asarray(inp["s5_b_re"][0], np.float32)
    bim = np.asarray(inp["s5_b_im"][0], np.float32)
    cre = np.asarray(inp["s5_c_re"][0], np.float32).transpose(0, 2, 1)
    cim = np.asarray(inp["s5_c_im"][0], np.float32).transpose(0, 2, 1)
    sh["s5B"] = f(np.concatenate([toB(lamr), toB(lami), logdtB, toB3(bre), toB3(bim), toB3(cre), toB3(cim)], axis=1))
    toA = lambda a: np.tile(a.reshape(4, 8, 1, 64), (1, 1, 16, 1)).transpose(1, 2, 0, 3).reshape(128, 256)
    toA3 = lambda a: a.reshape(4, 8, 64, 16).transpose(1, 3, 0, 2).reshape(128, 256)
    sh["s5A"] = f(np.concatenate([toA(lamr), toA(lami), toA(np.tile(logdt[:, None], (1, 64))), toA3(bre), toA3(bim)], axis=1))
    sh.update(_consts())
    sh["_scale"] = scale
    return sh


_NC_CACHE = {}


def kernel(**inputs):
    x = np.asarray(inputs["x"], np.float32)
    p = np.asarray(inputs["p"], np.float32)
    B, L, Dm = x.shape
    NT = L // TT
    sh = _prep_shared(inputs)
    sh.pop("_scale")
    if NT not in _NC_CACHE:
        _NC_CACHE[NT] = build(NT)
    nc = _NC_CACHE[NT]
    in_maps = []
    for b in range(B):
        m = dict(sh)
        m["x"] = np.ascontiguousarray(x[b])
        m["p"] = np.ascontiguousarray(p[:, b])
        in_maps.append(m)
    res = run_bass_kernel_spmd(nc, in_maps, core_ids=list(range(B)))
    out = np.stack([res.results[b]["y"] for b in range(B)], axis=0)
    return out.astype(np.float32)
```

```python
import math
from contextlib import ExitStack

import numpy as np
import concourse.bass as bass
import concourse.mybir as mybir
from concourse.bass_utils import run_bass_kernel_spmd

F32 = mybir.dt.float32
BF16 = mybir.dt.bfloat16
I32 = mybir.dt.int32
AF = mybir.ActivationFunctionType
ALU = mybir.AluOpType

TT = 512
EPS = 1e-6
KLIST = list(range(-7, 9))
TWO_PI = 2.0 * math.pi
import os
MLP_PIPE = bool(int(os.environ.get("K_MLPPIPE", "1")))
REC_ENG = os.environ.get("K_REC", "dve")
E_ON_POOL = bool(int(os.environ.get("K_EPOOL", "0")))
SAME_ENGINE_SYNC = set(os.environ.get("K_SES", "").split(","))


class Prog:
    def __init__(self, nc, stack):
        self.nc = nc
        self.stack = stack
        self.ops = {e: [] for e in ("pe", "act", "dve", "pool", "sp")}
        self.sems = {}
        self.cnt = {}
        self.keys = {}
        self.waited = {e: {} for e in self.ops}
        self.final = []
        self.stage = "setup"
        self.scopes = bool(int(os.environ.get("K_SCOPES", "0")))
        self.ses = set(SAME_ENGINE_SYNC)
        self.near = {"dve": int(os.environ.get("K_NEAR_DVE", "2")), "act": int(os.environ.get("K_NEAR_ACT", "1")), "pool": 2}

    def sem(self, name):
        if name not in self.sems:
            self.sems[name] = self.stack.enter_context(self.nc.semaphore(name))
            self.cnt[name] = 0
        return self.sems[name]

    def _resolve(self, eng, reads, writes, mysig):
        waits = {}

        def add(sig):
            if sig is None:
                return
            s, v = sig
            if s == eng:
                if eng == "pe":
                    return
                if eng not in self.ses and (self.cnt[eng] - v) > self.near.get(eng, 0):
                    return
            if waits.get(s, 0) < v:
                waits[s] = v

        for k in reads:
            st = self.keys.setdefault(k, [None, []])
            add(st[0])
        for k in writes:
            st = self.keys.setdefault(k, [None, []])
            add(st[0])
            for r in st[1]:
                add(r)
        out = []
        for s, v in waits.items():
            if self.waited[eng].get(s, 0) >= v:
                continue
            self.waited[eng][s] = v
            out.append((s, v))
        for k in reads:
            self.keys[k][1].append(mysig)
        for k in writes:
            self.keys[k][0] = mysig
            self.keys[k][1] = []
        return out

    def op(self, eng, fn, reads=(), writes=()):
        self.sem(eng)
        self.cnt[eng] += 1
        mysig = (eng, self.cnt[eng])
        waits = self._resolve(eng, reads, writes, mysig)
        self.ops[eng].append((waits, fn, (eng, 1), self.stage))

    def dma(self, q, semkey, fn, reads=(), writes=()):
        name = "d_" + semkey
        self.sem(name)
        self.cnt[name] += 16
        mysig = (name, self.cnt[name])
        waits = self._resolve(q, reads, writes, mysig)
        self.ops[q].append((waits, fn, (name, 16), "dma"))
        return mysig

    def finish(self, eng, sigs):
        self.final.append((eng, sigs))

    def emit(self):
        nc = self.nc
        engmap = {"pe": "tensor", "act": "scalar", "dve": "vector", "pool": "gpsimd", "sp": "sync"}
        block = self.stack.enter_context(nc.Block())
        for e, attr in engmap.items():
            ops = self.ops[e]
            finals = [s for (fe, s) in self.final if fe == e]
            if not ops and not finals:
                continue

            def body(engine, ops=ops, finals=finals):
                for waits, fn, (sname, inc), stage in ops:
                    if self.scopes:
                        with nc.named_scope(stage):
                            for s, v in waits:
                                engine.wait_ge(self.sems[s], v)
                            inst = fn(engine)
                            inst.then_inc(self.sems[sname], inc)
                    else:
                        for s, v in waits:
                            engine.wait_ge(self.sems[s], v)
                        inst = fn(engine)
                        inst.then_inc(self.sems[sname], inc)
                for sigs in finals:
                    for s, v in sigs:
                        engine.wait_ge(self.sems[s], v)

            getattr(block, attr)(body)


KG = 512


class Region:
    def __init__(self, arena, name, woff, nbytes):
        self.arena, self.name, self.woff, self.nbytes = arena, name, woff, nbytes

    def buf(self, boff, shape, dt):
        return Buf(self, boff, shape, dt)


class Buf:
    def __init__(self, region, boff, shape, dt):
        esz = 4 if dt in (F32, I32) else 2
        n = int(np.prod(shape))
        assert boff % 4 == 0 and boff + n * esz <= region.nbytes, (region.name, boff, shape, region.nbytes)
        w0 = region.woff + boff // 4
        nw = (n * esz + 3) // 4
        ap = region.arena[:, w0:w0 + nw]
        if dt != F32:
            ap = ap.bitcast(dt)
        if len(shape) > 1:
            names = " ".join("a%d" % i for i in range(len(shape)))
            kw = {"a%d" % i: shape[i] for i in range(1, len(shape))}
            ap = ap.rearrange("p (%s) -> p %s" % (names, names), **kw)
        self.ap, self.region, self.boff, self.shape, self.esz = ap, region, boff, tuple(shape), esz
        self.nbytes = n * esz
        self.blk = (n // shape[0]) * esz

    def keys(self, lo=None, hi=None):
        if lo is None:
            b0, b1 = self.boff, self.boff + self.nbytes
        else:
            hi = lo + 1 if hi is None else hi
            b0, b1 = self.boff + lo * self.blk, self.boff + hi * self.blk
        return [(self.region.name, k) for k in range(b0 // KG, (b1 + KG - 1) // KG)]


def _consts():
    c = {}
    c["ident"] = np.eye(128, dtype=np.float32)
    jj = np.arange(128)[:, None]
    tt = np.arange(128)[None, :]
    ones = np.ones((128, 128), np.float32)
    blockones = ((jj // 64) == (tt // 64)).astype(np.float32)
    negtri = -(jj >= tt).astype(np.float32)
    masklt = (jj < tt).astype(np.float32)
    c["cstb"] = np.concatenate([ones, blockones, negtri, masklt], axis=1)
    part = np.arange(128)
    g8 = part // 16
    bd = (g8[:, None, None] == np.arange(8)[None, :, None]) * np.ones((1, 1, 16))
    g2 = (part // 16) % 2
    g2col = (g2[:, None] == np.arange(2)[None, :]).astype(np.float32)
    pp3 = (part >= 96).astype(np.float32)[:, None]
    halfA = (part < 64).astype(np.float32)[:, None]
    halfB = (part >= 64).astype(np.float32)[:, None]
    kv = np.tile(np.array(KLIST, np.float32)[None, :], (128, 1))
    kf = kv / TWO_PI
    kvA = np.tile(np.arange(8, dtype=np.float32)[None, :], (128, 1))
    kfA = kvA / TWO_PI
    cnt = np.zeros((128, 4, 16), np.float32)
    for g, w in enumerate((2, 4, 8, 16)):
        cnt[:, g, :] = 1.0 / np.minimum(np.arange(16) + 1, w)
    bands = np.zeros((128, 16, 128), np.float64)
    tq = np.arange(128)[:, None]
    tt_ = np.arange(128)[None, :]
    for g, w in enumerate((2, 4, 8, 16)):
        main = ((tq <= tt_) & (tq > tt_ - w)) / float(w) - (tq == tt_)
        spill = ((tq - 128) > (tt_ - w)) / float(w)
        cntv = np.minimum(tt_ + 1, w)
        b0 = ((tq <= tt_) & (tq > tt_ - w)) / cntv - (tq == tt_)
        import ml_dtypes
        hi = b0.astype(np.float32).astype(ml_dtypes.bfloat16).astype(np.float64)
        lo = b0 - hi
        bands[:, 4 * g + 0] = main
        bands[:, 4 * g + 1] = spill
        bands[:, 4 * g + 2] = hi
        bands[:, 4 * g + 3] = lo
    c["bands"] = bands.reshape(128, 2048).astype(np.float32)
    c["cstf"] = np.concatenate([bd.reshape(128, 128).astype(np.float32), g2col, pp3, halfA, halfB,
                                kv, kf, kvA, kfA, cnt.reshape(128, 64)], axis=1).astype(np.float32)
    return c


CF = {}
_o = 0
for _n, _w in (("bd", 128), ("g2col", 2), ("pp3", 1), ("halfA", 1), ("halfB", 1), ("kv", 16), ("kf", 16),
               ("kvA", 8), ("kfA", 8), ("cnt", 64)):
    CF[_n] = (_o, _w)
    _o += _w
NCF = _o

VEC = {}
_o = 0
for _n, _w in (("ln_mix0", 8), ("ln_mlp0", 8), ("ln_ple0", 8), ("ln_mix1", 8), ("ln_mlp1", 8), ("ln_ple1", 8),
               ("pscale", 8), ("s5d", 4), ("qgA", 1), ("qgB", 1), ("kgA", 1)):
    VEC[_n] = (_o, _w)
    _o += _w
NVEC = _o


def build(NT, dbg=None):
    L = NT * TT
    nc = bass.Bass("TRN2", target_bir_lowering=False)
    D = lambda n, s: nc.dram_tensor(n, s, F32, kind="ExternalInput").ap()
    x_d = D("x", [L, 1024])
    p_d = D("p", [2, L, 256])
    w_in = D("w_in", [1024, 2048])
    w_out = D("w_out", [1024, 1024])
    w_glu = D("w_glu", [512, 512])
    w_up = D("w_up", [2, 1024, 4096])
    w_dn = D("w_dn", [2, 4096, 1024])
    w_gt = D("w_gt", [2, 1024, 1024])
    w_pu = D("w_pu", [2, 256, 1024])
    w_pool = D("w_pool", [4, 256, 256])
    vecs_d = D("vecs", [128, NVEC])
    s5b_d = D("s5B", [128, 48 + 4 * 256])
    s5a_d = D("s5A", [128, 5 * 256])
    ident_d = D("ident", [128, 128])
    cstb_d = D("cstb", [128, 512])
    cstf_d = D("cstf", [128, NCF])
    bands_d = D("bands", [128, 2048])
    carry_d = nc.dram_tensor("poolcarry", [128, 1024], BF16, kind="Internal").ap()
    y_d = nc.dram_tensor("y", [L, 1024], F32, kind="ExternalOutput").ap()
    dbg_d = None
    if dbg:
        dbg_d = nc.dram_tensor("dbg", [128, 8, 512], F32, kind="ExternalOutput").ap()

    with ExitStack() as st:
        P = Prog(nc, st)
        sizes = [("KV", 65536), ("VS", 16384), ("YW", 16384), ("KD", 8192), ("WS", 3 * 4096), ("HT", 16384),
                 ("CST", 6144), ("STG", 8192), ("HN", 8192), ("SQ", 8192), ("AT", 8192), ("UB", 8192),
                 ("SCR", 8192), ("S5B", 12800), ("RS", 4096)]
        total_w = sum(s for _, s in sizes) // 4
        arena = st.enter_context(nc.sbuf_tensor("arena", [128, total_w], F32))
        R = {}
        wo = 0
        for n, s in sizes:
            R[n] = Region(arena, n, wo, s)
            wo += s // 4
        PB = [st.enter_context(nc.psum_tensor("pb%d" % i, [128, 512], F32)) for i in range(8)]
        PK = [[("pb%d" % i, 0)] for i in range(8)]

        KT = R["KV"].buf(0, [4, 4096], BF16)
        VC = R["KV"].buf(32768, [32, 512], BF16)
        VS = R["VS"].buf(0, [4, 2, 8, 128], BF16)
        YW = R["YW"].buf(0, [16, 8, 2, 32], BF16)
        KD = R["KD"].buf(0, [4, 8, 128], BF16)
        WS = [R["WS"].buf(i * 4096, [8, 256], BF16) for i in range(3)]
        HT = R["HT"].buf(0, [8, 512], F32)
        ident = R["CST"].buf(0, [128], F32)
        vecs = R["CST"].buf(512, [NVEC], F32)
        cstb = R["CST"].buf(1024, [4, 128], BF16)
        cstf = R["CST"].buf(2048, [NCF], F32)
        assert NCF * 4 <= 1024
        Acf = R["CST"].buf(3072, [2, 16], F32)
        Xc = R["CST"].buf(3584, [2, 16], F32)
        LB = R["CST"].buf(4096, [8, 16], F32)
        CE = R["CST"].buf(4608, [2, 2, 4], F32)
        RT = R["CST"].buf(5120, [4, 16], F32)
        EINV = R["CST"].buf(5632, [4], F32)

        def vcol(name, i=0):
            o, w = VEC[name]
            return vecs.ap[:, o + i:o + i + 1]

        def cf(name):
            o, w = CF[name]
            return cstf.ap[:, o:o + w]

        ones_b = cstb.ap[:, 0, :]
        blockones_b = cstb.ap[:, 1, :]
        negtri_b = cstb.ap[:, 2, :]
        masklt_b = cstb.ap[:, 3, :]
        CK = cstb.keys() + cstf.keys() + vecs.keys() + ident.keys()

        def dve_tt(out, in0, in1, op, r, w, eng="dve"):
            P.op(eng, lambda e: e.tensor_tensor(out=out, in0=in0, in1=in1, op=op), reads=r, writes=w)

        def dve_ts(out, in0, s1, op0, r, w, s2=None, op1=None, eng="dve"):
            if op1 is None:
                P.op(eng, lambda e: e.tensor_scalar(out=out, in0=in0, scalar1=s1, scalar2=None, op0=op0), reads=r, writes=w)
            else:
                P.op(eng, lambda e: e.tensor_scalar(out=out, in0=in0, scalar1=s1, scalar2=s2, op0=op0, op1=op1), reads=r, writes=w)

        def dve_stt(out, in0, scalar, in1, op0, op1, r, w):
            P.op("dve", lambda e: e.scalar_tensor_tensor(out=out, in0=in0, scalar=scalar, in1=in1, op0=op0, op1=op1), reads=r, writes=w)

        def act(out, in_, func, r, w, scale=1.0, bias=0.0):
            if func == AF.Copy:
                P.op("act", lambda e: e.activation(out=out, in_=in_, func=func, scale=scale), reads=r, writes=w)
            else:
                P.op("act", lambda e: e.activation(out=out, in_=in_, func=func, scale=scale, bias=bias), reads=r, writes=w)

        def cp(out, in_, r, w, eng="dve"):
            P.op(eng, lambda e: e.tensor_copy(out=out, in_=in_), reads=r, writes=w)

        P.dma("sp", "c0", lambda e: e.dma_start(out=ident.ap, in_=ident_d[:, :]), writes=ident.keys())
        P.dma("sp", "c1", lambda e: e.dma_start(out=vecs.ap, in_=vecs_d[:, :]), writes=vecs.keys())
        P.dma("sp", "c2", lambda e: e.dma_start(out=cstf.ap, in_=cstf_d[:, :]), writes=cstf.keys())
        P.dma("pool", "c3", lambda e: e.dma_start(out=cstb.ap.rearrange("p a b -> p (a b)"), in_=cstb_d[:, :]), writes=cstb.keys())

        P.op("dve", lambda e: e.memset(EINV.ap, math.exp(-1.0)), writes=EINV.keys())
        o_q = VEC["qgA"][0]
        dve_ts(vecs.ap[:, o_q:o_q + 2], vecs.ap[:, o_q:o_q + 2], 0.125, ALU.mult, vecs.keys(), vecs.keys())

        KVr = R["KV"]

        def s5_setup():
            o = [0]

            def tmp(shape, dt=F32):
                b = KVr.buf(o[0], shape, dt)
                o[0] += ((b.nbytes + 511) // 512) * 512
                return b

            inB = tmp([48 + 1024])
            P.dma("sp", "s5in", lambda e: e.dma_start(out=inB.ap, in_=s5b_d[:, :]), writes=inB.keys())
            lamr = inB.ap[:, 0:16]
            lami = inB.ap[:, 16:32]
            logdt = inB.ap[:, 32:48]
            bre = inB.ap[:, 48:304].rearrange("p (q h) -> p q h", q=16)
            bim = inB.ap[:, 304:560].rearrange("p (q h) -> p q h", q=16)
            cre = inB.ap[:, 560:816].rearrange("p (q h) -> p q h", q=16)
            cim = inB.ap[:, 816:1072].rearrange("p (q h) -> p q h", q=16)
            kin = inB.keys()
            sm = tmp([8, 16])
            smk = sm.keys()
            dt_, lrdt, lidt, den, nr, fr, fi, t0 = [sm.ap[:, i, :] for i in range(8)]
            act(dt_, logdt, AF.Exp, kin, smk)
            dve_tt(lrdt, lamr, dt_, ALU.mult, kin + smk, smk)
            dve_tt(lidt, lami, dt_, ALU.mult, kin + smk, smk)
            NK = len(KLIST)
            big = [tmp([NK, 16]) for _ in range(6)]
            mag, Tt, t1, t2, cosv, sinv = big
            allk = sum([b.keys() for b in big], [])
            kv = cf("kv")
            kf = cf("kf")
            bc_q = lambda a: a.unsqueeze(1).to_broadcast([128, NK, 16])
            bc_k = lambda a: a.unsqueeze(2).to_broadcast([128, NK, 16])
            dve_tt(mag.ap, bc_q(lrdt), bc_k(kv), ALU.mult, smk + CK, allk)
            act(mag.ap, mag.ap, AF.Exp, allk, allk)
            dve_tt(Tt.ap, bc_q(lidt), bc_k(kf), ALU.mult, smk + CK, allk)

            def sincos(dst, shift, Tsrc, a1, a2, keys):
                dve_ts(a1.ap, Tsrc.ap, shift, ALU.add, keys, keys)
                cp(a2.ap.bitcast(I32), a1.ap, keys, keys)
                cp(a2.ap, a2.ap.bitcast(I32), keys, keys)
                dve_tt(a1.ap, a1.ap, a2.ap, ALU.subtract, keys, keys)
                dve_stt(a1.ap, a1.ap, 0.0, a1.ap, ALU.is_lt, ALU.add, keys, keys)
                act(dst.ap, a1.ap, AF.Sin, keys, keys, scale=TWO_PI * (1 - 1e-6), bias=-math.pi * (1 - 1e-6))

            sincos(cosv, 0.75 + 32.0, Tt, t1, t2, allk)
            sincos(sinv, 0.5 + 32.0, Tt, t1, t2, allk)
            dve_tt(cosv.ap, cosv.ap, mag.ap, ALU.mult, allk, allk)
            dve_tt(sinv.ap, sinv.ap, mag.ap, ALU.mult, allk, allk)
            Er = lambda k: cosv.ap[:, k + 7, :]
            Ei = lambda k: sinv.ap[:, k + 7, :]
            cp(Acf.ap[:, 0, :], Er(8), allk, Acf.keys())
            cp(Acf.ap[:, 1, :], Ei(8), allk, Acf.keys())
            dve_tt(den, lamr, lamr, ALU.mult, kin, smk)
            dve_tt(t0, lami, lami, ALU.mult, kin, smk)
            dve_tt(den, den, t0, ALU.add, smk, smk)
            P.op("dve", lambda e: e.reciprocal(out=den, in_=den), reads=smk, writes=smk)
            dve_ts(nr, Er(1), -1.0, ALU.add, allk, smk)
            dve_tt(fr, nr, lamr, ALU.mult, smk + kin, smk)
            dve_tt(t0, Ei(1), lami, ALU.mult, allk + kin, smk)
            dve_tt(fr, fr, t0, ALU.add, smk, smk)
            dve_tt(fr, fr, den, ALU.mult, smk, smk)
            dve_tt(fi, Ei(1), lamr, ALU.mult, allk + kin, smk)
            dve_tt(t0, nr, lami, ALU.mult, smk + kin, smk)
            dve_tt(fi, fi, t0, ALU.subtract, smk, smk)
            dve_tt(fi, fi, den, ALU.mult, smk, smk)
            bb = tmp([4, 16, 16])
            bbk = bb.keys()
            Bbr, Bbi, tA, tB = [bb.ap[:, i] for i in range(4)]
            bch = lambda a: a.unsqueeze(2).to_broadcast([128, 16, 16])
            dve_tt(Bbr, bre, bch(fr), ALU.mult, kin + smk, bbk)
            dve_tt(tA, bim, bch(fi), ALU.mult, kin + smk, bbk)
            dve_tt(Bbr, Bbr, tA, ALU.subtract, bbk, bbk)
            dve_tt(Bbi, bim, bch(fr), ALU.mult, kin + smk, bbk)
            dve_tt(tA, bre, bch(fi), ALU.mult, kin + smk, bbk)
            dve_tt(Bbi, Bbi, tA, ALU.add, bbk, bbk)
            Rr = tmp([9, 16, 16])
            Ri = tmp([9, 16, 16])
            Rt = tmp([9, 16, 16])
            rk = Rr.keys() + Ri.keys() + Rt.keys()
            bcC = lambda a: a.unsqueeze(1).to_broadcast([128, 9, 16, 16])
            bcE = lambda a: a.unsqueeze(3).to_broadcast([128, 9, 16, 16])
            Er9 = cosv.ap[:, 7:16, :]
            Ei9 = sinv.ap[:, 7:16, :]
            dve_tt(Rr.ap, bcC(cre), bcE(Er9), ALU.mult, kin + allk, rk)
            dve_tt(Rt.ap, bcC(cim), bcE(Ei9), ALU.mult, kin + allk, rk)
            dve_tt(Rr.ap, Rr.ap, Rt.ap, ALU.subtract, rk, rk)
            dve_tt(Ri.ap, bcC(cre), bcE(Ei9), ALU.mult, kin + allk, rk)
            dve_tt(Rt.ap, bcC(cim), bcE(Er9), ALU.mult, kin + allk, rk)
            dve_tt(Ri.ap, Ri.ap, Rt.ap, ALU.add, rk, rk)
            Bz = tmp([2, 16, 128])
            bzk = Bz.keys()
            P.op("dve", lambda e: e.memset(Bz.ap, 0.0), writes=bzk)
            for half in range(2):
                ps_ = slice(half * 64, half * 64 + 64)
                for pp in range(4):
                    cs_ = slice(pp * 32 + half * 16, pp * 32 + half * 16 + 16)
                    cp(Bz.ap[ps_, 0, pp:16:4, cs_], Bbr[ps_, pp:16:4, :], bbk, bzk)
                    dve_ts(Bz.ap[ps_, 1, pp:16:4, cs_], Bbi[ps_, pp:16:4, :], -1.0, ALU.mult, bbk, bzk)
            bd = cf("bd").rearrange("p (a b) -> p a b", a=8)
            for ct in range(4):
                bank = PB[ct]

                def mm(e, ct=ct, bank=bank):
                    last = None
                    for pp in range(4):
                        q = 4 * ct + pp
                        e.matmul(out=bank[:, 0:128], lhsT=Bz.ap[:, 0, q, :], rhs=Rr.ap[:, 0:8, q, :],
                                 start=(pp == 0), stop=False)
                        last = e.matmul(out=bank[:, 0:128], lhsT=Bz.ap[:, 1, q, :], rhs=Ri.ap[:, 0:8, q, :],
                                        start=False, stop=(pp == 3))
                    return last

                P.op("pe", mm, reads=bzk + rk, writes=PK[ct])
                src = bank[:, 0:128].rearrange("p (t h) -> p t h", t=8).unsqueeze(2).to_broadcast([128, 8, 8, 16])
                msk = bd.unsqueeze(1).to_broadcast([128, 8, 8, 16])
                dst = KD.ap[:, ct].rearrange("p t (g h) -> p t g h", g=8)
                dve_tt(dst, src, msk, ALU.mult, PK[ct] + CK, KD.keys())
            P.op("dve", lambda e: e.memset(YW.ap, 0.0), writes=YW.keys())
            for half in range(2):
                ps_ = slice(half * 64, half * 64 + 64)
                cs_ = slice(half * 16, half * 16 + 16)
                srcr = Rr.ap[ps_, 1:9].rearrange("p k q h -> p q k h")
                srci = Ri.ap[ps_, 1:9].rearrange("p k q h -> p q k h")
                cp(YW.ap[ps_, :, :, 0, cs_], srcr, rk, YW.keys())
                dve_ts(YW.ap[ps_, :, :, 1, cs_], srci, -1.0, ALU.mult, rk, YW.keys())
            o[0] = 0
            inA = tmp([5, 256])
            P.dma("sp", "s5in", lambda e: e.dma_start(out=inA.ap.rearrange("p a b -> p (a b)"), in_=s5a_d[:, :]), writes=inA.keys())
            kia = inA.keys()
            lamrA, lamiA, logdtA, breA, bimA = [inA.ap[:, i, :] for i in range(5)]
            smA = tmp([8, 256])
            sak = smA.keys()
            dtA, lrdtA, lidtA, denA, nrA, frA, fiA, t0A = [smA.ap[:, i, :] for i in range(8)]
            act(dtA, logdtA, AF.Exp, kia, sak)
            dve_tt(lrdtA, lamrA, dtA, ALU.mult, kia + sak, sak)
            dve_tt(lidtA, lamiA, dtA, ALU.mult, kia + sak, sak)
            bigA = [tmp([8, 256]) for _ in range(6)]
            magA, TA, t1A, t2A, cosA, sinA = bigA
            ak = sum([b.keys() for b in bigA], [])
            bq = lambda a: a.unsqueeze(1).to_broadcast([128, 8, 256])
            bk = lambda a: a.unsqueeze(2).to_broadcast([128, 8, 256])
            dve_tt(magA.ap, bq(lrdtA), bk(cf("kvA")), ALU.mult, sak + CK, ak)
            act(magA.ap, magA.ap, AF.Exp, ak, ak)
            dve_tt(TA.ap, bq(lidtA), bk(cf("kfA")), ALU.mult, sak + CK, ak)
            sincos(cosA, 0.75 + 32.0, TA, t1A, t2A, ak)
            sincos(sinA, 0.5 + 32.0, TA, t1A, t2A, ak)
            dve_tt(cosA.ap, cosA.ap, magA.ap, ALU.mult, ak, ak)
            dve_tt(sinA.ap, sinA.ap, magA.ap, ALU.mult, ak, ak)
            dve_tt(denA, lamrA, lamrA, ALU.mult, kia, sak)
            dve_tt(t0A, lamiA, lamiA, ALU.mult, kia, sak)
            dve_tt(denA, denA, t0A, ALU.add, sak, sak)
            P.op("dve", lambda e: e.reciprocal(out=denA, in_=denA), reads=sak, writes=sak)
            dve_ts(nrA, cosA.ap[:, 1, :], -1.0, ALU.add, ak, sak)
            dve_tt(frA, nrA, lamrA, ALU.mult, sak + kia, sak)
            dve_tt(t0A, sinA.ap[:, 1, :], lamiA, ALU.mult, ak + kia, sak)
            dve_tt(frA, frA, t0A, ALU.add, sak, sak)
            dve_tt(frA, frA, denA, ALU.mult, sak, sak)
            dve_tt(fiA, sinA.ap[:, 1, :], lamrA, ALU.mult, ak + kia, sak)
            dve_tt(t0A, nrA, lamiA, ALU.mult, sak + kia, sak)
            dve_tt(fiA, fiA, t0A, ALU.subtract, sak, sak)
            dve_tt(fiA, fiA, denA, ALU.mult, sak, sak)
            BbrA, BbiA = dtA, denA
            dve_tt(t0A, bimA, fiA, ALU.mult, kia + sak, sak)
            dve_tt(BbrA, breA, frA, ALU.mult, kia + sak, sak)
            dve_tt(BbrA, BbrA, t0A, ALU.subtract, sak, sak)
            dve_tt(t0A, breA, fiA, ALU.mult, kia + sak, sak)
            dve_tt(BbiA, bimA, frA, ALU.mult, kia + sak, sak)
            dve_tt(BbiA, BbiA, t0A, ALU.add, sak, sak)
            Wr, Wi, tW = magA, TA, t1A
            dve_tt(Wr.ap, bq(BbrA), cosA.ap, ALU.mult, sak + ak, ak)
            dve_tt(tW.ap, bq(BbiA), sinA.ap, ALU.mult, sak + ak, ak)
            dve_tt(Wr.ap, Wr.ap, tW.ap, ALU.subtract, ak, ak)
            dve_tt(Wi.ap, bq(BbrA), sinA.ap, ALU.mult, sak + ak, ak)
            dve_tt(tW.ap, bq(BbiA), cosA.ap, ALU.mult, sak + ak, ak)
            dve_tt(Wi.ap, Wi.ap, tW.ap, ALU.add, ak, ak)
            g2c = cf("g2col")
            for ri, Wx in enumerate((Wr, Wi)):
                src = Wx.ap.rearrange("p k (c s) -> p c k s", c=4)
                for g2p in range(2):
                    dst = VS.ap[:, :, ri, :, g2p * 64:(g2p + 1) * 64]
                    dve_ts(dst, src, g2c[:, g2p:g2p + 1], ALU.mult, ak + CK, VS.keys())
            P.op("dve", lambda e: e.memset(Xc.ap, 0.0), writes=Xc.keys())
            allkv = [("KV", k) for k in range(65536 // KG)]
            P.op("dve", lambda e: e.memset(LB.ap, 0.0), reads=allkv,
                 writes=LB.keys() + [("KTc", i) for i in range(NT)] + [("VCc", i) for i in range(NT)])

        P.ses = set(os.environ.get("K_SES_SETUP", "").split(","))
        s5_setup()
        P.ses = set(SAME_ENGINE_SYNC)

        WSPEC = {"w_in": w_in, "w_glu": w_glu, "w_out": w_out}
        for l_ in range(2):
            WSPEC["w_up%d" % l_] = w_up[l_]
            WSPEC["w_dn%d" % l_] = w_dn[l_]
            WSPEC["w_gt%d" % l_] = w_gt[l_]
            WSPEC["w_pu%d" % l_] = w_pu[l_]
        SCRT = {}
        WNKT = {}
        for name, W in WSPEC.items():
            K_, N_ = W.shape
            nkt = min(8, K_ // 128)
            WNKT[name] = nkt
            SCRT[name] = nc.dram_tensor("s_" + name, [N_ // 256, K_ // (128 * nkt), 128, nkt * 256], BF16, kind="Internal").ap()
        SCRT["pool"] = nc.dram_tensor("s_pool", [4, 1, 128, 512], BF16, kind="Internal").ap()
        WNKT["pool"] = 2

        CONV = []

        def convert(name, chunks):
            for (c, kc) in chunks:
                CONV.append((name, c, kc))

        convert("w_in", [(c, 0) for c in range(8)])
        convert("w_glu", [(c, 0) for c in range(2)])
        convert("w_out", [(c, 0) for c in range(4)])
        for l_ in range(2):
            if l_ == 1:
                convert("pool", [(g_, 0) for g_ in range(4)])
            for qq in range(4):
                convert("w_up%d" % l_, [(qq * 4 + c, 0) for c in range(4)])
                convert("w_dn%d" % l_, [(c, qq) for c in range(4)])
            for half in range(2):
                convert("w_gt%d" % l_, [(half * 2 + c, 0) for c in range(2)])
                convert("w_pu%d" % l_, [(half * 2 + c, 0) for c in range(2)])
        CONV_IDX = {k_: i_ for i_, k_ in enumerate(CONV)}
        cvp = [0]
        NCV = 8
        LOOKAHEAD = int(os.environ.get("K_LA", "6"))

        def conv_upto(idx):
            while cvp[0] <= min(idx, len(CONV) - 1):
                i_ = cvp[0]
                cvp[0] += 1
                name, c, kc = CONV[i_]
                nkt = WNKT[name]
                if name == "pool":
                    src = w_pool[c].rearrange("(k p) n -> p k n", p=128)
                else:
                    W = WSPEC[name]
                    src = W[kc * nkt * 128:(kc + 1) * nkt * 128, c * 256:(c + 1) * 256].rearrange("(k p) n -> p k n", p=128)
                dst = SCRT[name][c, kc].rearrange("p (k n) -> p k n", k=nkt)
                P.dma("pool", "cv%d" % (i_ % NCV), lambda e, src=src, dst=dst: e.dma_start(out=dst, in_=src),
                      writes=[("scr", name, c, kc), ("cvring", i_ % NCV)])

        wctr = [0]
        NSLOT = len(WS)
        PRE = {}

        def prefetch(chunks):
            for key in chunks:
                if key not in PRE:
                    PRE[key] = load_w(*key)

        def load_w(name, c, kc=0):
            s = wctr[0] % NSLOT
            wctr[0] += 1
            slot = WS[s]
            nkt = WNKT[name]
            conv_upto(CONV_IDX[(name, c, kc)] + LOOKAHEAD)
            src = SCRT[name][c, kc].rearrange("p (k n) -> p k n", k=nkt)
            P.dma("pool", "ws%d" % s, lambda e: e.dma_start(out=slot.ap[:, 0:nkt, :], in_=src),
                  reads=[("scr", name, c, kc)], writes=slot.keys())
            return slot

        pctr = [0]

        def banks(n):
            b = [(pctr[0] + i) % 8 for i in range(n)]
            pctr[0] = (pctr[0] + n) % 8
            return b

        def gemm_gen(name, c0, nchunks, kc, rhs_fn, rkeys_fn, evac):
            nkt = WNKT[name]
            for c in range(nchunks):
                slot = PRE.pop((name, c0 + c, kc), None)
                if slot is None:
                    slot = load_w(name, c0 + c, kc)
                bs = banks(2)
                for m in range(2):
                    b = bs[m]

                    def mm(e, m=m, b=b, slot=slot):
                        last = None
                        for kt in range(nkt):
                            last = e.matmul(out=PB[b][:, :], lhsT=slot.ap[:, kt, m * 128:(m + 1) * 128], rhs=rhs_fn(kt),
                                            start=(kt == 0), stop=(kt == nkt - 1))
                        return last

                    rk = sum([rkeys_fn(kt) for kt in range(nkt)], [])
                    P.op("pe", mm, reads=slot.keys() + rk, writes=PK[b])
                    evac(c * 2 + m, b)
                yield c

        def gemm_fm(*a):
            for _ in gemm_gen(*a):
                pass

        def interleave(g1, g2, lead=1):
            for _ in range(lead):
                next(g1, None)
            d1 = d2 = False
            while not (d1 and d2):
                if not d2:
                    d2 = next(g2, "done") == "done"
                if not d1:
                    d1 = next(g1, "done") == "done"

        STGb = [R["STG"].buf(i * 4096, [1024], F32) for i in range(2)]
        hn = R["HN"].buf(0, [8, 512], BF16)
        mixT = R["HN"].buf(0, [8, 512], BF16)
        sq = R["SQ"].buf(0, [8, 512], BF16)
        qz = R["SQ"].buf(0, [8, 512], BF16)
        gb = R["SQ"].buf(0, [4, 512], BF16)
        sig = R["SQ"].buf(0, [4, 512], F32)
        acc = R["AT"].buf(0, [4, 512], F32)
        y32 = R["AT"].buf(0, [4, 512], F32)
        aTs = [R["AT"].buf(0, [8, 512], BF16), R["S5B"].buf(0, [8, 512], BF16)]
        uTb = R["UB"].buf(0, [4, 512], BF16)
        uM = R["UB"].buf(4096, [4, 512], BF16)
        r32 = [R["UB"].buf(i * 2048, [512], F32) for i in range(4)]
        pT = R["STG"].buf(2048, [2, 512], BF16)
        pstg = [R["STG"].buf(i * 4096, [256], F32) for i in range(2)]
        e32 = R["SCR"].buf(0, [512], F32)
        spb = [R["SCR"].buf(2048 + i * 1024, [512], BF16) for i in range(3)]
        wlb = [R["SCR"].buf(5120 + i * 1024, [512], BF16) for i in range(3)]
        tmpf = [R["SCR"].buf(i * 2048, [512], F32) for i in range(4)]
        SX = R["S5B"].buf(0, [2, 16, 64], F32)
        Xp = R["S5B"].buf(8192, [2, 16, 64], BF16)
        hnp = [R["S5B"].buf(i * 2112, [528], F32) for i in range(2)]
        ptmp = [R["S5B"].buf(4224 + i * 2112, [528], F32) for i in range(2)]
        ypool = [R["S5B"].buf(8448 + i * 2048, [2, 512], BF16) for i in range(2)]
        hTok = R["AT"].buf(0, [4, 1024], BF16)
        ypool8 = R["UB"].buf(0, [8, 512], BF16)
        bandsb = R["S5B"].buf(0, [16, 128], BF16)
        carryb = R["S5B"].buf(4096, [1024], BF16)
        rstdc = R["RS"].buf(0, [4], F32)
        lnc = R["RS"].buf(512, [4], F32)
        rstd = R["RS"].buf(0, [512], F32)
        sqt = R["RS"].buf(2048, [512], F32)

        def dbg_dump(src_ap, keys, ti_sel, ti, slot=None):
            if dbg_d is None or ti != ti_sel:
                return
            dst = dbg_d[:, 0:src_ap.shape[1], :] if slot is None else dbg_d[:, slot, :]
            return P.dma("sp", "dbg", lambda e: e.dma_start(out=dst, in_=src_ap), reads=keys)

        NPOOL = int(os.environ.get("K_NPOOL", "3"))

        def rmsnorm(gain_name, out_fn, out_keys_fn, fp32_out=False):
            b = banks(1)[0]
            for dt in range(8):
                act(sq.ap[:, dt, :], HT.ap[:, dt, :], AF.Square, HT.keys(dt), sq.keys(dt))
                P.op("pe", lambda e, dt=dt: e.matmul(out=PB[b][:, :], lhsT=ones_b, rhs=sq.ap[:, dt, :], start=(dt == 0), stop=(dt == 7)),
                     reads=sq.keys(dt) + CK, writes=PK[b])
            act(sqt.ap, PB[b][:, :], AF.Ln, PK[b], sqt.keys(), scale=1.0 / 1024.0, bias=EPS)
            act(rstd.ap, sqt.ap, AF.Exp, sqt.keys(), rstd.keys(), scale=-0.5)
            for i, dt in enumerate(range(8 - NPOOL, 8)):
                act(tmpf[i].ap, HT.ap[:, dt, :], AF.Copy, HT.keys(dt) + CK, tmpf[i].keys(), scale=vcol(gain_name, dt))
            for i, dt in enumerate(range(8 - NPOOL, 8)):
                P.op("pool", lambda e, i=i, dt=dt: e.tensor_tensor(out=out_fn(dt), in0=tmpf[i].ap, in1=rstd.ap, op=ALU.mult),
                     reads=tmpf[i].keys() + rstd.keys(), writes=out_keys_fn(dt))
            for dt in range(8 - NPOOL):
                dve_stt(out_fn(dt), HT.ap[:, dt, :], vcol(gain_name, dt), rstd.ap, ALU.mult, ALU.mult,
                        HT.keys(dt) + rstd.keys() + CK, out_keys_fn(dt))

        final_sigs = []

        for ti in range(NT):
            t0 = ti * TT
            P.stage = "t%d_load" % ti
            for blk in range(4):
                sb_ = STGb[blk % 2]
                r0 = t0 + blk * 128
                P.dma("sp", "stg%d" % (blk % 2), lambda e, sb_=sb_, r0=r0: e.dma_start(out=sb_.ap, in_=x_d[r0:r0 + 128, :]), writes=sb_.keys())
                for half in range(2):
                    b = banks(1)[0]

                    def tr(e, sb_=sb_, half=half, b=b):
                        last = None
                        for j in range(4):
                            dt = half * 4 + j
                            last = e.transpose(out=PB[b][:, j * 128:(j + 1) * 128], in_=sb_.ap[:, dt * 128:(dt + 1) * 128], identity=ident.ap)
                        return last

                    P.op("pe", tr, reads=sb_.keys() + CK, writes=PK[b])
                    dst = HT.ap[:, half * 4:half * 4 + 4, blk * 128:(blk + 1) * 128]
                    src = PB[b][:, :].rearrange("p (j t) -> p j t", j=4)
                    wk = sum([HT.keys(half * 4 + j) for j in range(4)], [])
                    if half == 0:
                        act(dst, src, AF.Copy, PK[b], wk)
                    else:
                        cp(dst, src, PK[b], wk)

            for layer in range(2):
                if layer == 0:
                    P.stage = "t%d_inproj" % ti
                    prefetch([("w_in", c_, 0) for c_ in range(3)])
                    rmsnorm("ln_mix0", lambda dt: hn.ap[:, dt, :], lambda dt: hn.keys(dt))
                    hn_r = lambda kt: hn.ap[:, kt, :]
                    hn_k = lambda kt: hn.keys(kt)

                    def ev_u(m, b):
                        act(uTb.ap[:, m, :], PB[b][:, :], AF.Copy, PK[b], uTb.keys(m))

                    gemm_fm("w_in", 0, 2, 0, hn_r, hn_k, ev_u)
                    dve_ts(uM.ap[64:128], uTb.ap[64:128], cf("pp3")[64:128, :], ALU.mult, uTb.keys() + CK, uM.keys())

                    def qk_evac(is_q):
                        def ev(m, b):
                            s_ = tmpf[m % 2]
                            sb16 = s_.ap.bitcast(BF16)[:, 0:512]
                            act(sb16, PB[b][:, :], AF.Square, PK[b], s_.keys())
                            b2 = banks(1)[0]
                            P.op("pe", lambda e: e.matmul(out=PB[b2][:, :], lhsT=blockones_b, rhs=sb16, start=True, stop=True),
                                 reads=s_.keys() + CK, writes=PK[b2])
                            r_ = tmpf[2 + m % 2]
                            act(r_.ap, PB[b2][:, :], AF.Ln, PK[b2], r_.keys(), scale=1.0 / 64.0, bias=EPS)
                            act(r_.ap, r_.ap, AF.Exp, r_.keys(), r_.keys(), scale=-0.5)
                            if is_q:
                                dve_stt(qz.ap[:, 2 * m, :], PB[b][:, :], vcol("qgA"), r_.ap, ALU.mult, ALU.mult,
                                        PK[b] + r_.keys() + CK, qz.keys(2 * m))
                                dve_stt(qz.ap[:, 2 * m + 1, :], PB[b][:, :], vcol("qgB"), r_.ap, ALU.mult, ALU.mult,
                                        PK[b] + r_.keys() + CK, qz.keys(2 * m + 1))
                            else:
                                dve_stt(KT.ap[:, m, t0:t0 + 512], PB[b][:, :], vcol("kgA"), r_.ap, ALU.mult, ALU.mult,
                                        PK[b] + r_.keys() + CK, [("KTc", ti)])
                        return ev

                    gemm_fm("w_in", 2, 2, 0, hn_r, hn_k, qk_evac(True))
                    gemm_fm("w_in", 4, 2, 0, hn_r, hn_k, qk_evac(False))
                    for c in range(2):
                        slot = load_w("w_in", 6 + c, 0)
                        bs = banks(2)
                        for blk in range(4):
                            b = bs[blk // 2]
                            cs = slice((blk % 2) * 256, (blk % 2) * 256 + 256)

                            def mm(e, blk=blk, b=b, cs=cs, slot=slot):
                                last = None
                                for kt in range(8):
                                    last = e.matmul(out=PB[b][:, cs], lhsT=hn.ap[:, kt, blk * 128:(blk + 1) * 128], rhs=slot.ap[:, kt, :],
                                                    start=(kt == 0), stop=(kt == 7))
                                return last

                            P.op("pe", mm, reads=slot.keys() + hn.keys(), writes=[("pbh%d" % b, blk % 2)] + PK[b])
                        for blk in range(4):
                            b = bs[blk // 2]
                            cs = slice((blk % 2) * 256, (blk % 2) * 256 + 256)
                            dst = VC.ap[:, ti * 4 + blk, c * 256:(c + 1) * 256]
                            if blk % 2 == 0:
                                act(dst, PB[b][:, cs], AF.Copy, PK[b], [("VCc", ti)])
                            else:
                                cp(dst, PB[b][:, cs], PK[b], [("VCc", ti)])

                    P.stage = "t%d_att" % ti
                    sbanks = [6, 7, 6, 7]
                    t1_, t2_, t3_, t4_ = [RT.ap[:, i, :] for i in range(4)]
                    tk = RT.keys()
                    Ar = Acf.ap[:, 0, :]
                    Ai = Acf.ap[:, 1, :]
                    sxk = SX.keys()
                    for grp in range(2):
                        def mmS(e, grp=grp):
                            last = None
                            for pp in (2 * grp, 2 * grp + 1):
                                bnk = PB[6 + pp % 2]
                                for ct in range(4):
                                    for ri in range(2):
                                        col = (ct * 2 + ri) * 64
                                        for j in range(8):
                                            k = 7 - j
                                            if pp < 3:
                                                rows = slice(pp * 32, pp * 32 + 32)
                                                rhs = uTb.ap[rows, ct, j:512:8]
                                            else:
                                                rows = slice(64, 128)
                                                rhs = uM.ap[rows, ct, j:512:8]
                                            last = e.matmul(out=bnk[:, col:col + 64], lhsT=VS.ap[rows, ct, ri, k, :], rhs=rhs,
                                                            start=(j == 0), stop=(j == 7))
                            return last

                        P.op("pe", mmS, reads=uTb.keys() + uM.keys() + VS.keys(), writes=PK[6] + PK[7])
                        for pp in (2 * grp, 2 * grp + 1):
                            bk_ = 6 + pp % 2
                            src = PB[bk_][:, :].rearrange("p (c r n) -> p r c n", c=4, r=2)
                            for ri in range(2):
                                dst = SX.ap[:, ri, pp:16:4, :]
                                cp(dst, src[:, ri], PK[bk_], sxk)
                    cp(Xp.ap[:, :, :, 0], Xc.ap, Xc.keys(), Xp.keys())

                    def rec_step(c):
                        pr = Xc.ap[:, 0, :] if c == 0 else SX.ap[:, 0, :, c - 1]
                        pi = Xc.ap[:, 1, :] if c == 0 else SX.ap[:, 1, :, c - 1]
                        rk_ = (Xc.keys() if c == 0 else []) + sxk + Acf.keys() + tk

                        def rec(e, pr=pr, pi=pi, c=c):
                            e.tensor_tensor(out=t1_, in0=Ar, in1=pr, op=ALU.mult)
                            e.tensor_tensor(out=t2_, in0=Ai, in1=pi, op=ALU.mult)
                            e.tensor_tensor(out=t3_, in0=Ar, in1=pi, op=ALU.mult)
                            e.tensor_tensor(out=t4_, in0=Ai, in1=pr, op=ALU.mult)
                            e.tensor_tensor(out=t1_, in0=t1_, in1=t2_, op=ALU.subtract)
                            e.tensor_tensor(out=t3_, in0=t3_, in1=t4_, op=ALU.add)
                            e.tensor_tensor(out=SX.ap[:, 0, :, c], in0=SX.ap[:, 0, :, c], in1=t1_, op=ALU.add)
                            return e.tensor_tensor(out=SX.ap[:, 1, :, c], in0=SX.ap[:, 1, :, c], in1=t3_, op=ALU.add)

                        P.op(REC_ENG, rec, reads=rk_, writes=sxk + tk)

                    steps = []
                    for h in range(8):
                        for kb in range(4 * ti + 3, -1, -1):
                            steps.append((h, kb))
                    nst = len(steps)
                    rec_per = (64 + nst - 1) // nst
                    rec_done = [0]

                    def geom(i):
                        h, kb = steps[i]
                        b_ = kb - 4 * ti
                        diag = b_ >= 0
                        c0 = 128 * b_ if diag else 0
                        sb0 = b_ if diag else 0
                        return h, kb, diag, c0, sb0, 512 - c0

                    def stageA(i):
                        h, kb, diag, c0, sb0, N = geom(i)
                        zb = i % 2
                        sp_ = spb[i % 3]
                        kt_l = KT.ap[:, h // 2, kb * 128:(kb + 1) * 128]
                        q_r = qz.ap[:, h, c0:512]
                        P.op("pe", lambda e: e.matmul(out=PB[zb][:, 0:N], lhsT=kt_l, rhs=q_r, start=True, stop=True),
                             reads=[("KTc", kb // 4)] + qz.keys(h), writes=PK[zb])
                        act(e32.ap[:, 0:N], PB[zb][:, 0:N], AF.Exp, PK[zb], e32.keys())
                        act(sp_.ap[:, 0:N], e32.ap[:, 0:N], AF.Ln, e32.keys(), sp_.keys(), bias=1.0)
                        if diag:
                            dve_tt(sp_.ap[:, 0:128], sp_.ap[:, 0:128], masklt_b, ALU.mult, sp_.keys() + CK, sp_.keys())

                    def stageB(i):
                        h, kb, diag, c0, sb0, N = geom(i)
                        ab = 2 + i % 2
                        sp_ = spb[i % 3]
                        wl_ = wlb[i % 3]
                        kt_l = KT.ap[:, h // 2, kb * 128:(kb + 1) * 128]
                        q_r = qz.ap[:, h, c0:512]

                        def mm2(e):
                            e.matmul(out=PB[ab][:, 0:N], lhsT=kt_l, rhs=q_r, start=True, stop=False)
                            return e.matmul(out=PB[ab][:, 0:N], lhsT=negtri_b, rhs=sp_.ap[:, 0:N], start=False, stop=True)

                        P.op("pe", mm2, reads=[("KTc", kb // 4)] + qz.keys(h) + sp_.keys() + CK, writes=PK[ab])
                        act(wl_.ap[:, 0:N], PB[ab][:, 0:N], AF.Exp, PK[ab], wl_.keys())
                        if diag:
                            dve_tt(wl_.ap[:, 0:128], wl_.ap[:, 0:128], masklt_b, ALU.mult, wl_.keys() + CK, wl_.keys())

                    def stageC(i):
                        h, kb, diag, c0, sb0, N = geom(i)
                        vb = 4 + i % 2
                        sp_ = spb[i % 3]
                        wl_ = wlb[i % 3]
                        par = h % 2
                        Cst = CE.ap[:, par, 0, :]
                        Est = CE.ap[:, par, 1, :]
                        cek = [("CE", par)]
                        nsb = 4 - sb0
                        v_r = VC.ap[:, kb, h * 64:(h + 1) * 64]

                        def mm3(e):
                            last = None
                            for ii in range(nsb):
                                sbq = sb0 + ii
                                cs = slice(ii * 128, (ii + 1) * 128)
                                e.matmul(out=PB[vb][:, sbq * 80:sbq * 80 + 64], lhsT=wl_.ap[:, cs], rhs=v_r, start=True, stop=True)
                                last = e.matmul(out=PB[vb][:, sbq * 80 + 64:sbq * 80 + 65], lhsT=sp_.ap[:, cs], rhs=ones_b[:, 0:1], start=True, stop=True)
                            return last

                        P.op("pe", mm3, reads=wl_.keys() + sp_.keys() + [("VCc", kb // 4)] + CK, writes=PK[vb])
                        cont0 = sb0 + 1 if diag else 0
                        if cont0 < 4:
                            if E_ON_POOL:
                                P.op("pool", lambda e: e.tensor_tensor(out=Est[:, cont0:4], in0=EINV.ap[:, cont0:4], in1=Cst[:, cont0:4], op=ALU.pow),
                                     reads=cek + EINV.keys(), writes=cek)
                            else:
                                act(Est[:, cont0:4], Cst[:, cont0:4], AF.Exp, cek, cek, scale=-1.0)
                        for sbq in range(sb0, 4):
                            pv = PB[vb][:, sbq * 80:sbq * 80 + 64]
                            dst = acc.ap[:, sbq, h * 64:(h + 1) * 64]
                            if diag and sbq == sb0:
                                cp(dst, pv, PK[vb], acc.keys(sbq))
                            else:
                                dve_stt(dst, pv, Est[:, sbq:sbq + 1], dst, ALU.mult, ALU.add, PK[vb] + cek + acc.keys(sbq), acc.keys(sbq))
                        pcs = PB[vb][:, 0:320].rearrange("p (s c) -> p s c", c=80)[:, :, 64]
                        if diag:
                            cp(Cst[:, sb0:sb0 + 1], pcs[:, sb0:sb0 + 1], PK[vb], cek)
                            if sb0 + 1 < 4:
                                dve_tt(Cst[:, sb0 + 1:4], Cst[:, sb0 + 1:4], pcs[:, sb0 + 1:4], ALU.add, PK[vb] + cek, cek)
                        else:
                            dve_tt(Cst, Cst, pcs, ALU.add, PK[vb] + cek, cek)
                        for _ in range(rec_per):
                            if rec_done[0] < 64:
                                rec_step(rec_done[0])
                                rec_done[0] += 1

                    for i in range(nst + 2):
                        if i < nst:
                            stageA(i)
                        if 0 <= i - 1 < nst:
                            stageB(i - 1)
                        if 0 <= i - 2 < nst:
                            stageC(i - 2)
                    while rec_done[0] < 64:
                        rec_step(rec_done[0])
                        rec_done[0] += 1
                    cp(Xp.ap[:, :, :, 1:64], SX.ap[:, :, :, 0:63], sxk, Xp.keys())
                    cp(Xc.ap, SX.ap[:, :, :, 63], sxk, Xc.keys())
                    dbg_dump(acc.ap, acc.keys(), 0, ti) if dbg == "att" else None
                    for ft in range(4):
                        b = banks(1)[0]

                        def tr(e, ft=ft, b=b):
                            last = None
                            for sbq in range(4):
                                last = e.transpose(out=PB[b][:, sbq * 128:(sbq + 1) * 128], in_=acc.ap[:, sbq, ft * 128:(ft + 1) * 128], identity=ident.ap)
                            return last

                        P.op("pe", tr, reads=acc.keys() + CK, writes=PK[b])
                        act(mixT.ap[:, 4 + ft, :], PB[b][:, :], AF.Copy, PK[b], mixT.keys(4 + ft))

                    P.stage = "t%d_s5out" % ti
                    ybanks = banks(4)
                    for ct in range(4):
                        bnk = PB[ybanks[ct]]

                        def mmY(e, ct=ct, bnk=bnk):
                            last = None
                            for ip in range(8):
                                for j in range(ip + 1):
                                    e.matmul(out=bnk[:, ip:512:8], lhsT=KD.ap[:, ct, ip - j, :], rhs=uTb.ap[:, ct, j:512:8],
                                             start=(ip == 0 and j == 0), stop=False, skip_group_check=True)
                            for ip in range(8):
                                for pp in range(4):
                                    q = 4 * ct + pp
                                    for ri in range(2):
                                        last = e.matmul(out=bnk[pp * 32:(pp + 1) * 32, ip:512:8], lhsT=YW.ap[:, q, ip, ri, :], rhs=Xp.ap[:, ri, q, :],
                                                        start=False, stop=(ri == 1 and ip == 7 and pp == 3), tile_position=(0, pp * 32), skip_group_check=True)
                            return last

                        P.op("pe", mmY, reads=uTb.keys(ct) + KD.keys() + YW.keys() + Xp.keys(), writes=PK[ybanks[ct]])
                        dve_stt(y32.ap[:, ct, :], uTb.ap[:, ct, :], vcol("s5d", ct), bnk[:, :], ALU.mult, ALU.add,
                                PK[ybanks[ct]] + uTb.keys(ct) + CK, y32.keys(ct))
                        act(y32.ap[:, ct, :], y32.ap[:, ct, :], AF.Gelu, y32.keys(ct), y32.keys(ct))
                        cp(gb.ap[:, ct, :], y32.ap[:, ct, :], y32.keys(ct), gb.keys(ct))
                    dbg_dump(y32.ap, y32.keys(), 0, ti) if dbg == "s5" else None

                    def ev_glu(m, b):
                        t_ = tmpf[m % 2]
                        act(t_.ap, PB[b][:, :], AF.Sigmoid, PK[b], t_.keys())
                        dve_tt(mixT.ap[:, m, :], y32.ap[:, m, :], t_.ap, ALU.mult, y32.keys(m) + t_.keys(), mixT.keys(m))

                    gemm_fm("w_glu", 0, 2, 0, lambda kt: gb.ap[:, kt, :], lambda kt: gb.keys(kt), ev_glu)

                    def ev_res(m, b):
                        dve_tt(HT.ap[:, m, :], HT.ap[:, m, :], PB[b][:, :], ALU.add, PK[b] + HT.keys(m), HT.keys(m))

                    gemm_fm("w_out", 0, 4, 0, lambda kt: mixT.ap[:, kt, :], lambda kt: mixT.keys(kt), ev_res)
                    if dbg == "mix0":
                        dbg_dump(HT.ap, HT.keys(), NT - 1, ti)
                else:
                    P.stage = "t%d_pool" % ti
                    prefetch([("pool", g_, 0) for g_ in range(3)])
                    P.dma("pool", "bands", lambda e: e.dma_start(out=bandsb.ap.rearrange("p a b -> p (a b)"), in_=bands_d[:, :]), writes=bandsb.keys())
                    if ti > 0:
                        P.dma("sp", "carry", lambda e: e.dma_start(out=carryb.ap, in_=carry_d[:, :]), reads=[("carryd", 0)], writes=carryb.keys())
                    bss = banks(1)[0]
                    for dt in range(8):
                        act(sq.ap[:, dt, :], HT.ap[:, dt, :], AF.Square, HT.keys(dt), sq.keys(dt))

                    def mmss(e, bss=bss):
                        last = None
                        for blk in range(4):
                            for dt in range(8):
                                last = e.matmul(out=PB[bss][:, blk:blk + 1], lhsT=sq.ap[:, dt, blk * 128:(blk + 1) * 128], rhs=ones_b[:, 0:1],
                                                start=(dt == 0), stop=(dt == 7))
                        return last

                    P.op("pe", mmss, reads=sq.keys() + CK, writes=PK[bss])
                    act(lnc.ap, PB[bss][:, 0:4], AF.Ln, PK[bss], lnc.keys(), scale=1.0 / 1024.0, bias=EPS)
                    act(rstdc.ap, lnc.ap, AF.Exp, lnc.keys(), rstdc.keys(), scale=-0.5)
                    for blk in range(4):
                        bs = banks(2)
                        for half in range(2):
                            b = bs[half]

                            def trh(e, half=half, b=b, blk=blk):
                                last = None
                                for j in range(4):
                                    dt = half * 4 + j
                                    last = e.transpose(out=PB[b][:, j * 128:(j + 1) * 128], in_=HT.ap[:, dt, blk * 128:(blk + 1) * 128], identity=ident.ap)
                                return last

                            P.op("pe", trh, reads=HT.keys() + CK, writes=PK[b])
                            dst = hTok.ap[:, blk, half * 512:(half + 1) * 512]
                            if half == 0:
                                act(dst, PB[b][:, :], AF.Copy, PK[b] + rstdc.keys(), hTok.keys(blk), scale=rstdc.ap[:, blk:blk + 1])
                            else:
                                dve_ts(dst, PB[b][:, :], rstdc.ap[:, blk:blk + 1], ALU.mult, PK[b] + rstdc.keys(), hTok.keys(blk))
                    for dt in range(8):
                        g = dt // 2
                        b = banks(1)[0]

                        def mmb(e, dt=dt, g=g, b=b, ti=ti):
                            last = None
                            cs = slice(dt * 128, (dt + 1) * 128)
                            for blk in range(4):
                                out = PB[b][:, blk * 128:(blk + 1) * 128]
                                first_seq = (ti == 0 and blk == 0)
                                has_spill = not first_seq
                                if first_seq:
                                    e.matmul(out=out, lhsT=hTok.ap[:, 0, cs], rhs=bandsb.ap[:, 4 * g + 2, :], start=True, stop=False)
                                    last = e.matmul(out=out, lhsT=hTok.ap[:, 0, cs], rhs=bandsb.ap[:, 4 * g + 3, :], start=False, stop=True)
                                else:
                                    e.matmul(out=out, lhsT=hTok.ap[:, blk, cs], rhs=bandsb.ap[:, 4 * g + 0, :], start=True, stop=False)
                                    prev = carryb.ap[:, cs] if blk == 0 else hTok.ap[:, blk - 1, cs]
                                    last = e.matmul(out=out, lhsT=prev, rhs=bandsb.ap[:, 4 * g + 1, :], start=False, stop=True)
                            return last

                        P.op("pe", mmb, reads=hTok.keys() + bandsb.keys() + carryb.keys(), writes=PK[b])
                        if dt % 2 == 0:
                            act(ypool8.ap[:, dt, :], PB[b][:, :], AF.Copy, PK[b] + CK, ypool8.keys(dt), scale=vcol("ln_mix1", dt))
                        else:
                            dve_ts(ypool8.ap[:, dt, :], PB[b][:, :], vcol("ln_mix1", dt), ALU.mult, PK[b] + CK, ypool8.keys(dt))
                    P.dma("sp", "carryw", lambda e: e.dma_start(out=carry_d[:, :], in_=hTok.ap[:, 3, :]), reads=hTok.keys(3), writes=[("carryd", 0)])
                    for g in range(4):
                        slot = PRE.pop(("pool", g, 0), None) or load_w("pool", g, 0)
                        bs = banks(2)
                        for m in range(2):
                            b = bs[m]

                            def mm(e, m=m, b=b, slot=slot, g=g):
                                e.matmul(out=PB[b][:, :], lhsT=slot.ap[:, 0, m * 128:(m + 1) * 128], rhs=ypool8.ap[:, 2 * g, :], start=True, stop=False)
                                return e.matmul(out=PB[b][:, :], lhsT=slot.ap[:, 1, m * 128:(m + 1) * 128], rhs=ypool8.ap[:, 2 * g + 1, :], start=False, stop=True)

                            P.op("pe", mm, reads=slot.keys() + ypool8.keys(2 * g) + ypool8.keys(2 * g + 1), writes=PK[b])
                            dt = 2 * g + m
                            dve_stt(HT.ap[:, dt, :], PB[b][:, :], vcol("pscale", dt), HT.ap[:, dt, :], ALU.mult, ALU.add,
                                    PK[b] + HT.keys(dt) + CK, HT.keys(dt))

                if dbg == "pool" and layer == 1:
                    dbg_dump(HT.ap, HT.keys(), 0, ti)
                P.stage = "t%d_mlp%d" % (ti, layer)
                prefetch([("w_up%d" % layer, c_, 0) for c_ in range(3)])
                rmsnorm("ln_mlp%d" % layer, lambda dt: hn.ap[:, dt, :], lambda dt: hn.keys(dt))
                def up_gen(qq):
                    aT = aTs[qq % 2]

                    def ev_up(m, b, aT=aT):
                        r_ = r32[m % 4]
                        act(r_.ap, PB[b][:, :], AF.Relu, PK[b], r_.keys())
                        dve_tt(aT.ap[:, m, :], r_.ap, r_.ap, ALU.mult, r_.keys(), aT.keys(m))

                    return gemm_gen("w_up%d" % layer, qq * 4, 4, 0, lambda kt: hn.ap[:, kt, :], lambda kt: hn.keys(kt), ev_up)

                def dn_gen(qq):
                    aT = aTs[qq % 2]

                    def ev_res(m, b):
                        dve_tt(HT.ap[:, m, :], HT.ap[:, m, :], PB[b][:, :], ALU.add, PK[b] + HT.keys(m), HT.keys(m))

                    return gemm_gen("w_dn%d" % layer, 0, 4, qq, lambda kt, aT=aT: aT.ap[:, kt, :], lambda kt, aT=aT: aT.keys(kt), ev_res)

                if MLP_PIPE:
                    for _ in up_gen(0):
                        pass
                    for qq in range(4):
                        if qq < 3:
                            interleave(up_gen(qq + 1), dn_gen(qq), lead=1)
                        else:
                            for _ in dn_gen(qq):
                                pass
                else:
                    for qq in range(4):
                        for _ in up_gen(qq):
                            pass
                        for _ in dn_gen(qq):
                            pass

                if dbg == "mlp0" and layer == 0:
                    dbg_dump(HT.ap, HT.keys(), NT - 1, ti)
                P.stage = "t%d_ple%d" % (ti, layer)
                prefetch([("w_gt%d" % layer, c_, 0) for c_ in range(3)])
                rmsnorm("ln_ple%d" % layer, lambda dt: hn.ap[:, dt, :], lambda dt: hn.keys(dt))
                for blk in range(4):
                    sb_ = pstg[blk % 2]
                    r0 = t0 + blk * 128
                    P.dma("sp", "stg%d" % (blk % 2), lambda e, sb_=sb_, r0=r0, layer=layer: e.dma_start(out=sb_.ap, in_=p_d[layer, r0:r0 + 128, :]), writes=sb_.keys())
                    b = banks(1)[0]

                    def trp(e, sb_=sb_, b=b):
                        e.transpose(out=PB[b][:, 0:128], in_=sb_.ap[:, 0:128], identity=ident.ap)
                        return e.transpose(out=PB[b][:, 128:256], in_=sb_.ap[:, 128:256], identity=ident.ap)

                    P.op("pe", trp, reads=sb_.keys() + CK, writes=PK[b])
                    act(pT.ap[:, :, blk * 128:(blk + 1) * 128], PB[b][:, 0:256].rearrange("p (k t) -> p k t", k=2), AF.Copy, PK[b], pT.keys())
                sigs = [R["AT"].buf(0, [4, 512], F32), R["UB"].buf(0, [4, 512], F32)]

                def gate_gen(half):
                    sg = sigs[half]

                    def ev_gate(m, b, sg=sg):
                        act(sg.ap[:, m, :], PB[b][:, :], AF.Sigmoid, PK[b], sg.keys(m))

                    return gemm_gen("w_gt%d" % layer, half * 2, 2, 0, lambda kt: hn.ap[:, kt, :], lambda kt: hn.keys(kt), ev_gate)

                def pu_gen(half):
                    sg = sigs[half]

                    def ev_pu(m, b, half=half, sg=sg):
                        t_ = tmpf[2 + m % 2]
                        dve_tt(t_.ap, PB[b][:, :], sg.ap[:, m, :], ALU.mult, PK[b] + sg.keys(m), t_.keys())
                        dt = half * 4 + m
                        dve_tt(HT.ap[:, dt, :], HT.ap[:, dt, :], t_.ap, ALU.add, t_.keys() + HT.keys(dt), HT.keys(dt))

                    return gemm_gen("w_pu%d" % layer, half * 2, 2, 0, lambda kt: pT.ap[:, kt, :], lambda kt: pT.keys(), ev_pu)

                for _ in gate_gen(0):
                    pass
                interleave(gate_gen(1), pu_gen(0), lead=1)
                for _ in pu_gen(1):
                    pass
                if dbg == "l0" and layer == 0:
                    dbg_dump(HT.ap, HT.keys(), NT - 1, ti)

            P.stage = "t%d_out" % ti
            for blk in range(4):
                sb_ = STGb[blk % 2]
                r0 = t0 + blk * 128
                for half in range(2):
                    b = banks(1)[0]

                    def tro(e, half=half, b=b, blk=blk):
                        last = None
                        for j in range(4):
                            dt = half * 4 + j
                            last = e.transpose(out=PB[b][:, j * 128:(j + 1) * 128], in_=HT.ap[:, dt, blk * 128:(blk + 1) * 128], identity=ident.ap)
                        return last

                    P.op("pe", tro, reads=HT.keys() + CK, writes=PK[b])
                    if half == 0:
                        act(sb_.ap[:, 0:512], PB[b][:, :], AF.Copy, PK[b], sb_.keys())
                    else:
                        cp(sb_.ap[:, 512:1024], PB[b][:, :], PK[b], sb_.keys())
                sg = P.dma("sp", "stg%d" % (blk % 2), lambda e, sb_=sb_, r0=r0: e.dma_start(out=y_d[r0:r0 + 128, :], in_=sb_.ap), reads=sb_.keys())
                final_sigs.append(sg)

        fs = {}
        for s, v in final_sigs:
            fs[s] = max(fs.get(s, 0), v)
        if dbg_d is not None and "d_dbg" in P.cnt:
            fs["d_dbg"] = P.cnt["d_dbg"]
        P.finish("sp", list(fs.items()))
        P.emit()
    return nc


def _prep_shared(inp):
    f = lambda a: np.ascontiguousarray(np.asarray(a, dtype=np.float32))
    sh = {}
    sh["w_in"] = f(inp["w_in_even"][0])
    sh["w_out"] = f(inp["w_out_even"][0])
    sh["w_glu"] = f(inp["s5_w_glu"][0])
    sh["w_up"] = f(inp["w_mlp_up"])
    sh["w_dn"] = f(inp["w_mlp_down"])
    sh["w_gt"] = f(inp["w_ple_gate"])
    sh["w_pu"] = f(inp["w_ple_up"])
    sh["w_pool"] = f(inp["pool_w"][0])
    col8 = lambda v: np.asarray(v, np.float32).reshape(-1, 128).T
    qg = np.asarray(inp["sb_q_gain"][0], np.float32)
    kg = np.asarray(inp["sb_k_gain"][0], np.float32)
    qg128 = np.concatenate([qg, qg])
    kg128 = np.concatenate([kg, kg])
    half = (np.arange(128) < 64)
    scale = np.float32(64 ** -0.5)
    vec = np.concatenate([
        col8(inp["ln_mix_even"][0]), col8(inp["ln_mlp"][0]), col8(inp["ln_ple"][0]),
        col8(inp["ln_mix_odd"][0]), col8(inp["ln_mlp"][1]), col8(inp["ln_ple"][1]),
        col8(inp["pool_scale"][0]), col8(inp["s5_d"][0]),
        np.where(half, qg128, 0)[:, None], np.where(~half, qg128, 0)[:, None], kg128[:, None]], axis=1)
    sh["vecs"] = f(vec)
    lamr = np.asarray(inp["s5_lambda_re"][0], np.float32)
    lami = np.asarray(inp["s5_lambda_im"][0], np.float32)
    logdt = np.asarray(inp["s5_log_dt"][0], np.float32)
    toB = lambda a: a.reshape(16, 2, 64).transpose(1, 2, 0).reshape(128, 16)
    logdtB = toB(np.tile(logdt[:, None], (1, 64)))
    toB3 = lambda a: a.reshape(16, 2, 64, 16).transpose(1, 2, 0, 3).reshape(128, 256)
    bre = np.asarray(inp["s5_b_re"][0], np.float32)
    bim = np.asarray(inp["s5_b_im"][0], np.float32)
    cre = np.asarray(inp["s5_c_re"][0], np.float32).transpose(0, 2, 1)
    cim = np.asarray(inp["s5_c_im"][0], np.float32).transpose(0, 2, 1)
    sh["s5B"] = f(np.concatenate([toB(lamr), toB(lami), logdtB, toB3(bre), toB3(bim), toB3(cre), toB3(cim)], axis=1))
    toA = lambda a: np.tile(a.reshape(4, 8, 1, 64), (1, 1, 16, 1)).transpose(1, 2, 0, 3).reshape(128, 256)
    toA3 = lambda a: a.reshape(4, 8, 64, 16).transpose(1, 3, 0, 2).reshape(128, 256)
    sh["s5A"] = f(np.concatenate([toA(lamr), toA(lami), toA(np.tile(logdt[:, None], (1, 64))), toA3(bre), toA3(bim)], axis=1))
    sh.update(_consts())
    sh["_scale"] = scale
    return sh


_NC_CACHE = {}


def kernel(**inputs):
    x = np.asarray(inputs["x"], np.float32)
    p = np.asarray(inputs["p"], np.float32)
    B, L, Dm = x.shape
    NT = L // TT
    sh = _prep_shared(inputs)
    sh.pop("_scale")
    if NT not in _NC_CACHE:
        _NC_CACHE[NT] = build(NT)
    nc = _NC_CACHE[NT]
    in_maps = []
    for b in range(B):
        m = dict(sh)
        m["x"] = np.ascontiguousarray(x[b])
        m["p"] = np.ascontiguousarray(p[:, b])
        in_maps.append(m)
    res = run_bass_kernel_spmd(nc, in_maps, core_ids=list(range(B)))
    out = np.stack([res.results[b]["y"] for b in range(B)], axis=0)
    return out.astype(np.float32)
```

```python
import math
from contextlib import ExitStack

import numpy as np
import concourse.bass as bass
import concourse.mybir as mybir
from concourse.bass_utils import run_bass_kernel_spmd

F32 = mybir.dt.float32
BF16 = mybir.dt.bfloat16
I32 = mybir.dt.int32
AF = mybir.ActivationFunctionType
ALU = mybir.AluOpType

TT = 512
EPS = 1e-6
KLIST = list(range(-7, 9))
TWO_PI = 2.0 * math.pi
import os
MLP_PIPE = bool(int(os.environ.get("K_MLPPIPE", "1")))
REC_ENG = os.environ.get("K_REC", "dve")
E_ON_POOL = bool(int(os.environ.get("K_EPOOL", "0")))
SAME_ENGINE_SYNC = set(os.environ.get("K_SES", "").split(","))


class Prog:
    def __init__(self, nc, stack):
        self.nc = nc
        self.stack = stack
        self.ops = {e: [] for e in ("pe", "act", "dve", "pool", "sp")}
        self.sems = {}
        self.cnt = {}
        self.keys = {}
        self.waited = {e: {} for e in self.ops}
        self.final = []
        self.stage = "setup"
        self.scopes = bool(int(os.environ.get("K_SCOPES", "0")))
        self.ses = set(SAME_ENGINE_SYNC)
        self.near = {"dve": int(os.environ.get("K_NEAR_DVE", "2")), "act": int(os.environ.get("K_NEAR_ACT", "1")), "pool": 2}

    def sem(self, name):
        if name not in self.sems:
            self.sems[name] = self.stack.enter_context(self.nc.semaphore(name))
            self.cnt[name] = 0
        return self.sems[name]

    def _resolve(self, eng, reads, writes, mysig):
        waits = {}

        def add(sig):
            if sig is None:
                return
            s, v = sig
            if s == eng:
                if eng == "pe":
                    return
                if eng not in self.ses and (self.cnt[eng] - v) > self.near.get(eng, 0):
                    return
            if waits.get(s, 0) < v:
                waits[s] = v

        for k in reads:
            st = self.keys.setdefault(k, [None, []])
            add(st[0])
        for k in writes:
            st = self.keys.setdefault(k, [None, []])
            add(st[0])
            for r in st[1]:
                add(r)
        out = []
        for s, v in waits.items():
            if self.waited[eng].get(s, 0) >= v:
                continue
            self.waited[eng][s] = v
            out.append((s, v))
        for k in reads:
            self.keys[k][1].append(mysig)
        for k in writes:
            self.keys[k][0] = mysig
            self.keys[k][1] = []
        return out

    def op(self, eng, fn, reads=(), writes=()):
        self.sem(eng)
        self.cnt[eng] += 1
        mysig = (eng, self.cnt[eng])
        waits = self._resolve(eng, reads, writes, mysig)
        self.ops[eng].append((waits, fn, (eng, 1), self.stage))

    def dma(self, q, semkey, fn, reads=(), writes=()):
        name = "d_" + semkey
        self.sem(name)
        self.cnt[name] += 16
        mysig = (name, self.cnt[name])
        waits = self._resolve(q, reads, writes, mysig)
        self.ops[q].append((waits, fn, (name, 16), "dma"))
        return mysig

    def finish(self, eng, sigs):
        self.final.append((eng, sigs))

    def emit(self):
        nc = self.nc
        engmap = {"pe": "tensor", "act": "scalar", "dve": "vector", "pool": "gpsimd", "sp": "sync"}
        block = self.stack.enter_context(nc.Block())
        for e, attr in engmap.items():
            ops = self.ops[e]
            finals = [s for (fe, s) in self.final if fe == e]
            if not ops and not finals:
                continue

            def body(engine, ops=ops, finals=finals):
                for waits, fn, (sname, inc), stage in ops:
                    if self.scopes:
                        with nc.named_scope(stage):
                            for s, v in waits:
                                engine.wait_ge(self.sems[s], v)
                            inst = fn(engine)
                            inst.then_inc(self.sems[sname], inc)
                    else:
                        for s, v in waits:
                            engine.wait_ge(self.sems[s], v)
                        inst = fn(engine)
                        inst.then_inc(self.sems[sname], inc)
                for sigs in finals:
                    for s, v in sigs:
                        engine.wait_ge(self.sems[s], v)

            getattr(block, attr)(body)


KG = 512


class Region:
    def __init__(self, arena, name, woff, nbytes):
        self.arena, self.name, self.woff, self.nbytes = arena, name, woff, nbytes

    def buf(self, boff, shape, dt):
        return Buf(self, boff, shape, dt)


class Buf:
    def __init__(self, region, boff, shape, dt):
        esz = 4 if dt in (F32, I32) else 2
        n = int(np.prod(shape))
        assert boff % 4 == 0 and boff + n * esz <= region.nbytes, (region.name, boff, shape, region.nbytes)
        w0 = region.woff + boff // 4
        nw = (n * esz + 3) // 4
        ap = region.arena[:, w0:w0 + nw]
        if dt != F32:
            ap = ap.bitcast(dt)
        if len(shape) > 1:
            names = " ".join("a%d" % i for i in range(len(shape)))
            kw = {"a%d" % i: shape[i] for i in range(1, len(shape))}
            ap = ap.rearrange("p (%s) -> p %s" % (names, names), **kw)
        self.ap, self.region, self.boff, self.shape, self.esz = ap, region, boff, tuple(shape), esz
        self.nbytes = n * esz
        self.blk = (n // shape[0]) * esz

    def keys(self, lo=None, hi=None):
        if lo is None:
            b0, b1 = self.boff, self.boff + self.nbytes
        else:
            hi = lo + 1 if hi is None else hi
            b0, b1 = self.boff + lo * self.blk, self.boff + hi * self.blk
        return [(self.region.name, k) for k in range(b0 // KG, (b1 + KG - 1) // KG)]


def _consts():
    c = {}
    c["ident"] = np.eye(128, dtype=np.float32)
    jj = np.arange(128)[:, None]
    tt = np.arange(128)[None, :]
    ones = np.ones((128, 128), np.float32)
    blockones = ((jj // 64) == (tt // 64)).astype(np.float32)
    negtri = -(jj >= tt).astype(np.float32)
    masklt = (jj < tt).astype(np.float32)
    c["cstb"] = np.concatenate([ones, blockones, negtri, masklt], axis=1)
    part = np.arange(128)
    g8 = part // 16
    bd = (g8[:, None, None] == np.arange(8)[None, :, None]) * np.ones((1, 1, 16))
    g2 = (part // 16) % 2
    g2col = (g2[:, None] == np.arange(2)[None, :]).astype(np.float32)
    pp3 = (part >= 96).astype(np.float32)[:, None]
    halfA = (part < 64).astype(np.float32)[:, None]
    halfB = (part >= 64).astype(np.float32)[:, None]
    kv = np.tile(np.array(KLIST, np.float32)[None, :], (128, 1))
    kf = kv / TWO_PI
    kvA = np.tile(np.arange(8, dtype=np.float32)[None, :], (128, 1))
    kfA = kvA / TWO_PI
    cnt = np.zeros((128, 4, 16), np.float32)
    for g, w in enumerate((2, 4, 8, 16)):
        cnt[:, g, :] = 1.0 / np.minimum(np.arange(16) + 1, w)
    bands = np.zeros((128, 16, 128), np.float64)
    tq = np.arange(128)[:, None]
    tt_ = np.arange(128)[None, :]
    for g, w in enumerate((2, 4, 8, 16)):
        main = ((tq <= tt_) & (tq > tt_ - w)) / float(w) - (tq == tt_)
        spill = ((tq - 128) > (tt_ - w)) / float(w)
        cntv = np.minimum(tt_ + 1, w)
        b0 = ((tq <= tt_) & (tq > tt_ - w)) / cntv - (tq == tt_)
        import ml_dtypes
        hi = b0.astype(np.float32).astype(ml_dtypes.bfloat16).astype(np.float64)
        lo = b0 - hi
        bands[:, 4 * g + 0] = main
        bands[:, 4 * g + 1] = spill
        bands[:, 4 * g + 2] = hi
        bands[:, 4 * g + 3] = lo
    c["bands"] = bands.reshape(128, 2048).astype(np.float32)
    c["cstf"] = np.concatenate([bd.reshape(128, 128).astype(np.float32), g2col, pp3, halfA, halfB,
                                kv, kf, kvA, kfA, cnt.reshape(128, 64)], axis=1).astype(np.float32)
    return c


CF = {}
_o = 0
for _n, _w in (("bd", 128), ("g2col", 2), ("pp3", 1), ("halfA", 1), ("halfB", 1), ("kv", 16), ("kf", 16),
               ("kvA", 8), ("kfA", 8), ("cnt", 64)):
    CF[_n] = (_o, _w)
    _o += _w
NCF = _o

VEC = {}
_o = 0
for _n, _w in (("ln_mix0", 8), ("ln_mlp0", 8), ("ln_ple0", 8), ("ln_mix1", 8), ("ln_mlp1", 8), ("ln_ple1", 8),
               ("pscale", 8), ("s5d", 4), ("qgA", 1), ("qgB", 1), ("kgA", 1)):
    VEC[_n] = (_o, _w)
    _o += _w
NVEC = _o


def build(NT, dbg=None):
    L = NT * TT
    nc = bass.Bass("TRN2", target_bir_lowering=False)
    D = lambda n, s: nc.dram_tensor(n, s, F32, kind="ExternalInput").ap()
    x_d = D("x", [L, 1024])
    p_d = D("p", [2, L, 256])
    w_in = D("w_in", [1024, 2048])
    w_out = D("w_out", [1024, 1024])
    w_glu = D("w_glu", [512, 512])
    w_up = D("w_up", [2, 1024, 4096])
    w_dn = D("w_dn", [2, 4096, 1024])
    w_gt = D("w_gt", [2, 1024, 1024])
    w_pu = D("w_pu", [2, 256, 1024])
    w_pool = D("w_pool", [4, 256, 256])
    vecs_d = D("vecs", [128, NVEC])
    s5b_d = D("s5B", [128, 48 + 4 * 256])
    s5a_d = D("s5A", [128, 5 * 256])
    ident_d = D("ident", [128, 128])
    cstb_d = D("cstb", [128, 512])
    cstf_d = D("cstf", [128, NCF])
    bands_d = D("bands", [128, 2048])
    carry_d = nc.dram_tensor("poolcarry", [128, 1024], BF16, kind="Internal").ap()
    y_d = nc.dram_tensor("y", [L, 1024], F32, kind="ExternalOutput").ap()
    dbg_d = None
    if dbg:
        dbg_d = nc.dram_tensor("dbg", [128, 8, 512], F32, kind="ExternalOutput").ap()

    with ExitStack() as st:
        P = Prog(nc, st)
        sizes = [("KV", 65536), ("VS", 16384), ("YW", 16384), ("KD", 8192), ("WS", 3 * 4096), ("HT", 16384),
                 ("CST", 6144), ("STG", 8192), ("HN", 8192), ("SQ", 8192), ("AT", 8192), ("UB", 8192),
                 ("SCR", 8192), ("S5B", 12800), ("RS", 4096)]
        total_w = sum(s for _, s in sizes) // 4
        arena = st.enter_context(nc.sbuf_tensor("arena", [128, total_w], F32))
        R = {}
        wo = 0
        for n, s in sizes:
            R[n] = Region(arena, n, wo, s)
            wo += s // 4
        PB = [st.enter_context(nc.psum_tensor("pb%d" % i, [128, 512], F32)) for i in range(8)]
        PK = [[("pb%d" % i, 0)] for i in range(8)]

        KT = R["KV"].buf(0, [4, 4096], BF16)
        VC = R["KV"].buf(32768, [32, 512], BF16)
        VS = R["VS"].buf(0, [4, 2, 8, 128], BF16)
        YW = R["YW"].buf(0, [16, 8, 2, 32], BF16)
        KD = R["KD"].buf(0, [4, 8, 128], BF16)
        WS = [R["WS"].buf(i * 4096, [8, 256], BF16) for i in range(3)]
        HT = R["HT"].buf(0, [8, 512], F32)
        ident = R["CST"].buf(0, [128], F32)
        vecs = R["CST"].buf(512, [NVEC], F32)
        cstb = R["CST"].buf(1024, [4, 128], BF16)
        cstf = R["CST"].buf(2048, [NCF], F32)
        assert NCF * 4 <= 1024
        Acf = R["CST"].buf(3072, [2, 16], F32)
        Xc = R["CST"].buf(3584, [2, 16], F32)
        LB = R["CST"].buf(4096, [8, 16], F32)
        CE = R["CST"].buf(4608, [2, 2, 4], F32)
        RT = R["CST"].buf(5120, [4, 16], F32)
        EINV = R["CST"].buf(5632, [4], F32)

        def vcol(name, i=0):
            o, w = VEC[name]
            return vecs.ap[:, o + i:o + i + 1]

        def cf(name):
            o, w = CF[name]
            return cstf.ap[:, o:o + w]

        ones_b = cstb.ap[:, 0, :]
        blockones_b = cstb.ap[:, 1, :]
        negtri_b = cstb.ap[:, 2, :]
        masklt_b = cstb.ap[:, 3, :]
        CK = cstb.keys() + cstf.keys() + vecs.keys() + ident.keys()

        def dve_tt(out, in0, in1, op, r, w, eng="dve"):
            P.op(eng, lambda e: e.tensor_tensor(out=out, in0=in0, in1=in1, op=op), reads=r, writes=w)

        def dve_ts(out, in0, s1, op0, r, w, s2=None, op1=None, eng="dve"):
            if op1 is None:
                P.op(eng, lambda e: e.tensor_scalar(out=out, in0=in0, scalar1=s1, scalar2=None, op0=op0), reads=r, writes=w)
            else:
                P.op(eng, lambda e: e.tensor_scalar(out=out, in0=in0, scalar1=s1, scalar2=s2, op0=op0, op1=op1), reads=r, writes=w)

        def dve_stt(out, in0, scalar, in1, op0, op1, r, w):
            P.op("dve", lambda e: e.scalar_tensor_tensor(out=out, in0=in0, scalar=scalar, in1=in1, op0=op0, op1=op1), reads=r, writes=w)

        def act(out, in_, func, r, w, scale=1.0, bias=0.0):
            if func == AF.Copy:
                P.op("act", lambda e: e.activation(out=out, in_=in_, func=func, scale=scale), reads=r, writes=w)
            else:
                P.op("act", lambda e: e.activation(out=out, in_=in_, func=func, scale=scale, bias=bias), reads=r, writes=w)

        def cp(out, in_, r, w, eng="dve"):
            P.op(eng, lambda e: e.tensor_copy(out=out, in_=in_), reads=r, writes=w)

        P.dma("sp", "c0", lambda e: e.dma_start(out=ident.ap, in_=ident_d[:, :]), writes=ident.keys())
        P.dma("sp", "c1", lambda e: e.dma_start(out=vecs.ap, in_=vecs_d[:, :]), writes=vecs.keys())
        P.dma("sp", "c2", lambda e: e.dma_start(out=cstf.ap, in_=cstf_d[:, :]), writes=cstf.keys())
        P.dma("pool", "c3", lambda e: e.dma_start(out=cstb.ap.rearrange("p a b -> p (a b)"), in_=cstb_d[:, :]), writes=cstb.keys())

        P.op("dve", lambda e: e.memset(EINV.ap, math.exp(-1.0)), writes=EINV.keys())
        o_q = VEC["qgA"][0]
        dve_ts(vecs.ap[:, o_q:o_q + 2], vecs.ap[:, o_q:o_q + 2], 0.125, ALU.mult, vecs.keys(), vecs.keys())

        KVr = R["KV"]

        def s5_setup():
            o = [0]

            def tmp(shape, dt=F32):
                b = KVr.buf(o[0], shape, dt)
                o[0] += ((b.nbytes + 511) // 512) * 512
                return b

            inB = tmp([48 + 1024])
            P.dma("sp", "s5in", lambda e: e.dma_start(out=inB.ap, in_=s5b_d[:, :]), writes=inB.keys())
            lamr = inB.ap[:, 0:16]
            lami = inB.ap[:, 16:32]
            logdt = inB.ap[:, 32:48]
            bre = inB.ap[:, 48:304].rearrange("p (q h) -> p q h", q=16)
            bim = inB.ap[:, 304:560].rearrange("p (q h) -> p q h", q=16)
            cre = inB.ap[:, 560:816].rearrange("p (q h) -> p q h", q=16)
            cim = inB.ap[:, 816:1072].rearrange("p (q h) -> p q h", q=16)
            kin = inB.keys()
            sm = tmp([8, 16])
            smk = sm.keys()
            dt_, lrdt, lidt, den, nr, fr, fi, t0 = [sm.ap[:, i, :] for i in range(8)]
            act(dt_, logdt, AF.Exp, kin, smk)
            dve_tt(lrdt, lamr, dt_, ALU.mult, kin + smk, smk)
            dve_tt(lidt, lami, dt_, ALU.mult, kin + smk, smk)
            NK = len(KLIST)
            big = [tmp([NK, 16]) for _ in range(6)]
            mag, Tt, t1, t2, cosv, sinv = big
            allk = sum([b.keys() for b in big], [])
            kv = cf("kv")
            kf = cf("kf")
            bc_q = lambda a: a.unsqueeze(1).to_broadcast([128, NK, 16])
            bc_k = lambda a: a.unsqueeze(2).to_broadcast([128, NK, 16])
            dve_tt(mag.ap, bc_q(lrdt), bc_k(kv), ALU.mult, smk + CK, allk)
            act(mag.ap, mag.ap, AF.Exp, allk, allk)
            dve_tt(Tt.ap, bc_q(lidt), bc_k(kf), ALU.mult, smk + CK, allk)

            def sincos(dst, shift, Tsrc, a1, a2, keys):
                dve_ts(a1.ap, Tsrc.ap, shift, ALU.add, keys, keys)
                cp(a2.ap.bitcast(I32), a1.ap, keys, keys)
                cp(a2.ap, a2.ap.bitcast(I32), keys, keys)
                dve_tt(a1.ap, a1.ap, a2.ap, ALU.subtract, keys, keys)
                dve_stt(a1.ap, a1.ap, 0.0, a1.ap, ALU.is_lt, ALU.add, keys, keys)
                act(dst.ap, a1.ap, AF.Sin, keys, keys, scale=TWO_PI * (1 - 1e-6), bias=-math.pi * (1 - 1e-6))

            sincos(cosv, 0.75 + 32.0, Tt, t1, t2, allk)
            sincos(sinv, 0.5 + 32.0, Tt, t1, t2, allk)
            dve_tt(cosv.ap, cosv.ap, mag.ap, ALU.mult, allk, allk)
            dve_tt(sinv.ap, sinv.ap, mag.ap, ALU.mult, allk, allk)
            Er = lambda k: cosv.ap[:, k + 7, :]
            Ei = lambda k: sinv.ap[:, k + 7, :]
            cp(Acf.ap[:, 0, :], Er(8), allk, Acf.keys())
            cp(Acf.ap[:, 1, :], Ei(8), allk, Acf.keys())
            dve_tt(den, lamr, lamr, ALU.mult, kin, smk)
            dve_tt(t0, lami, lami, ALU.mult, kin, smk)
            dve_tt(den, den, t0, ALU.add, smk, smk)
            P.op("dve", lambda e: e.reciprocal(out=den, in_=den), reads=smk, writes=smk)
            dve_ts(nr, Er(1), -1.0, ALU.add, allk, smk)
            dve_tt(fr, nr, lamr, ALU.mult, smk + kin, smk)
            dve_tt(t0, Ei(1), lami, ALU.mult, allk + kin, smk)
            dve_tt(fr, fr, t0, ALU.add, smk, smk)
            dve_tt(fr, fr, den, ALU.mult, smk, smk)
            dve_tt(fi, Ei(1), lamr, ALU.mult, allk + kin, smk)
            dve_tt(t0, nr, lami, ALU.mult, smk + kin, smk)
            dve_tt(fi, fi, t0, ALU.subtract, smk, smk)
            dve_tt(fi, fi, den, ALU.mult, smk, smk)
            bb = tmp([4, 16, 16])
            bbk = bb.keys()
            Bbr, Bbi, tA, tB = [bb.ap[:, i] for i in range(4)]
            bch = lambda a: a.unsqueeze(2).to_broadcast([128, 16, 16])
            dve_tt(Bbr, bre, bch(fr), ALU.mult, kin + smk, bbk)
            dve_tt(tA, bim, bch(fi), ALU.mult, kin + smk, bbk)
            dve_tt(Bbr, Bbr, tA, ALU.subtract, bbk, bbk)
            dve_tt(Bbi, bim, bch(fr), ALU.mult, kin + smk, bbk)
            dve_tt(tA, bre, bch(fi), ALU.mult, kin + smk, bbk)
            dve_tt(Bbi, Bbi, tA, ALU.add, bbk, bbk)
            Rr = tmp([9, 16, 16])
            Ri = tmp([9, 16, 16])
            Rt = tmp([9, 16, 16])
            rk = Rr.keys() + Ri.keys() + Rt.keys()
            bcC = lambda a: a.unsqueeze(1).to_broadcast([128, 9, 16, 16])
            bcE = lambda a: a.unsqueeze(3).to_broadcast([128, 9, 16, 16])
            Er9 = cosv.ap[:, 7:16, :]
            Ei9 = sinv.ap[:, 7:16, :]
            dve_tt(Rr.ap, bcC(cre), bcE(Er9), ALU.mult, kin + allk, rk)
            dve_tt(Rt.ap, bcC(cim), bcE(Ei9), ALU.mult, kin + allk, rk)
            dve_tt(Rr.ap, Rr.ap, Rt.ap, ALU.subtract, rk, rk)
            dve_tt(Ri.ap, bcC(cre), bcE(Ei9), ALU.mult, kin + allk, rk)
            dve_tt(Rt.ap, bcC(cim), bcE(Er9), ALU.mult, kin + allk, rk)
            dve_tt(Ri.ap, Ri.ap, Rt.ap, ALU.add, rk, rk)
            Bz = tmp([2, 16, 128])
            bzk = Bz.keys()
            P.op("dve", lambda e: e.memset(Bz.ap, 0.0), writes=bzk)
            for half in range(2):
                ps_ = slice(half * 64, half * 64 + 64)
                for pp in range(4):
                    cs_ = slice(pp * 32 + half * 16, pp * 32 + half * 16 + 16)
                    cp(Bz.ap[ps_, 0, pp:16:4, cs_], Bbr[ps_, pp:16:4, :], bbk, bzk)
                    dve_ts(Bz.ap[ps_, 1, pp:16:4, cs_], Bbi[ps_, pp:16:4, :], -1.0, ALU.mult, bbk, bzk)
            bd = cf("bd").rearrange("p (a b) -> p a b", a=8)
            for ct in range(4):
                bank = PB[ct]

                def mm(e, ct=ct, bank=bank):
                    last = None
                    for pp in range(4):
                        q = 4 * ct + pp
                        e.matmul(out=bank[:, 0:128], lhsT=Bz.ap[:, 0, q, :], rhs=Rr.ap[:, 0:8, q, :],
                                 start=(pp == 0), stop=False)
                        last = e.matmul(out=bank[:, 0:128], lhsT=Bz.ap[:, 1, q, :], rhs=Ri.ap[:, 0:8, q, :],
                                        start=False, stop=(pp == 3))
                    return last

                P.op("pe", mm, reads=bzk + rk, writes=PK[ct])
                src = bank[:, 0:128].rearrange("p (t h) -> p t h", t=8).unsqueeze(2).to_broadcast([128, 8, 8, 16])
                msk = bd.unsqueeze(1).to_broadcast([128, 8, 8, 16])
                dst = KD.ap[:, ct].rearrange("p t (g h) -> p t g h", g=8)
                dve_tt(dst, src, msk, ALU.mult, PK[ct] + CK, KD.keys())
            P.op("dve", lambda e: e.memset(YW.ap, 0.0), writes=YW.keys())
            for half in range(2):
                ps_ = slice(half * 64, half * 64 + 64)
                cs_ = slice(half * 16, half * 16 + 16)
                srcr = Rr.ap[ps_, 1:9].rearrange("p k q h -> p q k h")
                srci = Ri.ap[ps_, 1:9].rearrange("p k q h -> p q k h")
                cp(YW.ap[ps_, :, :, 0, cs_], srcr, rk, YW.keys())
                dve_ts(YW.ap[ps_, :, :, 1, cs_], srci, -1.0, ALU.mult, rk, YW.keys())
            o[0] = 0
            inA = tmp([5, 256])
            P.dma("sp", "s5in", lambda e: e.dma_start(out=inA.ap.rearrange("p a b -> p (a b)"), in_=s5a_d[:, :]), writes=inA.keys())
            kia = inA.keys()
            lamrA, lamiA, logdtA, breA, bimA = [inA.ap[:, i, :] for i in range(5)]
            smA = tmp([8, 256])
            sak = smA.keys()
            dtA, lrdtA, lidtA, denA, nrA, frA, fiA, t0A = [smA.ap[:, i, :] for i in range(8)]
            act(dtA, logdtA, AF.Exp, kia, sak)
            dve_tt(lrdtA, lamrA, dtA, ALU.mult, kia + sak, sak)
            dve_tt(lidtA, lamiA, dtA, ALU.mult, kia + sak, sak)
            class _V:
                def __init__(self, ap):
                    self.ap = ap

            smB = tmp([6, 256])
            sbk = smB.keys()
            mag1, T1, u1, u2, c1, s1 = [_V(smB.ap[:, i, :]) for i in range(6)]
            act(mag1.ap, lrdtA, AF.Exp, sak, sbk)
            dve_ts(T1.ap, lidtA, 1.0 / TWO_PI, ALU.mult, sak, sbk)
            sincos(c1, 0.75 + 32.0, T1, u1, u2, sbk)
            sincos(s1, 0.5 + 32.0, T1, u1, u2, sbk)
            a1r, a1i = c1.ap, s1.ap
            dve_tt(a1r, a1r, mag1.ap, ALU.mult, sbk, sbk)
            dve_tt(a1i, a1i, mag1.ap, ALU.mult, sbk, sbk)
            dve_tt(denA, lamrA, lamrA, ALU.mult, kia, sak)
            dve_tt(t0A, lamiA, lamiA, ALU.mult, kia, sak)
            dve_tt(denA, denA, t0A, ALU.add, sak, sak)
            P.op("dve", lambda e: e.reciprocal(out=denA, in_=denA), reads=sak, writes=sak)
            dve_ts(nrA, a1r, -1.0, ALU.add, sbk, sak)
            dve_tt(frA, nrA, lamrA, ALU.mult, sak + kia, sak)
            dve_tt(t0A, a1i, lamiA, ALU.mult, sbk + kia, sak)
            dve_tt(frA, frA, t0A, ALU.add, sak, sak)
            dve_tt(frA, frA, denA, ALU.mult, sak, sak)
            dve_tt(fiA, a1i, lamrA, ALU.mult, sbk + kia, sak)
            dve_tt(t0A, nrA, lamiA, ALU.mult, sak + kia, sak)
            dve_tt(fiA, fiA, t0A, ALU.subtract, sak, sak)
            dve_tt(fiA, fiA, denA, ALU.mult, sak, sak)
            Wr = tmp([8, 256])
            Wi = tmp([8, 256])
            ak = Wr.keys() + Wi.keys()
            dve_tt(t0A, bimA, fiA, ALU.mult, kia + sak, sak)
            dve_tt(Wr.ap[:, 0, :], breA, frA, ALU.mult, kia + sak, ak)
            dve_tt(Wr.ap[:, 0, :], Wr.ap[:, 0, :], t0A, ALU.subtract, ak + sak, ak)
            dve_tt(t0A, breA, fiA, ALU.mult, kia + sak, sak)
            dve_tt(Wi.ap[:, 0, :], bimA, frA, ALU.mult, kia + sak, ak)
            dve_tt(Wi.ap[:, 0, :], Wi.ap[:, 0, :], t0A, ALU.add, ak + sak, ak)
            for k_ in range(7):
                dve_tt(t0A, Wi.ap[:, k_, :], a1i, ALU.mult, ak + sbk, sak)
                dve_tt(Wr.ap[:, k_ + 1, :], Wr.ap[:, k_, :], a1r, ALU.mult, ak + sbk, ak)
                dve_tt(Wr.ap[:, k_ + 1, :], Wr.ap[:, k_ + 1, :], t0A, ALU.subtract, ak + sak, ak)
                dve_tt(t0A, Wr.ap[:, k_, :], a1i, ALU.mult, ak + sbk, sak)
                dve_tt(Wi.ap[:, k_ + 1, :], Wi.ap[:, k_, :], a1r, ALU.mult, ak + sbk, ak)
                dve_tt(Wi.ap[:, k_ + 1, :], Wi.ap[:, k_ + 1, :], t0A, ALU.add, ak + sak, ak)
            g2c = cf("g2col")
            for ri, Wx in enumerate((Wr, Wi)):
                src = Wx.ap.rearrange("p k (c s) -> p c k s", c=4)
                for g2p in range(2):
                    dst = VS.ap[:, :, ri, :, g2p * 64:(g2p + 1) * 64]
                    dve_ts(dst, src, g2c[:, g2p:g2p + 1], ALU.mult, ak + CK, VS.keys())
            P.op("dve", lambda e: e.memset(Xc.ap, 0.0), writes=Xc.keys())
            allkv = [("KV", k) for k in range(65536 // KG)]
            P.op("dve", lambda e: e.memset(LB.ap, 0.0), reads=allkv,
                 writes=LB.keys() + [("KTc", i) for i in range(NT)] + [("VCc", i) for i in range(NT)])

        P.ses = set(os.environ.get("K_SES_SETUP", "").split(","))
        s5_setup()
        P.ses = set(SAME_ENGINE_SYNC)

        WSPEC = {"w_in": w_in, "w_glu": w_glu, "w_out": w_out}
        for l_ in range(2):
            WSPEC["w_up%d" % l_] = w_up[l_]
            WSPEC["w_dn%d" % l_] = w_dn[l_]
            WSPEC["w_gt%d" % l_] = w_gt[l_]
            WSPEC["w_pu%d" % l_] = w_pu[l_]
        SCRT = {}
        WNKT = {}
        for name, W in WSPEC.items():
            K_, N_ = W.shape
            nkt = min(8, K_ // 128)
            WNKT[name] = nkt
            SCRT[name] = nc.dram_tensor("s_" + name, [N_ // 256, K_ // (128 * nkt), 128, nkt * 256], BF16, kind="Internal").ap()
        SCRT["pool"] = nc.dram_tensor("s_pool", [4, 1, 128, 512], BF16, kind="Internal").ap()
        WNKT["pool"] = 2

        CONV = []

        def convert(name, chunks):
            for (c, kc) in chunks:
                CONV.append((name, c, kc))

        convert("w_in", [(c, 0) for c in range(8)])
        convert("w_glu", [(c, 0) for c in range(2)])
        convert("w_out", [(c, 0) for c in range(4)])
        for l_ in range(2):
            if l_ == 1:
                convert("pool", [(g_, 0) for g_ in range(4)])
            for qq in range(4):
                convert("w_up%d" % l_, [(qq * 4 + c, 0) for c in range(4)])
                convert("w_dn%d" % l_, [(c, qq) for c in range(4)])
            for half in range(2):
                convert("w_gt%d" % l_, [(half * 2 + c, 0) for c in range(2)])
                convert("w_pu%d" % l_, [(half * 2 + c, 0) for c in range(2)])
        CONV_IDX = {k_: i_ for i_, k_ in enumerate(CONV)}
        cvp = [0]
        NCV = 8
        LOOKAHEAD = int(os.environ.get("K_LA", "6"))

        def conv_upto(idx):
            while cvp[0] <= min(idx, len(CONV) - 1):
                i_ = cvp[0]
                cvp[0] += 1
                name, c, kc = CONV[i_]
                nkt = WNKT[name]
                if name == "pool":
                    src = w_pool[c].rearrange("(k p) n -> p k n", p=128)
                else:
                    W = WSPEC[name]
                    src = W[kc * nkt * 128:(kc + 1) * nkt * 128, c * 256:(c + 1) * 256].rearrange("(k p) n -> p k n", p=128)
                dst = SCRT[name][c, kc].rearrange("p (k n) -> p k n", k=nkt)
                P.dma("pool", "cv%d" % (i_ % NCV), lambda e, src=src, dst=dst: e.dma_start(out=dst, in_=src),
                      writes=[("scr", name, c, kc), ("cvring", i_ % NCV)])

        wctr = [0]
        NSLOT = len(WS)
        PRE = {}

        def prefetch(chunks):
            for key in chunks:
                if key not in PRE:
                    PRE[key] = load_w(*key)

        def load_w(name, c, kc=0):
            s = wctr[0] % NSLOT
            wctr[0] += 1
            slot = WS[s]
            nkt = WNKT[name]
            conv_upto(CONV_IDX[(name, c, kc)] + LOOKAHEAD)
            src = SCRT[name][c, kc].rearrange("p (k n) -> p k n", k=nkt)
            P.dma("pool", "ws%d" % s, lambda e: e.dma_start(out=slot.ap[:, 0:nkt, :], in_=src),
                  reads=[("scr", name, c, kc)], writes=slot.keys())
            return slot

        pctr = [0]

        def banks(n):
            b = [(pctr[0] + i) % 8 for i in range(n)]
            pctr[0] = (pctr[0] + n) % 8
            return b

        def gemm_gen(name, c0, nchunks, kc, rhs_fn, rkeys_fn, evac):
            nkt = WNKT[name]
            for c in range(nchunks):
                slot = PRE.pop((name, c0 + c, kc), None)
                if slot is None:
                    slot = load_w(name, c0 + c, kc)
                bs = banks(2)
                for m in range(2):
                    b = bs[m]

                    def mm(e, m=m, b=b, slot=slot):
                        last = None
                        for kt in range(nkt):
                            last = e.matmul(out=PB[b][:, :], lhsT=slot.ap[:, kt, m * 128:(m + 1) * 128], rhs=rhs_fn(kt),
                                            start=(kt == 0), stop=(kt == nkt - 1))
                        return last

                    rk = sum([rkeys_fn(kt) for kt in range(nkt)], [])
                    P.op("pe", mm, reads=slot.keys() + rk, writes=PK[b])
                    evac(c * 2 + m, b)
                yield c

        def gemm_fm(*a):
            for _ in gemm_gen(*a):
                pass

        def interleave(g1, g2, lead=1):
            for _ in range(lead):
                next(g1, None)
            d1 = d2 = False
            while not (d1 and d2):
                if not d2:
                    d2 = next(g2, "done") == "done"
                if not d1:
                    d1 = next(g1, "done") == "done"

        STGb = [R["STG"].buf(i * 4096, [1024], F32) for i in range(2)]
        hn = R["HN"].buf(0, [8, 512], BF16)
        mixT = R["HN"].buf(0, [8, 512], BF16)
        sq = R["SQ"].buf(0, [8, 512], BF16)
        qz = R["SQ"].buf(0, [8, 512], BF16)
        gb = R["SQ"].buf(0, [4, 512], BF16)
        sig = R["SQ"].buf(0, [4, 512], F32)
        acc = R["AT"].buf(0, [4, 512], F32)
        y32 = R["AT"].buf(0, [4, 512], F32)
        aTs = [R["AT"].buf(0, [8, 512], BF16), R["S5B"].buf(0, [8, 512], BF16)]
        uTb = R["UB"].buf(0, [4, 512], BF16)
        uM = R["UB"].buf(4096, [4, 512], BF16)
        r32 = [R["UB"].buf(i * 2048, [512], F32) for i in range(4)]
        pT = R["STG"].buf(2048, [2, 512], BF16)
        pstg = [R["STG"].buf(i * 4096, [256], F32) for i in range(2)]
        e32 = R["SCR"].buf(0, [512], F32)
        spb = [R["SCR"].buf(2048 + i * 1024, [512], BF16) for i in range(3)]
        wlb = [R["SCR"].buf(5120 + i * 1024, [512], BF16) for i in range(3)]
        tmpf = [R["SCR"].buf(i * 2048, [512], F32) for i in range(4)]
        SX = R["S5B"].buf(0, [2, 16, 64], F32)
        Xp = R["S5B"].buf(8192, [2, 16, 64], BF16)
        hnp = [R["S5B"].buf(i * 2112, [528], F32) for i in range(2)]
        ptmp = [R["S5B"].buf(4224 + i * 2112, [528], F32) for i in range(2)]
        ypool = [R["S5B"].buf(8448 + i * 2048, [2, 512], BF16) for i in range(2)]
        hTok = R["AT"].buf(0, [4, 1024], BF16)
        ypool8 = R["UB"].buf(0, [8, 512], BF16)
        bandsb = R["S5B"].buf(0, [16, 128], BF16)
        carryb = R["S5B"].buf(4096, [1024], BF16)
        rstdc = R["RS"].buf(0, [4], F32)
        lnc = R["RS"].buf(512, [4], F32)
        rstd = R["RS"].buf(0, [512], F32)
        sqt = R["RS"].buf(2048, [512], F32)

        def dbg_dump(src_ap, keys, ti_sel, ti, slot=None):
            if dbg_d is None or ti != ti_sel:
                return
            dst = dbg_d[:, 0:src_ap.shape[1], :] if slot is None else dbg_d[:, slot, :]
            return P.dma("sp", "dbg", lambda e: e.dma_start(out=dst, in_=src_ap), reads=keys)

        NPOOL = int(os.environ.get("K_NPOOL", "3"))

        def rmsnorm(gain_name, out_fn, out_keys_fn, fp32_out=False):
            b = banks(1)[0]
            for dt in range(8):
                act(sq.ap[:, dt, :], HT.ap[:, dt, :], AF.Square, HT.keys(dt), sq.keys(dt))
                P.op("pe", lambda e, dt=dt: e.matmul(out=PB[b][:, :], lhsT=ones_b, rhs=sq.ap[:, dt, :], start=(dt == 0), stop=(dt == 7)),
                     reads=sq.keys(dt) + CK, writes=PK[b])
            act(sqt.ap, PB[b][:, :], AF.Ln, PK[b], sqt.keys(), scale=1.0 / 1024.0, bias=EPS)
            act(rstd.ap, sqt.ap, AF.Exp, sqt.keys(), rstd.keys(), scale=-0.5)
            for i, dt in enumerate(range(8 - NPOOL, 8)):
                act(tmpf[i].ap, HT.ap[:, dt, :], AF.Copy, HT.keys(dt) + CK, tmpf[i].keys(), scale=vcol(gain_name, dt))
            for i, dt in enumerate(range(8 - NPOOL, 8)):
                P.op("pool", lambda e, i=i, dt=dt: e.tensor_tensor(out=out_fn(dt), in0=tmpf[i].ap, in1=rstd.ap, op=ALU.mult),
                     reads=tmpf[i].keys() + rstd.keys(), writes=out_keys_fn(dt))
            for dt in range(8 - NPOOL):
                dve_stt(out_fn(dt), HT.ap[:, dt, :], vcol(gain_name, dt), rstd.ap, ALU.mult, ALU.mult,
                        HT.keys(dt) + rstd.keys() + CK, out_keys_fn(dt))

        final_sigs = []

        for ti in range(NT):
            t0 = ti * TT
            P.stage = "t%d_load" % ti
            for blk in range(4):
                sb_ = STGb[blk % 2]
                r0 = t0 + blk * 128
                P.dma("sp", "stg%d" % (blk % 2), lambda e, sb_=sb_, r0=r0: e.dma_start(out=sb_.ap, in_=x_d[r0:r0 + 128, :]), writes=sb_.keys())
                for half in range(2):
                    b = banks(1)[0]

                    def tr(e, sb_=sb_, half=half, b=b):
                        last = None
                        for j in range(4):
                            dt = half * 4 + j
                            last = e.transpose(out=PB[b][:, j * 128:(j + 1) * 128], in_=sb_.ap[:, dt * 128:(dt + 1) * 128], identity=ident.ap)
                        return last

                    P.op("pe", tr, reads=sb_.keys() + CK, writes=PK[b])
                    dst = HT.ap[:, half * 4:half * 4 + 4, blk * 128:(blk + 1) * 128]
                    src = PB[b][:, :].rearrange("p (j t) -> p j t", j=4)
                    wk = sum([HT.keys(half * 4 + j) for j in range(4)], [])
                    if half == 0:
                        act(dst, src, AF.Copy, PK[b], wk)
                    else:
                        cp(dst, src, PK[b], wk)

            for layer in range(2):
                if layer == 0:
                    P.stage = "t%d_inproj" % ti
                    prefetch([("w_in", c_, 0) for c_ in range(3)])
                    rmsnorm("ln_mix0", lambda dt: hn.ap[:, dt, :], lambda dt: hn.keys(dt))
                    hn_r = lambda kt: hn.ap[:, kt, :]
                    hn_k = lambda kt: hn.keys(kt)

                    def ev_u(m, b):
                        act(uTb.ap[:, m, :], PB[b][:, :], AF.Copy, PK[b], uTb.keys(m))

                    gemm_fm("w_in", 0, 2, 0, hn_r, hn_k, ev_u)
                    dve_ts(uM.ap[64:128], uTb.ap[64:128], cf("pp3")[64:128, :], ALU.mult, uTb.keys() + CK, uM.keys())

                    def qk_evac(is_q):
                        def ev(m, b):
                            s_ = tmpf[m % 2]
                            sb16 = s_.ap.bitcast(BF16)[:, 0:512]
                            act(sb16, PB[b][:, :], AF.Square, PK[b], s_.keys())
                            b2 = banks(1)[0]
                            P.op("pe", lambda e: e.matmul(out=PB[b2][:, :], lhsT=blockones_b, rhs=sb16, start=True, stop=True),
                                 reads=s_.keys() + CK, writes=PK[b2])
                            r_ = tmpf[2 + m % 2]
                            act(r_.ap, PB[b2][:, :], AF.Ln, PK[b2], r_.keys(), scale=1.0 / 64.0, bias=EPS)
                            act(r_.ap, r_.ap, AF.Exp, r_.keys(), r_.keys(), scale=-0.5)
                            if is_q:
                                dve_stt(qz.ap[:, 2 * m, :], PB[b][:, :], vcol("qgA"), r_.ap, ALU.mult, ALU.mult,
                                        PK[b] + r_.keys() + CK, qz.keys(2 * m))
                                dve_stt(qz.ap[:, 2 * m + 1, :], PB[b][:, :], vcol("qgB"), r_.ap, ALU.mult, ALU.mult,
                                        PK[b] + r_.keys() + CK, qz.keys(2 * m + 1))
                            else:
                                dve_stt(KT.ap[:, m, t0:t0 + 512], PB[b][:, :], vcol("kgA"), r_.ap, ALU.mult, ALU.mult,
                                        PK[b] + r_.keys() + CK, [("KTc", ti)])
                        return ev

                    gemm_fm("w_in", 2, 2, 0, hn_r, hn_k, qk_evac(True))
                    gemm_fm("w_in", 4, 2, 0, hn_r, hn_k, qk_evac(False))
                    for c in range(2):
                        slot = load_w("w_in", 6 + c, 0)
                        bs = banks(2)
                        for blk in range(4):
                            b = bs[blk // 2]
                            cs = slice((blk % 2) * 256, (blk % 2) * 256 + 256)

                            def mm(e, blk=blk, b=b, cs=cs, slot=slot):
                                last = None
                                for kt in range(8):
                                    last = e.matmul(out=PB[b][:, cs], lhsT=hn.ap[:, kt, blk * 128:(blk + 1) * 128], rhs=slot.ap[:, kt, :],
                                                    start=(kt == 0), stop=(kt == 7))
                                return last

                            P.op("pe", mm, reads=slot.keys() + hn.keys(), writes=[("pbh%d" % b, blk % 2)] + PK[b])
                        for blk in range(4):
                            b = bs[blk // 2]
                            cs = slice((blk % 2) * 256, (blk % 2) * 256 + 256)
                            dst = VC.ap[:, ti * 4 + blk, c * 256:(c + 1) * 256]
                            if blk % 2 == 0:
                                act(dst, PB[b][:, cs], AF.Copy, PK[b], [("VCc", ti)])
                            else:
                                cp(dst, PB[b][:, cs], PK[b], [("VCc", ti)])

                    P.stage = "t%d_att" % ti
                    sbanks = [6, 7, 6, 7]
                    t1_, t2_, t3_, t4_ = [RT.ap[:, i, :] for i in range(4)]
                    tk = RT.keys()
                    Ar = Acf.ap[:, 0, :]
                    Ai = Acf.ap[:, 1, :]
                    sxk = SX.keys()
                    for grp in range(2):
                        def mmS(e, grp=grp):
                            last = None
                            for pp in (2 * grp, 2 * grp + 1):
                                bnk = PB[6 + pp % 2]
                                for ct in range(4):
                                    for ri in range(2):
                                        col = (ct * 2 + ri) * 64
                                        for j in range(8):
                                            k = 7 - j
                                            if pp < 3:
                                                rows = slice(pp * 32, pp * 32 + 32)
                                                rhs = uTb.ap[rows, ct, j:512:8]
                                            else:
                                                rows = slice(64, 128)
                                                rhs = uM.ap[rows, ct, j:512:8]
                                            last = e.matmul(out=bnk[:, col:col + 64], lhsT=VS.ap[rows, ct, ri, k, :], rhs=rhs,
                                                            start=(j == 0), stop=(j == 7))
                            return last

                        P.op("pe", mmS, reads=uTb.keys() + uM.keys() + VS.keys(), writes=PK[6] + PK[7])
                        for pp in (2 * grp, 2 * grp + 1):
                            bk_ = 6 + pp % 2
                            src = PB[bk_][:, :].rearrange("p (c r n) -> p r c n", c=4, r=2)
                            for ri in range(2):
                                dst = SX.ap[:, ri, pp:16:4, :]
                                cp(dst, src[:, ri], PK[bk_], sxk)
                    cp(Xp.ap[:, :, :, 0], Xc.ap, Xc.keys(), Xp.keys())

                    def rec_step(c):
                        pr = Xc.ap[:, 0, :] if c == 0 else SX.ap[:, 0, :, c - 1]
                        pi = Xc.ap[:, 1, :] if c == 0 else SX.ap[:, 1, :, c - 1]
                        rk_ = (Xc.keys() if c == 0 else []) + sxk + Acf.keys() + tk

                        def rec(e, pr=pr, pi=pi, c=c):
                            e.tensor_tensor(out=t1_, in0=Ar, in1=pr, op=ALU.mult)
                            e.tensor_tensor(out=t2_, in0=Ai, in1=pi, op=ALU.mult)
                            e.tensor_tensor(out=t3_, in0=Ar, in1=pi, op=ALU.mult)
                            e.tensor_tensor(out=t4_, in0=Ai, in1=pr, op=ALU.mult)
                            e.tensor_tensor(out=t1_, in0=t1_, in1=t2_, op=ALU.subtract)
                            e.tensor_tensor(out=t3_, in0=t3_, in1=t4_, op=ALU.add)
                            e.tensor_tensor(out=SX.ap[:, 0, :, c], in0=SX.ap[:, 0, :, c], in1=t1_, op=ALU.add)
                            return e.tensor_tensor(out=SX.ap[:, 1, :, c], in0=SX.ap[:, 1, :, c], in1=t3_, op=ALU.add)

                        P.op(REC_ENG, rec, reads=rk_, writes=sxk + tk)

                    steps = []
                    for h in range(8):
                        for kb in range(4 * ti + 3, -1, -1):
                            steps.append((h, kb))
                    nst = len(steps)
                    rec_per = (64 + nst - 1) // nst
                    rec_done = [0]

                    def geom(i):
                        h, kb = steps[i]
                        b_ = kb - 4 * ti
                        diag = b_ >= 0
                        c0 = 128 * b_ if diag else 0
                        sb0 = b_ if diag else 0
                        return h, kb, diag, c0, sb0, 512 - c0

                    def stageA(i):
                        h, kb, diag, c0, sb0, N = geom(i)
                        zb = i % 2
                        sp_ = spb[i % 3]
                        kt_l = KT.ap[:, h // 2, kb * 128:(kb + 1) * 128]
                        q_r = qz.ap[:, h, c0:512]
                        P.op("pe", lambda e: e.matmul(out=PB[zb][:, 0:N], lhsT=kt_l, rhs=q_r, start=True, stop=True),
                             reads=[("KTc", kb // 4)] + qz.keys(h), writes=PK[zb])
                        act(e32.ap[:, 0:N], PB[zb][:, 0:N], AF.Exp, PK[zb], e32.keys())
                        act(sp_.ap[:, 0:N], e32.ap[:, 0:N], AF.Ln, e32.keys(), sp_.keys(), bias=1.0)
                        if diag:
                            dve_tt(sp_.ap[:, 0:128], sp_.ap[:, 0:128], masklt_b, ALU.mult, sp_.keys() + CK, sp_.keys())

                    def stageB(i):
                        h, kb, diag, c0, sb0, N = geom(i)
                        ab = 2 + i % 2
                        sp_ = spb[i % 3]
                        wl_ = wlb[i % 3]
                        kt_l = KT.ap[:, h // 2, kb * 128:(kb + 1) * 128]
                        q_r = qz.ap[:, h, c0:512]

                        def mm2(e):
                            e.matmul(out=PB[ab][:, 0:N], lhsT=kt_l, rhs=q_r, start=True, stop=False)
                            return e.matmul(out=PB[ab][:, 0:N], lhsT=negtri_b, rhs=sp_.ap[:, 0:N], start=False, stop=True)

                        P.op("pe", mm2, reads=[("KTc", kb // 4)] + qz.keys(h) + sp_.keys() + CK, writes=PK[ab])
                        act(wl_.ap[:, 0:N], PB[ab][:, 0:N], AF.Exp, PK[ab], wl_.keys())
                        if diag:
                            dve_tt(wl_.ap[:, 0:128], wl_.ap[:, 0:128], masklt_b, ALU.mult, wl_.keys() + CK, wl_.keys())

                    def stageC(i):
                        h, kb, diag, c0, sb0, N = geom(i)
                        vb = 4 + i % 2
                        sp_ = spb[i % 3]
                        wl_ = wlb[i % 3]
                        par = h % 2
                        Cst = CE.ap[:, par, 0, :]
                        Est = CE.ap[:, par, 1, :]
                        cek = [("CE", par)]
                        nsb = 4 - sb0
                        v_r = VC.ap[:, kb, h * 64:(h + 1) * 64]

                        def mm3(e):
                            last = None
                            for ii in range(nsb):
                                sbq = sb0 + ii
                                cs = slice(ii * 128, (ii + 1) * 128)
                                e.matmul(out=PB[vb][:, sbq * 80:sbq * 80 + 64], lhsT=wl_.ap[:, cs], rhs=v_r, start=True, stop=True)
                                last = e.matmul(out=PB[vb][:, sbq * 80 + 64:sbq * 80 + 65], lhsT=sp_.ap[:, cs], rhs=ones_b[:, 0:1], start=True, stop=True)
                            return last

                        P.op("pe", mm3, reads=wl_.keys() + sp_.keys() + [("VCc", kb // 4)] + CK, writes=PK[vb])
                        cont0 = sb0 + 1 if diag else 0
                        if cont0 < 4:
                            if E_ON_POOL:
                                P.op("pool", lambda e: e.tensor_tensor(out=Est[:, cont0:4], in0=EINV.ap[:, cont0:4], in1=Cst[:, cont0:4], op=ALU.pow),
                                     reads=cek + EINV.keys(), writes=cek)
                            else:
                                act(Est[:, cont0:4], Cst[:, cont0:4], AF.Exp, cek, cek, scale=-1.0)
                        for sbq in range(sb0, 4):
                            pv = PB[vb][:, sbq * 80:sbq * 80 + 64]
                            dst = acc.ap[:, sbq, h * 64:(h + 1) * 64]
                            if diag and sbq == sb0:
                                cp(dst, pv, PK[vb], acc.keys(sbq))
                            else:
                                dve_stt(dst, pv, Est[:, sbq:sbq + 1], dst, ALU.mult, ALU.add, PK[vb] + cek + acc.keys(sbq), acc.keys(sbq))
                        pcs = PB[vb][:, 0:320].rearrange("p (s c) -> p s c", c=80)[:, :, 64]
                        if diag:
                            cp(Cst[:, sb0:sb0 + 1], pcs[:, sb0:sb0 + 1], PK[vb], cek)
                            if sb0 + 1 < 4:
                                dve_tt(Cst[:, sb0 + 1:4], Cst[:, sb0 + 1:4], pcs[:, sb0 + 1:4], ALU.add, PK[vb] + cek, cek)
                        else:
                            dve_tt(Cst, Cst, pcs, ALU.add, PK[vb] + cek, cek)
                        for _ in range(rec_per):
                            if rec_done[0] < 64:
                                rec_step(rec_done[0])
                                rec_done[0] += 1

                    for i in range(nst + 2):
                        if i < nst:
                            stageA(i)
                        if 0 <= i - 1 < nst:
                            stageB(i - 1)
                        if 0 <= i - 2 < nst:
                            stageC(i - 2)
                    while rec_done[0] < 64:
                        rec_step(rec_done[0])
                        rec_done[0] += 1
                    cp(Xp.ap[:, :, :, 1:64], SX.ap[:, :, :, 0:63], sxk, Xp.keys())
                    cp(Xc.ap, SX.ap[:, :, :, 63], sxk, Xc.keys())
                    dbg_dump(acc.ap, acc.keys(), 0, ti) if dbg == "att" else None
                    for ft in range(4):
                        b = banks(1)[0]

                        def tr(e, ft=ft, b=b):
                            last = None
                            for sbq in range(4):
                                last = e.transpose(out=PB[b][:, sbq * 128:(sbq + 1) * 128], in_=acc.ap[:, sbq, ft * 128:(ft + 1) * 128], identity=ident.ap)
                            return last

                        P.op("pe", tr, reads=acc.keys() + CK, writes=PK[b])
                        act(mixT.ap[:, 4 + ft, :], PB[b][:, :], AF.Copy, PK[b], mixT.keys(4 + ft))

                    P.stage = "t%d_s5out" % ti
                    ybanks = banks(4)
                    for ct in range(4):
                        bnk = PB[ybanks[ct]]

                        def mmY(e, ct=ct, bnk=bnk):
                            last = None
                            for ip in range(8):
                                for j in range(ip + 1):
                                    e.matmul(out=bnk[:, ip:512:8], lhsT=KD.ap[:, ct, ip - j, :], rhs=uTb.ap[:, ct, j:512:8],
                                             start=(ip == 0 and j == 0), stop=False, skip_group_check=True)
                            for ip in range(8):
                                for pp in range(4):
                                    q = 4 * ct + pp
                                    for ri in range(2):
                                        last = e.matmul(out=bnk[pp * 32:(pp + 1) * 32, ip:512:8], lhsT=YW.ap[:, q, ip, ri, :], rhs=Xp.ap[:, ri, q, :],
                                                        start=False, stop=(ri == 1 and ip == 7 and pp == 3), tile_position=(0, pp * 32), skip_group_check=True)
                            return last

                        P.op("pe", mmY, reads=uTb.keys(ct) + KD.keys() + YW.keys() + Xp.keys(), writes=PK[ybanks[ct]])
                        dve_stt(y32.ap[:, ct, :], uTb.ap[:, ct, :], vcol("s5d", ct), bnk[:, :], ALU.mult, ALU.add,
                                PK[ybanks[ct]] + uTb.keys(ct) + CK, y32.keys(ct))
                        act(y32.ap[:, ct, :], y32.ap[:, ct, :], AF.Gelu, y32.keys(ct), y32.keys(ct))
                        cp(gb.ap[:, ct, :], y32.ap[:, ct, :], y32.keys(ct), gb.keys(ct))
                    dbg_dump(y32.ap, y32.keys(), 0, ti) if dbg == "s5" else None

                    def ev_glu(m, b):
                        t_ = tmpf[m % 2]
                        act(t_.ap, PB[b][:, :], AF.Sigmoid, PK[b], t_.keys())
                        dve_tt(mixT.ap[:, m, :], y32.ap[:, m, :], t_.ap, ALU.mult, y32.keys(m) + t_.keys(), mixT.keys(m))

                    gemm_fm("w_glu", 0, 2, 0, lambda kt: gb.ap[:, kt, :], lambda kt: gb.keys(kt), ev_glu)

                    def ev_res(m, b):
                        dve_tt(HT.ap[:, m, :], HT.ap[:, m, :], PB[b][:, :], ALU.add, PK[b] + HT.keys(m), HT.keys(m))

                    gemm_fm("w_out", 0, 4, 0, lambda kt: mixT.ap[:, kt, :], lambda kt: mixT.keys(kt), ev_res)
                    if dbg == "mix0":
                        dbg_dump(HT.ap, HT.keys(), NT - 1, ti)
                else:
                    P.stage = "t%d_pool" % ti
                    prefetch([("pool", g_, 0) for g_ in range(3)])
                    P.dma("pool", "bands", lambda e: e.dma_start(out=bandsb.ap.rearrange("p a b -> p (a b)"), in_=bands_d[:, :]), writes=bandsb.keys())
                    if ti > 0:
                        P.dma("sp", "carry", lambda e: e.dma_start(out=carryb.ap, in_=carry_d[:, :]), reads=[("carryd", 0)], writes=carryb.keys())
                    bss = banks(1)[0]
                    for dt in range(8):
                        act(sq.ap[:, dt, :], HT.ap[:, dt, :], AF.Square, HT.keys(dt), sq.keys(dt))

                    def mmss(e, bss=bss):
                        last = None
                        for blk in range(4):
                            for dt in range(8):
                                last = e.matmul(out=PB[bss][:, blk:blk + 1], lhsT=sq.ap[:, dt, blk * 128:(blk + 1) * 128], rhs=ones_b[:, 0:1],
                                                start=(dt == 0), stop=(dt == 7))
                        return last

                    P.op("pe", mmss, reads=sq.keys() + CK, writes=PK[bss])
                    act(lnc.ap, PB[bss][:, 0:4], AF.Ln, PK[bss], lnc.keys(), scale=1.0 / 1024.0, bias=EPS)
                    act(rstdc.ap, lnc.ap, AF.Exp, lnc.keys(), rstdc.keys(), scale=-0.5)
                    for blk in range(4):
                        bs = banks(2)
                        for half in range(2):
                            b = bs[half]

                            def trh(e, half=half, b=b, blk=blk):
                                last = None
                                for j in range(4):
                                    dt = half * 4 + j
                                    last = e.transpose(out=PB[b][:, j * 128:(j + 1) * 128], in_=HT.ap[:, dt, blk * 128:(blk + 1) * 128], identity=ident.ap)
                                return last

                            P.op("pe", trh, reads=HT.keys() + CK, writes=PK[b])
                            dst = hTok.ap[:, blk, half * 512:(half + 1) * 512]
                            if half == 0:
                                act(dst, PB[b][:, :], AF.Copy, PK[b] + rstdc.keys(), hTok.keys(blk), scale=rstdc.ap[:, blk:blk + 1])
                            else:
                                dve_ts(dst, PB[b][:, :], rstdc.ap[:, blk:blk + 1], ALU.mult, PK[b] + rstdc.keys(), hTok.keys(blk))
                    for dt in range(8):
                        g = dt // 2
                        b = banks(1)[0]

                        def mmb(e, dt=dt, g=g, b=b, ti=ti):
                            last = None
                            cs = slice(dt * 128, (dt + 1) * 128)
                            for blk in range(4):
                                out = PB[b][:, blk * 128:(blk + 1) * 128]
                                first_seq = (ti == 0 and blk == 0)
                                has_spill = not first_seq
                                if first_seq:
                                    e.matmul(out=out, lhsT=hTok.ap[:, 0, cs], rhs=bandsb.ap[:, 4 * g + 2, :], start=True, stop=False)
                                    last = e.matmul(out=out, lhsT=hTok.ap[:, 0, cs], rhs=bandsb.ap[:, 4 * g + 3, :], start=False, stop=True)
                                else:
                                    e.matmul(out=out, lhsT=hTok.ap[:, blk, cs], rhs=bandsb.ap[:, 4 * g + 0, :], start=True, stop=False)
                                    prev = carryb.ap[:, cs] if blk == 0 else hTok.ap[:, blk - 1, cs]
                                    last = e.matmul(out=out, lhsT=prev, rhs=bandsb.ap[:, 4 * g + 1, :], start=False, stop=True)
                            return last

                        P.op("pe", mmb, reads=hTok.keys() + bandsb.keys() + carryb.keys(), writes=PK[b])
                        if dt % 2 == 0:
                            act(ypool8.ap[:, dt, :], PB[b][:, :], AF.Copy, PK[b] + CK, ypool8.keys(dt), scale=vcol("ln_mix1", dt))
                        else:
                            dve_ts(ypool8.ap[:, dt, :], PB[b][:, :], vcol("ln_mix1", dt), ALU.mult, PK[b] + CK, ypool8.keys(dt))
                    P.dma("sp", "carryw", lambda e: e.dma_start(out=carry_d[:, :], in_=hTok.ap[:, 3, :]), reads=hTok.keys(3), writes=[("carryd", 0)])
                    for g in range(4):
                        slot = PRE.pop(("pool", g, 0), None) or load_w("pool", g, 0)
                        bs = banks(2)
                        for m in range(2):
                            b = bs[m]

                            def mm(e, m=m, b=b, slot=slot, g=g):
                                e.matmul(out=PB[b][:, :], lhsT=slot.ap[:, 0, m * 128:(m + 1) * 128], rhs=ypool8.ap[:, 2 * g, :], start=True, stop=False)
                                return e.matmul(out=PB[b][:, :], lhsT=slot.ap[:, 1, m * 128:(m + 1) * 128], rhs=ypool8.ap[:, 2 * g + 1, :], start=False, stop=True)

                            P.op("pe", mm, reads=slot.keys() + ypool8.keys(2 * g) + ypool8.keys(2 * g + 1), writes=PK[b])
                            dt = 2 * g + m
                            dve_stt(HT.ap[:, dt, :], PB[b][:, :], vcol("pscale", dt), HT.ap[:, dt, :], ALU.mult, ALU.add,
                                    PK[b] + HT.keys(dt) + CK, HT.keys(dt))

                if dbg == "pool" and layer == 1:
                    dbg_dump(HT.ap, HT.keys(), 0, ti)
                P.stage = "t%d_mlp%d" % (ti, layer)
                prefetch([("w_up%d" % layer, c_, 0) for c_ in range(3)])
                rmsnorm("ln_mlp%d" % layer, lambda dt: hn.ap[:, dt, :], lambda dt: hn.keys(dt))
                def up_gen(qq):
                    aT = aTs[qq % 2]

                    def ev_up(m, b, aT=aT):
                        r_ = r32[m % 4]
                        act(r_.ap, PB[b][:, :], AF.Relu, PK[b], r_.keys())
                        dve_tt(aT.ap[:, m, :], r_.ap, r_.ap, ALU.mult, r_.keys(), aT.keys(m))

                    return gemm_gen("w_up%d" % layer, qq * 4, 4, 0, lambda kt: hn.ap[:, kt, :], lambda kt: hn.keys(kt), ev_up)

                def dn_gen(qq):
                    aT = aTs[qq % 2]

                    def ev_res(m, b):
                        dve_tt(HT.ap[:, m, :], HT.ap[:, m, :], PB[b][:, :], ALU.add, PK[b] + HT.keys(m), HT.keys(m))

                    return gemm_gen("w_dn%d" % layer, 0, 4, qq, lambda kt, aT=aT: aT.ap[:, kt, :], lambda kt, aT=aT: aT.keys(kt), ev_res)

                if MLP_PIPE:
                    for _ in up_gen(0):
                        pass
                    for qq in range(4):
                        if qq < 3:
                            interleave(up_gen(qq + 1), dn_gen(qq), lead=1)
                        else:
                            for _ in dn_gen(qq):
                                pass
                else:
                    for qq in range(4):
                        for _ in up_gen(qq):
                            pass
                        for _ in dn_gen(qq):
                            pass

                if dbg == "mlp0" and layer == 0:
                    dbg_dump(HT.ap, HT.keys(), NT - 1, ti)
                P.stage = "t%d_ple%d" % (ti, layer)
                prefetch([("w_gt%d" % layer, c_, 0) for c_ in range(3)])
                rmsnorm("ln_ple%d" % layer, lambda dt: hn.ap[:, dt, :], lambda dt: hn.keys(dt))
                for blk in range(4):
                    sb_ = pstg[blk % 2]
                    r0 = t0 + blk * 128
                    P.dma("sp", "stg%d" % (blk % 2), lambda e, sb_=sb_, r0=r0, layer=layer: e.dma_start(out=sb_.ap, in_=p_d[layer, r0:r0 + 128, :]), writes=sb_.keys())
                    b = banks(1)[0]

                    def trp(e, sb_=sb_, b=b):
                        e.transpose(out=PB[b][:, 0:128], in_=sb_.ap[:, 0:128], identity=ident.ap)
                        return e.transpose(out=PB[b][:, 128:256], in_=sb_.ap[:, 128:256], identity=ident.ap)

                    P.op("pe", trp, reads=sb_.keys() + CK, writes=PK[b])
                    act(pT.ap[:, :, blk * 128:(blk + 1) * 128], PB[b][:, 0:256].rearrange("p (k t) -> p k t", k=2), AF.Copy, PK[b], pT.keys())
                sigs = [R["AT"].buf(0, [4, 512], F32), R["UB"].buf(0, [4, 512], F32)]

                def gate_gen(half):
                    sg = sigs[half]

                    def ev_gate(m, b, sg=sg):
                        act(sg.ap[:, m, :], PB[b][:, :], AF.Sigmoid, PK[b], sg.keys(m))

                    return gemm_gen("w_gt%d" % layer, half * 2, 2, 0, lambda kt: hn.ap[:, kt, :], lambda kt: hn.keys(kt), ev_gate)

                def pu_gen(half):
                    sg = sigs[half]

                    def ev_pu(m, b, half=half, sg=sg):
                        t_ = tmpf[2 + m % 2]
                        dve_tt(t_.ap, PB[b][:, :], sg.ap[:, m, :], ALU.mult, PK[b] + sg.keys(m), t_.keys())
                        dt = half * 4 + m
                        dve_tt(HT.ap[:, dt, :], HT.ap[:, dt, :], t_.ap, ALU.add, t_.keys() + HT.keys(dt), HT.keys(dt))

                    return gemm_gen("w_pu%d" % layer, half * 2, 2, 0, lambda kt: pT.ap[:, kt, :], lambda kt: pT.keys(), ev_pu)

                for _ in gate_gen(0):
                    pass
                interleave(gate_gen(1), pu_gen(0), lead=1)
                for _ in pu_gen(1):
                    pass
                if dbg == "l0" and layer == 0:
                    dbg_dump(HT.ap, HT.keys(), NT - 1, ti)

            P.stage = "t%d_out" % ti
            for blk in range(4):
                sb_ = STGb[blk % 2]
                r0 = t0 + blk * 128
                for half in range(2):
                    b = banks(1)[0]

                    def tro(e, half=half, b=b, blk=blk):
                        last = None
                        for j in range(4):
                            dt = half * 4 + j
                            last = e.transpose(out=PB[b][:, j * 128:(j + 1) * 128], in_=HT.ap[:, dt, blk * 128:(blk + 1) * 128], identity=ident.ap)
                        return last

                    P.op("pe", tro, reads=HT.keys() + CK, writes=PK[b])
                    if half == 0:
                        act(sb_.ap[:, 0:512], PB[b][:, :], AF.Copy, PK[b], sb_.keys())
                    else:
                        cp(sb_.ap[:, 512:1024], PB[b][:, :], PK[b], sb_.keys())
                sg = P.dma("sp", "stg%d" % (blk % 2), lambda e, sb_=sb_, r0=r0: e.dma_start(out=y_d[r0:r0 + 128, :], in_=sb_.ap), reads=sb_.keys())
                final_sigs.append(sg)

        fs = {}
        for s, v in final_sigs:
            fs[s] = max(fs.get(s, 0), v)
        if dbg_d is not None and "d_dbg" in P.cnt:
            fs["d_dbg"] = P.cnt["d_dbg"]
        P.finish("sp", list(fs.items()))
        P.emit()
    return nc


def _prep_shared(inp):
    f = lambda a: np.ascontiguousarray(np.asarray(a, dtype=np.float32))
    sh = {}
    sh["w_in"] = f(inp["w_in_even"][0])
    sh["w_out"] = f(inp["w_out_even"][0])
    sh["w_glu"] = f(inp["s5_w_glu"][0])
    sh["w_up"] = f(inp["w_mlp_up"])
    sh["w_dn"] = f(inp["w_mlp_down"])
    sh["w_gt"] = f(inp["w_ple_gate"])
    sh["w_pu"] = f(inp["w_ple_up"])
    sh["w_pool"] = f(inp["pool_w"][0])
    col8 = lambda v: np.asarray(v, np.float32).reshape(-1, 128).T
    qg = np.asarray(inp["sb_q_gain"][0], np.float32)
    kg = np.asarray(inp["sb_k_gain"][0], np.float32)
    qg128 = np.concatenate([qg, qg])
    kg128 = np.concatenate([kg, kg])
    half = (np.arange(128) < 64)
    scale = np.float32(64 ** -0.5)
    vec = np.concatenate([
        col8(inp["ln_mix_even"][0]), col8(inp["ln_mlp"][0]), col8(inp["ln_ple"][0]),
        col8(inp["ln_mix_odd"][0]), col8(inp["ln_mlp"][1]), col8(inp["ln_ple"][1]),
        col8(inp["pool_scale"][0]), col8(inp["s5_d"][0]),
        np.where(half, qg128, 0)[:, None], np.where(~half, qg128, 0)[:, None], kg128[:, None]], axis=1)
    sh["vecs"] = f(vec)
    lamr = np.asarray(inp["s5_lambda_re"][0], np.float32)
    lami = np.asarray(inp["s5_lambda_im"][0], np.float32)
    logdt = np.asarray(inp["s5_log_dt"][0], np.float32)
    toB = lambda a: a.reshape(16, 2, 64).transpose(1, 2, 0).reshape(128, 16)
    logdtB = toB(np.tile(logdt[:, None], (1, 64)))
    toB3 = lambda a: a.reshape(16, 2, 64, 16).transpose(1, 2, 0, 3).reshape(128, 256)
    bre = np.asarray(inp["s5_b_re"][0], np.float32)
    bim = np.asarray(inp["s5_b_im"][0], np.float32)
    cre = np.asarray(inp["s5_c_re"][0], np.float32).transpose(0, 2, 1)
    cim = np.asarray(inp["s5_c_im"][0], np.float32).transpose(0, 2, 1)
    sh["s5B"] = f(np.concatenate([toB(lamr), toB(lami), logdtB, toB3(bre), toB3(bim), toB3(cre), toB3(cim)], axis=1))
    toA = lambda a: np.tile(a.reshape(4, 8, 1, 64), (1, 1, 16, 1)).transpose(1, 2, 0, 3).reshape(128, 256)
    toA3 = lambda a: a.reshape(4, 8, 64, 16).transpose(1, 3, 0, 2).reshape(128, 256)
    sh["s5A"] = f(np.concatenate([toA(lamr), toA(lami), toA(np.tile(logdt[:, None], (1, 64))), toA3(bre), toA3(bim)], axis=1))
    sh.update(_consts())
    sh["_scale"] = scale
    return sh


_NC_CACHE = {}


def kernel(**inputs):
    x = np.asarray(inputs["x"], np.float32)
    p = np.asarray(inputs["p"], np.float32)
    B, L, Dm = x.shape
    NT = L // TT
    sh = _prep_shared(inputs)
    sh.pop("_scale")
    if NT not in _NC_CACHE:
        _NC_CACHE[NT] = build(NT)
    nc = _NC_CACHE[NT]
    in_maps = []
    for b in range(B):
        m = dict(sh)
        m["x"] = np.ascontiguousarray(x[b])
        m["p"] = np.ascontiguousarray(p[:, b])
        in_maps.append(m)
    res = run_bass_kernel_spmd(nc, in_maps, core_ids=list(range(B)))
    out = np.stack([res.results[b]["y"] for b in range(B)], axis=0)
    return out.astype(np.float32)
```

```python
import math
from contextlib import ExitStack

import numpy as np
import concourse.bass as bass
import concourse.mybir as mybir
from concourse.bass_utils import run_bass_kernel_spmd

F32 = mybir.dt.float32
BF16 = mybir.dt.bfloat16
I32 = mybir.dt.int32
AF = mybir.ActivationFunctionType
ALU = mybir.AluOpType

TT = 512
EPS = 1e-6
KLIST = list(range(-7, 9))
TWO_PI = 2.0 * math.pi
import os
MLP_PIPE = bool(int(os.environ.get("K_MLPPIPE", "1")))
REC_ENG = os.environ.get("K_REC", "dve")
E_ON_POOL = bool(int(os.environ.get("K_EPOOL", "0")))
SAME_ENGINE_SYNC = set(os.environ.get("K_SES", "").split(","))


class Prog:
    def __init__(self, nc, stack):
        self.nc = nc
        self.stack = stack
        self.ops = {e: [] for e in ("pe", "act", "dve", "pool", "sp")}
        self.sems = {}
        self.cnt = {}
        self.keys = {}
        self.waited = {e: {} for e in self.ops}
        self.final = []
        self.stage = "setup"
        self.scopes = bool(int(os.environ.get("K_SCOPES", "0")))
        self.ses = set(SAME_ENGINE_SYNC)
        self.near = {"dve": int(os.environ.get("K_NEAR_DVE", "2")), "act": int(os.environ.get("K_NEAR_ACT", "1")), "pool": 2}

    def sem(self, name):
        if name not in self.sems:
            self.sems[name] = self.stack.enter_context(self.nc.semaphore(name))
            self.cnt[name] = 0
        return self.sems[name]

    def _resolve(self, eng, reads, writes, mysig):
        waits = {}

        def add(sig):
            if sig is None:
                return
            s, v = sig
            if s == eng:
                if eng == "pe":
                    return
                if eng not in self.ses and (self.cnt[eng] - v) > self.near.get(eng, 0):
                    return
            if waits.get(s, 0) < v:
                waits[s] = v

        for k in reads:
            st = self.keys.setdefault(k, [None, []])
            add(st[0])
        for k in writes:
            st = self.keys.setdefault(k, [None, []])
            add(st[0])
            for r in st[1]:
                add(r)
        out = []
        for s, v in waits.items():
            if self.waited[eng].get(s, 0) >= v:
                continue
            self.waited[eng][s] = v
            out.append((s, v))
        for k in reads:
            self.keys[k][1].append(mysig)
        for k in writes:
            self.keys[k][0] = mysig
            self.keys[k][1] = []
        return out

    def op(self, eng, fn, reads=(), writes=()):
        self.sem(eng)
        self.cnt[eng] += 1
        mysig = (eng, self.cnt[eng])
        waits = self._resolve(eng, reads, writes, mysig)
        self.ops[eng].append((waits, fn, (eng, 1), self.stage))

    def dma(self, q, semkey, fn, reads=(), writes=()):
        name = "d_" + semkey
        self.sem(name)
        self.cnt[name] += 16
        mysig = (name, self.cnt[name])
        waits = self._resolve(q, reads, writes, mysig)
        self.ops[q].append((waits, fn, (name, 16), "dma"))
        return mysig

    def finish(self, eng, sigs):
        self.final.append((eng, sigs))

    def emit(self):
        nc = self.nc
        engmap = {"pe": "tensor", "act": "scalar", "dve": "vector", "pool": "gpsimd", "sp": "sync"}
        block = self.stack.enter_context(nc.Block())
        for e, attr in engmap.items():
            ops = self.ops[e]
            finals = [s for (fe, s) in self.final if fe == e]
            if not ops and not finals:
                continue

            def body(engine, ops=ops, finals=finals):
                for waits, fn, (sname, inc), stage in ops:
                    if self.scopes:
                        with nc.named_scope(stage):
                            for s, v in waits:
                                engine.wait_ge(self.sems[s], v)
                            inst = fn(engine)
                            inst.then_inc(self.sems[sname], inc)
                    else:
                        for s, v in waits:
                            engine.wait_ge(self.sems[s], v)
                        inst = fn(engine)
                        inst.then_inc(self.sems[sname], inc)
                for sigs in finals:
                    for s, v in sigs:
                        engine.wait_ge(self.sems[s], v)

            getattr(block, attr)(body)


KG = 512


class Region:
    def __init__(self, arena, name, woff, nbytes):
        self.arena, self.name, self.woff, self.nbytes = arena, name, woff, nbytes

    def buf(self, boff, shape, dt):
        return Buf(self, boff, shape, dt)


class Buf:
    def __init__(self, region, boff, shape, dt):
        esz = 4 if dt in (F32, I32) else 2
        n = int(np.prod(shape))
        assert boff % 4 == 0 and boff + n * esz <= region.nbytes, (region.name, boff, shape, region.nbytes)
        w0 = region.woff + boff // 4
        nw = (n * esz + 3) // 4
        ap = region.arena[:, w0:w0 + nw]
        if dt != F32:
            ap = ap.bitcast(dt)
        if len(shape) > 1:
            names = " ".join("a%d" % i for i in range(len(shape)))
            kw = {"a%d" % i: shape[i] for i in range(1, len(shape))}
            ap = ap.rearrange("p (%s) -> p %s" % (names, names), **kw)
        self.ap, self.region, self.boff, self.shape, self.esz = ap, region, boff, tuple(shape), esz
        self.nbytes = n * esz
        self.blk = (n // shape[0]) * esz

    def keys(self, lo=None, hi=None):
        if lo is None:
            b0, b1 = self.boff, self.boff + self.nbytes
        else:
            hi = lo + 1 if hi is None else hi
            b0, b1 = self.boff + lo * self.blk, self.boff + hi * self.blk
        return [(self.region.name, k) for k in range(b0 // KG, (b1 + KG - 1) // KG)]


def _consts():
    c = {}
    c["ident"] = np.eye(128, dtype=np.float32)
    jj = np.arange(128)[:, None]
    tt = np.arange(128)[None, :]
    ones = np.ones((128, 128), np.float32)
    blockones = ((jj // 64) == (tt // 64)).astype(np.float32)
    negtri = -(jj >= tt).astype(np.float32)
    masklt = (jj < tt).astype(np.float32)
    c["cstb"] = np.concatenate([ones, blockones, negtri, masklt], axis=1)
    part = np.arange(128)
    g8 = part // 16
    bd = (g8[:, None, None] == np.arange(8)[None, :, None]) * np.ones((1, 1, 16))
    g2 = (part // 16) % 2
    g2col = (g2[:, None] == np.arange(2)[None, :]).astype(np.float32)
    pp3 = (part >= 96).astype(np.float32)[:, None]
    halfA = (part < 64).astype(np.float32)[:, None]
    halfB = (part >= 64).astype(np.float32)[:, None]
    kv = np.tile(np.array(KLIST, np.float32)[None, :], (128, 1))
    kf = kv / TWO_PI
    kvA = np.tile(np.arange(8, dtype=np.float32)[None, :], (128, 1))
    kfA = kvA / TWO_PI
    cnt = np.zeros((128, 4, 16), np.float32)
    for g, w in enumerate((2, 4, 8, 16)):
        cnt[:, g, :] = 1.0 / np.minimum(np.arange(16) + 1, w)
    bands = np.zeros((128, 16, 128), np.float64)
    tq = np.arange(128)[:, None]
    tt_ = np.arange(128)[None, :]
    for g, w in enumerate((2, 4, 8, 16)):
        main = ((tq <= tt_) & (tq > tt_ - w)) / float(w) - (tq == tt_)
        spill = ((tq - 128) > (tt_ - w)) / float(w)
        cntv = np.minimum(tt_ + 1, w)
        b0 = ((tq <= tt_) & (tq > tt_ - w)) / cntv - (tq == tt_)
        import ml_dtypes
        hi = b0.astype(np.float32).astype(ml_dtypes.bfloat16).astype(np.float64)
        lo = b0 - hi
        bands[:, 4 * g + 0] = main
        bands[:, 4 * g + 1] = spill
        bands[:, 4 * g + 2] = hi
        bands[:, 4 * g + 3] = lo
    c["bands"] = bands.reshape(128, 2048).astype(np.float32)
    c["cstf"] = np.concatenate([bd.reshape(128, 128).astype(np.float32), g2col, pp3, halfA, halfB,
                                kv, kf, kvA, kfA, cnt.reshape(128, 64)], axis=1).astype(np.float32)
    return c


CF = {}
_o = 0
for _n, _w in (("bd", 128), ("g2col", 2), ("pp3", 1), ("halfA", 1), ("halfB", 1), ("kv", 16), ("kf", 16),
               ("kvA", 8), ("kfA", 8), ("cnt", 64)):
    CF[_n] = (_o, _w)
    _o += _w
NCF = _o

VEC = {}
_o = 0
for _n, _w in (("ln_mix0", 8), ("ln_mlp0", 8), ("ln_ple0", 8), ("ln_mix1", 8), ("ln_mlp1", 8), ("ln_ple1", 8),
               ("pscale", 8), ("s5d", 4), ("qgA", 1), ("qgB", 1), ("kgA", 1)):
    VEC[_n] = (_o, _w)
    _o += _w
NVEC = _o


def build(NT, dbg=None):
    L = NT * TT
    nc = bass.Bass("TRN2", target_bir_lowering=False)
    D = lambda n, s: nc.dram_tensor(n, s, F32, kind="ExternalInput").ap()
    x_d = D("x", [L, 1024])
    p_d = D("p", [2, L, 256])
    w_in = D("w_in", [1024, 2048])
    w_out = D("w_out", [1024, 1024])
    w_glu = D("w_glu", [512, 512])
    w_up = D("w_up", [2, 1024, 4096])
    w_dn = D("w_dn", [2, 4096, 1024])
    w_gt = D("w_gt", [2, 1024, 1024])
    w_pu = D("w_pu", [2, 256, 1024])
    w_pool = D("w_pool", [4, 256, 256])
    vecs_d = D("vecs", [128, NVEC])
    s5b_d = D("s5B", [128, 48 + 4 * 256])
    s5a_d = D("s5A", [128, 5 * 256])
    ident_d = D("ident", [128, 128])
    cstb_d = D("cstb", [128, 512])
    cstf_d = D("cstf", [128, NCF])
    bands_d = D("bands", [128, 2048])
    carry_d = nc.dram_tensor("poolcarry", [128, 1024], BF16, kind="Internal").ap()
    y_d = nc.dram_tensor("y", [L, 1024], F32, kind="ExternalOutput").ap()
    dbg_d = None
    if dbg:
        dbg_d = nc.dram_tensor("dbg", [128, 8, 512], F32, kind="ExternalOutput").ap()

    with ExitStack() as st:
        P = Prog(nc, st)
        sizes = [("KV", 65536), ("VS", 16384), ("YW", 16384), ("KD", 8192), ("WS", 3 * 4096), ("HT", 16384),
                 ("CST", 6144), ("STG", 8192), ("HN", 8192), ("SQ", 8192), ("AT", 8192), ("UB", 8192),
                 ("SCR", 8192), ("S5B", 12800), ("RS", 4096)]
        total_w = sum(s for _, s in sizes) // 4
        arena = st.enter_context(nc.sbuf_tensor("arena", [128, total_w], F32))
        R = {}
        wo = 0
        for n, s in sizes:
            R[n] = Region(arena, n, wo, s)
            wo += s // 4
        PB = [st.enter_context(nc.psum_tensor("pb%d" % i, [128, 512], F32)) for i in range(8)]
        PK = [[("pb%d" % i, 0)] for i in range(8)]

        KT = R["KV"].buf(0, [4, 4096], BF16)
        VC = R["KV"].buf(32768, [32, 512], BF16)
        VS = R["VS"].buf(0, [4, 2, 8, 128], BF16)
        YW = R["YW"].buf(0, [16, 8, 2, 32], BF16)
        KD = R["KD"].buf(0, [4, 8, 128], BF16)
        WS = [R["WS"].buf(i * 4096, [8, 256], BF16) for i in range(3)]
        HT = R["HT"].buf(0, [8, 512], F32)
        ident = R["CST"].buf(0, [128], F32)
        vecs = R["CST"].buf(512, [NVEC], F32)
        cstb = R["CST"].buf(1024, [4, 128], BF16)
        cstf = R["CST"].buf(2048, [NCF], F32)
        assert NCF * 4 <= 1024
        Acf = R["CST"].buf(3072, [2, 16], F32)
        Xc = R["CST"].buf(3584, [2, 16], F32)
        LB = R["CST"].buf(4096, [8, 16], F32)
        CE = R["CST"].buf(4608, [2, 2, 4], F32)
        RT = R["CST"].buf(5120, [4, 16], F32)
        EINV = R["CST"].buf(5632, [4], F32)

        def vcol(name, i=0):
            o, w = VEC[name]
            return vecs.ap[:, o + i:o + i + 1]

        def cf(name):
            o, w = CF[name]
            return cstf.ap[:, o:o + w]

        ones_b = cstb.ap[:, 0, :]
        blockones_b = cstb.ap[:, 1, :]
        negtri_b = cstb.ap[:, 2, :]
        masklt_b = cstb.ap[:, 3, :]
        CK = cstb.keys() + cstf.keys() + vecs.keys() + ident.keys()

        def dve_tt(out, in0, in1, op, r, w, eng="dve"):
            P.op(eng, lambda e: e.tensor_tensor(out=out, in0=in0, in1=in1, op=op), reads=r, writes=w)

        def dve_ts(out, in0, s1, op0, r, w, s2=None, op1=None, eng="dve"):
            if op1 is None:
                P.op(eng, lambda e: e.tensor_scalar(out=out, in0=in0, scalar1=s1, scalar2=None, op0=op0), reads=r, writes=w)
            else:
                P.op(eng, lambda e: e.tensor_scalar(out=out, in0=in0, scalar1=s1, scalar2=s2, op0=op0, op1=op1), reads=r, writes=w)

        def dve_stt(out, in0, scalar, in1, op0, op1, r, w):
            P.op("dve", lambda e: e.scalar_tensor_tensor(out=out, in0=in0, scalar=scalar, in1=in1, op0=op0, op1=op1), reads=r, writes=w)

        def act(out, in_, func, r, w, scale=1.0, bias=0.0):
            if func == AF.Copy:
                P.op("act", lambda e: e.activation(out=out, in_=in_, func=func, scale=scale), reads=r, writes=w)
            else:
                P.op("act", lambda e: e.activation(out=out, in_=in_, func=func, scale=scale, bias=bias), reads=r, writes=w)

        def cp(out, in_, r, w, eng="dve"):
            P.op(eng, lambda e: e.tensor_copy(out=out, in_=in_), reads=r, writes=w)

        P.dma("sp", "c0", lambda e: e.dma_start(out=ident.ap, in_=ident_d[:, :]), writes=ident.keys())
        P.dma("sp", "c1", lambda e: e.dma_start(out=vecs.ap, in_=vecs_d[:, :]), writes=vecs.keys())
        P.dma("sp", "c2", lambda e: e.dma_start(out=cstf.ap, in_=cstf_d[:, :]), writes=cstf.keys())
        P.dma("pool", "c3", lambda e: e.dma_start(out=cstb.ap.rearrange("p a b -> p (a b)"), in_=cstb_d[:, :]), writes=cstb.keys())

        P.op("dve", lambda e: e.memset(EINV.ap, math.exp(-1.0)), writes=EINV.keys())
        o_q = VEC["qgA"][0]
        dve_ts(vecs.ap[:, o_q:o_q + 2], vecs.ap[:, o_q:o_q + 2], 0.125, ALU.mult, vecs.keys(), vecs.keys())

        KVr = R["KV"]

        def s5_setup():
            o = [0]

            def tmp(shape, dt=F32):
                b = KVr.buf(o[0], shape, dt)
                o[0] += ((b.nbytes + 511) // 512) * 512
                return b

            inB = tmp([48 + 1024])
            P.dma("sp", "s5in", lambda e: e.dma_start(out=inB.ap, in_=s5b_d[:, :]), writes=inB.keys())
            lamr = inB.ap[:, 0:16]
            lami = inB.ap[:, 16:32]
            logdt = inB.ap[:, 32:48]
            bre = inB.ap[:, 48:304].rearrange("p (q h) -> p q h", q=16)
            bim = inB.ap[:, 304:560].rearrange("p (q h) -> p q h", q=16)
            cre = inB.ap[:, 560:816].rearrange("p (q h) -> p q h", q=16)
            cim = inB.ap[:, 816:1072].rearrange("p (q h) -> p q h", q=16)
            kin = inB.keys()
            sm = tmp([8, 16])
            smk = sm.keys()
            dt_, lrdt, lidt, den, nr, fr, fi, t0 = [sm.ap[:, i, :] for i in range(8)]
            act(dt_, logdt, AF.Exp, kin, smk)
            dve_tt(lrdt, lamr, dt_, ALU.mult, kin + smk, smk)
            dve_tt(lidt, lami, dt_, ALU.mult, kin + smk, smk)
            NK = len(KLIST)
            big = [tmp([NK, 16]) for _ in range(6)]
            mag, Tt, t1, t2, cosv, sinv = big
            allk = sum([b.keys() for b in big], [])
            kv = cf("kv")
            kf = cf("kf")
            bc_q = lambda a: a.unsqueeze(1).to_broadcast([128, NK, 16])
            bc_k = lambda a: a.unsqueeze(2).to_broadcast([128, NK, 16])
            dve_tt(mag.ap, bc_q(lrdt), bc_k(kv), ALU.mult, smk + CK, allk)
            act(mag.ap, mag.ap, AF.Exp, allk, allk)
            dve_tt(Tt.ap, bc_q(lidt), bc_k(kf), ALU.mult, smk + CK, allk)

            def sincos(dst, shift, Tsrc, a1, a2, keys):
                dve_ts(a1.ap, Tsrc.ap, shift, ALU.add, keys, keys)
                cp(a2.ap.bitcast(I32), a1.ap, keys, keys)
                cp(a2.ap, a2.ap.bitcast(I32), keys, keys)
                dve_tt(a1.ap, a1.ap, a2.ap, ALU.subtract, keys, keys)
                dve_stt(a1.ap, a1.ap, 0.0, a1.ap, ALU.is_lt, ALU.add, keys, keys)
                act(dst.ap, a1.ap, AF.Sin, keys, keys, scale=TWO_PI * (1 - 1e-6), bias=-math.pi * (1 - 1e-6))

            sincos(cosv, 0.75 + 32.0, Tt, t1, t2, allk)
            sincos(sinv, 0.5 + 32.0, Tt, t1, t2, allk)
            dve_tt(cosv.ap, cosv.ap, mag.ap, ALU.mult, allk, allk)
            dve_tt(sinv.ap, sinv.ap, mag.ap, ALU.mult, allk, allk)
            Er = lambda k: cosv.ap[:, k + 7, :]
            Ei = lambda k: sinv.ap[:, k + 7, :]
            cp(Acf.ap[:, 0, :], Er(8), allk, Acf.keys())
            cp(Acf.ap[:, 1, :], Ei(8), allk, Acf.keys())
            dve_tt(den, lamr, lamr, ALU.mult, kin, smk)
            dve_tt(t0, lami, lami, ALU.mult, kin, smk)
            dve_tt(den, den, t0, ALU.add, smk, smk)
            P.op("dve", lambda e: e.reciprocal(out=den, in_=den), reads=smk, writes=smk)
            dve_ts(nr, Er(1), -1.0, ALU.add, allk, smk)
            dve_tt(fr, nr, lamr, ALU.mult, smk + kin, smk)
            dve_tt(t0, Ei(1), lami, ALU.mult, allk + kin, smk)
            dve_tt(fr, fr, t0, ALU.add, smk, smk)
            dve_tt(fr, fr, den, ALU.mult, smk, smk)
            dve_tt(fi, Ei(1), lamr, ALU.mult, allk + kin, smk)
            dve_tt(t0, nr, lami, ALU.mult, smk + kin, smk)
            dve_tt(fi, fi, t0, ALU.subtract, smk, smk)
            dve_tt(fi, fi, den, ALU.mult, smk, smk)
            bb = tmp([4, 16, 16])
            bbk = bb.keys()
            Bbr, Bbi, tA, tB = [bb.ap[:, i] for i in range(4)]
            bch = lambda a: a.unsqueeze(2).to_broadcast([128, 16, 16])
            dve_tt(Bbr, bre, bch(fr), ALU.mult, kin + smk, bbk)
            dve_tt(tA, bim, bch(fi), ALU.mult, kin + smk, bbk)
            dve_tt(Bbr, Bbr, tA, ALU.subtract, bbk, bbk)
            dve_tt(Bbi, bim, bch(fr), ALU.mult, kin + smk, bbk)
            dve_tt(tA, bre, bch(fi), ALU.mult, kin + smk, bbk)
            dve_tt(Bbi, Bbi, tA, ALU.add, bbk, bbk)
            Rr = tmp([9, 16, 16])
            Ri = tmp([9, 16, 16])
            Rt = tmp([9, 16, 16])
            rk = Rr.keys() + Ri.keys() + Rt.keys()
            bcC = lambda a: a.unsqueeze(1).to_broadcast([128, 9, 16, 16])
            bcE = lambda a: a.unsqueeze(3).to_broadcast([128, 9, 16, 16])
            Er9 = cosv.ap[:, 7:16, :]
            Ei9 = sinv.ap[:, 7:16, :]
            dve_tt(Rr.ap, bcC(cre), bcE(Er9), ALU.mult, kin + allk, rk)
            dve_tt(Rt.ap, bcC(cim), bcE(Ei9), ALU.mult, kin + allk, rk)
            dve_tt(Rr.ap, Rr.ap, Rt.ap, ALU.subtract, rk, rk)
            dve_tt(Ri.ap, bcC(cre), bcE(Ei9), ALU.mult, kin + allk, rk)
            dve_tt(Rt.ap, bcC(cim), bcE(Er9), ALU.mult, kin + allk, rk)
            dve_tt(Ri.ap, Ri.ap, Rt.ap, ALU.add, rk, rk)
            Bz = tmp([2, 16, 128])
            bzk = Bz.keys()
            P.op("dve", lambda e: e.memset(Bz.ap, 0.0), writes=bzk)
            for half in range(2):
                ps_ = slice(half * 64, half * 64 + 64)
                for pp in range(4):
                    cs_ = slice(pp * 32 + half * 16, pp * 32 + half * 16 + 16)
                    cp(Bz.ap[ps_, 0, pp:16:4, cs_], Bbr[ps_, pp:16:4, :], bbk, bzk)
                    dve_ts(Bz.ap[ps_, 1, pp:16:4, cs_], Bbi[ps_, pp:16:4, :], -1.0, ALU.mult, bbk, bzk)
            bd = cf("bd").rearrange("p (a b) -> p a b", a=8)
            for ct in range(4):
                bank = PB[ct]

                def mm(e, ct=ct, bank=bank):
                    last = None
                    for pp in range(4):
                        q = 4 * ct + pp
                        e.matmul(out=bank[:, 0:128], lhsT=Bz.ap[:, 0, q, :], rhs=Rr.ap[:, 0:8, q, :],
                                 start=(pp == 0), stop=False)
                        last = e.matmul(out=bank[:, 0:128], lhsT=Bz.ap[:, 1, q, :], rhs=Ri.ap[:, 0:8, q, :],
                                        start=False, stop=(pp == 3))
                    return last

                P.op("pe", mm, reads=bzk + rk, writes=PK[ct])
                src = bank[:, 0:128].rearrange("p (t h) -> p t h", t=8).unsqueeze(2).to_broadcast([128, 8, 8, 16])
                msk = bd.unsqueeze(1).to_broadcast([128, 8, 8, 16])
                dst = KD.ap[:, ct].rearrange("p t (g h) -> p t g h", g=8)
                dve_tt(dst, src, msk, ALU.mult, PK[ct] + CK, KD.keys())
            P.op("dve", lambda e: e.memset(YW.ap, 0.0), writes=YW.keys())
            for half in range(2):
                ps_ = slice(half * 64, half * 64 + 64)
                cs_ = slice(half * 16, half * 16 + 16)
                srcr = Rr.ap[ps_, 1:9].rearrange("p k q h -> p q k h")
                srci = Ri.ap[ps_, 1:9].rearrange("p k q h -> p q k h")
                cp(YW.ap[ps_, :, :, 0, cs_], srcr, rk, YW.keys())
                dve_ts(YW.ap[ps_, :, :, 1, cs_], srci, -1.0, ALU.mult, rk, YW.keys())
            o[0] = 0
            inA = tmp([5, 256])
            P.dma("sp", "s5in", lambda e: e.dma_start(out=inA.ap.rearrange("p a b -> p (a b)"), in_=s5a_d[:, :]), writes=inA.keys())
            kia = inA.keys()
            lamrA, lamiA, logdtA, breA, bimA = [inA.ap[:, i, :] for i in range(5)]
            smA = tmp([8, 256])
            sak = smA.keys()
            dtA, lrdtA, lidtA, denA, nrA, frA, fiA, t0A = [smA.ap[:, i, :] for i in range(8)]
            act(dtA, logdtA, AF.Exp, kia, sak)
            dve_tt(lrdtA, lamrA, dtA, ALU.mult, kia + sak, sak)
            dve_tt(lidtA, lamiA, dtA, ALU.mult, kia + sak, sak)
            class _V:
                def __init__(self, ap):
                    self.ap = ap

            smB = tmp([6, 256])
            sbk = smB.keys()
            mag1, T1, u1, u2, c1, s1 = [_V(smB.ap[:, i, :]) for i in range(6)]
            act(mag1.ap, lrdtA, AF.Exp, sak, sbk)
            dve_ts(T1.ap, lidtA, 1.0 / TWO_PI, ALU.mult, sak, sbk)
            sincos(c1, 0.75 + 32.0, T1, u1, u2, sbk)
            sincos(s1, 0.5 + 32.0, T1, u1, u2, sbk)
            a1r, a1i = c1.ap, s1.ap
            dve_tt(a1r, a1r, mag1.ap, ALU.mult, sbk, sbk)
            dve_tt(a1i, a1i, mag1.ap, ALU.mult, sbk, sbk)
            dve_tt(denA, lamrA, lamrA, ALU.mult, kia, sak)
            dve_tt(t0A, lamiA, lamiA, ALU.mult, kia, sak)
            dve_tt(denA, denA, t0A, ALU.add, sak, sak)
            P.op("dve", lambda e: e.reciprocal(out=denA, in_=denA), reads=sak, writes=sak)
            dve_ts(nrA, a1r, -1.0, ALU.add, sbk, sak)
            dve_tt(frA, nrA, lamrA, ALU.mult, sak + kia, sak)
            dve_tt(t0A, a1i, lamiA, ALU.mult, sbk + kia, sak)
            dve_tt(frA, frA, t0A, ALU.add, sak, sak)
            dve_tt(frA, frA, denA, ALU.mult, sak, sak)
            dve_tt(fiA, a1i, lamrA, ALU.mult, sbk + kia, sak)
            dve_tt(t0A, nrA, lamiA, ALU.mult, sak + kia, sak)
            dve_tt(fiA, fiA, t0A, ALU.subtract, sak, sak)
            dve_tt(fiA, fiA, denA, ALU.mult, sak, sak)
            Wr = tmp([8, 256])
            Wi = tmp([8, 256])
            ak = Wr.keys() + Wi.keys()
            dve_tt(t0A, bimA, fiA, ALU.mult, kia + sak, sak)
            dve_tt(Wr.ap[:, 0, :], breA, frA, ALU.mult, kia + sak, ak)
            dve_tt(Wr.ap[:, 0, :], Wr.ap[:, 0, :], t0A, ALU.subtract, ak + sak, ak)
            dve_tt(t0A, breA, fiA, ALU.mult, kia + sak, sak)
            dve_tt(Wi.ap[:, 0, :], bimA, frA, ALU.mult, kia + sak, ak)
            dve_tt(Wi.ap[:, 0, :], Wi.ap[:, 0, :], t0A, ALU.add, ak + sak, ak)
            for k_ in range(7):
                dve_tt(t0A, Wi.ap[:, k_, :], a1i, ALU.mult, ak + sbk, sak)
                dve_tt(Wr.ap[:, k_ + 1, :], Wr.ap[:, k_, :], a1r, ALU.mult, ak + sbk, ak)
                dve_tt(Wr.ap[:, k_ + 1, :], Wr.ap[:, k_ + 1, :], t0A, ALU.subtract, ak + sak, ak)
                dve_tt(t0A, Wr.ap[:, k_, :], a1i, ALU.mult, ak + sbk, sak)
                dve_tt(Wi.ap[:, k_ + 1, :], Wi.ap[:, k_, :], a1r, ALU.mult, ak + sbk, ak)
                dve_tt(Wi.ap[:, k_ + 1, :], Wi.ap[:, k_ + 1, :], t0A, ALU.add, ak + sak, ak)
            g2c = cf("g2col")
            for ri, Wx in enumerate((Wr, Wi)):
                src = Wx.ap.rearrange("p k (c s) -> p c k s", c=4)
                for g2p in range(2):
                    dst = VS.ap[:, :, ri, :, g2p * 64:(g2p + 1) * 64]
                    dve_ts(dst, src, g2c[:, g2p:g2p + 1], ALU.mult, ak + CK, VS.keys())
            P.op("dve", lambda e: e.memset(Xc.ap, 0.0), writes=Xc.keys())
            allkv = [("KV", k) for k in range(65536 // KG)]
            P.op("dve", lambda e: e.memset(LB.ap, 0.0), reads=allkv,
                 writes=LB.keys() + [("KTc", i) for i in range(NT)] + [("VCc", i) for i in range(NT)])

        P.ses = set(os.environ.get("K_SES_SETUP", "").split(","))
        s5_setup()
        P.ses = set(SAME_ENGINE_SYNC)

        WSPEC = {"w_in": w_in, "w_glu": w_glu, "w_out": w_out}
        for l_ in range(2):
            WSPEC["w_up%d" % l_] = w_up[l_]
            WSPEC["w_dn%d" % l_] = w_dn[l_]
            WSPEC["w_gt%d" % l_] = w_gt[l_]
            WSPEC["w_pu%d" % l_] = w_pu[l_]
        SCRT = {}
        WNKT = {}
        for name, W in WSPEC.items():
            K_, N_ = W.shape
            nkt = min(8, K_ // 128)
            WNKT[name] = nkt
            SCRT[name] = nc.dram_tensor("s_" + name, [N_ // 256, K_ // (128 * nkt), 128, nkt * 256], BF16, kind="Internal").ap()
        SCRT["pool"] = nc.dram_tensor("s_pool", [4, 1, 128, 512], BF16, kind="Internal").ap()
        WNKT["pool"] = 2

        CONV = []

        def convert(name, chunks):
            for (c, kc) in chunks:
                CONV.append((name, c, kc))

        convert("w_in", [(c, 0) for c in range(8)])
        convert("w_glu", [(c, 0) for c in range(2)])
        convert("w_out", [(c, 0) for c in range(4)])
        for l_ in range(2):
            if l_ == 1:
                convert("pool", [(g_, 0) for g_ in range(4)])
            for qq in range(4):
                convert("w_up%d" % l_, [(qq * 4 + c, 0) for c in range(4)])
                convert("w_dn%d" % l_, [(c, qq) for c in range(4)])
            for half in range(2):
                convert("w_gt%d" % l_, [(half * 2 + c, 0) for c in range(2)])
                convert("w_pu%d" % l_, [(half * 2 + c, 0) for c in range(2)])
        CONV_IDX = {k_: i_ for i_, k_ in enumerate(CONV)}
        cvp = [0]
        NCV = 8
        LOOKAHEAD = int(os.environ.get("K_LA", "6"))

        def conv_upto(idx):
            while cvp[0] <= min(idx, len(CONV) - 1):
                i_ = cvp[0]
                cvp[0] += 1
                name, c, kc = CONV[i_]
                nkt = WNKT[name]
                if name == "pool":
                    src = w_pool[c].rearrange("(k p) n -> p k n", p=128)
                else:
                    W = WSPEC[name]
                    src = W[kc * nkt * 128:(kc + 1) * nkt * 128, c * 256:(c + 1) * 256].rearrange("(k p) n -> p k n", p=128)
                dst = SCRT[name][c, kc].rearrange("p (k n) -> p k n", k=nkt)
                P.dma("pool", "cv%d" % (i_ % NCV), lambda e, src=src, dst=dst: e.dma_start(out=dst, in_=src),
                      writes=[("scr", name, c, kc), ("cvring", i_ % NCV)])

        wctr = [0]
        NSLOT = len(WS)
        PRE = {}

        def prefetch(chunks):
            for key in chunks:
                if key not in PRE:
                    PRE[key] = load_w(*key)

        def load_w(name, c, kc=0):
            s = wctr[0] % NSLOT
            wctr[0] += 1
            slot = WS[s]
            nkt = WNKT[name]
            conv_upto(CONV_IDX[(name, c, kc)] + LOOKAHEAD)
            src = SCRT[name][c, kc].rearrange("p (k n) -> p k n", k=nkt)
            P.dma("pool", "ws%d" % s, lambda e: e.dma_start(out=slot.ap[:, 0:nkt, :], in_=src),
                  reads=[("scr", name, c, kc)], writes=slot.keys())
            return slot

        pctr = [0]

        def banks(n):
            b = [(pctr[0] + i) % 8 for i in range(n)]
            pctr[0] = (pctr[0] + n) % 8
            return b

        def gemm_gen(name, c0, nchunks, kc, rhs_fn, rkeys_fn, evac):
            nkt = WNKT[name]
            for c in range(nchunks):
                slot = PRE.pop((name, c0 + c, kc), None)
                if slot is None:
                    slot = load_w(name, c0 + c, kc)
                bs = banks(2)
                for m in range(2):
                    b = bs[m]

                    def mm(e, m=m, b=b, slot=slot):
                        last = None
                        for kt in range(nkt):
                            last = e.matmul(out=PB[b][:, :], lhsT=slot.ap[:, kt, m * 128:(m + 1) * 128], rhs=rhs_fn(kt),
                                            start=(kt == 0), stop=(kt == nkt - 1))
                        return last

                    rk = sum([rkeys_fn(kt) for kt in range(nkt)], [])
                    P.op("pe", mm, reads=slot.keys() + rk, writes=PK[b])
                    evac(c * 2 + m, b)
                yield c

        def gemm_fm(*a):
            for _ in gemm_gen(*a):
                pass

        def interleave(g1, g2, lead=1):
            for _ in range(lead):
                next(g1, None)
            d1 = d2 = False
            while not (d1 and d2):
                if not d2:
                    d2 = next(g2, "done") == "done"
                if not d1:
                    d1 = next(g1, "done") == "done"

        STGb = [R["STG"].buf(i * 4096, [1024], F32) for i in range(2)]
        hn = R["HN"].buf(0, [8, 512], BF16)
        mixT = R["HN"].buf(0, [8, 512], BF16)
        sq = R["SQ"].buf(0, [8, 512], BF16)
        qz = R["SQ"].buf(0, [8, 512], BF16)
        gb = R["SQ"].buf(0, [4, 512], BF16)
        sig = R["SQ"].buf(0, [4, 512], F32)
        acc = R["AT"].buf(0, [4, 512], F32)
        y32 = R["AT"].buf(0, [4, 512], F32)
        aTs = [R["AT"].buf(0, [8, 512], BF16), R["S5B"].buf(0, [8, 512], BF16)]
        uTb = R["UB"].buf(0, [4, 512], BF16)
        uM = R["UB"].buf(4096, [4, 512], BF16)
        r32 = [R["UB"].buf(i * 2048, [512], F32) for i in range(4)]
        pT = R["STG"].buf(2048, [2, 512], BF16)
        pstg = [R["STG"].buf(i * 4096, [256], F32) for i in range(2)]
        e32 = R["SCR"].buf(0, [512], F32)
        spb = [R["SCR"].buf(2048 + i * 1024, [512], BF16) for i in range(3)]
        wlb = [R["SCR"].buf(5120 + i * 1024, [512], BF16) for i in range(3)]
        tmpf = [R["SCR"].buf(i * 2048, [512], F32) for i in range(4)]
        SX = R["S5B"].buf(0, [2, 16, 64], F32)
        Xp = R["S5B"].buf(8192, [2, 16, 64], BF16)
        hnp = [R["S5B"].buf(i * 2112, [528], F32) for i in range(2)]
        ptmp = [R["S5B"].buf(4224 + i * 2112, [528], F32) for i in range(2)]
        ypool = [R["S5B"].buf(8448 + i * 2048, [2, 512], BF16) for i in range(2)]
        hTok = R["AT"].buf(0, [4, 1024], BF16)
        ypool8 = R["UB"].buf(0, [8, 512], BF16)
        bandsb = R["S5B"].buf(0, [16, 128], BF16)
        carryb = R["S5B"].buf(4096, [1024], BF16)
        rstdc = R["RS"].buf(0, [4], F32)
        lnc = R["RS"].buf(512, [4], F32)
        rstd = R["RS"].buf(0, [512], F32)
        sqt = R["RS"].buf(2048, [512], F32)

        def dbg_dump(src_ap, keys, ti_sel, ti, slot=None):
            if dbg_d is None or ti != ti_sel:
                return
            dst = dbg_d[:, 0:src_ap.shape[1], :] if slot is None else dbg_d[:, slot, :]
            return P.dma("sp", "dbg", lambda e: e.dma_start(out=dst, in_=src_ap), reads=keys)

        NPOOL = int(os.environ.get("K_NPOOL", "0"))

        def rmsnorm(gain_name, out_fn, out_keys_fn, fp32_out=False):
            b = banks(1)[0]
            for dt in range(8):
                act(sq.ap[:, dt, :], HT.ap[:, dt, :], AF.Square, HT.keys(dt), sq.keys(dt))
                P.op("pe", lambda e, dt=dt: e.matmul(out=PB[b][:, :], lhsT=ones_b, rhs=sq.ap[:, dt, :], start=(dt == 0), stop=(dt == 7)),
                     reads=sq.keys(dt) + CK, writes=PK[b])
            act(sqt.ap, PB[b][:, :], AF.Ln, PK[b], sqt.keys(), scale=1.0 / 1024.0, bias=EPS)
            act(rstd.ap, sqt.ap, AF.Exp, sqt.keys(), rstd.keys(), scale=-0.5)
            for i, dt in enumerate(range(8 - NPOOL, 8)):
                act(tmpf[i].ap, HT.ap[:, dt, :], AF.Copy, HT.keys(dt) + CK, tmpf[i].keys(), scale=vcol(gain_name, dt))
            for i, dt in enumerate(range(8 - NPOOL, 8)):
                P.op("pool", lambda e, i=i, dt=dt: e.tensor_tensor(out=out_fn(dt), in0=tmpf[i].ap, in1=rstd.ap, op=ALU.mult),
                     reads=tmpf[i].keys() + rstd.keys(), writes=out_keys_fn(dt))
            for dt in range(8 - NPOOL):
                dve_stt(out_fn(dt), HT.ap[:, dt, :], vcol(gain_name, dt), rstd.ap, ALU.mult, ALU.mult,
                        HT.keys(dt) + rstd.keys() + CK, out_keys_fn(dt))

        final_sigs = []

        for ti in range(NT):
            t0 = ti * TT
            P.stage = "t%d_load" % ti
            for blk in range(4):
                sb_ = STGb[blk % 2]
                r0 = t0 + blk * 128
                P.dma("sp", "stg%d" % (blk % 2), lambda e, sb_=sb_, r0=r0: e.dma_start(out=sb_.ap, in_=x_d[r0:r0 + 128, :]), writes=sb_.keys())
                for half in range(2):
                    b = banks(1)[0]

                    def tr(e, sb_=sb_, half=half, b=b):
                        last = None
                        for j in range(4):
                            dt = half * 4 + j
                            last = e.transpose(out=PB[b][:, j * 128:(j + 1) * 128], in_=sb_.ap[:, dt * 128:(dt + 1) * 128], identity=ident.ap)
                        return last

                    P.op("pe", tr, reads=sb_.keys() + CK, writes=PK[b])
                    dst = HT.ap[:, half * 4:half * 4 + 4, blk * 128:(blk + 1) * 128]
                    src = PB[b][:, :].rearrange("p (j t) -> p j t", j=4)
                    wk = sum([HT.keys(half * 4 + j) for j in range(4)], [])
                    if half == 0:
                        act(dst, src, AF.Copy, PK[b], wk)
                    else:
                        cp(dst, src, PK[b], wk)

            for layer in range(2):
                if layer == 0:
                    P.stage = "t%d_inproj" % ti
                    prefetch([("w_in", c_, 0) for c_ in range(3)])
                    rmsnorm("ln_mix0", lambda dt: hn.ap[:, dt, :], lambda dt: hn.keys(dt))
                    hn_r = lambda kt: hn.ap[:, kt, :]
                    hn_k = lambda kt: hn.keys(kt)

                    def ev_u(m, b):
                        act(uTb.ap[:, m, :], PB[b][:, :], AF.Copy, PK[b], uTb.keys(m))

                    gemm_fm("w_in", 0, 2, 0, hn_r, hn_k, ev_u)
                    dve_ts(uM.ap[64:128], uTb.ap[64:128], cf("pp3")[64:128, :], ALU.mult, uTb.keys() + CK, uM.keys())

                    def qk_evac(is_q):
                        def ev(m, b):
                            s_ = tmpf[m % 2]
                            sb16 = s_.ap.bitcast(BF16)[:, 0:512]
                            act(sb16, PB[b][:, :], AF.Square, PK[b], s_.keys())
                            b2 = banks(1)[0]
                            P.op("pe", lambda e: e.matmul(out=PB[b2][:, :], lhsT=blockones_b, rhs=sb16, start=True, stop=True),
                                 reads=s_.keys() + CK, writes=PK[b2])
                            r_ = tmpf[2 + m % 2]
                            act(r_.ap, PB[b2][:, :], AF.Ln, PK[b2], r_.keys(), scale=1.0 / 64.0, bias=EPS)
                            act(r_.ap, r_.ap, AF.Exp, r_.keys(), r_.keys(), scale=-0.5)
                            if is_q:
                                dve_stt(qz.ap[:, 2 * m, :], PB[b][:, :], vcol("qgA"), r_.ap, ALU.mult, ALU.mult,
                                        PK[b] + r_.keys() + CK, qz.keys(2 * m))
                                dve_stt(qz.ap[:, 2 * m + 1, :], PB[b][:, :], vcol("qgB"), r_.ap, ALU.mult, ALU.mult,
                                        PK[b] + r_.keys() + CK, qz.keys(2 * m + 1))
                            else:
                                dve_stt(KT.ap[:, m, t0:t0 + 512], PB[b][:, :], vcol("kgA"), r_.ap, ALU.mult, ALU.mult,
                                        PK[b] + r_.keys() + CK, [("KTc", ti)])
                        return ev

                    gemm_fm("w_in", 2, 2, 0, hn_r, hn_k, qk_evac(True))
                    gemm_fm("w_in", 4, 2, 0, hn_r, hn_k, qk_evac(False))
                    for c in range(2):
                        slot = load_w("w_in", 6 + c, 0)
                        bs = banks(2)
                        for blk in range(4):
                            b = bs[blk // 2]
                            cs = slice((blk % 2) * 256, (blk % 2) * 256 + 256)

                            def mm(e, blk=blk, b=b, cs=cs, slot=slot):
                                last = None
                                for kt in range(8):
                                    last = e.matmul(out=PB[b][:, cs], lhsT=hn.ap[:, kt, blk * 128:(blk + 1) * 128], rhs=slot.ap[:, kt, :],
                                                    start=(kt == 0), stop=(kt == 7))
                                return last

                            P.op("pe", mm, reads=slot.keys() + hn.keys(), writes=[("pbh%d" % b, blk % 2)] + PK[b])
                        for blk in range(4):
                            b = bs[blk // 2]
                            cs = slice((blk % 2) * 256, (blk % 2) * 256 + 256)
                            dst = VC.ap[:, ti * 4 + blk, c * 256:(c + 1) * 256]
                            if blk % 2 == 0:
                                act(dst, PB[b][:, cs], AF.Copy, PK[b], [("VCc", ti)])
                            else:
                                cp(dst, PB[b][:, cs], PK[b], [("VCc", ti)])

                    P.stage = "t%d_att" % ti
                    sbanks = [6, 7, 6, 7]
                    t1_, t2_, t3_, t4_ = [RT.ap[:, i, :] for i in range(4)]
                    tk = RT.keys()
                    Ar = Acf.ap[:, 0, :]
                    Ai = Acf.ap[:, 1, :]
                    sxk = SX.keys()
                    for grp in range(2):
                        def mmS(e, grp=grp):
                            last = None
                            for pp in (2 * grp, 2 * grp + 1):
                                bnk = PB[6 + pp % 2]
                                for ct in range(4):
                                    for ri in range(2):
                                        col = (ct * 2 + ri) * 64
                                        for j in range(8):
                                            k = 7 - j
                                            if pp < 3:
                                                rows = slice(pp * 32, pp * 32 + 32)
                                                rhs = uTb.ap[rows, ct, j:512:8]
                                            else:
                                                rows = slice(64, 128)
                                                rhs = uM.ap[rows, ct, j:512:8]
                                            last = e.matmul(out=bnk[:, col:col + 64], lhsT=VS.ap[rows, ct, ri, k, :], rhs=rhs,
                                                            start=(j == 0), stop=(j == 7))
                            return last

                        P.op("pe", mmS, reads=uTb.keys() + uM.keys() + VS.keys(), writes=PK[6] + PK[7])
                        for pp in (2 * grp, 2 * grp + 1):
                            bk_ = 6 + pp % 2
                            src = PB[bk_][:, :].rearrange("p (c r n) -> p r c n", c=4, r=2)
                            for ri in range(2):
                                dst = SX.ap[:, ri, pp:16:4, :]
                                cp(dst, src[:, ri], PK[bk_], sxk)
                    cp(Xp.ap[:, :, :, 0], Xc.ap, Xc.keys(), Xp.keys())

                    def rec_step(c):
                        pr = Xc.ap[:, 0, :] if c == 0 else SX.ap[:, 0, :, c - 1]
                        pi = Xc.ap[:, 1, :] if c == 0 else SX.ap[:, 1, :, c - 1]
                        rk_ = (Xc.keys() if c == 0 else []) + sxk + Acf.keys() + tk

                        def rec(e, pr=pr, pi=pi, c=c):
                            e.tensor_tensor(out=t1_, in0=Ar, in1=pr, op=ALU.mult)
                            e.tensor_tensor(out=t2_, in0=Ai, in1=pi, op=ALU.mult)
                            e.tensor_tensor(out=t3_, in0=Ar, in1=pi, op=ALU.mult)
                            e.tensor_tensor(out=t4_, in0=Ai, in1=pr, op=ALU.mult)
                            e.tensor_tensor(out=t1_, in0=t1_, in1=t2_, op=ALU.subtract)
                            e.tensor_tensor(out=t3_, in0=t3_, in1=t4_, op=ALU.add)
                            e.tensor_tensor(out=SX.ap[:, 0, :, c], in0=SX.ap[:, 0, :, c], in1=t1_, op=ALU.add)
                            return e.tensor_tensor(out=SX.ap[:, 1, :, c], in0=SX.ap[:, 1, :, c], in1=t3_, op=ALU.add)

                        P.op(REC_ENG, rec, reads=rk_, writes=sxk + tk)

                    steps = []
                    for h in range(8):
                        for kb in range(4 * ti + 3, -1, -1):
                            steps.append((h, kb))
                    nst = len(steps)
                    rec_per = (64 + nst - 1) // nst
                    rec_done = [0]

                    def geom(i):
                        h, kb = steps[i]
                        b_ = kb - 4 * ti
                        diag = b_ >= 0
                        c0 = 128 * b_ if diag else 0
                        sb0 = b_ if diag else 0
                        return h, kb, diag, c0, sb0, 512 - c0

                    def stageA(i):
                        h, kb, diag, c0, sb0, N = geom(i)
                        zb = i % 2
                        sp_ = spb[i % 3]
                        kt_l = KT.ap[:, h // 2, kb * 128:(kb + 1) * 128]
                        q_r = qz.ap[:, h, c0:512]
                        P.op("pe", lambda e: e.matmul(out=PB[zb][:, 0:N], lhsT=kt_l, rhs=q_r, start=True, stop=True),
                             reads=[("KTc", kb // 4)] + qz.keys(h), writes=PK[zb])
                        act(e32.ap[:, 0:N], PB[zb][:, 0:N], AF.Exp, PK[zb], e32.keys())
                        act(sp_.ap[:, 0:N], e32.ap[:, 0:N], AF.Ln, e32.keys(), sp_.keys(), bias=1.0)
                        if diag:
                            dve_tt(sp_.ap[:, 0:128], sp_.ap[:, 0:128], masklt_b, ALU.mult, sp_.keys() + CK, sp_.keys())

                    def stageB(i):
                        h, kb, diag, c0, sb0, N = geom(i)
                        ab = 2 + i % 2
                        sp_ = spb[i % 3]
                        wl_ = wlb[i % 3]
                        kt_l = KT.ap[:, h // 2, kb * 128:(kb + 1) * 128]
                        q_r = qz.ap[:, h, c0:512]

                        def mm2(e):
                            e.matmul(out=PB[ab][:, 0:N], lhsT=kt_l, rhs=q_r, start=True, stop=False)
                            return e.matmul(out=PB[ab][:, 0:N], lhsT=negtri_b, rhs=sp_.ap[:, 0:N], start=False, stop=True)

                        P.op("pe", mm2, reads=[("KTc", kb // 4)] + qz.keys(h) + sp_.keys() + CK, writes=PK[ab])
                        act(wl_.ap[:, 0:N], PB[ab][:, 0:N], AF.Exp, PK[ab], wl_.keys())
                        if diag:
                            dve_tt(wl_.ap[:, 0:128], wl_.ap[:, 0:128], masklt_b, ALU.mult, wl_.keys() + CK, wl_.keys())

                    def stageC(i):
                        h, kb, diag, c0, sb0, N = geom(i)
                        vb = 4 + i % 2
                        sp_ = spb[i % 3]
                        wl_ = wlb[i % 3]
                        par = h % 2
                        Cst = CE.ap[:, par, 0, :]
                        Est = CE.ap[:, par, 1, :]
                        cek = [("CE", par)]
                        nsb = 4 - sb0
                        v_r = VC.ap[:, kb, h * 64:(h + 1) * 64]

                        def mm3(e):
                            last = None
                            for ii in range(nsb):
                                sbq = sb0 + ii
                                cs = slice(ii * 128, (ii + 1) * 128)
                                e.matmul(out=PB[vb][:, sbq * 80:sbq * 80 + 64], lhsT=wl_.ap[:, cs], rhs=v_r, start=True, stop=True)
                                last = e.matmul(out=PB[vb][:, sbq * 80 + 64:sbq * 80 + 65], lhsT=sp_.ap[:, cs], rhs=ones_b[:, 0:1], start=True, stop=True)
                            return last

                        P.op("pe", mm3, reads=wl_.keys() + sp_.keys() + [("VCc", kb // 4)] + CK, writes=PK[vb])
                        cont0 = sb0 + 1 if diag else 0
                        if cont0 < 4:
                            if E_ON_POOL:
                                P.op("pool", lambda e: e.tensor_tensor(out=Est[:, cont0:4], in0=EINV.ap[:, cont0:4], in1=Cst[:, cont0:4], op=ALU.pow),
                                     reads=cek + EINV.keys(), writes=cek)
                            else:
                                act(Est[:, cont0:4], Cst[:, cont0:4], AF.Exp, cek, cek, scale=-1.0)
                        for sbq in range(sb0, 4):
                            pv = PB[vb][:, sbq * 80:sbq * 80 + 64]
                            dst = acc.ap[:, sbq, h * 64:(h + 1) * 64]
                            if diag and sbq == sb0:
                                cp(dst, pv, PK[vb], acc.keys(sbq))
                            else:
                                dve_stt(dst, pv, Est[:, sbq:sbq + 1], dst, ALU.mult, ALU.add, PK[vb] + cek + acc.keys(sbq), acc.keys(sbq))
                        pcs = PB[vb][:, 0:320].rearrange("p (s c) -> p s c", c=80)[:, :, 64]
                        if diag:
                            cp(Cst[:, sb0:sb0 + 1], pcs[:, sb0:sb0 + 1], PK[vb], cek)
                            if sb0 + 1 < 4:
                                dve_tt(Cst[:, sb0 + 1:4], Cst[:, sb0 + 1:4], pcs[:, sb0 + 1:4], ALU.add, PK[vb] + cek, cek)
                        else:
                            dve_tt(Cst, Cst, pcs, ALU.add, PK[vb] + cek, cek)
                        for _ in range(rec_per):
                            if rec_done[0] < 64:
                                rec_step(rec_done[0])
                                rec_done[0] += 1

                    for i in range(nst + 2):
                        if i < nst:
                            stageA(i)
                        if 0 <= i - 1 < nst:
                            stageB(i - 1)
                        if 0 <= i - 2 < nst:
                            stageC(i - 2)
                    while rec_done[0] < 64:
                        rec_step(rec_done[0])
                        rec_done[0] += 1
                    cp(Xp.ap[:, :, :, 1:64], SX.ap[:, :, :, 0:63], sxk, Xp.keys())
                    cp(Xc.ap, SX.ap[:, :, :, 63], sxk, Xc.keys())
                    dbg_dump(acc.ap, acc.keys(), 0, ti) if dbg == "att" else None
                    for ft in range(4):
                        b = banks(1)[0]

                        def tr(e, ft=ft, b=b):
                            last = None
                            for sbq in range(4):
                                last = e.transpose(out=PB[b][:, sbq * 128:(sbq + 1) * 128], in_=acc.ap[:, sbq, ft * 128:(ft + 1) * 128], identity=ident.ap)
                            return last

                        P.op("pe", tr, reads=acc.keys() + CK, writes=PK[b])
                        act(mixT.ap[:, 4 + ft, :], PB[b][:, :], AF.Copy, PK[b], mixT.keys(4 + ft))

                    P.stage = "t%d_s5out" % ti
                    ybanks = banks(4)
                    for ct in range(4):
                        bnk = PB[ybanks[ct]]

                        def mmY(e, ct=ct, bnk=bnk):
                            last = None
                            for ip in range(8):
                                for j in range(ip + 1):
                                    e.matmul(out=bnk[:, ip:512:8], lhsT=KD.ap[:, ct, ip - j, :], rhs=uTb.ap[:, ct, j:512:8],
                                             start=(ip == 0 and j == 0), stop=False, skip_group_check=True)
                            for ip in range(8):
                                for pp in range(4):
                                    q = 4 * ct + pp
                                    for ri in range(2):
                                        last = e.matmul(out=bnk[pp * 32:(pp + 1) * 32, ip:512:8], lhsT=YW.ap[:, q, ip, ri, :], rhs=Xp.ap[:, ri, q, :],
                                                        start=False, stop=(ri == 1 and ip == 7 and pp == 3), tile_position=(0, pp * 32), skip_group_check=True)
                            return last

                        P.op("pe", mmY, reads=uTb.keys(ct) + KD.keys() + YW.keys() + Xp.keys(), writes=PK[ybanks[ct]])
                        dve_stt(y32.ap[:, ct, :], uTb.ap[:, ct, :], vcol("s5d", ct), bnk[:, :], ALU.mult, ALU.add,
                                PK[ybanks[ct]] + uTb.keys(ct) + CK, y32.keys(ct))
                        act(y32.ap[:, ct, :], y32.ap[:, ct, :], AF.Gelu, y32.keys(ct), y32.keys(ct))
                        cp(gb.ap[:, ct, :], y32.ap[:, ct, :], y32.keys(ct), gb.keys(ct))
                    dbg_dump(y32.ap, y32.keys(), 0, ti) if dbg == "s5" else None

                    def ev_glu(m, b):
                        t_ = tmpf[m % 2]
                        act(t_.ap, PB[b][:, :], AF.Sigmoid, PK[b], t_.keys())
                        dve_tt(mixT.ap[:, m, :], y32.ap[:, m, :], t_.ap, ALU.mult, y32.keys(m) + t_.keys(), mixT.keys(m))

                    gemm_fm("w_glu", 0, 2, 0, lambda kt: gb.ap[:, kt, :], lambda kt: gb.keys(kt), ev_glu)

                    def ev_res(m, b):
                        dve_tt(HT.ap[:, m, :], HT.ap[:, m, :], PB[b][:, :], ALU.add, PK[b] + HT.keys(m), HT.keys(m))

                    gemm_fm("w_out", 0, 4, 0, lambda kt: mixT.ap[:, kt, :], lambda kt: mixT.keys(kt), ev_res)
                    if dbg == "mix0":
                        dbg_dump(HT.ap, HT.keys(), NT - 1, ti)
                else:
                    P.stage = "t%d_pool" % ti
                    prefetch([("pool", g_, 0) for g_ in range(3)])
                    P.dma("pool", "bands", lambda e: e.dma_start(out=bandsb.ap.rearrange("p a b -> p (a b)"), in_=bands_d[:, :]), writes=bandsb.keys())
                    if ti > 0:
                        P.dma("sp", "carry", lambda e: e.dma_start(out=carryb.ap, in_=carry_d[:, :]), reads=[("carryd", 0)], writes=carryb.keys())
                    bss = banks(1)[0]
                    for dt in range(8):
                        act(sq.ap[:, dt, :], HT.ap[:, dt, :], AF.Square, HT.keys(dt), sq.keys(dt))

                    def mmss(e, bss=bss):
                        last = None
                        for blk in range(4):
                            for dt in range(8):
                                last = e.matmul(out=PB[bss][:, blk:blk + 1], lhsT=sq.ap[:, dt, blk * 128:(blk + 1) * 128], rhs=ones_b[:, 0:1],
                                                start=(dt == 0), stop=(dt == 7))
                        return last

                    P.op("pe", mmss, reads=sq.keys() + CK, writes=PK[bss])
                    act(lnc.ap, PB[bss][:, 0:4], AF.Ln, PK[bss], lnc.keys(), scale=1.0 / 1024.0, bias=EPS)
                    act(rstdc.ap, lnc.ap, AF.Exp, lnc.keys(), rstdc.keys(), scale=-0.5)
                    for blk in range(4):
                        bs = banks(2)
                        for half in range(2):
                            b = bs[half]

                            def trh(e, half=half, b=b, blk=blk):
                                last = None
                                for j in range(4):
                                    dt = half * 4 + j
                                    last = e.transpose(out=PB[b][:, j * 128:(j + 1) * 128], in_=HT.ap[:, dt, blk * 128:(blk + 1) * 128], identity=ident.ap)
                                return last

                            P.op("pe", trh, reads=HT.keys() + CK, writes=PK[b])
                            dst = hTok.ap[:, blk, half * 512:(half + 1) * 512]
                            if half == 0:
                                act(dst, PB[b][:, :], AF.Copy, PK[b] + rstdc.keys(), hTok.keys(blk), scale=rstdc.ap[:, blk:blk + 1])
                            else:
                                dve_ts(dst, PB[b][:, :], rstdc.ap[:, blk:blk + 1], ALU.mult, PK[b] + rstdc.keys(), hTok.keys(blk))
                    for dt in range(8):
                        g = dt // 2
                        b = banks(1)[0]

                        def mmb(e, dt=dt, g=g, b=b, ti=ti):
                            last = None
                            cs = slice(dt * 128, (dt + 1) * 128)
                            for blk in range(4):
                                out = PB[b][:, blk * 128:(blk + 1) * 128]
                                first_seq = (ti == 0 and blk == 0)
                                has_spill = not first_seq
                                if first_seq:
                                    e.matmul(out=out, lhsT=hTok.ap[:, 0, cs], rhs=bandsb.ap[:, 4 * g + 2, :], start=True, stop=False)
                                    last = e.matmul(out=out, lhsT=hTok.ap[:, 0, cs], rhs=bandsb.ap[:, 4 * g + 3, :], start=False, stop=True)
                                else:
                                    e.matmul(out=out, lhsT=hTok.ap[:, blk, cs], rhs=bandsb.ap[:, 4 * g + 0, :], start=True, stop=False)
                                    prev = carryb.ap[:, cs] if blk == 0 else hTok.ap[:, blk - 1, cs]
                                    last = e.matmul(out=out, lhsT=prev, rhs=bandsb.ap[:, 4 * g + 1, :], start=False, stop=True)
                            return last

                        P.op("pe", mmb, reads=hTok.keys() + bandsb.keys() + carryb.keys(), writes=PK[b])
                        if dt % 2 == 0:
                            act(ypool8.ap[:, dt, :], PB[b][:, :], AF.Copy, PK[b] + CK, ypool8.keys(dt), scale=vcol("ln_mix1", dt))
                        else:
                            dve_ts(ypool8.ap[:, dt, :], PB[b][:, :], vcol("ln_mix1", dt), ALU.mult, PK[b] + CK, ypool8.keys(dt))
                    P.dma("sp", "carryw", lambda e: e.dma_start(out=carry_d[:, :], in_=hTok.ap[:, 3, :]), reads=hTok.keys(3), writes=[("carryd", 0)])
                    for g in range(4):
                        slot = PRE.pop(("pool", g, 0), None) or load_w("pool", g, 0)
                        bs = banks(2)
                        for m in range(2):
                            b = bs[m]

                            def mm(e, m=m, b=b, slot=slot, g=g):
                                e.matmul(out=PB[b][:, :], lhsT=slot.ap[:, 0, m * 128:(m + 1) * 128], rhs=ypool8.ap[:, 2 * g, :], start=True, stop=False)
                                return e.matmul(out=PB[b][:, :], lhsT=slot.ap[:, 1, m * 128:(m + 1) * 128], rhs=ypool8.ap[:, 2 * g + 1, :], start=False, stop=True)

                            P.op("pe", mm, reads=slot.keys() + ypool8.keys(2 * g) + ypool8.keys(2 * g + 1), writes=PK[b])
                            dt = 2 * g + m
                            dve_stt(HT.ap[:, dt, :], PB[b][:, :], vcol("pscale", dt), HT.ap[:, dt, :], ALU.mult, ALU.add,
                                    PK[b] + HT.keys(dt) + CK, HT.keys(dt))

                if dbg == "pool" and layer == 1:
                    dbg_dump(HT.ap, HT.keys(), 0, ti)
                P.stage = "t%d_mlp%d" % (ti, layer)
                prefetch([("w_up%d" % layer, c_, 0) for c_ in range(3)])
                rmsnorm("ln_mlp%d" % layer, lambda dt: hn.ap[:, dt, :], lambda dt: hn.keys(dt))
                def up_gen(qq):
                    aT = aTs[qq % 2]

                    def ev_up(m, b, aT=aT):
                        r_ = r32[m % 4]
                        act(r_.ap, PB[b][:, :], AF.Relu, PK[b], r_.keys())
                        dve_tt(aT.ap[:, m, :], r_.ap, r_.ap, ALU.mult, r_.keys(), aT.keys(m))

                    return gemm_gen("w_up%d" % layer, qq * 4, 4, 0, lambda kt: hn.ap[:, kt, :], lambda kt: hn.keys(kt), ev_up)

                def dn_gen(qq):
                    aT = aTs[qq % 2]

                    def ev_res(m, b):
                        dve_tt(HT.ap[:, m, :], HT.ap[:, m, :], PB[b][:, :], ALU.add, PK[b] + HT.keys(m), HT.keys(m))

                    return gemm_gen("w_dn%d" % layer, 0, 4, qq, lambda kt, aT=aT: aT.ap[:, kt, :], lambda kt, aT=aT: aT.keys(kt), ev_res)

                if MLP_PIPE:
                    for _ in up_gen(0):
                        pass
                    for qq in range(4):
                        if qq < 3:
                            interleave(up_gen(qq + 1), dn_gen(qq), lead=1)
                        else:
                            for _ in dn_gen(qq):
                                pass
                else:
                    for qq in range(4):
                        for _ in up_gen(qq):
                            pass
                        for _ in dn_gen(qq):
                            pass

                if dbg == "mlp0" and layer == 0:
                    dbg_dump(HT.ap, HT.keys(), NT - 1, ti)
                P.stage = "t%d_ple%d" % (ti, layer)
                prefetch([("w_gt%d" % layer, c_, 0) for c_ in range(3)])
                rmsnorm("ln_ple%d" % layer, lambda dt: hn.ap[:, dt, :], lambda dt: hn.keys(dt))
                for blk in range(4):
                    sb_ = pstg[blk % 2]
                    r0 = t0 + blk * 128
                    P.dma("sp", "stg%d" % (blk % 2), lambda e, sb_=sb_, r0=r0, layer=layer: e.dma_start(out=sb_.ap, in_=p_d[layer, r0:r0 + 128, :]), writes=sb_.keys())
                    b = banks(1)[0]

                    def trp(e, sb_=sb_, b=b):
                        e.transpose(out=PB[b][:, 0:128], in_=sb_.ap[:, 0:128], identity=ident.ap)
                        return e.transpose(out=PB[b][:, 128:256], in_=sb_.ap[:, 128:256], identity=ident.ap)

                    P.op("pe", trp, reads=sb_.keys() + CK, writes=PK[b])
                    act(pT.ap[:, :, blk * 128:(blk + 1) * 128], PB[b][:, 0:256].rearrange("p (k t) -> p k t", k=2), AF.Copy, PK[b], pT.keys())
                sigs = [R["AT"].buf(0, [4, 512], F32), R["UB"].buf(0, [4, 512], F32)]

                def gate_gen(half):
                    sg = sigs[half]

                    def ev_gate(m, b, sg=sg):
                        act(sg.ap[:, m, :], PB[b][:, :], AF.Sigmoid, PK[b], sg.keys(m))

                    return gemm_gen("w_gt%d" % layer, half * 2, 2, 0, lambda kt: hn.ap[:, kt, :], lambda kt: hn.keys(kt), ev_gate)

                def pu_gen(half):
                    sg = sigs[half]

                    def ev_pu(m, b, half=half, sg=sg):
                        t_ = tmpf[2 + m % 2]
                        dve_tt(t_.ap, PB[b][:, :], sg.ap[:, m, :], ALU.mult, PK[b] + sg.keys(m), t_.keys())
                        dt = half * 4 + m
                        dve_tt(HT.ap[:, dt, :], HT.ap[:, dt, :], t_.ap, ALU.add, t_.keys() + HT.keys(dt), HT.keys(dt))

                    return gemm_gen("w_pu%d" % layer, half * 2, 2, 0, lambda kt: pT.ap[:, kt, :], lambda kt: pT.keys(), ev_pu)

                for _ in gate_gen(0):
                    pass
                interleave(gate_gen(1), pu_gen(0), lead=1)
                for _ in pu_gen(1):
                    pass
                if dbg == "l0" and layer == 0:
                    dbg_dump(HT.ap, HT.keys(), NT - 1, ti)

            P.stage = "t%d_out" % ti
            for blk in range(4):
                sb_ = STGb[blk % 2]
                r0 = t0 + blk * 128
                for half in range(2):
                    b = banks(1)[0]

                    def tro(e, half=half, b=b, blk=blk):
                        last = None
                        for j in range(4):
                            dt = half * 4 + j
                            last = e.transpose(out=PB[b][:, j * 128:(j + 1) * 128], in_=HT.ap[:, dt, blk * 128:(blk + 1) * 128], identity=ident.ap)
                        return last

                    P.op("pe", tro, reads=HT.keys() + CK, writes=PK[b])
                    if half == 0:
                        act(sb_.ap[:, 0:512], PB[b][:, :], AF.Copy, PK[b], sb_.keys())
                    else:
                        cp(sb_.ap[:, 512:1024], PB[b][:, :], PK[b], sb_.keys())
                sg = P.dma("sp", "stg%d" % (blk % 2), lambda e, sb_=sb_, r0=r0: e.dma_start(out=y_d[r0:r0 + 128, :], in_=sb_.ap), reads=sb_.keys())
                final_sigs.append(sg)

        fs = {}
        for s, v in final_sigs:
            fs[s] = max(fs.get(s, 0), v)
        if dbg_d is not None and "d_dbg" in P.cnt:
            fs["d_dbg"] = P.cnt["d_dbg"]
        P.finish("sp", list(fs.items()))
        P.emit()
    return nc


def _prep_shared(inp):
    f = lambda a: np.ascontiguousarray(np.asarray(a, dtype=np.float32))
    sh = {}
    sh["w_in"] = f(inp["w_in_even"][0])
    sh["w_out"] = f(inp["w_out_even"][0])
    sh["w_glu"] = f(inp["s5_w_glu"][0])
    sh["w_up"] = f(inp["w_mlp_up"])
    sh["w_dn"] = f(inp["w_mlp_down"])
    sh["w_gt"] = f(inp["w_ple_gate"])
    sh["w_pu"] = f(inp["w_ple_up"])
    sh["w_pool"] = f(inp["pool_w"][0])
    col8 = lambda v: np.asarray(v, np.float32).reshape(-1, 128).T
    qg = np.asarray(inp["sb_q_gain"][0], np.float32)
    kg = np.asarray(inp["sb_k_gain"][0], np.float32)
    qg128 = np.concatenate([qg, qg])
    kg128 = np.concatenate([kg, kg])
    half = (np.arange(128) < 64)
    scale = np.float32(64 ** -0.5)
    vec = np.concatenate([
        col8(inp["ln_mix_even"][0]), col8(inp["ln_mlp"][0]), col8(inp["ln_ple"][0]),
        col8(inp["ln_mix_odd"][0]), col8(inp["ln_mlp"][1]), col8(inp["ln_ple"][1]),
        col8(inp["pool_scale"][0]), col8(inp["s5_d"][0]),
        np.where(half, qg128, 0)[:, None], np.where(~half, qg128, 0)[:, None], kg128[:, None]], axis=1)
    sh["vecs"] = f(vec)
    lamr = np.asarray(inp["s5_lambda_re"][0], np.float32)
    lami = np.asarray(inp["s5_lambda_im"][0], np.float32)
    logdt = np.asarray(inp["s5_log_dt"][0], np.float32)
    toB = lambda a: a.reshape(16, 2, 64).transpose(1, 2, 0).reshape(128, 16)
    logdtB = toB(np.tile(logdt[:, None], (1, 64)))
    toB3 = lambda a: a.reshape(16, 2, 64, 16).transpose(1, 2, 0, 3).reshape(128, 256)
    bre = np.asarray(inp["s5_b_re"][0], np.float32)
    bim = np.asarray(inp["s5_b_im"][0], np.float32)
    cre = np.asarray(inp["s5_c_re"][0], np.float32).transpose(0, 2, 1)
    cim = np.asarray(inp["s5_c_im"][0], np.float32).transpose(0, 2, 1)
    sh["s5B"] = f(np.concatenate([toB(lamr), toB(lami), logdtB, toB3(bre), toB3(bim), toB3(cre), toB3(cim)], axis=1))
    toA = lambda a: np.tile(a.reshape(4, 8, 1, 64), (1, 1, 16, 1)).transpose(1, 2, 0, 3).reshape(128, 256)
    toA3 = lambda a: a.reshape(4, 8, 64, 16).transpose(1, 3, 0, 2).reshape(128, 256)
    sh["s5A"] = f(np.concatenate([toA(lamr), toA(lami), toA(np.tile(logdt[:, None], (1, 64))), toA3(bre), toA3(bim)], axis=1))
    sh.update(_consts())
    sh["_scale"] = scale
    return sh


_NC_CACHE = {}


def kernel(**inputs):
    x = np.asarray(inputs["x"], np.float32)
    p = np.asarray(inputs["p"], np.float32)
    B, L, Dm = x.shape
    NT = L // TT
    sh = _prep_shared(inputs)
    sh.pop("_scale")
    if NT not in _NC_CACHE:
        _NC_CACHE[NT] = build(NT)
    nc = _NC_CACHE[NT]
    in_maps = []
    for b in range(B):
        m = dict(sh)
        m["x"] = np.ascontiguousarray(x[b])
        m["p"] = np.ascontiguousarray(p[:, b])
        in_maps.append(m)
    res = run_bass_kernel_spmd(nc, in_maps, core_ids=list(range(B)))
    out = np.stack([res.results[b]["y"] for b in range(B)], axis=0)
    return out.astype(np.float32)
```

```python
import math
from contextlib import ExitStack

import numpy as np
import concourse.bass as bass
import concourse.mybir as mybir
from concourse.bass_utils import run_bass_kernel_spmd

F32 = mybir.dt.float32
BF16 = mybir.dt.bfloat16
I32 = mybir.dt.int32
AF = mybir.ActivationFunctionType
ALU = mybir.AluOpType

TT = 512
EPS = 1e-6
KLIST = list(range(-7, 9))
TWO_PI = 2.0 * math.pi
import os
NJUNK = int(os.environ.get("K_JUNK", "24"))
MLP_PIPE = bool(int(os.environ.get("K_MLPPIPE", "1")))
REC_ENG = os.environ.get("K_REC", "dve")
E_ON_POOL = bool(int(os.environ.get("K_EPOOL", "0")))
SAME_ENGINE_SYNC = set(os.environ.get("K_SES", "").split(","))


class Prog:
    def __init__(self, nc, stack):
        self.nc = nc
        self.stack = stack
        self.ops = {e: [] for e in ("pe", "act", "dve", "pool", "sp")}
        self.sems = {}
        self.cnt = {}
        self.keys = {}
        self.waited = {e: {} for e in self.ops}
        self.final = []
        self.stage = "setup"
        self.scopes = bool(int(os.environ.get("K_SCOPES", "0")))
        self.ses = set(SAME_ENGINE_SYNC)
        self.near = {"dve": int(os.environ.get("K_NEAR_DVE", "2")), "act": int(os.environ.get("K_NEAR_ACT", "1")), "pool": 2}

    def sem(self, name):
        if name not in self.sems:
            self.sems[name] = self.stack.enter_context(self.nc.semaphore(name))
            self.cnt[name] = 0
        return self.sems[name]

    def _resolve(self, eng, reads, writes, mysig):
        waits = {}

        def add(sig):
            if sig is None:
                return
            s, v = sig
            if s == eng:
                if eng == "pe":
                    return
                if eng not in self.ses and (self.cnt[eng] - v) > self.near.get(eng, 0):
                    return
            if waits.get(s, 0) < v:
                waits[s] = v

        for k in reads:
            st = self.keys.setdefault(k, [None, []])
            add(st[0])
        for k in writes:
            st = self.keys.setdefault(k, [None, []])
            add(st[0])
            for r in st[1]:
                add(r)
        out = []
        for s, v in waits.items():
            if self.waited[eng].get(s, 0) >= v:
                continue
            self.waited[eng][s] = v
            out.append((s, v))
        for k in reads:
            self.keys[k][1].append(mysig)
        for k in writes:
            self.keys[k][0] = mysig
            self.keys[k][1] = []
        return out

    def op(self, eng, fn, reads=(), writes=()):
        self.sem(eng)
        self.cnt[eng] += 1
        mysig = (eng, self.cnt[eng])
        waits = self._resolve(eng, reads, writes, mysig)
        self.ops[eng].append((waits, fn, (eng, 1), self.stage))

    def dma(self, q, semkey, fn, reads=(), writes=()):
        name = "d_" + semkey
        self.sem(name)
        self.cnt[name] += 16
        mysig = (name, self.cnt[name])
        waits = self._resolve(q, reads, writes, mysig)
        self.ops[q].append((waits, fn, (name, 16), "dma"))
        return mysig

    def finish(self, eng, sigs):
        self.final.append((eng, sigs))

    def emit(self):
        nc = self.nc
        engmap = {"pe": "tensor", "act": "scalar", "dve": "vector", "pool": "gpsimd", "sp": "sync"}
        block = self.stack.enter_context(nc.Block())
        for e, attr in engmap.items():
            ops = self.ops[e]
            finals = [s for (fe, s) in self.final if fe == e]
            if not ops and not finals:
                continue

            def body(engine, ops=ops, finals=finals):
                for waits, fn, (sname, inc), stage in ops:
                    if self.scopes:
                        with nc.named_scope(stage):
                            for s, v in waits:
                                engine.wait_ge(self.sems[s], v)
                            inst = fn(engine)
                            inst.then_inc(self.sems[sname], inc)
                    else:
                        for s, v in waits:
                            engine.wait_ge(self.sems[s], v)
                        inst = fn(engine)
                        inst.then_inc(self.sems[sname], inc)
                for sigs in finals:
                    for s, v in sigs:
                        engine.wait_ge(self.sems[s], v)

            getattr(block, attr)(body)


KG = 512


class Region:
    def __init__(self, arena, name, woff, nbytes):
        self.arena, self.name, self.woff, self.nbytes = arena, name, woff, nbytes

    def buf(self, boff, shape, dt):
        return Buf(self, boff, shape, dt)


class Buf:
    def __init__(self, region, boff, shape, dt):
        esz = 4 if dt in (F32, I32) else 2
        n = int(np.prod(shape))
        assert boff % 4 == 0 and boff + n * esz <= region.nbytes, (region.name, boff, shape, region.nbytes)
        w0 = region.woff + boff // 4
        nw = (n * esz + 3) // 4
        ap = region.arena[:, w0:w0 + nw]
        if dt != F32:
            ap = ap.bitcast(dt)
        if len(shape) > 1:
            names = " ".join("a%d" % i for i in range(len(shape)))
            kw = {"a%d" % i: shape[i] for i in range(1, len(shape))}
            ap = ap.rearrange("p (%s) -> p %s" % (names, names), **kw)
        self.ap, self.region, self.boff, self.shape, self.esz = ap, region, boff, tuple(shape), esz
        self.nbytes = n * esz
        self.blk = (n // shape[0]) * esz

    def keys(self, lo=None, hi=None):
        if lo is None:
            b0, b1 = self.boff, self.boff + self.nbytes
        else:
            hi = lo + 1 if hi is None else hi
            b0, b1 = self.boff + lo * self.blk, self.boff + hi * self.blk
        return [(self.region.name, k) for k in range(b0 // KG, (b1 + KG - 1) // KG)]


def _consts():
    c = {}
    c["ident"] = np.eye(128, dtype=np.float32)
    jj = np.arange(128)[:, None]
    tt = np.arange(128)[None, :]
    ones = np.ones((128, 128), np.float32)
    blockones = ((jj // 64) == (tt // 64)).astype(np.float32)
    negtri = -(jj >= tt).astype(np.float32)
    masklt = (jj < tt).astype(np.float32)
    c["cstb"] = np.concatenate([ones, blockones, negtri, masklt], axis=1)
    part = np.arange(128)
    g8 = part // 16
    bd = (g8[:, None, None] == np.arange(8)[None, :, None]) * np.ones((1, 1, 16))
    g2 = (part // 16) % 2
    g2col = (g2[:, None] == np.arange(2)[None, :]).astype(np.float32)
    pp3 = (part >= 96).astype(np.float32)[:, None]
    halfA = (part < 64).astype(np.float32)[:, None]
    halfB = (part >= 64).astype(np.float32)[:, None]
    kv = np.tile(np.array(KLIST, np.float32)[None, :], (128, 1))
    kf = kv / TWO_PI
    kvA = np.tile(np.arange(8, dtype=np.float32)[None, :], (128, 1))
    kfA = kvA / TWO_PI
    cnt = np.zeros((128, 4, 16), np.float32)
    for g, w in enumerate((2, 4, 8, 16)):
        cnt[:, g, :] = 1.0 / np.minimum(np.arange(16) + 1, w)
    bands = np.zeros((128, 16, 128), np.float64)
    tq = np.arange(128)[:, None]
    tt_ = np.arange(128)[None, :]
    for g, w in enumerate((2, 4, 8, 16)):
        main = ((tq <= tt_) & (tq > tt_ - w)) / float(w) - (tq == tt_)
        spill = ((tq - 128) > (tt_ - w)) / float(w)
        cntv = np.minimum(tt_ + 1, w)
        b0 = ((tq <= tt_) & (tq > tt_ - w)) / cntv - (tq == tt_)
        import ml_dtypes
        hi = b0.astype(np.float32).astype(ml_dtypes.bfloat16).astype(np.float64)
        lo = b0 - hi
        bands[:, 4 * g + 0] = main
        bands[:, 4 * g + 1] = spill
        bands[:, 4 * g + 2] = hi
        bands[:, 4 * g + 3] = lo
    c["bands"] = bands.reshape(128, 2048).astype(np.float32)
    c["cstf"] = np.concatenate([bd.reshape(128, 128).astype(np.float32), g2col, pp3, halfA, halfB,
                                kv, kf, kvA, kfA, cnt.reshape(128, 64)], axis=1).astype(np.float32)
    return c


CF = {}
_o = 0
for _n, _w in (("bd", 128), ("g2col", 2), ("pp3", 1), ("halfA", 1), ("halfB", 1), ("kv", 16), ("kf", 16),
               ("kvA", 8), ("kfA", 8), ("cnt", 64)):
    CF[_n] = (_o, _w)
    _o += _w
NCF = _o

VEC = {}
_o = 0
for _n, _w in (("ln_mix0", 8), ("ln_mlp0", 8), ("ln_ple0", 8), ("ln_mix1", 8), ("ln_mlp1", 8), ("ln_ple1", 8),
               ("pscale", 8), ("s5d", 4), ("qgA", 1), ("qgB", 1), ("kgA", 1)):
    VEC[_n] = (_o, _w)
    _o += _w
NVEC = _o


def build(NT, dbg=None):
    L = NT * TT
    nc = bass.Bass("TRN2", target_bir_lowering=False)
    D = lambda n, s: nc.dram_tensor(n, s, F32, kind="ExternalInput").ap()
    x_d = D("x", [L, 1024])
    p_d = D("p", [2, L, 256])
    w_in = D("w_in", [1024, 2048])
    w_out = D("w_out", [1024, 1024])
    w_glu = D("w_glu", [512, 512])
    w_up = D("w_up", [2, 1024, 4096])
    w_dn = D("w_dn", [2, 4096, 1024])
    w_gt = D("w_gt", [2, 1024, 1024])
    w_pu = D("w_pu", [2, 256, 1024])
    w_pool = D("w_pool", [4, 256, 256])
    vecs_d = D("vecs", [128, NVEC])
    s5b_d = D("s5B", [128, 48 + 4 * 256])
    s5a_d = D("s5A", [128, 5 * 256])
    ident_d = D("ident", [128, 128])
    cstb_d = D("cstb", [128, 512])
    cstf_d = D("cstf", [128, NCF])
    bands_d = D("bands", [128, 2048])
    carry_d = nc.dram_tensor("poolcarry", [128, 1024], BF16, kind="Internal").ap()
    y_d = nc.dram_tensor("y", [L, 1024], F32, kind="ExternalOutput").ap()
    dbg_d = None
    if dbg:
        dbg_d = nc.dram_tensor("dbg", [128, 8, 512], F32, kind="ExternalOutput").ap()

    with ExitStack() as st:
        P = Prog(nc, st)
        sizes = [("KV", 65536), ("VS", 16384), ("YW", 16384), ("KD", 8192), ("WS", 3 * 4096), ("HT", 16384),
                 ("CST", 6144), ("STG", 8192), ("HN", 8192), ("SQ", 8192), ("AT", 8192), ("UB", 8192),
                 ("SCR", 8192), ("S5B", 12800), ("RS", 4096)]
        total_w = sum(s for _, s in sizes) // 4
        arena = st.enter_context(nc.sbuf_tensor("arena", [128, total_w], F32))
        R = {}
        wo = 0
        for n, s in sizes:
            R[n] = Region(arena, n, wo, s)
            wo += s // 4
        PB = [st.enter_context(nc.psum_tensor("pb%d" % i, [128, 512], F32)) for i in range(8)]
        PK = [[("pb%d" % i, 0)] for i in range(8)]

        KT = R["KV"].buf(0, [4, 4096], BF16)
        VC = R["KV"].buf(32768, [32, 512], BF16)
        VS = R["VS"].buf(0, [4, 2, 8, 128], BF16)
        YW = R["YW"].buf(0, [16, 8, 2, 32], BF16)
        KD = R["KD"].buf(0, [4, 8, 128], BF16)
        WS = [R["WS"].buf(i * 4096, [8, 256], BF16) for i in range(3)]
        HT = R["HT"].buf(0, [8, 512], F32)
        ident = R["CST"].buf(0, [128], F32)
        vecs = R["CST"].buf(512, [NVEC], F32)
        cstb = R["CST"].buf(1024, [4, 128], BF16)
        cstf = R["CST"].buf(2048, [NCF], F32)
        assert NCF * 4 <= 1024
        Acf = R["CST"].buf(3072, [2, 16], F32)
        Xc = R["CST"].buf(3584, [2, 16], F32)
        LB = R["CST"].buf(4096, [8, 16], F32)
        CE = R["CST"].buf(4608, [2, 2, 4], F32)
        RT = R["CST"].buf(5120, [4, 16], F32)
        EINV = R["CST"].buf(5632, [4], F32)

        def vcol(name, i=0):
            o, w = VEC[name]
            return vecs.ap[:, o + i:o + i + 1]

        def cf(name):
            o, w = CF[name]
            return cstf.ap[:, o:o + w]

        ones_b = cstb.ap[:, 0, :]
        blockones_b = cstb.ap[:, 1, :]
        negtri_b = cstb.ap[:, 2, :]
        masklt_b = cstb.ap[:, 3, :]
        CK = cstb.keys() + cstf.keys() + vecs.keys() + ident.keys()

        def dve_tt(out, in0, in1, op, r, w, eng="dve"):
            P.op(eng, lambda e: e.tensor_tensor(out=out, in0=in0, in1=in1, op=op), reads=r, writes=w)

        def dve_ts(out, in0, s1, op0, r, w, s2=None, op1=None, eng="dve"):
            if op1 is None:
                P.op(eng, lambda e: e.tensor_scalar(out=out, in0=in0, scalar1=s1, scalar2=None, op0=op0), reads=r, writes=w)
            else:
                P.op(eng, lambda e: e.tensor_scalar(out=out, in0=in0, scalar1=s1, scalar2=s2, op0=op0, op1=op1), reads=r, writes=w)

        def dve_stt(out, in0, scalar, in1, op0, op1, r, w):
            P.op("dve", lambda e: e.scalar_tensor_tensor(out=out, in0=in0, scalar=scalar, in1=in1, op0=op0, op1=op1), reads=r, writes=w)

        def act(out, in_, func, r, w, scale=1.0, bias=0.0):
            if func == AF.Copy:
                P.op("act", lambda e: e.activation(out=out, in_=in_, func=func, scale=scale), reads=r, writes=w)
            else:
                P.op("act", lambda e: e.activation(out=out, in_=in_, func=func, scale=scale, bias=bias), reads=r, writes=w)

        def cp(out, in_, r, w, eng="dve"):
            P.op(eng, lambda e: e.tensor_copy(out=out, in_=in_), reads=r, writes=w)

        P.dma("sp", "c0", lambda e: e.dma_start(out=ident.ap, in_=ident_d[:, :]), writes=ident.keys())
        P.dma("sp", "c1", lambda e: e.dma_start(out=vecs.ap, in_=vecs_d[:, :]), writes=vecs.keys())
        P.dma("sp", "c2", lambda e: e.dma_start(out=cstf.ap, in_=cstf_d[:, :]), writes=cstf.keys())
        P.dma("pool", "c3", lambda e: e.dma_start(out=cstb.ap.rearrange("p a b -> p (a b)"), in_=cstb_d[:, :]), writes=cstb.keys())

        P.op("dve", lambda e: e.memset(EINV.ap, math.exp(-1.0)), writes=EINV.keys())
        o_q = VEC["qgA"][0]
        dve_ts(vecs.ap[:, o_q:o_q + 2], vecs.ap[:, o_q:o_q + 2], 0.125, ALU.mult, vecs.keys(), vecs.keys())

        KVr = R["KV"]

        def s5_setup():
            o = [0]

            def tmp(shape, dt=F32):
                b = KVr.buf(o[0], shape, dt)
                o[0] += ((b.nbytes + 511) // 512) * 512
                return b

            inB = tmp([48 + 1024])
            P.dma("sp", "s5in", lambda e: e.dma_start(out=inB.ap, in_=s5b_d[:, :]), writes=inB.keys())
            lamr = inB.ap[:, 0:16]
            lami = inB.ap[:, 16:32]
            logdt = inB.ap[:, 32:48]
            bre = inB.ap[:, 48:304].rearrange("p (q h) -> p q h", q=16)
            bim = inB.ap[:, 304:560].rearrange("p (q h) -> p q h", q=16)
            cre = inB.ap[:, 560:816].rearrange("p (q h) -> p q h", q=16)
            cim = inB.ap[:, 816:1072].rearrange("p (q h) -> p q h", q=16)
            kin = inB.keys()
            sm = tmp([8, 16])
            smk = sm.keys()
            dt_, lrdt, lidt, den, nr, fr, fi, t0 = [sm.ap[:, i, :] for i in range(8)]
            act(dt_, logdt, AF.Exp, kin, smk)
            dve_tt(lrdt, lamr, dt_, ALU.mult, kin + smk, smk)
            dve_tt(lidt, lami, dt_, ALU.mult, kin + smk, smk)
            NK = len(KLIST)
            big = [tmp([NK, 16]) for _ in range(6)]
            mag, Tt, t1, t2, cosv, sinv = big
            allk = sum([b.keys() for b in big], [])
            kv = cf("kv")
            kf = cf("kf")
            bc_q = lambda a: a.unsqueeze(1).to_broadcast([128, NK, 16])
            bc_k = lambda a: a.unsqueeze(2).to_broadcast([128, NK, 16])
            dve_tt(mag.ap, bc_q(lrdt), bc_k(kv), ALU.mult, smk + CK, allk)
            act(mag.ap, mag.ap, AF.Exp, allk, allk)
            dve_tt(Tt.ap, bc_q(lidt), bc_k(kf), ALU.mult, smk + CK, allk)

            def sincos(dst, shift, Tsrc, a1, a2, keys):
                dve_ts(a1.ap, Tsrc.ap, shift, ALU.add, keys, keys)
                cp(a2.ap.bitcast(I32), a1.ap, keys, keys)
                cp(a2.ap, a2.ap.bitcast(I32), keys, keys)
                dve_tt(a1.ap, a1.ap, a2.ap, ALU.subtract, keys, keys)
                dve_stt(a1.ap, a1.ap, 0.0, a1.ap, ALU.is_lt, ALU.add, keys, keys)
                act(dst.ap, a1.ap, AF.Sin, keys, keys, scale=TWO_PI * (1 - 1e-6), bias=-math.pi * (1 - 1e-6))

            sincos(cosv, 0.75 + 32.0, Tt, t1, t2, allk)
            sincos(sinv, 0.5 + 32.0, Tt, t1, t2, allk)
            dve_tt(cosv.ap, cosv.ap, mag.ap, ALU.mult, allk, allk)
            dve_tt(sinv.ap, sinv.ap, mag.ap, ALU.mult, allk, allk)
            Er = lambda k: cosv.ap[:, k + 7, :]
            Ei = lambda k: sinv.ap[:, k + 7, :]
            cp(Acf.ap[:, 0, :], Er(8), allk, Acf.keys())
            cp(Acf.ap[:, 1, :], Ei(8), allk, Acf.keys())
            dve_tt(den, lamr, lamr, ALU.mult, kin, smk)
            dve_tt(t0, lami, lami, ALU.mult, kin, smk)
            dve_tt(den, den, t0, ALU.add, smk, smk)
            P.op("dve", lambda e: e.reciprocal(out=den, in_=den), reads=smk, writes=smk)
            dve_ts(nr, Er(1), -1.0, ALU.add, allk, smk)
            dve_tt(fr, nr, lamr, ALU.mult, smk + kin, smk)
            dve_tt(t0, Ei(1), lami, ALU.mult, allk + kin, smk)
            dve_tt(fr, fr, t0, ALU.add, smk, smk)
            dve_tt(fr, fr, den, ALU.mult, smk, smk)
            dve_tt(fi, Ei(1), lamr, ALU.mult, allk + kin, smk)
            dve_tt(t0, nr, lami, ALU.mult, smk + kin, smk)
            dve_tt(fi, fi, t0, ALU.subtract, smk, smk)
            dve_tt(fi, fi, den, ALU.mult, smk, smk)
            bb = tmp([4, 16, 16])
            bbk = bb.keys()
            Bbr, Bbi, tA, tB = [bb.ap[:, i] for i in range(4)]
            bch = lambda a: a.unsqueeze(2).to_broadcast([128, 16, 16])
            dve_tt(Bbr, bre, bch(fr), ALU.mult, kin + smk, bbk)
            dve_tt(tA, bim, bch(fi), ALU.mult, kin + smk, bbk)
            dve_tt(Bbr, Bbr, tA, ALU.subtract, bbk, bbk)
            dve_tt(Bbi, bim, bch(fr), ALU.mult, kin + smk, bbk)
            dve_tt(tA, bre, bch(fi), ALU.mult, kin + smk, bbk)
            dve_tt(Bbi, Bbi, tA, ALU.add, bbk, bbk)
            Rr = tmp([9, 16, 16])
            Ri = tmp([9, 16, 16])
            Rt = tmp([9, 16, 16])
            rk = Rr.keys() + Ri.keys() + Rt.keys()
            bcC = lambda a: a.unsqueeze(1).to_broadcast([128, 9, 16, 16])
            bcE = lambda a: a.unsqueeze(3).to_broadcast([128, 9, 16, 16])
            Er9 = cosv.ap[:, 7:16, :]
            Ei9 = sinv.ap[:, 7:16, :]
            dve_tt(Rr.ap, bcC(cre), bcE(Er9), ALU.mult, kin + allk, rk)
            dve_tt(Rt.ap, bcC(cim), bcE(Ei9), ALU.mult, kin + allk, rk)
            dve_tt(Rr.ap, Rr.ap, Rt.ap, ALU.subtract, rk, rk)
            dve_tt(Ri.ap, bcC(cre), bcE(Ei9), ALU.mult, kin + allk, rk)
            dve_tt(Rt.ap, bcC(cim), bcE(Er9), ALU.mult, kin + allk, rk)
            dve_tt(Ri.ap, Ri.ap, Rt.ap, ALU.add, rk, rk)
            Bz = tmp([2, 16, 128])
            bzk = Bz.keys()
            P.op("dve", lambda e: e.memset(Bz.ap, 0.0), writes=bzk)
            for half in range(2):
                ps_ = slice(half * 64, half * 64 + 64)
                for pp in range(4):
                    cs_ = slice(pp * 32 + half * 16, pp * 32 + half * 16 + 16)
                    cp(Bz.ap[ps_, 0, pp:16:4, cs_], Bbr[ps_, pp:16:4, :], bbk, bzk)
                    dve_ts(Bz.ap[ps_, 1, pp:16:4, cs_], Bbi[ps_, pp:16:4, :], -1.0, ALU.mult, bbk, bzk)
            bd = cf("bd").rearrange("p (a b) -> p a b", a=8)
            for ct in range(4):
                bank = PB[ct]

                def mm(e, ct=ct, bank=bank):
                    last = None
                    for pp in range(4):
                        q = 4 * ct + pp
                        e.matmul(out=bank[:, 0:128], lhsT=Bz.ap[:, 0, q, :], rhs=Rr.ap[:, 0:8, q, :],
                                 start=(pp == 0), stop=False)
                        last = e.matmul(out=bank[:, 0:128], lhsT=Bz.ap[:, 1, q, :], rhs=Ri.ap[:, 0:8, q, :],
                                        start=False, stop=(pp == 3))
                    return last

                P.op("pe", mm, reads=bzk + rk, writes=PK[ct])
                src = bank[:, 0:128].rearrange("p (t h) -> p t h", t=8).unsqueeze(2).to_broadcast([128, 8, 8, 16])
                msk = bd.unsqueeze(1).to_broadcast([128, 8, 8, 16])
                dst = KD.ap[:, ct].rearrange("p t (g h) -> p t g h", g=8)
                dve_tt(dst, src, msk, ALU.mult, PK[ct] + CK, KD.keys())
            P.op("dve", lambda e: e.memset(YW.ap, 0.0), writes=YW.keys())
            for half in range(2):
                ps_ = slice(half * 64, half * 64 + 64)
                cs_ = slice(half * 16, half * 16 + 16)
                srcr = Rr.ap[ps_, 1:9].rearrange("p k q h -> p q k h")
                srci = Ri.ap[ps_, 1:9].rearrange("p k q h -> p q k h")
                cp(YW.ap[ps_, :, :, 0, cs_], srcr, rk, YW.keys())
                dve_ts(YW.ap[ps_, :, :, 1, cs_], srci, -1.0, ALU.mult, rk, YW.keys())
            o[0] = 0
            inA = tmp([5, 256])
            P.dma("sp", "s5in", lambda e: e.dma_start(out=inA.ap.rearrange("p a b -> p (a b)"), in_=s5a_d[:, :]), writes=inA.keys())
            kia = inA.keys()
            lamrA, lamiA, logdtA, breA, bimA = [inA.ap[:, i, :] for i in range(5)]
            smA = tmp([8, 256])
            sak = smA.keys()
            dtA, lrdtA, lidtA, denA, nrA, frA, fiA, t0A = [smA.ap[:, i, :] for i in range(8)]
            act(dtA, logdtA, AF.Exp, kia, sak)
            dve_tt(lrdtA, lamrA, dtA, ALU.mult, kia + sak, sak)
            dve_tt(lidtA, lamiA, dtA, ALU.mult, kia + sak, sak)
            class _V:
                def __init__(self, ap):
                    self.ap = ap

            smB = tmp([6, 256])
            sbk = smB.keys()
            mag1, T1, u1, u2, c1, s1 = [_V(smB.ap[:, i, :]) for i in range(6)]
            act(mag1.ap, lrdtA, AF.Exp, sak, sbk)
            dve_ts(T1.ap, lidtA, 1.0 / TWO_PI, ALU.mult, sak, sbk)
            sincos(c1, 0.75 + 32.0, T1, u1, u2, sbk)
            sincos(s1, 0.5 + 32.0, T1, u1, u2, sbk)
            a1r, a1i = c1.ap, s1.ap
            dve_tt(a1r, a1r, mag1.ap, ALU.mult, sbk, sbk)
            dve_tt(a1i, a1i, mag1.ap, ALU.mult, sbk, sbk)
            dve_tt(denA, lamrA, lamrA, ALU.mult, kia, sak)
            dve_tt(t0A, lamiA, lamiA, ALU.mult, kia, sak)
            dve_tt(denA, denA, t0A, ALU.add, sak, sak)
            P.op("dve", lambda e: e.reciprocal(out=denA, in_=denA), reads=sak, writes=sak)
            dve_ts(nrA, a1r, -1.0, ALU.add, sbk, sak)
            dve_tt(frA, nrA, lamrA, ALU.mult, sak + kia, sak)
            dve_tt(t0A, a1i, lamiA, ALU.mult, sbk + kia, sak)
            dve_tt(frA, frA, t0A, ALU.add, sak, sak)
            dve_tt(frA, frA, denA, ALU.mult, sak, sak)
            dve_tt(fiA, a1i, lamrA, ALU.mult, sbk + kia, sak)
            dve_tt(t0A, nrA, lamiA, ALU.mult, sak + kia, sak)
            dve_tt(fiA, fiA, t0A, ALU.subtract, sak, sak)
            dve_tt(fiA, fiA, denA, ALU.mult, sak, sak)
            Wr = tmp([8, 256])
            Wi = tmp([8, 256])
            ak = Wr.keys() + Wi.keys()
            dve_tt(t0A, bimA, fiA, ALU.mult, kia + sak, sak)
            dve_tt(Wr.ap[:, 0, :], breA, frA, ALU.mult, kia + sak, ak)
            dve_tt(Wr.ap[:, 0, :], Wr.ap[:, 0, :], t0A, ALU.subtract, ak + sak, ak)
            dve_tt(t0A, breA, fiA, ALU.mult, kia + sak, sak)
            dve_tt(Wi.ap[:, 0, :], bimA, frA, ALU.mult, kia + sak, ak)
            dve_tt(Wi.ap[:, 0, :], Wi.ap[:, 0, :], t0A, ALU.add, ak + sak, ak)
            for k_ in range(7):
                dve_tt(t0A, Wi.ap[:, k_, :], a1i, ALU.mult, ak + sbk, sak)
                dve_tt(Wr.ap[:, k_ + 1, :], Wr.ap[:, k_, :], a1r, ALU.mult, ak + sbk, ak)
                dve_tt(Wr.ap[:, k_ + 1, :], Wr.ap[:, k_ + 1, :], t0A, ALU.subtract, ak + sak, ak)
                dve_tt(t0A, Wr.ap[:, k_, :], a1i, ALU.mult, ak + sbk, sak)
                dve_tt(Wi.ap[:, k_ + 1, :], Wi.ap[:, k_, :], a1r, ALU.mult, ak + sbk, ak)
                dve_tt(Wi.ap[:, k_ + 1, :], Wi.ap[:, k_ + 1, :], t0A, ALU.add, ak + sak, ak)
            g2c = cf("g2col")
            for ri, Wx in enumerate((Wr, Wi)):
                src = Wx.ap.rearrange("p k (c s) -> p c k s", c=4)
                for g2p in range(2):
                    dst = VS.ap[:, :, ri, :, g2p * 64:(g2p + 1) * 64]
                    dve_ts(dst, src, g2c[:, g2p:g2p + 1], ALU.mult, ak + CK, VS.keys())
            P.op("dve", lambda e: e.memset(Xc.ap, 0.0), writes=Xc.keys())
            allkv = [("KV", k) for k in range(65536 // KG)]
            P.op("dve", lambda e: e.memset(LB.ap, 0.0), reads=allkv,
                 writes=LB.keys() + [("KTc", i) for i in range(NT)] + [("VCc", i) for i in range(NT)])

        P.ses = set(os.environ.get("K_SES_SETUP", "").split(","))
        s5_setup()
        P.ses = set(SAME_ENGINE_SYNC)

        WSPEC = {"w_in": w_in, "w_glu": w_glu, "w_out": w_out}
        for l_ in range(2):
            WSPEC["w_up%d" % l_] = w_up[l_]
            WSPEC["w_dn%d" % l_] = w_dn[l_]
            WSPEC["w_gt%d" % l_] = w_gt[l_]
            WSPEC["w_pu%d" % l_] = w_pu[l_]
        SCRT = {}
        WNKT = {}
        for name, W in WSPEC.items():
            K_, N_ = W.shape
            nkt = min(8, K_ // 128)
            WNKT[name] = nkt
            SCRT[name] = nc.dram_tensor("s_" + name, [N_ // 256, K_ // (128 * nkt), 128, nkt * 256], BF16, kind="Internal").ap()
        SCRT["pool"] = nc.dram_tensor("s_pool", [4, 1, 128, 512], BF16, kind="Internal").ap()
        WNKT["pool"] = 2

        CONV = []

        def convert(name, chunks):
            for (c, kc) in chunks:
                CONV.append((name, c, kc))

        convert("w_in", [(c, 0) for c in range(8)])
        convert("w_glu", [(c, 0) for c in range(2)])
        convert("w_out", [(c, 0) for c in range(4)])
        for l_ in range(2):
            if l_ == 1:
                convert("pool", [(g_, 0) for g_ in range(4)])
            for qq in range(4):
                convert("w_up%d" % l_, [(qq * 4 + c, 0) for c in range(4)])
                convert("w_dn%d" % l_, [(c, qq) for c in range(4)])
            for half in range(2):
                convert("w_gt%d" % l_, [(half * 2 + c, 0) for c in range(2)])
                convert("w_pu%d" % l_, [(half * 2 + c, 0) for c in range(2)])
        CONV_IDX = {k_: i_ for i_, k_ in enumerate(CONV)}
        cvp = [0]
        NCV = 8
        LOOKAHEAD = int(os.environ.get("K_LA", "6"))

        def conv_upto(idx):
            while cvp[0] <= min(idx, len(CONV) - 1):
                i_ = cvp[0]
                cvp[0] += 1
                name, c, kc = CONV[i_]
                nkt = WNKT[name]
                if name == "pool":
                    src = w_pool[c].rearrange("(k p) n -> p k n", p=128)
                else:
                    W = WSPEC[name]
                    src = W[kc * nkt * 128:(kc + 1) * nkt * 128, c * 256:(c + 1) * 256].rearrange("(k p) n -> p k n", p=128)
                dst = SCRT[name][c, kc].rearrange("p (k n) -> p k n", k=nkt)
                P.dma("pool", "cv%d" % (i_ % NCV), lambda e, src=src, dst=dst: e.dma_start(out=dst, in_=src),
                      writes=[("scr", name, c, kc), ("cvring", i_ % NCV)])

        wctr = [0]
        NSLOT = len(WS)
        PRE = {}

        def prefetch(chunks):
            for key in chunks:
                if key not in PRE:
                    PRE[key] = load_w(*key)

        def load_w(name, c, kc=0):
            s = wctr[0] % NSLOT
            wctr[0] += 1
            slot = WS[s]
            nkt = WNKT[name]
            conv_upto(CONV_IDX[(name, c, kc)] + LOOKAHEAD)
            src = SCRT[name][c, kc].rearrange("p (k n) -> p k n", k=nkt)
            P.dma("pool", "ws%d" % s, lambda e: e.dma_start(out=slot.ap[:, 0:nkt, :], in_=src),
                  reads=[("scr", name, c, kc)], writes=slot.keys())
            return slot

        pctr = [0]

        def banks(n):
            b = [(pctr[0] + i) % 8 for i in range(n)]
            pctr[0] = (pctr[0] + n) % 8
            return b

        def gemm_gen(name, c0, nchunks, kc, rhs_fn, rkeys_fn, evac):
            nkt = WNKT[name]
            for c in range(nchunks):
                slot = PRE.pop((name, c0 + c, kc), None)
                if slot is None:
                    slot = load_w(name, c0 + c, kc)
                bs = banks(2)
                for m in range(2):
                    b = bs[m]

                    def mm(e, m=m, b=b, slot=slot):
                        last = None
                        for kt in range(nkt):
                            last = e.matmul(out=PB[b][:, :], lhsT=slot.ap[:, kt, m * 128:(m + 1) * 128], rhs=rhs_fn(kt),
                                            start=(kt == 0), stop=(kt == nkt - 1))
                        return last

                    rk = sum([rkeys_fn(kt) for kt in range(nkt)], [])
                    P.op("pe", mm, reads=slot.keys() + rk, writes=PK[b])
                    evac(c * 2 + m, b)
                yield c

        def gemm_fm(*a):
            for _ in gemm_gen(*a):
                pass

        def interleave(g1, g2, lead=1):
            for _ in range(lead):
                next(g1, None)
            d1 = d2 = False
            while not (d1 and d2):
                if not d2:
                    d2 = next(g2, "done") == "done"
                if not d1:
                    d1 = next(g1, "done") == "done"

        STGb = [R["STG"].buf(i * 4096, [1024], F32) for i in range(2)]
        hn = R["HN"].buf(0, [8, 512], BF16)
        mixT = R["HN"].buf(0, [8, 512], BF16)
        sq = R["SQ"].buf(0, [8, 512], BF16)
        qz = R["SQ"].buf(0, [8, 512], BF16)
        gb = R["SQ"].buf(0, [4, 512], BF16)
        sig = R["SQ"].buf(0, [4, 512], F32)
        acc = R["AT"].buf(0, [4, 512], F32)
        y32 = R["AT"].buf(0, [4, 512], F32)
        aTs = [R["AT"].buf(0, [8, 512], BF16), R["S5B"].buf(0, [8, 512], BF16)]
        uTb = R["UB"].buf(0, [4, 512], BF16)
        uM = R["UB"].buf(4096, [4, 512], BF16)
        r32 = [R["UB"].buf(i * 2048, [512], F32) for i in range(4)]
        pT = R["STG"].buf(2048, [2, 512], BF16)
        pstg = [R["STG"].buf(i * 4096, [256], F32) for i in range(2)]
        e32 = R["SCR"].buf(0, [512], F32)
        spb = [R["SCR"].buf(2048 + i * 1024, [512], BF16) for i in range(3)]
        wlb = [R["SCR"].buf(5120 + i * 1024, [512], BF16) for i in range(3)]
        tmpf = [R["SCR"].buf(i * 2048, [512], F32) for i in range(4)]
        SX = R["S5B"].buf(0, [2, 16, 64], F32)
        Xp = R["S5B"].buf(8192, [2, 16, 64], BF16)
        hnp = [R["S5B"].buf(i * 2112, [528], F32) for i in range(2)]
        ptmp = [R["S5B"].buf(4224 + i * 2112, [528], F32) for i in range(2)]
        ypool = [R["S5B"].buf(8448 + i * 2048, [2, 512], BF16) for i in range(2)]
        hTok = R["AT"].buf(0, [4, 1024], BF16)
        ypool8 = R["UB"].buf(0, [8, 512], BF16)
        bandsb = R["S5B"].buf(0, [16, 128], BF16)
        carryb = R["S5B"].buf(4096, [1024], BF16)
        rstdc = R["RS"].buf(0, [4], F32)
        lnc = R["RS"].buf(512, [4], F32)
        rstd = R["RS"].buf(0, [512], F32)
        sqt = R["RS"].buf(2048, [512], F32)

        def dbg_dump(src_ap, keys, ti_sel, ti, slot=None):
            if dbg_d is None or ti != ti_sel:
                return
            dst = dbg_d[:, 0:src_ap.shape[1], :] if slot is None else dbg_d[:, slot, :]
            return P.dma("sp", "dbg", lambda e: e.dma_start(out=dst, in_=src_ap), reads=keys)

        NPOOL = int(os.environ.get("K_NPOOL", "0"))

        def rmsnorm(gain_name, out_fn, out_keys_fn, fp32_out=False):
            b = banks(1)[0]
            for dt in range(8):
                act(sq.ap[:, dt, :], HT.ap[:, dt, :], AF.Square, HT.keys(dt), sq.keys(dt))
                P.op("pe", lambda e, dt=dt: e.matmul(out=PB[b][:, :], lhsT=ones_b, rhs=sq.ap[:, dt, :], start=(dt == 0), stop=(dt == 7)),
                     reads=sq.keys(dt) + CK, writes=PK[b])
            if NJUNK > 0:
                bj = pctr[0] % 8
                cflat = cstb.ap.rearrange("p a b -> p (a b)")

                def junk(e, bj=bj):
                    last = None
                    for _ in range(NJUNK):
                        last = e.matmul(out=PB[bj][:, :], lhsT=ones_b, rhs=cflat, start=True, stop=True)
                    return last

                P.op("pe", junk, reads=CK, writes=PK[bj])
            act(sqt.ap, PB[b][:, :], AF.Ln, PK[b], sqt.keys(), scale=1.0 / 1024.0, bias=EPS)
            act(rstd.ap, sqt.ap, AF.Exp, sqt.keys(), rstd.keys(), scale=-0.5)
            for i, dt in enumerate(range(8 - NPOOL, 8)):
                act(tmpf[i].ap, HT.ap[:, dt, :], AF.Copy, HT.keys(dt) + CK, tmpf[i].keys(), scale=vcol(gain_name, dt))
            for i, dt in enumerate(range(8 - NPOOL, 8)):
                P.op("pool", lambda e, i=i, dt=dt: e.tensor_tensor(out=out_fn(dt), in0=tmpf[i].ap, in1=rstd.ap, op=ALU.mult),
                     reads=tmpf[i].keys() + rstd.keys(), writes=out_keys_fn(dt))
            for dt in range(8 - NPOOL):
                dve_stt(out_fn(dt), HT.ap[:, dt, :], vcol(gain_name, dt), rstd.ap, ALU.mult, ALU.mult,
                        HT.keys(dt) + rstd.keys() + CK, out_keys_fn(dt))

        final_sigs = []

        for ti in range(NT):
            t0 = ti * TT
            P.stage = "t%d_load" % ti
            for blk in range(4):
                sb_ = STGb[blk % 2]
                r0 = t0 + blk * 128
                P.dma("sp", "stg%d" % (blk % 2), lambda e, sb_=sb_, r0=r0: e.dma_start(out=sb_.ap, in_=x_d[r0:r0 + 128, :]), writes=sb_.keys())
                for half in range(2):
                    b = banks(1)[0]

                    def tr(e, sb_=sb_, half=half, b=b):
                        last = None
                        for j in range(4):
                            dt = half * 4 + j
                            last = e.transpose(out=PB[b][:, j * 128:(j + 1) * 128], in_=sb_.ap[:, dt * 128:(dt + 1) * 128], identity=ident.ap)
                        return last

                    P.op("pe", tr, reads=sb_.keys() + CK, writes=PK[b])
                    dst = HT.ap[:, half * 4:half * 4 + 4, blk * 128:(blk + 1) * 128]
                    src = PB[b][:, :].rearrange("p (j t) -> p j t", j=4)
                    wk = sum([HT.keys(half * 4 + j) for j in range(4)], [])
                    if half == 0:
                        act(dst, src, AF.Copy, PK[b], wk)
                    else:
                        cp(dst, src, PK[b], wk)

            for layer in range(2):
                if layer == 0:
                    P.stage = "t%d_inproj" % ti
                    prefetch([("w_in", c_, 0) for c_ in range(3)])
                    rmsnorm("ln_mix0", lambda dt: hn.ap[:, dt, :], lambda dt: hn.keys(dt))
                    hn_r = lambda kt: hn.ap[:, kt, :]
                    hn_k = lambda kt: hn.keys(kt)

                    def ev_u(m, b):
                        act(uTb.ap[:, m, :], PB[b][:, :], AF.Copy, PK[b], uTb.keys(m))

                    gemm_fm("w_in", 0, 2, 0, hn_r, hn_k, ev_u)
                    dve_ts(uM.ap[64:128], uTb.ap[64:128], cf("pp3")[64:128, :], ALU.mult, uTb.keys() + CK, uM.keys())

                    def qk_evac(is_q):
                        def ev(m, b):
                            s_ = tmpf[m % 2]
                            sb16 = s_.ap.bitcast(BF16)[:, 0:512]
                            act(sb16, PB[b][:, :], AF.Square, PK[b], s_.keys())
                            b2 = banks(1)[0]
                            P.op("pe", lambda e: e.matmul(out=PB[b2][:, :], lhsT=blockones_b, rhs=sb16, start=True, stop=True),
                                 reads=s_.keys() + CK, writes=PK[b2])
                            r_ = tmpf[2 + m % 2]
                            act(r_.ap, PB[b2][:, :], AF.Ln, PK[b2], r_.keys(), scale=1.0 / 64.0, bias=EPS)
                            act(r_.ap, r_.ap, AF.Exp, r_.keys(), r_.keys(), scale=-0.5)
                            if is_q:
                                dve_stt(qz.ap[:, 2 * m, :], PB[b][:, :], vcol("qgA"), r_.ap, ALU.mult, ALU.mult,
                                        PK[b] + r_.keys() + CK, qz.keys(2 * m))
                                dve_stt(qz.ap[:, 2 * m + 1, :], PB[b][:, :], vcol("qgB"), r_.ap, ALU.mult, ALU.mult,
                                        PK[b] + r_.keys() + CK, qz.keys(2 * m + 1))
                            else:
                                dve_stt(KT.ap[:, m, t0:t0 + 512], PB[b][:, :], vcol("kgA"), r_.ap, ALU.mult, ALU.mult,
                                        PK[b] + r_.keys() + CK, [("KTc", ti)])
                        return ev

                    gemm_fm("w_in", 2, 2, 0, hn_r, hn_k, qk_evac(True))
                    gemm_fm("w_in", 4, 2, 0, hn_r, hn_k, qk_evac(False))
                    for c in range(2):
                        slot = load_w("w_in", 6 + c, 0)
                        bs = banks(2)
                        for blk in range(4):
                            b = bs[blk // 2]
                            cs = slice((blk % 2) * 256, (blk % 2) * 256 + 256)

                            def mm(e, blk=blk, b=b, cs=cs, slot=slot):
                                last = None
                                for kt in range(8):
                                    last = e.matmul(out=PB[b][:, cs], lhsT=hn.ap[:, kt, blk * 128:(blk + 1) * 128], rhs=slot.ap[:, kt, :],
                                                    start=(kt == 0), stop=(kt == 7))
                                return last

                            P.op("pe", mm, reads=slot.keys() + hn.keys(), writes=[("pbh%d" % b, blk % 2)] + PK[b])
                        for blk in range(4):
                            b = bs[blk // 2]
                            cs = slice((blk % 2) * 256, (blk % 2) * 256 + 256)
                            dst = VC.ap[:, ti * 4 + blk, c * 256:(c + 1) * 256]
                            if blk % 2 == 0:
                                act(dst, PB[b][:, cs], AF.Copy, PK[b], [("VCc", ti)])
                            else:
                                cp(dst, PB[b][:, cs], PK[b], [("VCc", ti)])

                    P.stage = "t%d_att" % ti
                    sbanks = [6, 7, 6, 7]
                    t1_, t2_, t3_, t4_ = [RT.ap[:, i, :] for i in range(4)]
                    tk = RT.keys()
                    Ar = Acf.ap[:, 0, :]
                    Ai = Acf.ap[:, 1, :]
                    sxk = SX.keys()
                    for grp in range(2):
                        def mmS(e, grp=grp):
                            last = None
                            for pp in (2 * grp, 2 * grp + 1):
                                bnk = PB[6 + pp % 2]
                                for ct in range(4):
                                    for ri in range(2):
                                        col = (ct * 2 + ri) * 64
                                        for j in range(8):
                                            k = 7 - j
                                            if pp < 3:
                                                rows = slice(pp * 32, pp * 32 + 32)
                                                rhs = uTb.ap[rows, ct, j:512:8]
                                            else:
                                                rows = slice(64, 128)
                                                rhs = uM.ap[rows, ct, j:512:8]
                                            last = e.matmul(out=bnk[:, col:col + 64], lhsT=VS.ap[rows, ct, ri, k, :], rhs=rhs,
                                                            start=(j == 0), stop=(j == 7))
                            return last

                        P.op("pe", mmS, reads=uTb.keys() + uM.keys() + VS.keys(), writes=PK[6] + PK[7])
                        for pp in (2 * grp, 2 * grp + 1):
                            bk_ = 6 + pp % 2
                            src = PB[bk_][:, :].rearrange("p (c r n) -> p r c n", c=4, r=2)
                            for ri in range(2):
                                dst = SX.ap[:, ri, pp:16:4, :]
                                cp(dst, src[:, ri], PK[bk_], sxk)
                    cp(Xp.ap[:, :, :, 0], Xc.ap, Xc.keys(), Xp.keys())

                    def rec_step(c):
                        pr = Xc.ap[:, 0, :] if c == 0 else SX.ap[:, 0, :, c - 1]
                        pi = Xc.ap[:, 1, :] if c == 0 else SX.ap[:, 1, :, c - 1]
                        rk_ = (Xc.keys() if c == 0 else []) + sxk + Acf.keys() + tk

                        def rec(e, pr=pr, pi=pi, c=c):
                            e.tensor_tensor(out=t1_, in0=Ar, in1=pr, op=ALU.mult)
                            e.tensor_tensor(out=t2_, in0=Ai, in1=pi, op=ALU.mult)
                            e.tensor_tensor(out=t3_, in0=Ar, in1=pi, op=ALU.mult)
                            e.tensor_tensor(out=t4_, in0=Ai, in1=pr, op=ALU.mult)
                            e.tensor_tensor(out=t1_, in0=t1_, in1=t2_, op=ALU.subtract)
                            e.tensor_tensor(out=t3_, in0=t3_, in1=t4_, op=ALU.add)
                            e.tensor_tensor(out=SX.ap[:, 0, :, c], in0=SX.ap[:, 0, :, c], in1=t1_, op=ALU.add)
                            return e.tensor_tensor(out=SX.ap[:, 1, :, c], in0=SX.ap[:, 1, :, c], in1=t3_, op=ALU.add)

                        P.op(REC_ENG, rec, reads=rk_, writes=sxk + tk)

                    steps = []
                    for h in range(8):
                        for kb in range(4 * ti + 3, -1, -1):
                            steps.append((h, kb))
                    nst = len(steps)
                    rec_per = (64 + nst - 1) // nst
                    rec_done = [0]

                    def geom(i):
                        h, kb = steps[i]
                        b_ = kb - 4 * ti
                        diag = b_ >= 0
                        c0 = 128 * b_ if diag else 0
                        sb0 = b_ if diag else 0
                        return h, kb, diag, c0, sb0, 512 - c0

                    def stageA(i):
                        h, kb, diag, c0, sb0, N = geom(i)
                        zb = i % 2
                        sp_ = spb[i % 3]
                        kt_l = KT.ap[:, h // 2, kb * 128:(kb + 1) * 128]
                        q_r = qz.ap[:, h, c0:512]
                        P.op("pe", lambda e: e.matmul(out=PB[zb][:, 0:N], lhsT=kt_l, rhs=q_r, start=True, stop=True),
                             reads=[("KTc", kb // 4)] + qz.keys(h), writes=PK[zb])
                        act(e32.ap[:, 0:N], PB[zb][:, 0:N], AF.Exp, PK[zb], e32.keys())
                        act(sp_.ap[:, 0:N], e32.ap[:, 0:N], AF.Ln, e32.keys(), sp_.keys(), bias=1.0)
                        if diag:
                            dve_tt(sp_.ap[:, 0:128], sp_.ap[:, 0:128], masklt_b, ALU.mult, sp_.keys() + CK, sp_.keys())

                    def stageB(i):
                        h, kb, diag, c0, sb0, N = geom(i)
                        ab = 2 + i % 2
                        sp_ = spb[i % 3]
                        wl_ = wlb[i % 3]
                        kt_l = KT.ap[:, h // 2, kb * 128:(kb + 1) * 128]
                        q_r = qz.ap[:, h, c0:512]

                        def mm2(e):
                            e.matmul(out=PB[ab][:, 0:N], lhsT=kt_l, rhs=q_r, start=True, stop=False)
                            return e.matmul(out=PB[ab][:, 0:N], lhsT=negtri_b, rhs=sp_.ap[:, 0:N], start=False, stop=True)

                        P.op("pe", mm2, reads=[("KTc", kb // 4)] + qz.keys(h) + sp_.keys() + CK, writes=PK[ab])
                        act(wl_.ap[:, 0:N], PB[ab][:, 0:N], AF.Exp, PK[ab], wl_.keys())
                        if diag:
                            dve_tt(wl_.ap[:, 0:128], wl_.ap[:, 0:128], masklt_b, ALU.mult, wl_.keys() + CK, wl_.keys())

                    def stageC(i):
                        h, kb, diag, c0, sb0, N = geom(i)
                        vb = 4 + i % 2
                        sp_ = spb[i % 3]
                        wl_ = wlb[i % 3]
                        par = h % 2
                        Cst = CE.ap[:, par, 0, :]
                        Est = CE.ap[:, par, 1, :]
                        cek = [("CE", par)]
                        nsb = 4 - sb0
                        v_r = VC.ap[:, kb, h * 64:(h + 1) * 64]

                        def mm3(e):
                            last = None
                            for ii in range(nsb):
                                sbq = sb0 + ii
                                cs = slice(ii * 128, (ii + 1) * 128)
                                e.matmul(out=PB[vb][:, sbq * 80:sbq * 80 + 64], lhsT=wl_.ap[:, cs], rhs=v_r, start=True, stop=True)
                                last = e.matmul(out=PB[vb][:, sbq * 80 + 64:sbq * 80 + 65], lhsT=sp_.ap[:, cs], rhs=ones_b[:, 0:1], start=True, stop=True)
                            return last

                        P.op("pe", mm3, reads=wl_.keys() + sp_.keys() + [("VCc", kb // 4)] + CK, writes=PK[vb])
                        cont0 = sb0 + 1 if diag else 0
                        if cont0 < 4:
                            if E_ON_POOL:
                                P.op("pool", lambda e: e.tensor_tensor(out=Est[:, cont0:4], in0=EINV.ap[:, cont0:4], in1=Cst[:, cont0:4], op=ALU.pow),
                                     reads=cek + EINV.keys(), writes=cek)
                            else:
                                act(Est[:, cont0:4], Cst[:, cont0:4], AF.Exp, cek, cek, scale=-1.0)
                        for sbq in range(sb0, 4):
                            pv = PB[vb][:, sbq * 80:sbq * 80 + 64]
                            dst = acc.ap[:, sbq, h * 64:(h + 1) * 64]
                            if diag and sbq == sb0:
                                cp(dst, pv, PK[vb], acc.keys(sbq))
                            else:
                                dve_stt(dst, pv, Est[:, sbq:sbq + 1], dst, ALU.mult, ALU.add, PK[vb] + cek + acc.keys(sbq), acc.keys(sbq))
                        pcs = PB[vb][:, 0:320].rearrange("p (s c) -> p s c", c=80)[:, :, 64]
                        if diag:
                            cp(Cst[:, sb0:sb0 + 1], pcs[:, sb0:sb0 + 1], PK[vb], cek)
                            if sb0 + 1 < 4:
                                dve_tt(Cst[:, sb0 + 1:4], Cst[:, sb0 + 1:4], pcs[:, sb0 + 1:4], ALU.add, PK[vb] + cek, cek)
                        else:
                            dve_tt(Cst, Cst, pcs, ALU.add, PK[vb] + cek, cek)
                        for _ in range(rec_per):
                            if rec_done[0] < 64:
                                rec_step(rec_done[0])
                                rec_done[0] += 1

                    for i in range(nst + 2):
                        if i < nst:
                            stageA(i)
                        if 0 <= i - 1 < nst:
                            stageB(i - 1)
                        if 0 <= i - 2 < nst:
                            stageC(i - 2)
                    while rec_done[0] < 64:
                        rec_step(rec_done[0])
                        rec_done[0] += 1
                    cp(Xp.ap[:, :, :, 1:64], SX.ap[:, :, :, 0:63], sxk, Xp.keys())
                    cp(Xc.ap, SX.ap[:, :, :, 63], sxk, Xc.keys())
                    dbg_dump(acc.ap, acc.keys(), 0, ti) if dbg == "att" else None
                    for ft in range(4):
                        b = banks(1)[0]

                        def tr(e, ft=ft, b=b):
                            last = None
                            for sbq in range(4):
                                last = e.transpose(out=PB[b][:, sbq * 128:(sbq + 1) * 128], in_=acc.ap[:, sbq, ft * 128:(ft + 1) * 128], identity=ident.ap)
                            return last

                        P.op("pe", tr, reads=acc.keys() + CK, writes=PK[b])
                        act(mixT.ap[:, 4 + ft, :], PB[b][:, :], AF.Copy, PK[b], mixT.keys(4 + ft))

                    P.stage = "t%d_s5out" % ti
                    ybanks = banks(4)
                    for ct in range(4):
                        bnk = PB[ybanks[ct]]

                        def mmY(e, ct=ct, bnk=bnk):
                            last = None
                            for ip in range(8):
                                for j in range(ip + 1):
                                    e.matmul(out=bnk[:, ip:512:8], lhsT=KD.ap[:, ct, ip - j, :], rhs=uTb.ap[:, ct, j:512:8],
                                             start=(ip == 0 and j == 0), stop=False, skip_group_check=True)
                            for ip in range(8):
                                for pp in range(4):
                                    q = 4 * ct + pp
                                    for ri in range(2):
                                        last = e.matmul(out=bnk[pp * 32:(pp + 1) * 32, ip:512:8], lhsT=YW.ap[:, q, ip, ri, :], rhs=Xp.ap[:, ri, q, :],
                                                        start=False, stop=(ri == 1 and ip == 7 and pp == 3), tile_position=(0, pp * 32), skip_group_check=True)
                            return last

                        P.op("pe", mmY, reads=uTb.keys(ct) + KD.keys() + YW.keys() + Xp.keys(), writes=PK[ybanks[ct]])
                        dve_stt(y32.ap[:, ct, :], uTb.ap[:, ct, :], vcol("s5d", ct), bnk[:, :], ALU.mult, ALU.add,
                                PK[ybanks[ct]] + uTb.keys(ct) + CK, y32.keys(ct))
                        act(y32.ap[:, ct, :], y32.ap[:, ct, :], AF.Gelu, y32.keys(ct), y32.keys(ct))
                        cp(gb.ap[:, ct, :], y32.ap[:, ct, :], y32.keys(ct), gb.keys(ct))
                    dbg_dump(y32.ap, y32.keys(), 0, ti) if dbg == "s5" else None

                    def ev_glu(m, b):
                        t_ = tmpf[m % 2]
                        act(t_.ap, PB[b][:, :], AF.Sigmoid, PK[b], t_.keys())
                        dve_tt(mixT.ap[:, m, :], y32.ap[:, m, :], t_.ap, ALU.mult, y32.keys(m) + t_.keys(), mixT.keys(m))

                    gemm_fm("w_glu", 0, 2, 0, lambda kt: gb.ap[:, kt, :], lambda kt: gb.keys(kt), ev_glu)

                    def ev_res(m, b):
                        dve_tt(HT.ap[:, m, :], HT.ap[:, m, :], PB[b][:, :], ALU.add, PK[b] + HT.keys(m), HT.keys(m))

                    gemm_fm("w_out", 0, 4, 0, lambda kt: mixT.ap[:, kt, :], lambda kt: mixT.keys(kt), ev_res)
                    if dbg == "mix0":
                        dbg_dump(HT.ap, HT.keys(), NT - 1, ti)
                else:
                    P.stage = "t%d_pool" % ti
                    prefetch([("pool", g_, 0) for g_ in range(3)])
                    P.dma("pool", "bands", lambda e: e.dma_start(out=bandsb.ap.rearrange("p a b -> p (a b)"), in_=bands_d[:, :]), writes=bandsb.keys())
                    if ti > 0:
                        P.dma("sp", "carry", lambda e: e.dma_start(out=carryb.ap, in_=carry_d[:, :]), reads=[("carryd", 0)], writes=carryb.keys())
                    bss = banks(1)[0]
                    for dt in range(8):
                        act(sq.ap[:, dt, :], HT.ap[:, dt, :], AF.Square, HT.keys(dt), sq.keys(dt))

                    def mmss(e, bss=bss):
                        last = None
                        for blk in range(4):
                            for dt in range(8):
                                last = e.matmul(out=PB[bss][:, blk:blk + 1], lhsT=sq.ap[:, dt, blk * 128:(blk + 1) * 128], rhs=ones_b[:, 0:1],
                                                start=(dt == 0), stop=(dt == 7))
                        return last

                    P.op("pe", mmss, reads=sq.keys() + CK, writes=PK[bss])
                    act(lnc.ap, PB[bss][:, 0:4], AF.Ln, PK[bss], lnc.keys(), scale=1.0 / 1024.0, bias=EPS)
                    act(rstdc.ap, lnc.ap, AF.Exp, lnc.keys(), rstdc.keys(), scale=-0.5)
                    for blk in range(4):
                        bs = banks(2)
                        for half in range(2):
                            b = bs[half]

                            def trh(e, half=half, b=b, blk=blk):
                                last = None
                                for j in range(4):
                                    dt = half * 4 + j
                                    last = e.transpose(out=PB[b][:, j * 128:(j + 1) * 128], in_=HT.ap[:, dt, blk * 128:(blk + 1) * 128], identity=ident.ap)
                                return last

                            P.op("pe", trh, reads=HT.keys() + CK, writes=PK[b])
                            dst = hTok.ap[:, blk, half * 512:(half + 1) * 512]
                            if half == 0:
                                act(dst, PB[b][:, :], AF.Copy, PK[b] + rstdc.keys(), hTok.keys(blk), scale=rstdc.ap[:, blk:blk + 1])
                            else:
                                dve_ts(dst, PB[b][:, :], rstdc.ap[:, blk:blk + 1], ALU.mult, PK[b] + rstdc.keys(), hTok.keys(blk))
                    for dt in range(8):
                        g = dt // 2
                        b = banks(1)[0]

                        def mmb(e, dt=dt, g=g, b=b, ti=ti):
                            last = None
                            cs = slice(dt * 128, (dt + 1) * 128)
                            for blk in range(4):
                                out = PB[b][:, blk * 128:(blk + 1) * 128]
                                first_seq = (ti == 0 and blk == 0)
                                has_spill = not first_seq
                                if first_seq:
                                    e.matmul(out=out, lhsT=hTok.ap[:, 0, cs], rhs=bandsb.ap[:, 4 * g + 2, :], start=True, stop=False)
                                    last = e.matmul(out=out, lhsT=hTok.ap[:, 0, cs], rhs=bandsb.ap[:, 4 * g + 3, :], start=False, stop=True)
                                else:
                                    e.matmul(out=out, lhsT=hTok.ap[:, blk, cs], rhs=bandsb.ap[:, 4 * g + 0, :], start=True, stop=False)
                                    prev = carryb.ap[:, cs] if blk == 0 else hTok.ap[:, blk - 1, cs]
                                    last = e.matmul(out=out, lhsT=prev, rhs=bandsb.ap[:, 4 * g + 1, :], start=False, stop=True)
                            return last

                        P.op("pe", mmb, reads=hTok.keys() + bandsb.keys() + carryb.keys(), writes=PK[b])
                        if dt % 2 == 0:
                            act(ypool8.ap[:, dt, :], PB[b][:, :], AF.Copy, PK[b] + CK, ypool8.keys(dt), scale=vcol("ln_mix1", dt))
                        else:
                            dve_ts(ypool8.ap[:, dt, :], PB[b][:, :], vcol("ln_mix1", dt), ALU.mult, PK[b] + CK, ypool8.keys(dt))
                    P.dma("sp", "carryw", lambda e: e.dma_start(out=carry_d[:, :], in_=hTok.ap[:, 3, :]), reads=hTok.keys(3), writes=[("carryd", 0)])
                    for g in range(4):
                        slot = PRE.pop(("pool", g, 0), None) or load_w("pool", g, 0)
                        bs = banks(2)
                        for m in range(2):
                            b = bs[m]

                            def mm(e, m=m, b=b, slot=slot, g=g):
                                e.matmul(out=PB[b][:, :], lhsT=slot.ap[:, 0, m * 128:(m + 1) * 128], rhs=ypool8.ap[:, 2 * g, :], start=True, stop=False)
                                return e.matmul(out=PB[b][:, :], lhsT=slot.ap[:, 1, m * 128:(m + 1) * 128], rhs=ypool8.ap[:, 2 * g + 1, :], start=False, stop=True)

                            P.op("pe", mm, reads=slot.keys() + ypool8.keys(2 * g) + ypool8.keys(2 * g + 1), writes=PK[b])
                            dt = 2 * g + m
                            dve_stt(HT.ap[:, dt, :], PB[b][:, :], vcol("pscale", dt), HT.ap[:, dt, :], ALU.mult, ALU.add,
                                    PK[b] + HT.keys(dt) + CK, HT.keys(dt))

                if dbg == "pool" and layer == 1:
                    dbg_dump(HT.ap, HT.keys(), 0, ti)
                P.stage = "t%d_mlp%d" % (ti, layer)
                prefetch([("w_up%d" % layer, c_, 0) for c_ in range(3)])
                rmsnorm("ln_mlp%d" % layer, lambda dt: hn.ap[:, dt, :], lambda dt: hn.keys(dt))
                def up_gen(qq):
                    aT = aTs[qq % 2]

                    def ev_up(m, b, aT=aT):
                        r_ = r32[m % 4]
                        act(r_.ap, PB[b][:, :], AF.Relu, PK[b], r_.keys())
                        dve_tt(aT.ap[:, m, :], r_.ap, r_.ap, ALU.mult, r_.keys(), aT.keys(m))

                    return gemm_gen("w_up%d" % layer, qq * 4, 4, 0, lambda kt: hn.ap[:, kt, :], lambda kt: hn.keys(kt), ev_up)

                def dn_gen(qq):
                    aT = aTs[qq % 2]

                    def ev_res(m, b):
                        dve_tt(HT.ap[:, m, :], HT.ap[:, m, :], PB[b][:, :], ALU.add, PK[b] + HT.keys(m), HT.keys(m))

                    return gemm_gen("w_dn%d" % layer, 0, 4, qq, lambda kt, aT=aT: aT.ap[:, kt, :], lambda kt, aT=aT: aT.keys(kt), ev_res)

                if MLP_PIPE:
                    for _ in up_gen(0):
                        pass
                    for qq in range(4):
                        if qq < 3:
                            interleave(up_gen(qq + 1), dn_gen(qq), lead=1)
                        else:
                            for _ in dn_gen(qq):
                                pass
                else:
                    for qq in range(4):
                        for _ in up_gen(qq):
                            pass
                        for _ in dn_gen(qq):
                            pass

                if dbg == "mlp0" and layer == 0:
                    dbg_dump(HT.ap, HT.keys(), NT - 1, ti)
                P.stage = "t%d_ple%d" % (ti, layer)
                prefetch([("w_gt%d" % layer, c_, 0) for c_ in range(3)])
                rmsnorm("ln_ple%d" % layer, lambda dt: hn.ap[:, dt, :], lambda dt: hn.keys(dt))
                for blk in range(4):
                    sb_ = pstg[blk % 2]
                    r0 = t0 + blk * 128
                    P.dma("sp", "stg%d" % (blk % 2), lambda e, sb_=sb_, r0=r0, layer=layer: e.dma_start(out=sb_.ap, in_=p_d[layer, r0:r0 + 128, :]), writes=sb_.keys())
                    b = banks(1)[0]

                    def trp(e, sb_=sb_, b=b):
                        e.transpose(out=PB[b][:, 0:128], in_=sb_.ap[:, 0:128], identity=ident.ap)
                        return e.transpose(out=PB[b][:, 128:256], in_=sb_.ap[:, 128:256], identity=ident.ap)

                    P.op("pe", trp, reads=sb_.keys() + CK, writes=PK[b])
                    act(pT.ap[:, :, blk * 128:(blk + 1) * 128], PB[b][:, 0:256].rearrange("p (k t) -> p k t", k=2), AF.Copy, PK[b], pT.keys())
                sigs = [R["AT"].buf(0, [4, 512], F32), R["UB"].buf(0, [4, 512], F32)]

                def gate_gen(half):
                    sg = sigs[half]

                    def ev_gate(m, b, sg=sg):
                        act(sg.ap[:, m, :], PB[b][:, :], AF.Sigmoid, PK[b], sg.keys(m))

                    return gemm_gen("w_gt%d" % layer, half * 2, 2, 0, lambda kt: hn.ap[:, kt, :], lambda kt: hn.keys(kt), ev_gate)

                def pu_gen(half):
                    sg = sigs[half]

                    def ev_pu(m, b, half=half, sg=sg):
                        t_ = tmpf[2 + m % 2]
                        dve_tt(t_.ap, PB[b][:, :], sg.ap[:, m, :], ALU.mult, PK[b] + sg.keys(m), t_.keys())
                        dt = half * 4 + m
                        dve_tt(HT.ap[:, dt, :], HT.ap[:, dt, :], t_.ap, ALU.add, t_.keys() + HT.keys(dt), HT.keys(dt))

                    return gemm_gen("w_pu%d" % layer, half * 2, 2, 0, lambda kt: pT.ap[:, kt, :], lambda kt: pT.keys(), ev_pu)

                for _ in gate_gen(0):
                    pass
                interleave(gate_gen(1), pu_gen(0), lead=1)
                for _ in pu_gen(1):
                    pass
                if dbg == "l0" and layer == 0:
                    dbg_dump(HT.ap, HT.keys(), NT - 1, ti)

            P.stage = "t%d_out" % ti
            for blk in range(4):
                sb_ = STGb[blk % 2]
                r0 = t0 + blk * 128
                for half in range(2):
                    b = banks(1)[0]

                    def tro(e, half=half, b=b, blk=blk):
                        last = None
                        for j in range(4):
                            dt = half * 4 + j
                            last = e.transpose(out=PB[b][:, j * 128:(j + 1) * 128], in_=HT.ap[:, dt, blk * 128:(blk + 1) * 128], identity=ident.ap)
                        return last

                    P.op("pe", tro, reads=HT.keys() + CK, writes=PK[b])
                    if half == 0:
                        act(sb_.ap[:, 0:512], PB[b][:, :], AF.Copy, PK[b], sb_.keys())
                    else:
                        cp(sb_.ap[:, 512:1024], PB[b][:, :], PK[b], sb_.keys())
                sg = P.dma("sp", "stg%d" % (blk % 2), lambda e, sb_=sb_, r0=r0: e.dma_start(out=y_d[r0:r0 + 128, :], in_=sb_.ap), reads=sb_.keys())
                final_sigs.append(sg)

        fs = {}
        for s, v in final_sigs:
            fs[s] = max(fs.get(s, 0), v)
        if dbg_d is not None and "d_dbg" in P.cnt:
            fs["d_dbg"] = P.cnt["d_dbg"]
        P.finish("sp", list(fs.items()))
        P.emit()
    return nc


def _prep_shared(inp):
    f = lambda a: np.ascontiguousarray(np.asarray(a, dtype=np.float32))
    sh = {}
    sh["w_in"] = f(inp["w_in_even"][0])
    sh["w_out"] = f(inp["w_out_even"][0])
    sh["w_glu"] = f(inp["s5_w_glu"][0])
    sh["w_up"] = f(inp["w_mlp_up"])
    sh["w_dn"] = f(inp["w_mlp_down"])
    sh["w_gt"] = f(inp["w_ple_gate"])
    sh["w_pu"] = f(inp["w_ple_up"])
    sh["w_pool"] = f(inp["pool_w"][0])
    col8 = lambda v: np.asarray(v, np.float32).reshape(-1, 128).T
    qg = np.asarray(inp["sb_q_gain"][0], np.float32)
    kg = np.asarray(inp["sb_k_gain"][0], np.float32)
    qg128 = np.concatenate([qg, qg])
    kg128 = np.concatenate([kg, kg])
    half = (np.arange(128) < 64)
    scale = np.float32(64 ** -0.5)
    vec = np.concatenate([
        col8(inp["ln_mix_even"][0]), col8(inp["ln_mlp"][0]), col8(inp["ln_ple"][0]),
        col8(inp["ln_mix_odd"][0]), col8(inp["ln_mlp"][1]), col8(inp["ln_ple"][1]),
        col8(inp["pool_scale"][0]), col8(inp["s5_d"][0]),
        np.where(half, qg128, 0)[:, None], np.where(~half, qg128, 0)[:, None], kg128[:, None]], axis=1)
    sh["vecs"] = f(vec)
    lamr = np.asarray(inp["s5_lambda_re"][0], np.float32)
    lami = np.asarray(inp["s5_lambda_im"][0], np.float32)
    logdt = np.asarray(inp["s5_log_dt"][0], np.float32)
    toB = lambda a: a.reshape(16, 2, 64).transpose(1, 2, 0).reshape(128, 16)
    logdtB = toB(np.tile(logdt[:, None], (1, 64)))
    toB3 = lambda a: a.reshape(16, 2, 64, 16).transpose(1, 2, 0, 3).reshape(128, 256)
    bre = np.asarray(inp["s5_b_re"][0], np.float32)
    bim = np.asarray(inp["s5_b_im"][0], np.float32)
    cre = np.asarray(inp["s5_c_re"][0], np.float32).transpose(0, 2, 1)
    cim = np.asarray(inp["s5_c_im"][0], np.float32).transpose(0, 2, 1)
    sh["s5B"] = f(np.concatenate([toB(lamr), toB(lami), logdtB, toB3(bre), toB3(bim), toB3(cre), toB3(cim)], axis=1))
    toA = lambda a: np.tile(a.reshape(4, 8, 1, 64), (1, 1, 16, 1)).transpose(1, 2, 0, 3).reshape(128, 256)
    toA3 = lambda a: a.reshape(4, 8, 64, 16).transpose(1, 3, 0, 2).reshape(128, 256)
    sh["s5A"] = f(np.concatenate([toA(lamr), toA(lami), toA(np.tile(logdt[:, None], (1, 64))), toA3(bre), toA3(bim)], axis=1))
    sh.update(_consts())
    sh["_scale"] = scale
    return sh


_NC_CACHE = {}


def kernel(**inputs):
    x = np.asarray(inputs["x"], np.float32)
    p = np.asarray(inputs["p"], np.float32)
    B, L, Dm = x.shape
    NT = L // TT
    sh = _prep_shared(inputs)
    sh.pop("_scale")
    if NT not in _NC_CACHE:
        _NC_CACHE[NT] = build(NT)
    nc = _NC_CACHE[NT]
    in_maps = []
    for b in range(B):
        m = dict(sh)
        m["x"] = np.ascontiguousarray(x[b])
        m["p"] = np.ascontiguousarray(p[:, b])
        in_maps.append(m)
    res = run_bass_kernel_spmd(nc, in_maps, core_ids=list(range(B)))
    out = np.stack([res.results[b]["y"] for b in range(B)], axis=0)
    return out.astype(np.float32)
```

```python
import math
from contextlib import ExitStack

import numpy as np
import concourse.bass as bass
import concourse.mybir as mybir
from concourse.bass_utils import run_bass_kernel_spmd

F32 = mybir.dt.float32
BF16 = mybir.dt.bfloat16
I32 = mybir.dt.int32
AF = mybir.ActivationFunctionType
ALU = mybir.AluOpType

TT = 512
EPS = 1e-6
KLIST = list(range(-7, 9))
TWO_PI = 2.0 * math.pi
import os
NJUNK = int(os.environ.get("K_JUNK", "24"))
MLP_PIPE = bool(int(os.environ.get("K_MLPPIPE", "1")))
REC_ENG = os.environ.get("K_REC", "dve")
E_ON_POOL = bool(int(os.environ.get("K_EPOOL", "0")))
SAME_ENGINE_SYNC = set(os.environ.get("K_SES", "").split(","))


class Prog:
    def __init__(self, nc, stack):
        self.nc = nc
        self.stack = stack
        self.ops = {e: [] for e in ("pe", "act", "dve", "pool", "sp")}
        self.sems = {}
        self.cnt = {}
        self.keys = {}
        self.waited = {e: {} for e in self.ops}
        self.final = []
        self.stage = "setup"
        self.scopes = bool(int(os.environ.get("K_SCOPES", "0")))
        self.ses = set(SAME_ENGINE_SYNC)
        self.near = {"dve": int(os.environ.get("K_NEAR_DVE", "2")), "act": int(os.environ.get("K_NEAR_ACT", "1")), "pool": 2}

    def sem(self, name):
        if name not in self.sems:
            self.sems[name] = self.stack.enter_context(self.nc.semaphore(name))
            self.cnt[name] = 0
        return self.sems[name]

    def _resolve(self, eng, reads, writes, mysig):
        waits = {}

        def add(sig):
            if sig is None:
                return
            s, v = sig
            if s == eng:
                if eng == "pe":
                    return
                if eng not in self.ses and (self.cnt[eng] - v) > self.near.get(eng, 0):
                    return
            if waits.get(s, 0) < v:
                waits[s] = v

        for k in reads:
            st = self.keys.setdefault(k, [None, []])
            add(st[0])
        for k in writes:
            st = self.keys.setdefault(k, [None, []])
            add(st[0])
            for r in st[1]:
                add(r)
        out = []
        for s, v in waits.items():
            if self.waited[eng].get(s, 0) >= v:
                continue
            self.waited[eng][s] = v
            out.append((s, v))
        for k in reads:
            self.keys[k][1].append(mysig)
        for k in writes:
            self.keys[k][0] = mysig
            self.keys[k][1] = []
        return out

    def op(self, eng, fn, reads=(), writes=()):
        self.sem(eng)
        self.cnt[eng] += 1
        mysig = (eng, self.cnt[eng])
        waits = self._resolve(eng, reads, writes, mysig)
        self.ops[eng].append((waits, fn, (eng, 1), self.stage))

    def dma(self, q, semkey, fn, reads=(), writes=()):
        name = "d_" + semkey
        self.sem(name)
        self.cnt[name] += 16
        mysig = (name, self.cnt[name])
        waits = self._resolve(q, reads, writes, mysig)
        self.ops[q].append((waits, fn, (name, 16), "dma"))
        return mysig

    def finish(self, eng, sigs):
        self.final.append((eng, sigs))

    def emit(self):
        nc = self.nc
        engmap = {"pe": "tensor", "act": "scalar", "dve": "vector", "pool": "gpsimd", "sp": "sync"}
        block = self.stack.enter_context(nc.Block())
        for e, attr in engmap.items():
            ops = self.ops[e]
            finals = [s for (fe, s) in self.final if fe == e]
            if not ops and not finals:
                continue

            def body(engine, ops=ops, finals=finals):
                for waits, fn, (sname, inc), stage in ops:
                    if self.scopes:
                        with nc.named_scope(stage):
                            for s, v in waits:
                                engine.wait_ge(self.sems[s], v)
                            inst = fn(engine)
                            inst.then_inc(self.sems[sname], inc)
                    else:
                        for s, v in waits:
                            engine.wait_ge(self.sems[s], v)
                        inst = fn(engine)
                        inst.then_inc(self.sems[sname], inc)
                for sigs in finals:
                    for s, v in sigs:
                        engine.wait_ge(self.sems[s], v)

            getattr(block, attr)(body)


KG = 512


class Region:
    def __init__(self, arena, name, woff, nbytes):
        self.arena, self.name, self.woff, self.nbytes = arena, name, woff, nbytes

    def buf(self, boff, shape, dt):
        return Buf(self, boff, shape, dt)


class Buf:
    def __init__(self, region, boff, shape, dt):
        esz = 4 if dt in (F32, I32) else 2
        n = int(np.prod(shape))
        assert boff % 4 == 0 and boff + n * esz <= region.nbytes, (region.name, boff, shape, region.nbytes)
        w0 = region.woff + boff // 4
        nw = (n * esz + 3) // 4
        ap = region.arena[:, w0:w0 + nw]
        if dt != F32:
            ap = ap.bitcast(dt)
        if len(shape) > 1:
            names = " ".join("a%d" % i for i in range(len(shape)))
            kw = {"a%d" % i: shape[i] for i in range(1, len(shape))}
            ap = ap.rearrange("p (%s) -> p %s" % (names, names), **kw)
        self.ap, self.region, self.boff, self.shape, self.esz = ap, region, boff, tuple(shape), esz
        self.nbytes = n * esz
        self.blk = (n // shape[0]) * esz

    def keys(self, lo=None, hi=None):
        if lo is None:
            b0, b1 = self.boff, self.boff + self.nbytes
        else:
            hi = lo + 1 if hi is None else hi
            b0, b1 = self.boff + lo * self.blk, self.boff + hi * self.blk
        return [(self.region.name, k) for k in range(b0 // KG, (b1 + KG - 1) // KG)]


def _consts():
    c = {}
    c["ident"] = np.eye(128, dtype=np.float32)
    jj = np.arange(128)[:, None]
    tt = np.arange(128)[None, :]
    ones = np.ones((128, 128), np.float32)
    blockones = ((jj // 64) == (tt // 64)).astype(np.float32)
    negtri = -(jj >= tt).astype(np.float32)
    masklt = (jj < tt).astype(np.float32)
    c["cstb"] = np.concatenate([ones, blockones, negtri, masklt], axis=1)
    part = np.arange(128)
    g8 = part // 16
    bd = (g8[:, None, None] == np.arange(8)[None, :, None]) * np.ones((1, 1, 16))
    g2 = (part // 16) % 2
    g2col = (g2[:, None] == np.arange(2)[None, :]).astype(np.float32)
    pp3 = (part >= 96).astype(np.float32)[:, None]
    halfA = (part < 64).astype(np.float32)[:, None]
    halfB = (part >= 64).astype(np.float32)[:, None]
    kv = np.tile(np.array(KLIST, np.float32)[None, :], (128, 1))
    kf = kv / TWO_PI
    kvA = np.tile(np.arange(8, dtype=np.float32)[None, :], (128, 1))
    kfA = kvA / TWO_PI
    cnt = np.zeros((128, 4, 16), np.float32)
    for g, w in enumerate((2, 4, 8, 16)):
        cnt[:, g, :] = 1.0 / np.minimum(np.arange(16) + 1, w)
    bands = np.zeros((128, 16, 128), np.float64)
    tq = np.arange(128)[:, None]
    tt_ = np.arange(128)[None, :]
    for g, w in enumerate((2, 4, 8, 16)):
        main = ((tq <= tt_) & (tq > tt_ - w)) / float(w) - (tq == tt_)
        spill = ((tq - 128) > (tt_ - w)) / float(w)
        cntv = np.minimum(tt_ + 1, w)
        b0 = ((tq <= tt_) & (tq > tt_ - w)) / cntv - (tq == tt_)
        import ml_dtypes
        hi = b0.astype(np.float32).astype(ml_dtypes.bfloat16).astype(np.float64)
        lo = b0 - hi
        bands[:, 4 * g + 0] = main
        bands[:, 4 * g + 1] = spill
        bands[:, 4 * g + 2] = hi
        bands[:, 4 * g + 3] = lo
    c["bands"] = bands.reshape(128, 2048).astype(np.float32)
    c["cstf"] = np.concatenate([bd.reshape(128, 128).astype(np.float32), g2col, pp3, halfA, halfB,
                                kv, kf, kvA, kfA, cnt.reshape(128, 64)], axis=1).astype(np.float32)
    return c


CF = {}
_o = 0
for _n, _w in (("bd", 128), ("g2col", 2), ("pp3", 1), ("halfA", 1), ("halfB", 1), ("kv", 16), ("kf", 16),
               ("kvA", 8), ("kfA", 8), ("cnt", 64)):
    CF[_n] = (_o, _w)
    _o += _w
NCF = _o

VEC = {}
_o = 0
for _n, _w in (("ln_mix0", 8), ("ln_mlp0", 8), ("ln_ple0", 8), ("ln_mix1", 8), ("ln_mlp1", 8), ("ln_ple1", 8),
               ("pscale", 8), ("s5d", 4), ("qgA", 1), ("qgB", 1), ("kgA", 1)):
    VEC[_n] = (_o, _w)
    _o += _w
NVEC = _o


def build(NT, dbg=None):
    L = NT * TT
    nc = bass.Bass("TRN2", target_bir_lowering=False)
    D = lambda n, s: nc.dram_tensor(n, s, F32, kind="ExternalInput").ap()
    x_d = D("x", [L, 1024])
    p_d = D("p", [2, L, 256])
    w_in = D("w_in", [1024, 2048])
    w_out = D("w_out", [1024, 1024])
    w_glu = D("w_glu", [512, 512])
    w_up = D("w_up", [2, 1024, 4096])
    w_dn = D("w_dn", [2, 4096, 1024])
    w_gt = D("w_gt", [2, 1024, 1024])
    w_pu = D("w_pu", [2, 256, 1024])
    w_pool = D("w_pool", [4, 256, 256])
    vecs_d = D("vecs", [128, NVEC])
    s5b_d = D("s5B", [128, 48 + 4 * 256])
    s5a_d = D("s5A", [128, 5 * 256])
    ident_d = D("ident", [128, 128])
    cstb_d = D("cstb", [128, 512])
    cstf_d = D("cstf", [128, NCF])
    bands_d = D("bands", [128, 2048])
    carry_d = nc.dram_tensor("poolcarry", [128, 1024], BF16, kind="Internal").ap()
    y_d = nc.dram_tensor("y", [L, 1024], F32, kind="ExternalOutput").ap()
    dbg_d = None
    if dbg:
        dbg_d = nc.dram_tensor("dbg", [128, 8, 512], F32, kind="ExternalOutput").ap()

    with ExitStack() as st:
        P = Prog(nc, st)
        sizes = [("KV", 65536), ("VS", 16384), ("YW", 16384), ("KD", 8192), ("WS", 3 * 4096), ("HT", 16384),
                 ("CST", 6144), ("STG", 8192), ("HN", 8192), ("SQ", 8192), ("AT", 8192), ("UB", 8192),
                 ("SCR", 8192), ("S5B", 12800), ("RS", 4096)]
        total_w = sum(s for _, s in sizes) // 4
        arena = st.enter_context(nc.sbuf_tensor("arena", [128, total_w], F32))
        R = {}
        wo = 0
        for n, s in sizes:
            R[n] = Region(arena, n, wo, s)
            wo += s // 4
        PB = [st.enter_context(nc.psum_tensor("pb%d" % i, [128, 512], F32)) for i in range(8)]
        PK = [[("pb%d" % i, 0)] for i in range(8)]

        KT = R["KV"].buf(0, [4, 4096], BF16)
        VC = R["KV"].buf(32768, [32, 512], BF16)
        VS = R["VS"].buf(0, [4, 2, 8, 128], BF16)
        YW = R["YW"].buf(0, [16, 8, 2, 32], BF16)
        KD = R["KD"].buf(0, [4, 8, 128], BF16)
        WS = [R["WS"].buf(i * 4096, [8, 256], BF16) for i in range(3)]
        HT = R["HT"].buf(0, [8, 512], F32)
        ident = R["CST"].buf(0, [128], F32)
        vecs = R["CST"].buf(512, [NVEC], F32)
        cstb = R["CST"].buf(1024, [4, 128], BF16)
        cstf = R["CST"].buf(2048, [NCF], F32)
        assert NCF * 4 <= 1024
        Acf = R["CST"].buf(3072, [2, 16], F32)
        Xc = R["CST"].buf(3584, [2, 16], F32)
        LB = R["CST"].buf(4096, [8, 16], F32)
        CE = R["CST"].buf(4608, [2, 2, 4], F32)
        RT = R["CST"].buf(5120, [4, 16], F32)
        EINV = R["CST"].buf(5632, [4], F32)

        def vcol(name, i=0):
            o, w = VEC[name]
            return vecs.ap[:, o + i:o + i + 1]

        def cf(name):
            o, w = CF[name]
            return cstf.ap[:, o:o + w]

        ones_b = cstb.ap[:, 0, :]
        blockones_b = cstb.ap[:, 1, :]
        negtri_b = cstb.ap[:, 2, :]
        masklt_b = cstb.ap[:, 3, :]
        CK = cstb.keys() + cstf.keys() + vecs.keys() + ident.keys()

        def dve_tt(out, in0, in1, op, r, w, eng="dve"):
            P.op(eng, lambda e: e.tensor_tensor(out=out, in0=in0, in1=in1, op=op), reads=r, writes=w)

        def dve_ts(out, in0, s1, op0, r, w, s2=None, op1=None, eng="dve"):
            if op1 is None:
                P.op(eng, lambda e: e.tensor_scalar(out=out, in0=in0, scalar1=s1, scalar2=None, op0=op0), reads=r, writes=w)
            else:
                P.op(eng, lambda e: e.tensor_scalar(out=out, in0=in0, scalar1=s1, scalar2=s2, op0=op0, op1=op1), reads=r, writes=w)

        def dve_stt(out, in0, scalar, in1, op0, op1, r, w):
            P.op("dve", lambda e: e.scalar_tensor_tensor(out=out, in0=in0, scalar=scalar, in1=in1, op0=op0, op1=op1), reads=r, writes=w)

        def act(out, in_, func, r, w, scale=1.0, bias=0.0):
            if func == AF.Copy:
                P.op("act", lambda e: e.activation(out=out, in_=in_, func=func, scale=scale), reads=r, writes=w)
            else:
                P.op("act", lambda e: e.activation(out=out, in_=in_, func=func, scale=scale, bias=bias), reads=r, writes=w)

        def cp(out, in_, r, w, eng="dve"):
            P.op(eng, lambda e: e.tensor_copy(out=out, in_=in_), reads=r, writes=w)

        P.dma("sp", "c0", lambda e: e.dma_start(out=ident.ap, in_=ident_d[:, :]), writes=ident.keys())
        P.dma("sp", "c1", lambda e: e.dma_start(out=vecs.ap, in_=vecs_d[:, :]), writes=vecs.keys())
        P.dma("sp", "c2", lambda e: e.dma_start(out=cstf.ap, in_=cstf_d[:, :]), writes=cstf.keys())
        P.dma("pool", "c3", lambda e: e.dma_start(out=cstb.ap.rearrange("p a b -> p (a b)"), in_=cstb_d[:, :]), writes=cstb.keys())

        P.op("dve", lambda e: e.memset(EINV.ap, math.exp(-1.0)), writes=EINV.keys())
        o_q = VEC["qgA"][0]
        dve_ts(vecs.ap[:, o_q:o_q + 2], vecs.ap[:, o_q:o_q + 2], 0.125, ALU.mult, vecs.keys(), vecs.keys())

        KVr = R["KV"]

        def s5_setup():
            o = [0]

            def tmp(shape, dt=F32):
                b = KVr.buf(o[0], shape, dt)
                o[0] += ((b.nbytes + 511) // 512) * 512
                return b

            inB = tmp([48 + 1024])
            P.dma("sp", "s5in", lambda e: e.dma_start(out=inB.ap, in_=s5b_d[:, :]), writes=inB.keys())
            lamr = inB.ap[:, 0:16]
            lami = inB.ap[:, 16:32]
            logdt = inB.ap[:, 32:48]
            bre = inB.ap[:, 48:304].rearrange("p (q h) -> p q h", q=16)
            bim = inB.ap[:, 304:560].rearrange("p (q h) -> p q h", q=16)
            cre = inB.ap[:, 560:816].rearrange("p (q h) -> p q h", q=16)
            cim = inB.ap[:, 816:1072].rearrange("p (q h) -> p q h", q=16)
            kin = inB.keys()
            sm = tmp([8, 16])
            smk = sm.keys()
            dt_, lrdt, lidt, den, nr, fr, fi, t0 = [sm.ap[:, i, :] for i in range(8)]
            act(dt_, logdt, AF.Exp, kin, smk)
            dve_tt(lrdt, lamr, dt_, ALU.mult, kin + smk, smk)
            dve_tt(lidt, lami, dt_, ALU.mult, kin + smk, smk)
            NK = len(KLIST)
            big = [tmp([NK, 16]) for _ in range(6)]
            mag, Tt, t1, t2, cosv, sinv = big
            allk = sum([b.keys() for b in big], [])
            kv = cf("kv")
            kf = cf("kf")
            bc_q = lambda a: a.unsqueeze(1).to_broadcast([128, NK, 16])
            bc_k = lambda a: a.unsqueeze(2).to_broadcast([128, NK, 16])
            dve_tt(mag.ap, bc_q(lrdt), bc_k(kv), ALU.mult, smk + CK, allk)
            act(mag.ap, mag.ap, AF.Exp, allk, allk)
            dve_tt(Tt.ap, bc_q(lidt), bc_k(kf), ALU.mult, smk + CK, allk)

            def sincos(dst, shift, Tsrc, a1, a2, keys):
                dve_ts(a1.ap, Tsrc.ap, shift, ALU.add, keys, keys)
                cp(a2.ap.bitcast(I32), a1.ap, keys, keys)
                cp(a2.ap, a2.ap.bitcast(I32), keys, keys)
                dve_tt(a1.ap, a1.ap, a2.ap, ALU.subtract, keys, keys)
                dve_stt(a1.ap, a1.ap, 0.0, a1.ap, ALU.is_lt, ALU.add, keys, keys)
                act(dst.ap, a1.ap, AF.Sin, keys, keys, scale=TWO_PI * (1 - 1e-6), bias=-math.pi * (1 - 1e-6))

            sincos(cosv, 0.75 + 32.0, Tt, t1, t2, allk)
            sincos(sinv, 0.5 + 32.0, Tt, t1, t2, allk)
            dve_tt(cosv.ap, cosv.ap, mag.ap, ALU.mult, allk, allk)
            dve_tt(sinv.ap, sinv.ap, mag.ap, ALU.mult, allk, allk)
            Er = lambda k: cosv.ap[:, k + 7, :]
            Ei = lambda k: sinv.ap[:, k + 7, :]
            cp(Acf.ap[:, 0, :], Er(8), allk, Acf.keys())
            cp(Acf.ap[:, 1, :], Ei(8), allk, Acf.keys())
            dve_tt(den, lamr, lamr, ALU.mult, kin, smk)
            dve_tt(t0, lami, lami, ALU.mult, kin, smk)
            dve_tt(den, den, t0, ALU.add, smk, smk)
            P.op("dve", lambda e: e.reciprocal(out=den, in_=den), reads=smk, writes=smk)
            dve_ts(nr, Er(1), -1.0, ALU.add, allk, smk)
            dve_tt(fr, nr, lamr, ALU.mult, smk + kin, smk)
            dve_tt(t0, Ei(1), lami, ALU.mult, allk + kin, smk)
            dve_tt(fr, fr, t0, ALU.add, smk, smk)
            dve_tt(fr, fr, den, ALU.mult, smk, smk)
            dve_tt(fi, Ei(1), lamr, ALU.mult, allk + kin, smk)
            dve_tt(t0, nr, lami, ALU.mult, smk + kin, smk)
            dve_tt(fi, fi, t0, ALU.subtract, smk, smk)
            dve_tt(fi, fi, den, ALU.mult, smk, smk)
            bb = tmp([4, 16, 16])
            bbk = bb.keys()
            Bbr, Bbi, tA, tB = [bb.ap[:, i] for i in range(4)]
            bch = lambda a: a.unsqueeze(2).to_broadcast([128, 16, 16])
            dve_tt(Bbr, bre, bch(fr), ALU.mult, kin + smk, bbk)
            dve_tt(tA, bim, bch(fi), ALU.mult, kin + smk, bbk)
            dve_tt(Bbr, Bbr, tA, ALU.subtract, bbk, bbk)
            dve_tt(Bbi, bim, bch(fr), ALU.mult, kin + smk, bbk)
            dve_tt(tA, bre, bch(fi), ALU.mult, kin + smk, bbk)
            dve_tt(Bbi, Bbi, tA, ALU.add, bbk, bbk)
            Rr = tmp([9, 16, 16])
            Ri = tmp([9, 16, 16])
            Rt = tmp([9, 16, 16])
            rk = Rr.keys() + Ri.keys() + Rt.keys()
            bcC = lambda a: a.unsqueeze(1).to_broadcast([128, 9, 16, 16])
            bcE = lambda a: a.unsqueeze(3).to_broadcast([128, 9, 16, 16])
            Er9 = cosv.ap[:, 7:16, :]
            Ei9 = sinv.ap[:, 7:16, :]
            dve_tt(Rr.ap, bcC(cre), bcE(Er9), ALU.mult, kin + allk, rk)
            dve_tt(Rt.ap, bcC(cim), bcE(Ei9), ALU.mult, kin + allk, rk)
            dve_tt(Rr.ap, Rr.ap, Rt.ap, ALU.subtract, rk, rk)
            dve_tt(Ri.ap, bcC(cre), bcE(Ei9), ALU.mult, kin + allk, rk)
            dve_tt(Rt.ap, bcC(cim), bcE(Er9), ALU.mult, kin + allk, rk)
            dve_tt(Ri.ap, Ri.ap, Rt.ap, ALU.add, rk, rk)
            Bz = tmp([2, 16, 128])
            bzk = Bz.keys()
            P.op("dve", lambda e: e.memset(Bz.ap, 0.0), writes=bzk)
            for half in range(2):
                ps_ = slice(half * 64, half * 64 + 64)
                for pp in range(4):
                    cs_ = slice(pp * 32 + half * 16, pp * 32 + half * 16 + 16)
                    cp(Bz.ap[ps_, 0, pp:16:4, cs_], Bbr[ps_, pp:16:4, :], bbk, bzk)
                    dve_ts(Bz.ap[ps_, 1, pp:16:4, cs_], Bbi[ps_, pp:16:4, :], -1.0, ALU.mult, bbk, bzk)
            bd = cf("bd").rearrange("p (a b) -> p a b", a=8)
            for ct in range(4):
                bank = PB[ct]

                def mm(e, ct=ct, bank=bank):
                    last = None
                    for pp in range(4):
                        q = 4 * ct + pp
                        e.matmul(out=bank[:, 0:128], lhsT=Bz.ap[:, 0, q, :], rhs=Rr.ap[:, 0:8, q, :],
                                 start=(pp == 0), stop=False)
                        last = e.matmul(out=bank[:, 0:128], lhsT=Bz.ap[:, 1, q, :], rhs=Ri.ap[:, 0:8, q, :],
                                        start=False, stop=(pp == 3))
                    return last

                P.op("pe", mm, reads=bzk + rk, writes=PK[ct])
                src = bank[:, 0:128].rearrange("p (t h) -> p t h", t=8).unsqueeze(2).to_broadcast([128, 8, 8, 16])
                msk = bd.unsqueeze(1).to_broadcast([128, 8, 8, 16])
                dst = KD.ap[:, ct].rearrange("p t (g h) -> p t g h", g=8)
                dve_tt(dst, src, msk, ALU.mult, PK[ct] + CK, KD.keys())
            P.op("dve", lambda e: e.memset(YW.ap, 0.0), writes=YW.keys())
            for half in range(2):
                ps_ = slice(half * 64, half * 64 + 64)
                cs_ = slice(half * 16, half * 16 + 16)
                srcr = Rr.ap[ps_, 1:9].rearrange("p k q h -> p q k h")
                srci = Ri.ap[ps_, 1:9].rearrange("p k q h -> p q k h")
                cp(YW.ap[ps_, :, :, 0, cs_], srcr, rk, YW.keys())
                dve_ts(YW.ap[ps_, :, :, 1, cs_], srci, -1.0, ALU.mult, rk, YW.keys())
            o[0] = 0
            inA = tmp([5, 256])
            P.dma("sp", "s5in", lambda e: e.dma_start(out=inA.ap.rearrange("p a b -> p (a b)"), in_=s5a_d[:, :]), writes=inA.keys())
            kia = inA.keys()
            lamrA, lamiA, logdtA, breA, bimA = [inA.ap[:, i, :] for i in range(5)]
            smA = tmp([8, 256])
            sak = smA.keys()
            dtA, lrdtA, lidtA, denA, nrA, frA, fiA, t0A = [smA.ap[:, i, :] for i in range(8)]
            act(dtA, logdtA, AF.Exp, kia, sak)
            dve_tt(lrdtA, lamrA, dtA, ALU.mult, kia + sak, sak)
            dve_tt(lidtA, lamiA, dtA, ALU.mult, kia + sak, sak)
            class _V:
                def __init__(self, ap):
                    self.ap = ap

            smB = tmp([6, 256])
            sbk = smB.keys()
            mag1, T1, u1, u2, c1, s1 = [_V(smB.ap[:, i, :]) for i in range(6)]
            act(mag1.ap, lrdtA, AF.Exp, sak, sbk)
            dve_ts(T1.ap, lidtA, 1.0 / TWO_PI, ALU.mult, sak, sbk)
            sincos(c1, 0.75 + 32.0, T1, u1, u2, sbk)
            sincos(s1, 0.5 + 32.0, T1, u1, u2, sbk)
            a1r, a1i = c1.ap, s1.ap
            dve_tt(a1r, a1r, mag1.ap, ALU.mult, sbk, sbk)
            dve_tt(a1i, a1i, mag1.ap, ALU.mult, sbk, sbk)
            dve_tt(denA, lamrA, lamrA, ALU.mult, kia, sak)
            dve_tt(t0A, lamiA, lamiA, ALU.mult, kia, sak)
            dve_tt(denA, denA, t0A, ALU.add, sak, sak)
            P.op("dve", lambda e: e.reciprocal(out=denA, in_=denA), reads=sak, writes=sak)
            dve_ts(nrA, a1r, -1.0, ALU.add, sbk, sak)
            dve_tt(frA, nrA, lamrA, ALU.mult, sak + kia, sak)
            dve_tt(t0A, a1i, lamiA, ALU.mult, sbk + kia, sak)
            dve_tt(frA, frA, t0A, ALU.add, sak, sak)
            dve_tt(frA, frA, denA, ALU.mult, sak, sak)
            dve_tt(fiA, a1i, lamrA, ALU.mult, sbk + kia, sak)
            dve_tt(t0A, nrA, lamiA, ALU.mult, sak + kia, sak)
            dve_tt(fiA, fiA, t0A, ALU.subtract, sak, sak)
            dve_tt(fiA, fiA, denA, ALU.mult, sak, sak)
            Wr = tmp([8, 256])
            Wi = tmp([8, 256])
            ak = Wr.keys() + Wi.keys()
            dve_tt(t0A, bimA, fiA, ALU.mult, kia + sak, sak)
            dve_tt(Wr.ap[:, 0, :], breA, frA, ALU.mult, kia + sak, ak)
            dve_tt(Wr.ap[:, 0, :], Wr.ap[:, 0, :], t0A, ALU.subtract, ak + sak, ak)
            dve_tt(t0A, breA, fiA, ALU.mult, kia + sak, sak)
            dve_tt(Wi.ap[:, 0, :], bimA, frA, ALU.mult, kia + sak, ak)
            dve_tt(Wi.ap[:, 0, :], Wi.ap[:, 0, :], t0A, ALU.add, ak + sak, ak)
            for k_ in range(7):
                dve_tt(t0A, Wi.ap[:, k_, :], a1i, ALU.mult, ak + sbk, sak)
                dve_tt(Wr.ap[:, k_ + 1, :], Wr.ap[:, k_, :], a1r, ALU.mult, ak + sbk, ak)
                dve_tt(Wr.ap[:, k_ + 1, :], Wr.ap[:, k_ + 1, :], t0A, ALU.subtract, ak + sak, ak)
                dve_tt(t0A, Wr.ap[:, k_, :], a1i, ALU.mult, ak + sbk, sak)
                dve_tt(Wi.ap[:, k_ + 1, :], Wi.ap[:, k_, :], a1r, ALU.mult, ak + sbk, ak)
                dve_tt(Wi.ap[:, k_ + 1, :], Wi.ap[:, k_ + 1, :], t0A, ALU.add, ak + sak, ak)
            g2c = cf("g2col")
            for ri, Wx in enumerate((Wr, Wi)):
                src = Wx.ap.rearrange("p k (c s) -> p c k s", c=4)
                for g2p in range(2):
                    dst = VS.ap[:, :, ri, :, g2p * 64:(g2p + 1) * 64]
                    dve_ts(dst, src, g2c[:, g2p:g2p + 1], ALU.mult, ak + CK, VS.keys())
            P.op("dve", lambda e: e.memset(Xc.ap, 0.0), writes=Xc.keys())
            allkv = [("KV", k) for k in range(65536 // KG)]
            P.op("dve", lambda e: e.memset(LB.ap, 0.0), reads=allkv,
                 writes=LB.keys() + [("KTc", i) for i in range(NT)] + [("VCc", i) for i in range(NT)])

        P.ses = set(os.environ.get("K_SES_SETUP", "").split(","))
        s5_setup()
        P.ses = set(SAME_ENGINE_SYNC)

        WSPEC = {"w_in": w_in, "w_glu": w_glu, "w_out": w_out}
        for l_ in range(2):
            WSPEC["w_up%d" % l_] = w_up[l_]
            WSPEC["w_dn%d" % l_] = w_dn[l_]
            WSPEC["w_gt%d" % l_] = w_gt[l_]
            WSPEC["w_pu%d" % l_] = w_pu[l_]
        SCRT = {}
        WNKT = {}
        for name, W in WSPEC.items():
            K_, N_ = W.shape
            nkt = min(8, K_ // 128)
            WNKT[name] = nkt
            SCRT[name] = nc.dram_tensor("s_" + name, [N_ // 256, K_ // (128 * nkt), 128, nkt * 256], BF16, kind="Internal").ap()
        SCRT["pool"] = nc.dram_tensor("s_pool", [4, 1, 128, 512], BF16, kind="Internal").ap()
        WNKT["pool"] = 2

        CONV = []

        def convert(name, chunks):
            for (c, kc) in chunks:
                CONV.append((name, c, kc))

        convert("w_in", [(c, 0) for c in range(8)])
        convert("w_glu", [(c, 0) for c in range(2)])
        convert("w_out", [(c, 0) for c in range(4)])
        for l_ in range(2):
            if l_ == 1:
                convert("pool", [(g_, 0) for g_ in range(4)])
            for qq in range(4):
                convert("w_up%d" % l_, [(qq * 4 + c, 0) for c in range(4)])
                convert("w_dn%d" % l_, [(c, qq) for c in range(4)])
            for half in range(2):
                convert("w_gt%d" % l_, [(half * 2 + c, 0) for c in range(2)])
                convert("w_pu%d" % l_, [(half * 2 + c, 0) for c in range(2)])
        CONV_IDX = {k_: i_ for i_, k_ in enumerate(CONV)}
        cvp = [0]
        NCV = 16
        LOOKAHEAD = int(os.environ.get("K_LA", "12"))

        def conv_upto(idx):
            while cvp[0] <= min(idx, len(CONV) - 1):
                i_ = cvp[0]
                cvp[0] += 1
                name, c, kc = CONV[i_]
                nkt = WNKT[name]
                if name == "pool":
                    src = w_pool[c].rearrange("(k p) n -> p k n", p=128)
                else:
                    W = WSPEC[name]
                    src = W[kc * nkt * 128:(kc + 1) * nkt * 128, c * 256:(c + 1) * 256].rearrange("(k p) n -> p k n", p=128)
                dst = SCRT[name][c, kc].rearrange("p (k n) -> p k n", k=nkt)
                P.dma("pool", "cv%d" % (i_ % NCV), lambda e, src=src, dst=dst: e.dma_start(out=dst, in_=src),
                      writes=[("scr", name, c, kc), ("cvring", i_ % NCV)])

        wctr = [0]
        NSLOT = len(WS)
        PRE = {}

        def prefetch(chunks):
            for key in chunks:
                if key not in PRE:
                    PRE[key] = load_w(*key)

        def load_w(name, c, kc=0):
            s = wctr[0] % NSLOT
            wctr[0] += 1
            slot = WS[s]
            nkt = WNKT[name]
            conv_upto(CONV_IDX[(name, c, kc)] + LOOKAHEAD)
            src = SCRT[name][c, kc].rearrange("p (k n) -> p k n", k=nkt)
            P.dma("pool", "ws%d" % s, lambda e: e.dma_start(out=slot.ap[:, 0:nkt, :], in_=src),
                  reads=[("scr", name, c, kc)], writes=slot.keys())
            return slot

        pctr = [0]

        def banks(n):
            b = [(pctr[0] + i) % 8 for i in range(n)]
            pctr[0] = (pctr[0] + n) % 8
            return b

        def gemm_gen(name, c0, nchunks, kc, rhs_fn, rkeys_fn, evac):
            nkt = WNKT[name]
            for c in range(nchunks):
                slot = PRE.pop((name, c0 + c, kc), None)
                if slot is None:
                    slot = load_w(name, c0 + c, kc)
                bs = banks(2)
                for m in range(2):
                    b = bs[m]

                    def mm(e, m=m, b=b, slot=slot):
                        last = None
                        for kt in range(nkt):
                            last = e.matmul(out=PB[b][:, :], lhsT=slot.ap[:, kt, m * 128:(m + 1) * 128], rhs=rhs_fn(kt),
                                            start=(kt == 0), stop=(kt == nkt - 1))
                        return last

                    rk = sum([rkeys_fn(kt) for kt in range(nkt)], [])
                    P.op("pe", mm, reads=slot.keys() + rk, writes=PK[b])
                    evac(c * 2 + m, b)
                yield c

        def gemm_fm(*a):
            for _ in gemm_gen(*a):
                pass

        def interleave(g1, g2, lead=1):
            for _ in range(lead):
                next(g1, None)
            d1 = d2 = False
            while not (d1 and d2):
                if not d2:
                    d2 = next(g2, "done") == "done"
                if not d1:
                    d1 = next(g1, "done") == "done"

        STGb = [R["STG"].buf(i * 4096, [1024], F32) for i in range(2)]
        hn = R["HN"].buf(0, [8, 512], BF16)
        mixT = R["HN"].buf(0, [8, 512], BF16)
        sq = R["SQ"].buf(0, [8, 512], BF16)
        qz = R["SQ"].buf(0, [8, 512], BF16)
        gb = R["SQ"].buf(0, [4, 512], BF16)
        sig = R["SQ"].buf(0, [4, 512], F32)
        acc = R["AT"].buf(0, [4, 512], F32)
        y32 = R["AT"].buf(0, [4, 512], F32)
        aTs = [R["AT"].buf(0, [8, 512], BF16), R["S5B"].buf(0, [8, 512], BF16)]
        uTb = R["UB"].buf(0, [4, 512], BF16)
        uM = R["UB"].buf(4096, [4, 512], BF16)
        r32 = [R["UB"].buf(i * 2048, [512], F32) for i in range(4)]
        pT = R["STG"].buf(2048, [2, 512], BF16)
        pstg = [R["STG"].buf(i * 4096, [256], F32) for i in range(2)]
        e32 = R["SCR"].buf(0, [512], F32)
        spb = [R["SCR"].buf(2048 + i * 1024, [512], BF16) for i in range(3)]
        wlb = [R["SCR"].buf(5120 + i * 1024, [512], BF16) for i in range(3)]
        tmpf = [R["SCR"].buf(i * 2048, [512], F32) for i in range(4)]
        SX = R["S5B"].buf(0, [2, 16, 64], F32)
        Xp = R["S5B"].buf(8192, [2, 16, 64], BF16)
        hnp = [R["S5B"].buf(i * 2112, [528], F32) for i in range(2)]
        ptmp = [R["S5B"].buf(4224 + i * 2112, [528], F32) for i in range(2)]
        ypool = [R["S5B"].buf(8448 + i * 2048, [2, 512], BF16) for i in range(2)]
        hTok = R["AT"].buf(0, [4, 1024], BF16)
        ypool8 = R["UB"].buf(0, [8, 512], BF16)
        bandsb = R["S5B"].buf(0, [16, 128], BF16)
        carryb = R["S5B"].buf(4096, [1024], BF16)
        rstdc = R["RS"].buf(0, [4], F32)
        lnc = R["RS"].buf(512, [4], F32)
        rstd = R["RS"].buf(0, [512], F32)
        sqt = R["RS"].buf(2048, [512], F32)

        def dbg_dump(src_ap, keys, ti_sel, ti, slot=None):
            if dbg_d is None or ti != ti_sel:
                return
            dst = dbg_d[:, 0:src_ap.shape[1], :] if slot is None else dbg_d[:, slot, :]
            return P.dma("sp", "dbg", lambda e: e.dma_start(out=dst, in_=src_ap), reads=keys)

        NPOOL = int(os.environ.get("K_NPOOL", "0"))

        def rmsnorm(gain_name, out_fn, out_keys_fn, fp32_out=False):
            b = banks(1)[0]
            for dt in range(8):
                act(sq.ap[:, dt, :], HT.ap[:, dt, :], AF.Square, HT.keys(dt), sq.keys(dt))
                P.op("pe", lambda e, dt=dt: e.matmul(out=PB[b][:, :], lhsT=ones_b, rhs=sq.ap[:, dt, :], start=(dt == 0), stop=(dt == 7)),
                     reads=sq.keys(dt) + CK, writes=PK[b])
            if NJUNK > 0:
                bj = pctr[0] % 8
                cflat = cstb.ap.rearrange("p a b -> p (a b)")

                def junk(e, bj=bj):
                    last = None
                    for _ in range(NJUNK):
                        last = e.matmul(out=PB[bj][:, :], lhsT=ones_b, rhs=cflat, start=True, stop=True)
                    return last

                P.op("pe", junk, reads=CK, writes=PK[bj])
            act(sqt.ap, PB[b][:, :], AF.Ln, PK[b], sqt.keys(), scale=1.0 / 1024.0, bias=EPS)
            act(rstd.ap, sqt.ap, AF.Exp, sqt.keys(), rstd.keys(), scale=-0.5)
            for i, dt in enumerate(range(8 - NPOOL, 8)):
                act(tmpf[i].ap, HT.ap[:, dt, :], AF.Copy, HT.keys(dt) + CK, tmpf[i].keys(), scale=vcol(gain_name, dt))
            for i, dt in enumerate(range(8 - NPOOL, 8)):
                P.op("pool", lambda e, i=i, dt=dt: e.tensor_tensor(out=out_fn(dt), in0=tmpf[i].ap, in1=rstd.ap, op=ALU.mult),
                     reads=tmpf[i].keys() + rstd.keys(), writes=out_keys_fn(dt))
            for dt in range(8 - NPOOL):
                dve_stt(out_fn(dt), HT.ap[:, dt, :], vcol(gain_name, dt), rstd.ap, ALU.mult, ALU.mult,
                        HT.keys(dt) + rstd.keys() + CK, out_keys_fn(dt))

        final_sigs = []

        for ti in range(NT):
            t0 = ti * TT
            P.stage = "t%d_load" % ti
            for blk in range(4):
                sb_ = STGb[blk % 2]
                r0 = t0 + blk * 128
                P.dma("sp", "stg%d" % (blk % 2), lambda e, sb_=sb_, r0=r0: e.dma_start(out=sb_.ap, in_=x_d[r0:r0 + 128, :]), writes=sb_.keys())
                for half in range(2):
                    b = banks(1)[0]

                    def tr(e, sb_=sb_, half=half, b=b):
                        last = None
                        for j in range(4):
                            dt = half * 4 + j
                            last = e.transpose(out=PB[b][:, j * 128:(j + 1) * 128], in_=sb_.ap[:, dt * 128:(dt + 1) * 128], identity=ident.ap)
                        return last

                    P.op("pe", tr, reads=sb_.keys() + CK, writes=PK[b])
                    dst = HT.ap[:, half * 4:half * 4 + 4, blk * 128:(blk + 1) * 128]
                    src = PB[b][:, :].rearrange("p (j t) -> p j t", j=4)
                    wk = sum([HT.keys(half * 4 + j) for j in range(4)], [])
                    if half == 0:
                        act(dst, src, AF.Copy, PK[b], wk)
                    else:
                        cp(dst, src, PK[b], wk)

            for layer in range(2):
                if layer == 0:
                    P.stage = "t%d_inproj" % ti
                    prefetch([("w_in", c_, 0) for c_ in range(3)])
                    rmsnorm("ln_mix0", lambda dt: hn.ap[:, dt, :], lambda dt: hn.keys(dt))
                    hn_r = lambda kt: hn.ap[:, kt, :]
                    hn_k = lambda kt: hn.keys(kt)

                    def ev_u(m, b):
                        act(uTb.ap[:, m, :], PB[b][:, :], AF.Copy, PK[b], uTb.keys(m))

                    gemm_fm("w_in", 0, 2, 0, hn_r, hn_k, ev_u)
                    dve_ts(uM.ap[64:128], uTb.ap[64:128], cf("pp3")[64:128, :], ALU.mult, uTb.keys() + CK, uM.keys())

                    def qk_evac(is_q):
                        def ev(m, b):
                            s_ = tmpf[m % 2]
                            sb16 = s_.ap.bitcast(BF16)[:, 0:512]
                            act(sb16, PB[b][:, :], AF.Square, PK[b], s_.keys())
                            b2 = banks(1)[0]
                            P.op("pe", lambda e: e.matmul(out=PB[b2][:, :], lhsT=blockones_b, rhs=sb16, start=True, stop=True),
                                 reads=s_.keys() + CK, writes=PK[b2])
                            r_ = tmpf[2 + m % 2]
                            act(r_.ap, PB[b2][:, :], AF.Ln, PK[b2], r_.keys(), scale=1.0 / 64.0, bias=EPS)
                            act(r_.ap, r_.ap, AF.Exp, r_.keys(), r_.keys(), scale=-0.5)
                            if is_q:
                                dve_stt(qz.ap[:, 2 * m, :], PB[b][:, :], vcol("qgA"), r_.ap, ALU.mult, ALU.mult,
                                        PK[b] + r_.keys() + CK, qz.keys(2 * m))
                                dve_stt(qz.ap[:, 2 * m + 1, :], PB[b][:, :], vcol("qgB"), r_.ap, ALU.mult, ALU.mult,
                                        PK[b] + r_.keys() + CK, qz.keys(2 * m + 1))
                            else:
                                dve_stt(KT.ap[:, m, t0:t0 + 512], PB[b][:, :], vcol("kgA"), r_.ap, ALU.mult, ALU.mult,
                                        PK[b] + r_.keys() + CK, [("KTc", ti)])
                        return ev

                    gemm_fm("w_in", 2, 2, 0, hn_r, hn_k, qk_evac(True))
                    gemm_fm("w_in", 4, 2, 0, hn_r, hn_k, qk_evac(False))
                    for c in range(2):
                        slot = load_w("w_in", 6 + c, 0)
                        bs = banks(2)
                        for blk in range(4):
                            b = bs[blk // 2]
                            cs = slice((blk % 2) * 256, (blk % 2) * 256 + 256)

                            def mm(e, blk=blk, b=b, cs=cs, slot=slot):
                                last = None
                                for kt in range(8):
                                    last = e.matmul(out=PB[b][:, cs], lhsT=hn.ap[:, kt, blk * 128:(blk + 1) * 128], rhs=slot.ap[:, kt, :],
                                                    start=(kt == 0), stop=(kt == 7))
                                return last

                            P.op("pe", mm, reads=slot.keys() + hn.keys(), writes=[("pbh%d" % b, blk % 2)] + PK[b])
                        for blk in range(4):
                            b = bs[blk // 2]
                            cs = slice((blk % 2) * 256, (blk % 2) * 256 + 256)
                            dst = VC.ap[:, ti * 4 + blk, c * 256:(c + 1) * 256]
                            if blk % 2 == 0:
                                act(dst, PB[b][:, cs], AF.Copy, PK[b], [("VCc", ti)])
                            else:
                                cp(dst, PB[b][:, cs], PK[b], [("VCc", ti)])

                    P.stage = "t%d_att" % ti
                    sbanks = [6, 7, 6, 7]
                    t1_, t2_, t3_, t4_ = [RT.ap[:, i, :] for i in range(4)]
                    tk = RT.keys()
                    Ar = Acf.ap[:, 0, :]
                    Ai = Acf.ap[:, 1, :]
                    sxk = SX.keys()
                    for grp in range(2):
                        def mmS(e, grp=grp):
                            last = None
                            for pp in (2 * grp, 2 * grp + 1):
                                bnk = PB[6 + pp % 2]
                                for ct in range(4):
                                    for ri in range(2):
                                        col = (ct * 2 + ri) * 64
                                        for j in range(8):
                                            k = 7 - j
                                            if pp < 3:
                                                rows = slice(pp * 32, pp * 32 + 32)
                                                rhs = uTb.ap[rows, ct, j:512:8]
                                            else:
                                                rows = slice(64, 128)
                                                rhs = uM.ap[rows, ct, j:512:8]
                                            last = e.matmul(out=bnk[:, col:col + 64], lhsT=VS.ap[rows, ct, ri, k, :], rhs=rhs,
                                                            start=(j == 0), stop=(j == 7))
                            return last

                        P.op("pe", mmS, reads=uTb.keys() + uM.keys() + VS.keys(), writes=PK[6] + PK[7])
                        for pp in (2 * grp, 2 * grp + 1):
                            bk_ = 6 + pp % 2
                            src = PB[bk_][:, :].rearrange("p (c r n) -> p r c n", c=4, r=2)
                            for ri in range(2):
                                dst = SX.ap[:, ri, pp:16:4, :]
                                cp(dst, src[:, ri], PK[bk_], sxk)
                    cp(Xp.ap[:, :, :, 0], Xc.ap, Xc.keys(), Xp.keys())

                    def rec_step(c):
                        pr = Xc.ap[:, 0, :] if c == 0 else SX.ap[:, 0, :, c - 1]
                        pi = Xc.ap[:, 1, :] if c == 0 else SX.ap[:, 1, :, c - 1]
                        rk_ = (Xc.keys() if c == 0 else []) + sxk + Acf.keys() + tk

                        def rec(e, pr=pr, pi=pi, c=c):
                            e.tensor_tensor(out=t1_, in0=Ar, in1=pr, op=ALU.mult)
                            e.tensor_tensor(out=t2_, in0=Ai, in1=pi, op=ALU.mult)
                            e.tensor_tensor(out=t3_, in0=Ar, in1=pi, op=ALU.mult)
                            e.tensor_tensor(out=t4_, in0=Ai, in1=pr, op=ALU.mult)
                            e.tensor_tensor(out=t1_, in0=t1_, in1=t2_, op=ALU.subtract)
                            e.tensor_tensor(out=t3_, in0=t3_, in1=t4_, op=ALU.add)
                            e.tensor_tensor(out=SX.ap[:, 0, :, c], in0=SX.ap[:, 0, :, c], in1=t1_, op=ALU.add)
                            return e.tensor_tensor(out=SX.ap[:, 1, :, c], in0=SX.ap[:, 1, :, c], in1=t3_, op=ALU.add)

                        P.op(REC_ENG, rec, reads=rk_, writes=sxk + tk)

                    steps = []
                    for h in range(8):
                        for kb in range(4 * ti + 3, -1, -1):
                            steps.append((h, kb))
                    nst = len(steps)
                    rec_per = (64 + nst - 1) // nst
                    rec_done = [0]

                    def geom(i):
                        h, kb = steps[i]
                        b_ = kb - 4 * ti
                        diag = b_ >= 0
                        c0 = 128 * b_ if diag else 0
                        sb0 = b_ if diag else 0
                        return h, kb, diag, c0, sb0, 512 - c0

                    def stageA(i):
                        h, kb, diag, c0, sb0, N = geom(i)
                        zb = i % 2
                        sp_ = spb[i % 3]
                        kt_l = KT.ap[:, h // 2, kb * 128:(kb + 1) * 128]
                        q_r = qz.ap[:, h, c0:512]
                        P.op("pe", lambda e: e.matmul(out=PB[zb][:, 0:N], lhsT=kt_l, rhs=q_r, start=True, stop=True),
                             reads=[("KTc", kb // 4)] + qz.keys(h), writes=PK[zb])
                        act(e32.ap[:, 0:N], PB[zb][:, 0:N], AF.Exp, PK[zb], e32.keys())
                        act(sp_.ap[:, 0:N], e32.ap[:, 0:N], AF.Ln, e32.keys(), sp_.keys(), bias=1.0)
                        if diag:
                            dve_tt(sp_.ap[:, 0:128], sp_.ap[:, 0:128], masklt_b, ALU.mult, sp_.keys() + CK, sp_.keys())

                    def stageB(i):
                        h, kb, diag, c0, sb0, N = geom(i)
                        ab = 2 + i % 2
                        sp_ = spb[i % 3]
                        wl_ = wlb[i % 3]
                        kt_l = KT.ap[:, h // 2, kb * 128:(kb + 1) * 128]
                        q_r = qz.ap[:, h, c0:512]

                        def mm2(e):
                            e.matmul(out=PB[ab][:, 0:N], lhsT=kt_l, rhs=q_r, start=True, stop=False)
                            return e.matmul(out=PB[ab][:, 0:N], lhsT=negtri_b, rhs=sp_.ap[:, 0:N], start=False, stop=True)

                        P.op("pe", mm2, reads=[("KTc", kb // 4)] + qz.keys(h) + sp_.keys() + CK, writes=PK[ab])
                        act(wl_.ap[:, 0:N], PB[ab][:, 0:N], AF.Exp, PK[ab], wl_.keys())
                        if diag:
                            dve_tt(wl_.ap[:, 0:128], wl_.ap[:, 0:128], masklt_b, ALU.mult, wl_.keys() + CK, wl_.keys())

                    def stageC(i):
                        h, kb, diag, c0, sb0, N = geom(i)
                        vb = 4 + i % 2
                        sp_ = spb[i % 3]
                        wl_ = wlb[i % 3]
                        par = h % 2
                        Cst = CE.ap[:, par, 0, :]
                        Est = CE.ap[:, par, 1, :]
                        cek = [("CE", par)]
                        nsb = 4 - sb0
                        v_r = VC.ap[:, kb, h * 64:(h + 1) * 64]

                        def mm3(e):
                            last = None
                            for ii in range(nsb):
                                sbq = sb0 + ii
                                cs = slice(ii * 128, (ii + 1) * 128)
                                e.matmul(out=PB[vb][:, sbq * 80:sbq * 80 + 64], lhsT=wl_.ap[:, cs], rhs=v_r, start=True, stop=True)
                                last = e.matmul(out=PB[vb][:, sbq * 80 + 64:sbq * 80 + 65], lhsT=sp_.ap[:, cs], rhs=ones_b[:, 0:1], start=True, stop=True)
                            return last

                        P.op("pe", mm3, reads=wl_.keys() + sp_.keys() + [("VCc", kb // 4)] + CK, writes=PK[vb])
                        cont0 = sb0 + 1 if diag else 0
                        if cont0 < 4:
                            if E_ON_POOL:
                                P.op("pool", lambda e: e.tensor_tensor(out=Est[:, cont0:4], in0=EINV.ap[:, cont0:4], in1=Cst[:, cont0:4], op=ALU.pow),
                                     reads=cek + EINV.keys(), writes=cek)
                            else:
                                act(Est[:, cont0:4], Cst[:, cont0:4], AF.Exp, cek, cek, scale=-1.0)
                        for sbq in range(sb0, 4):
                            pv = PB[vb][:, sbq * 80:sbq * 80 + 64]
                            dst = acc.ap[:, sbq, h * 64:(h + 1) * 64]
                            if diag and sbq == sb0:
                                cp(dst, pv, PK[vb], acc.keys(sbq))
                            else:
                                dve_stt(dst, pv, Est[:, sbq:sbq + 1], dst, ALU.mult, ALU.add, PK[vb] + cek + acc.keys(sbq), acc.keys(sbq))
                        pcs = PB[vb][:, 0:320].rearrange("p (s c) -> p s c", c=80)[:, :, 64]
                        if diag:
                            cp(Cst[:, sb0:sb0 + 1], pcs[:, sb0:sb0 + 1], PK[vb], cek)
                            if sb0 + 1 < 4:
                                dve_tt(Cst[:, sb0 + 1:4], Cst[:, sb0 + 1:4], pcs[:, sb0 + 1:4], ALU.add, PK[vb] + cek, cek)
                        else:
                            dve_tt(Cst, Cst, pcs, ALU.add, PK[vb] + cek, cek)
                        for _ in range(rec_per):
                            if rec_done[0] < 64:
                                rec_step(rec_done[0])
                                rec_done[0] += 1

                    for i in range(nst + 2):
                        if i < nst:
                            stageA(i)
                        if 0 <= i - 1 < nst:
                            stageB(i - 1)
                        if 0 <= i - 2 < nst:
                            stageC(i - 2)
                    while rec_done[0] < 64:
                        rec_step(rec_done[0])
                        rec_done[0] += 1
                    cp(Xp.ap[:, :, :, 1:64], SX.ap[:, :, :, 0:63], sxk, Xp.keys())
                    cp(Xc.ap, SX.ap[:, :, :, 63], sxk, Xc.keys())
                    dbg_dump(acc.ap, acc.keys(), 0, ti) if dbg == "att" else None
                    for ft in range(4):
                        b = banks(1)[0]

                        def tr(e, ft=ft, b=b):
                            last = None
                            for sbq in range(4):
                                last = e.transpose(out=PB[b][:, sbq * 128:(sbq + 1) * 128], in_=acc.ap[:, sbq, ft * 128:(ft + 1) * 128], identity=ident.ap)
                            return last

                        P.op("pe", tr, reads=acc.keys() + CK, writes=PK[b])
                        act(mixT.ap[:, 4 + ft, :], PB[b][:, :], AF.Copy, PK[b], mixT.keys(4 + ft))

                    P.stage = "t%d_s5out" % ti
                    ybanks = banks(4)
                    for ct in range(4):
                        bnk = PB[ybanks[ct]]

                        def mmY(e, ct=ct, bnk=bnk):
                            last = None
                            for ip in range(8):
                                for j in range(ip + 1):
                                    e.matmul(out=bnk[:, ip:512:8], lhsT=KD.ap[:, ct, ip - j, :], rhs=uTb.ap[:, ct, j:512:8],
                                             start=(ip == 0 and j == 0), stop=False, skip_group_check=True)
                            for ip in range(8):
                                for pp in range(4):
                                    q = 4 * ct + pp
                                    for ri in range(2):
                                        last = e.matmul(out=bnk[pp * 32:(pp + 1) * 32, ip:512:8], lhsT=YW.ap[:, q, ip, ri, :], rhs=Xp.ap[:, ri, q, :],
                                                        start=False, stop=(ri == 1 and ip == 7 and pp == 3), tile_position=(0, pp * 32), skip_group_check=True)
                            return last

                        P.op("pe", mmY, reads=uTb.keys(ct) + KD.keys() + YW.keys() + Xp.keys(), writes=PK[ybanks[ct]])
                        dve_stt(y32.ap[:, ct, :], uTb.ap[:, ct, :], vcol("s5d", ct), bnk[:, :], ALU.mult, ALU.add,
                                PK[ybanks[ct]] + uTb.keys(ct) + CK, y32.keys(ct))
                        act(y32.ap[:, ct, :], y32.ap[:, ct, :], AF.Gelu, y32.keys(ct), y32.keys(ct))
                        cp(gb.ap[:, ct, :], y32.ap[:, ct, :], y32.keys(ct), gb.keys(ct))
                    dbg_dump(y32.ap, y32.keys(), 0, ti) if dbg == "s5" else None

                    def ev_glu(m, b):
                        t_ = tmpf[m % 2]
                        act(t_.ap, PB[b][:, :], AF.Sigmoid, PK[b], t_.keys())
                        dve_tt(mixT.ap[:, m, :], y32.ap[:, m, :], t_.ap, ALU.mult, y32.keys(m) + t_.keys(), mixT.keys(m))

                    gemm_fm("w_glu", 0, 2, 0, lambda kt: gb.ap[:, kt, :], lambda kt: gb.keys(kt), ev_glu)

                    def ev_res(m, b):
                        dve_tt(HT.ap[:, m, :], HT.ap[:, m, :], PB[b][:, :], ALU.add, PK[b] + HT.keys(m), HT.keys(m))

                    gemm_fm("w_out", 0, 4, 0, lambda kt: mixT.ap[:, kt, :], lambda kt: mixT.keys(kt), ev_res)
                    if dbg == "mix0":
                        dbg_dump(HT.ap, HT.keys(), NT - 1, ti)
                else:
                    P.stage = "t%d_pool" % ti
                    prefetch([("pool", g_, 0) for g_ in range(3)])
                    P.dma("pool", "bands", lambda e: e.dma_start(out=bandsb.ap.rearrange("p a b -> p (a b)"), in_=bands_d[:, :]), writes=bandsb.keys())
                    if ti > 0:
                        P.dma("sp", "carry", lambda e: e.dma_start(out=carryb.ap, in_=carry_d[:, :]), reads=[("carryd", 0)], writes=carryb.keys())
                    bss = banks(1)[0]
                    for dt in range(8):
                        act(sq.ap[:, dt, :], HT.ap[:, dt, :], AF.Square, HT.keys(dt), sq.keys(dt))

                    def mmss(e, bss=bss):
                        last = None
                        for blk in range(4):
                            for dt in range(8):
                                last = e.matmul(out=PB[bss][:, blk:blk + 1], lhsT=sq.ap[:, dt, blk * 128:(blk + 1) * 128], rhs=ones_b[:, 0:1],
                                                start=(dt == 0), stop=(dt == 7))
                        return last

                    P.op("pe", mmss, reads=sq.keys() + CK, writes=PK[bss])
                    act(lnc.ap, PB[bss][:, 0:4], AF.Ln, PK[bss], lnc.keys(), scale=1.0 / 1024.0, bias=EPS)
                    act(rstdc.ap, lnc.ap, AF.Exp, lnc.keys(), rstdc.keys(), scale=-0.5)
                    for blk in range(4):
                        bs = banks(2)
                        for half in range(2):
                            b = bs[half]

                            def trh(e, half=half, b=b, blk=blk):
                                last = None
                                for j in range(4):
                                    dt = half * 4 + j
                                    last = e.transpose(out=PB[b][:, j * 128:(j + 1) * 128], in_=HT.ap[:, dt, blk * 128:(blk + 1) * 128], identity=ident.ap)
                                return last

                            P.op("pe", trh, reads=HT.keys() + CK, writes=PK[b])
                            dst = hTok.ap[:, blk, half * 512:(half + 1) * 512]
                            if half == 0:
                                act(dst, PB[b][:, :], AF.Copy, PK[b] + rstdc.keys(), hTok.keys(blk), scale=rstdc.ap[:, blk:blk + 1])
                            else:
                                dve_ts(dst, PB[b][:, :], rstdc.ap[:, blk:blk + 1], ALU.mult, PK[b] + rstdc.keys(), hTok.keys(blk))
                    for dt in range(8):
                        g = dt // 2
                        b = banks(1)[0]

                        def mmb(e, dt=dt, g=g, b=b, ti=ti):
                            last = None
                            cs = slice(dt * 128, (dt + 1) * 128)
                            for blk in range(4):
                                out = PB[b][:, blk * 128:(blk + 1) * 128]
                                first_seq = (ti == 0 and blk == 0)
                                has_spill = not first_seq
                                if first_seq:
                                    e.matmul(out=out, lhsT=hTok.ap[:, 0, cs], rhs=bandsb.ap[:, 4 * g + 2, :], start=True, stop=False)
                                    last = e.matmul(out=out, lhsT=hTok.ap[:, 0, cs], rhs=bandsb.ap[:, 4 * g + 3, :], start=False, stop=True)
                                else:
                                    e.matmul(out=out, lhsT=hTok.ap[:, blk, cs], rhs=bandsb.ap[:, 4 * g + 0, :], start=True, stop=False)
                                    prev = carryb.ap[:, cs] if blk == 0 else hTok.ap[:, blk - 1, cs]
                                    last = e.matmul(out=out, lhsT=prev, rhs=bandsb.ap[:, 4 * g + 1, :], start=False, stop=True)
                            return last

                        P.op("pe", mmb, reads=hTok.keys() + bandsb.keys() + carryb.keys(), writes=PK[b])
                        if dt % 2 == 0:
                            act(ypool8.ap[:, dt, :], PB[b][:, :], AF.Copy, PK[b] + CK, ypool8.keys(dt), scale=vcol("ln_mix1", dt))
                        else:
                            dve_ts(ypool8.ap[:, dt, :], PB[b][:, :], vcol("ln_mix1", dt), ALU.mult, PK[b] + CK, ypool8.keys(dt))
                    P.dma("sp", "carryw", lambda e: e.dma_start(out=carry_d[:, :], in_=hTok.ap[:, 3, :]), reads=hTok.keys(3), writes=[("carryd", 0)])
                    for g in range(4):
                        slot = PRE.pop(("pool", g, 0), None) or load_w("pool", g, 0)
                        bs = banks(2)
                        for m in range(2):
                            b = bs[m]

                            def mm(e, m=m, b=b, slot=slot, g=g):
                                e.matmul(out=PB[b][:, :], lhsT=slot.ap[:, 0, m * 128:(m + 1) * 128], rhs=ypool8.ap[:, 2 * g, :], start=True, stop=False)
                                return e.matmul(out=PB[b][:, :], lhsT=slot.ap[:, 1, m * 128:(m + 1) * 128], rhs=ypool8.ap[:, 2 * g + 1, :], start=False, stop=True)

                            P.op("pe", mm, reads=slot.keys() + ypool8.keys(2 * g) + ypool8.keys(2 * g + 1), writes=PK[b])
                            dt = 2 * g + m
                            dve_stt(HT.ap[:, dt, :], PB[b][:, :], vcol("pscale", dt), HT.ap[:, dt, :], ALU.mult, ALU.add,
                                    PK[b] + HT.keys(dt) + CK, HT.keys(dt))

                if dbg == "pool" and layer == 1:
                    dbg_dump(HT.ap, HT.keys(), 0, ti)
                P.stage = "t%d_mlp%d" % (ti, layer)
                prefetch([("w_up%d" % layer, c_, 0) for c_ in range(3)])
                rmsnorm("ln_mlp%d" % layer, lambda dt: hn.ap[:, dt, :], lambda dt: hn.keys(dt))
                def up_gen(qq):
                    aT = aTs[qq % 2]

                    def ev_up(m, b, aT=aT):
                        r_ = r32[m % 4]
                        act(r_.ap, PB[b][:, :], AF.Relu, PK[b], r_.keys())
                        dve_tt(aT.ap[:, m, :], r_.ap, r_.ap, ALU.mult, r_.keys(), aT.keys(m))

                    return gemm_gen("w_up%d" % layer, qq * 4, 4, 0, lambda kt: hn.ap[:, kt, :], lambda kt: hn.keys(kt), ev_up)

                def dn_gen(qq):
                    aT = aTs[qq % 2]

                    def ev_res(m, b):
                        dve_tt(HT.ap[:, m, :], HT.ap[:, m, :], PB[b][:, :], ALU.add, PK[b] + HT.keys(m), HT.keys(m))

                    return gemm_gen("w_dn%d" % layer, 0, 4, qq, lambda kt, aT=aT: aT.ap[:, kt, :], lambda kt, aT=aT: aT.keys(kt), ev_res)

                if MLP_PIPE:
                    for _ in up_gen(0):
                        pass
                    for qq in range(4):
                        if qq < 3:
                            interleave(up_gen(qq + 1), dn_gen(qq), lead=1)
                        else:
                            for _ in dn_gen(qq):
                                pass
                else:
                    for qq in range(4):
                        for _ in up_gen(qq):
                            pass
                        for _ in dn_gen(qq):
                            pass

                if dbg == "mlp0" and layer == 0:
                    dbg_dump(HT.ap, HT.keys(), NT - 1, ti)
                P.stage = "t%d_ple%d" % (ti, layer)
                prefetch([("w_gt%d" % layer, c_, 0) for c_ in range(3)])
                rmsnorm("ln_ple%d" % layer, lambda dt: hn.ap[:, dt, :], lambda dt: hn.keys(dt))
                for blk in range(4):
                    sb_ = pstg[blk % 2]
                    r0 = t0 + blk * 128
                    P.dma("sp", "stg%d" % (blk % 2), lambda e, sb_=sb_, r0=r0, layer=layer: e.dma_start(out=sb_.ap, in_=p_d[layer, r0:r0 + 128, :]), writes=sb_.keys())
                    b = banks(1)[0]

                    def trp(e, sb_=sb_, b=b):
                        e.transpose(out=PB[b][:, 0:128], in_=sb_.ap[:, 0:128], identity=ident.ap)
                        return e.transpose(out=PB[b][:, 128:256], in_=sb_.ap[:, 128:256], identity=ident.ap)

                    P.op("pe", trp, reads=sb_.keys() + CK, writes=PK[b])
                    act(pT.ap[:, :, blk * 128:(blk + 1) * 128], PB[b][:, 0:256].rearrange("p (k t) -> p k t", k=2), AF.Copy, PK[b], pT.keys())
                sigs = [R["AT"].buf(0, [4, 512], F32), R["UB"].buf(0, [4, 512], F32)]

                def gate_gen(half):
                    sg = sigs[half]

                    def ev_gate(m, b, sg=sg):
                        act(sg.ap[:, m, :], PB[b][:, :], AF.Sigmoid, PK[b], sg.keys(m))

                    return gemm_gen("w_gt%d" % layer, half * 2, 2, 0, lambda kt: hn.ap[:, kt, :], lambda kt: hn.keys(kt), ev_gate)

                def pu_gen(half):
                    sg = sigs[half]

                    def ev_pu(m, b, half=half, sg=sg):
                        t_ = tmpf[2 + m % 2]
                        dve_tt(t_.ap, PB[b][:, :], sg.ap[:, m, :], ALU.mult, PK[b] + sg.keys(m), t_.keys())
                        dt = half * 4 + m
                        dve_tt(HT.ap[:, dt, :], HT.ap[:, dt, :], t_.ap, ALU.add, t_.keys() + HT.keys(dt), HT.keys(dt))

                    return gemm_gen("w_pu%d" % layer, half * 2, 2, 0, lambda kt: pT.ap[:, kt, :], lambda kt: pT.keys(), ev_pu)

                for _ in gate_gen(0):
                    pass
                interleave(gate_gen(1), pu_gen(0), lead=1)
                for _ in pu_gen(1):
                    pass
                if dbg == "l0" and layer == 0:
                    dbg_dump(HT.ap, HT.keys(), NT - 1, ti)

            P.stage = "t%d_out" % ti
            for blk in range(4):
                sb_ = STGb[blk % 2]
                r0 = t0 + blk * 128
                for half in range(2):
                    b = banks(1)[0]

                    def tro(e, half=half, b=b, blk=blk):
                        last = None
                        for j in range(4):
                            dt = half * 4 + j
                            last = e.transpose(out=PB[b][:, j * 128:(j + 1) * 128], in_=HT.ap[:, dt, blk * 128:(blk + 1) * 128], identity=ident.ap)
                        return last

                    P.op("pe", tro, reads=HT.keys() + CK, writes=PK[b])
                    if half == 0:
                        act(sb_.ap[:, 0:512], PB[b][:, :], AF.Copy, PK[b], sb_.keys())
                    else:
                        cp(sb_.ap[:, 512:1024], PB[b][:, :], PK[b], sb_.keys())
                sg = P.dma("sp", "stg%d" % (blk % 2), lambda e, sb_=sb_, r0=r0: e.dma_start(out=y_d[r0:r0 + 128, :], in_=sb_.ap), reads=sb_.keys())
                final_sigs.append(sg)

        fs = {}
        for s, v in final_sigs:
            fs[s] = max(fs.get(s, 0), v)
        if dbg_d is not None and "d_dbg" in P.cnt:
            fs["d_dbg"] = P.cnt["d_dbg"]
        P.finish("sp", list(fs.items()))
        P.emit()
    return nc


def _prep_shared(inp):
    f = lambda a: np.ascontiguousarray(np.asarray(a, dtype=np.float32))
    sh = {}
    sh["w_in"] = f(inp["w_in_even"][0])
    sh["w_out"] = f(inp["w_out_even"][0])
    sh["w_glu"] = f(inp["s5_w_glu"][0])
    sh["w_up"] = f(inp["w_mlp_up"])
    sh["w_dn"] = f(inp["w_mlp_down"])
    sh["w_gt"] = f(inp["w_ple_gate"])
    sh["w_pu"] = f(inp["w_ple_up"])
    sh["w_pool"] = f(inp["pool_w"][0])
    col8 = lambda v: np.asarray(v, np.float32).reshape(-1, 128).T
    qg = np.asarray(inp["sb_q_gain"][0], np.float32)
    kg = np.asarray(inp["sb_k_gain"][0], np.float32)
    qg128 = np.concatenate([qg, qg])
    kg128 = np.concatenate([kg, kg])
    half = (np.arange(128) < 64)
    scale = np.float32(64 ** -0.5)
    vec = np.concatenate([
        col8(inp["ln_mix_even"][0]), col8(inp["ln_mlp"][0]), col8(inp["ln_ple"][0]),
        col8(inp["ln_mix_odd"][0]), col8(inp["ln_mlp"][1]), col8(inp["ln_ple"][1]),
        col8(inp["pool_scale"][0]), col8(inp["s5_d"][0]),
        np.where(half, qg128, 0)[:, None], np.where(~half, qg128, 0)[:, None], kg128[:, None]], axis=1)
    sh["vecs"] = f(vec)
    lamr = np.asarray(inp["s5_lambda_re"][0], np.float32)
    lami = np.asarray(inp["s5_lambda_im"][0], np.float32)
    logdt = np.asarray(inp["s5_log_dt"][0], np.float32)
    toB = lambda a: a.reshape(16, 2, 64).transpose(1, 2, 0).reshape(128, 16)
    logdtB = toB(np.tile(logdt[:, None], (1, 64)))
    toB3 = lambda a: a.reshape(16, 2, 64, 16).transpose(1, 2, 0, 3).reshape(128, 256)
    bre = np.asarray(inp["s5_b_re"][0], np.float32)
    bim = np.asarray(inp["s5_b_im"][0], np.float32)
    cre = np.asarray(inp["s5_c_re"][0], np.float32).transpose(0, 2, 1)
    cim = np.asarray(inp["s5_c_im"][0], np.float32).transpose(0, 2, 1)
    sh["s5B"] = f(np.concatenate([toB(lamr), toB(lami), logdtB, toB3(bre), toB3(bim), toB3(cre), toB3(cim)], axis=1))
    toA = lambda a: np.tile(a.reshape(4, 8, 1, 64), (1, 1, 16, 1)).transpose(1, 2, 0, 3).reshape(128, 256)
    toA3 = lambda a: a.reshape(4, 8, 64, 16).transpose(1, 3, 0, 2).reshape(128, 256)
    sh["s5A"] = f(np.concatenate([toA(lamr), toA(lami), toA(np.tile(logdt[:, None], (1, 64))), toA3(bre), toA3(bim)], axis=1))
    sh.update(_consts())
    sh["_scale"] = scale
    return sh


_NC_CACHE = {}


def kernel(**inputs):
    x = np.asarray(inputs["x"], np.float32)
    p = np.asarray(inputs["p"], np.float32)
    B, L, Dm = x.shape
    NT = L // TT
    sh = _prep_shared(inputs)
    sh.pop("_scale")
    if NT not in _NC_CACHE:
        _NC_CACHE[NT] = build(NT)
    nc = _NC_CACHE[NT]
    in_maps = []
    for b in range(B):
        m = dict(sh)
        m["x"] = np.ascontiguousarray(x[b])
        m["p"] = np.ascontiguousarray(p[:, b])
        in_maps.append(m)
    res = run_bass_kernel_spmd(nc, in_maps, core_ids=list(range(B)))
    out = np.stack([res.results[b]["y"] for b in range(B)], axis=0)
    return out.astype(np.float32)
```
